# Optimizing a Trainium2 kernel written in Bass

```python
import math
import jax
import jax.numpy as jnp
from jax import lax
import numpy as np

D_MODEL = 1024
BATCH = 8
SEQ = 2048
DEPTH = 2
DEC_BATCH = 128
DEC_SEQ = 4
PAST_LEN = 16384
PAGE_SIZE = 128

HEAD_DIM = 64
N_MIXERS = 4
HEADS = D_MODEL // (N_MIXERS * HEAD_DIM)
RET_DK = HEAD_DIM
RET_DV = HEAD_DIM
GLA_DK = HEAD_DIM // 2
GLA_DV = HEAD_DIM
GLA_GATE_RANK = 16
GLA_GATE_NORM = 16.0
HG_DK = HEAD_DIM
HG_DV = HEAD_DIM
GDN_DK = HEAD_DIM
GDN_DV = HEAD_DIM
CONV_W = 4
GDN_CONV_CH = HEADS * (2 * GDN_DK + GDN_DV)
MIX_WIDTH = HEADS * (RET_DV + GLA_DV + HG_DV + GDN_DV)
D_FF = ((8 * D_MODEL // 3 + 127) // 128) * 128
CHUNK = 64
ROPE_BASE = 10000.0
LN_EPS = 1e-5
RMS_EPS = 1e-6
ALPHA = (2.0 * DEPTH) ** 0.25
DEEPNORM_BETA = (8.0 * DEPTH) ** -0.25
N_MOD = 9
MIX_COLS = (
    HEADS * RET_DK, HEADS * RET_DK, HEADS * RET_DV, HEADS * RET_DV,
    HEADS * GLA_DK, HEADS * GLA_DK, HEADS * GLA_DV, GLA_GATE_RANK, HEADS * GLA_DV,
    HEADS * HG_DK, HEADS * HG_DK, HEADS * HG_DV, HEADS * HG_DV,
    GDN_CONV_CH, HEADS, HEADS, HEADS * GDN_DV,
)
IN_COLS = sum(MIX_COLS)

kernel_name = 'hybrid_ret_gla_hgrn2_gdn_decode_step'


def _layer_norm(x, g, b):
    xf = x.astype(jnp.float32)
    mu = jnp.mean(xf, axis=-1, keepdims=True)
    var = jnp.mean(jnp.square(xf - mu), axis=-1, keepdims=True)
    y = (xf - mu) * lax.rsqrt(var + LN_EPS)
    return (y * g.astype(jnp.float32) + b.astype(jnp.float32)).astype(x.dtype)


def _rms(x, g=None):
    xf = x.astype(jnp.float32)
    y = xf * lax.rsqrt(jnp.mean(jnp.square(xf), axis=-1, keepdims=True) + RMS_EPS)
    return y if g is None else y * g.astype(jnp.float32)


def _l2norm(x):
    xf = x.astype(jnp.float32)
    return xf * lax.rsqrt(jnp.sum(jnp.square(xf), axis=-1, keepdims=True) + RMS_EPS)


def _rotary(x, pos):
    half = x.shape[-1] // 2
    inv = ROPE_BASE ** (-jnp.arange(half, dtype=jnp.float32) / half)
    ang = pos[:, None] * inv[None, :]
    cos = jnp.cos(ang)[None, :, None, :]
    sin = jnp.sin(ang)[None, :, None, :]
    x1, x2 = x[..., :half], x[..., half:]
    return jnp.concatenate([x1 * cos - x2 * sin, x1 * sin + x2 * cos], axis=-1)


def _swiglu(h, wi, wo):
    a, b = jnp.split(h @ wi, 2, axis=-1)
    return (jax.nn.silu(a) * b) @ wo


def _split_cols(t):
    out, start = [], 0
    for w in MIX_COLS:
        out.append(t[..., start:start + w])
        start += w
    return out


def _chunking(T):
    c = min(CHUNK, T)
    return c, -(-T // c) * c


def _prep(t, tp, c):
    t = jnp.swapaxes(t, 1, 2).astype(jnp.float32)
    b, h, T, d = t.shape
    t = jnp.pad(t, ((0, 0), (0, 0), (0, tp - T), (0, 0)))
    return jnp.moveaxis(t.reshape(b, h, tp // c, c, d), 2, 0)


def _unchunk(o, T):
    nc, b, h, c, d = o.shape
    o = jnp.moveaxis(o, 0, 2).reshape(b, h, nc * c, d)[:, :, :T]
    return jnp.swapaxes(o, 1, 2)


def _gla_scan(q, k, v, g, s0):
    T = q.shape[1]
    c, tp = _chunking(T)
    xs = tuple(_prep(t, tp, c) for t in (q, k, v, g))
    causal = jnp.tril(jnp.ones((c, c), dtype=bool))[:, :, None]

    def step(S, inp):
        qc, kc, vc, gc = inp
        G = jnp.cumsum(gc, axis=2)
        diff = G[:, :, :, None, :] - G[:, :, None, :, :]
        decay = jnp.where(causal, jnp.exp(jnp.where(causal, diff, 0.0)), 0.0)
        att = jnp.einsum('bhtd,bhsd,bhtsd->bhts', qc, kc, decay)
        o = att @ vc + (qc * jnp.exp(G)) @ S
        gl = G[:, :, -1:, :]
        S = jnp.exp(gl[:, :, 0, :, None]) * S + jnp.einsum('bhsd,bhsv->bhdv', kc * jnp.exp(gl - G), vc)
        return S, o

    s, o = lax.scan(step, s0.astype(jnp.float32), xs)
    return _unchunk(o, T), s.astype(s0.dtype)


def _gdn_scan(q, k, v, beta, g, s0):
    T = q.shape[1]
    c, tp = _chunking(T)
    xs = tuple(_prep(t, tp, c) for t in (q, k, v, beta[..., None], g[..., None]))
    incl = jnp.tril(jnp.ones((c, c), dtype=bool))
    strict = jnp.tril(jnp.ones((c, c), dtype=bool), -1)
    eye = jnp.eye(c, dtype=jnp.float32)

    def step(S, inp):
        qc, kc, vc, bc, gc = inp
        G = jnp.cumsum(gc, axis=2)
        diff = G - jnp.swapaxes(G, -1, -2)
        L = jnp.where(incl, jnp.exp(jnp.where(incl, diff, 0.0)), 0.0)
        a = jnp.where(strict, bc * (kc @ jnp.swapaxes(kc, -1, -2)) * L, 0.0)
        tm = eye + a
        u = lax.linalg.triangular_solve(tm, bc * vc, left_side=True, lower=True, unit_diagonal=True)
        w = lax.linalg.triangular_solve(tm, bc * jnp.exp(G) * kc, left_side=True, lower=True, unit_diagonal=True)
        delta = u - w @ S
        o = ((qc @ jnp.swapaxes(kc, -1, -2)) * L) @ delta + (qc * jnp.exp(G)) @ S
        gl = G[:, :, -1:, :]
        S = jnp.exp(gl) * S + jnp.swapaxes(kc * jnp.exp(gl - G), -1, -2) @ delta
        return S, o

    s, o = lax.scan(step, s0.astype(jnp.float32), xs)
    return _unchunk(o, T), s.astype(s0.dtype)


def _causal_conv(u, buf, w):
    T = u.shape[1]
    up = jnp.concatenate([buf.astype(u.dtype), u], axis=1)
    out = sum(up[:, i:i + T] * w[i] for i in range(CONV_W))
    return out, up[:, -(CONV_W - 1):].astype(buf.dtype)


def _token_mixing(h, pos, lb_l, st, l, p):
    B, T, _ = h.shape
    f32 = jnp.float32
    ret_s, gla_s, hg_s, gdn_s, conv_s = st
    (rq, rk, rv, rg, aq, ak, av, alr, ag, hq, hf, hi, hg, dqkv, db, da, dg) = _split_cols(h @ p['w_in'][l])
    heads = lambda t: t.reshape(B, T, HEADS, -1)

    q = _rotary(heads(rq).astype(f32), pos)
    k = _rotary(heads(rk).astype(f32), pos) * RET_DK ** -0.5
    ret_decay = jnp.log(1.0 - 2.0 ** (-5.0 - jnp.arange(HEADS, dtype=f32)))
    g = jnp.broadcast_to(ret_decay[:, None], (B, T, HEADS, RET_DK))
    o_ret, ret_s = _gla_scan(q, k, heads(rv), g, ret_s)
    o_ret = _rms(o_ret) * jax.nn.silu(heads(rg).astype(f32))

    gk = jax.nn.log_sigmoid((alr @ p['gla_wg'][l] + p['gla_bg'][l]).astype(f32)) / GLA_GATE_NORM
    o_gla, gla_s = _gla_scan(heads(aq).astype(f32) * GLA_DK ** -0.5, heads(ak), heads(av), heads(gk), gla_s)
    o_gla = _rms(o_gla, p['gla_norm'][l]) * jax.nn.silu(heads(ag).astype(f32))

    zf = heads(hf.astype(f32))
    lbh = lb_l.reshape(HEADS, HG_DK)
    log_f = jnp.log(lbh + (1.0 - lbh) * jax.nn.sigmoid(zf))
    k_h = (1.0 - lbh) * jax.nn.sigmoid(-zf)
    q_h = jax.nn.silu(heads(hq).astype(f32)) * HG_DK ** -0.5
    o_hg, hg_s = _gla_scan(q_h, k_h, heads(hi), log_f, hg_s)
    o_hg = _rms(o_hg, p['hg_norm'][l]) * jax.nn.silu(heads(hg).astype(f32))

    u, conv_s = _causal_conv(dqkv, conv_s, p['gdn_conv'][l])
    u = jax.nn.silu(u)
    uq, uk, uv = jnp.split(u, [HEADS * GDN_DK, 2 * HEADS * GDN_DK], axis=-1)
    q_d = _l2norm(heads(uq)) * GDN_DK ** -0.5
    k_d = _l2norm(heads(uk))
    beta = jax.nn.sigmoid(db.astype(f32))
    g_d = -jnp.exp(p['gdn_a_log'][l].astype(f32)) * jax.nn.softplus(da.astype(f32) + p['gdn_dt_bias'][l].astype(f32))
    o_gdn, gdn_s = _gdn_scan(q_d, k_d, heads(uv), beta, g_d, gdn_s)
    o_gdn = _rms(o_gdn, p['gdn_norm'][l]) * jax.nn.silu(heads(dg).astype(f32))

    o = jnp.concatenate([t.reshape(B, T, -1) for t in (o_ret, o_gla, o_hg, o_gdn)], axis=-1).astype(h.dtype)
    return o @ p['w_out'][l], (ret_s, gla_s, hg_s, gdn_s, conv_s)


def _layer(x, c, pos, l, st, p, lb):
    mod = jax.nn.silu(c) @ p['ada_w'][l] + p['ada_b'][l]
    sh1, sc1, gt1, sh2, sc2, gt2, sh3, sc3, gt3 = jnp.split(mod[:, None, :], N_MOD, axis=-1)
    h = x * (1.0 + sc1) + sh1
    x = _layer_norm(ALPHA * x + 0.5 * (1.0 + gt1) * _swiglu(h, p['ffn1_wi'][l], p['ffn1_wo'][l]),
                    p['ln_g'][l, 0], p['ln_b'][l, 0])
    h = x * (1.0 + sc2) + sh2
    m, new_st = _token_mixing(h, pos, lb[l], st, l, p)
    x = _layer_norm(ALPHA * x + (1.0 + gt2) * m, p['ln_g'][l, 1], p['ln_b'][l, 1])
    h = x * (1.0 + sc3) + sh3
    x = _layer_norm(ALPHA * x + 0.5 * (1.0 + gt3) * _swiglu(h, p['ffn2_wi'][l], p['ffn2_wo'][l]),
                    p['ln_g'][l, 2], p['ln_b'][l, 2])
    return x, new_st


def _trunk(x, c, pos, states, p, lb):
    new = []
    for l in range(DEPTH):
        x, st = _layer(x, c, pos, l, tuple(s[l] for s in states), p, lb)
        new.append(st)
    return x, tuple(jnp.stack([n[i] for n in new]) for i in range(len(states)))


def setup_inputs(seed: int = 0) -> dict:
    key = jax.random.key(seed)
    ks = jax.random.split(key, 32)
    f32 = jnp.float32
    d = D_MODEL

    def nrm(k, shape, s):
        return jax.random.normal(k, shape, f32) * s

    u_a = jax.random.uniform(ks[26], (DEPTH, HEADS), f32, 1.0, 16.0)
    u_dt = jax.random.uniform(ks[27], (DEPTH, HEADS), f32)
    dt = jnp.exp(u_dt * (math.log(0.1) - math.log(0.001)) + math.log(0.001))
    return {
        'x_prompt': nrm(ks[0], (BATCH, SEQ, d), 1.0),
        'x_sample': nrm(ks[1], (DEC_BATCH, DEC_SEQ, d), 1.0),
        'state_ret': nrm(ks[2], (DEPTH, DEC_BATCH, HEADS, RET_DK, RET_DV), 0.5),
        'state_gla': nrm(ks[3], (DEPTH, DEC_BATCH, HEADS, GLA_DK, GLA_DV), 0.5),
        'state_hgrn': nrm(ks[4], (DEPTH, DEC_BATCH, HEADS, HG_DK, HG_DV), 0.5),
        'state_gdn': nrm(ks[5], (DEPTH, DEC_BATCH, HEADS, GDN_DK, GDN_DV), 0.5),
        'state_gdn_conv': nrm(ks[6], (DEPTH, DEC_BATCH, CONV_W - 1, GDN_CONV_CH), 1.0),
        'c_prompt': nrm(ks[7], (BATCH, d), 1.0),
        'c_sample': nrm(ks[8], (DEC_BATCH, d), 1.0),
        'ada_w': nrm(ks[9], (DEPTH, d, N_MOD * d), 0.1 * d ** -0.5),
        'ada_b': nrm(ks[10], (DEPTH, N_MOD * d), 0.02),
        'ln_g': 1.0 + nrm(ks[11], (DEPTH, 3, d), 0.02),
        'ln_b': nrm(ks[12], (DEPTH, 3, d), 0.02),
        'ffn1_wi': nrm(ks[13], (DEPTH, d, 2 * D_FF), d ** -0.5),
        'ffn1_wo': nrm(ks[14], (DEPTH, D_FF, d), D_FF ** -0.5 * DEEPNORM_BETA),
        'ffn2_wi': nrm(ks[15], (DEPTH, d, 2 * D_FF), d ** -0.5),
        'ffn2_wo': nrm(ks[16], (DEPTH, D_FF, d), D_FF ** -0.5 * DEEPNORM_BETA),
        'w_in': nrm(ks[17], (DEPTH, d, IN_COLS), d ** -0.5),
        'gla_wg': nrm(ks[18], (DEPTH, GLA_GATE_RANK, HEADS * GLA_DK), GLA_GATE_RANK ** -0.5),
        'gla_bg': nrm(ks[19], (DEPTH, HEADS * GLA_DK), 0.02),
        'hg_lb': nrm(ks[20], (DEPTH, HEADS * HG_DK), 0.5),
        'gdn_conv': nrm(ks[21], (DEPTH, CONV_W, GDN_CONV_CH), CONV_W ** -0.5),
        'gdn_a_log': jnp.log(u_a),
        'gdn_dt_bias': dt + jnp.log(-jnp.expm1(-dt)),
        'gla_norm': 1.0 + nrm(ks[22], (DEPTH, GLA_DV), 0.02),
        'hg_norm': 1.0 + nrm(ks[23], (DEPTH, HG_DV), 0.02),
        'gdn_norm': 1.0 + nrm(ks[24], (DEPTH, GDN_DV), 0.02),
        'w_out': nrm(ks[25], (DEPTH, MIX_WIDTH, d), MIX_WIDTH ** -0.5 * DEEPNORM_BETA),
    }


def reference(x_prompt, x_sample, state_ret, state_gla, state_hgrn, state_gdn, state_gdn_conv,
              c_prompt, c_sample, ada_w, ada_b, ln_g, ln_b, ffn1_wi, ffn1_wo, ffn2_wi, ffn2_wo,
              w_in, gla_wg, gla_bg, hg_lb, gdn_conv, gdn_a_log, gdn_dt_bias,
              gla_norm, hg_norm, gdn_norm, w_out):
    p = {'ada_w': ada_w, 'ada_b': ada_b, 'ln_g': ln_g, 'ln_b': ln_b,
         'ffn1_wi': ffn1_wi, 'ffn1_wo': ffn1_wo, 'ffn2_wi': ffn2_wi, 'ffn2_wo': ffn2_wo,
         'w_in': w_in, 'gla_wg': gla_wg, 'gla_bg': gla_bg, 'gdn_conv': gdn_conv,
         'gdn_a_log': gdn_a_log, 'gdn_dt_bias': gdn_dt_bias, 'gla_norm': gla_norm,
         'hg_norm': hg_norm, 'gdn_norm': gdn_norm, 'w_out': w_out}
    plb = jax.nn.softmax(hg_lb.astype(jnp.float32), axis=0)
    lb = jnp.cumsum(plb, axis=0) - plb[0]

    sample_states = (state_ret, state_gla, state_hgrn, state_gdn, state_gdn_conv)
    bp = x_prompt.shape[0]
    prompt_states = tuple(jnp.zeros((DEPTH, bp) + s.shape[2:], x_prompt.dtype) for s in sample_states)
    pos_p = jnp.arange(x_prompt.shape[1], dtype=jnp.float32)
    pos_s = PAST_LEN + jnp.arange(x_sample.shape[1], dtype=jnp.float32)

    y_prompt, (p_ret, p_gla, p_hg, p_gdn, p_conv) = _trunk(x_prompt, c_prompt, pos_p, prompt_states, p, lb)
    y_sample, (s_ret, s_gla, s_hg, s_gdn, s_conv) = _trunk(x_sample, c_sample, pos_s, sample_states, p, lb)
    return (y_prompt, y_sample, p_ret, p_gla, p_hg, p_gdn, p_conv, s_ret, s_gla, s_hg, s_gdn, s_conv)
```

```python
import numpy as np
from contextlib import ExitStack
import concourse.bass as bass
import concourse.mybir as mybir
from concourse.bass_utils import run_bass_kernel_spmd

F32 = mybir.dt.float32
BF16 = mybir.dt.bfloat16
AF = mybir.ActivationFunctionType
ALU = mybir.AluOpType

NCORES = 8
D = 1024
KC = 8
DFF = 2816
NJ = 22
TP = 2048
NSQ = 16
TS = 4
NTOK = TP + NSQ * TS
DEPTH = 2
ALPHA = float((2.0 * DEPTH) ** 0.25)
LN_EPS = 1e-5
RMS_EPS = 1e-6
TBS = [(0, 512), (512, 512), (1024, 512), (1536, 512), (2048, 64)]
JG = [list(range(0, 8)), list(range(8, 15)), list(range(15, 22))]
MB = 256
NCH = MB // 64
NPB = TP // MB
NSB = NSQ // NCH
NW = 12
NDS = 8
import os as _os
CUT = int(_os.environ.get('KCUT', '0'))
CUTN = int(_os.environ.get('KCUTN', '-1'))
STRICT = int(_os.environ.get('KSTRICT', '1'))
CUTG = int(_os.environ.get('KCUTG', '0'))
ATTACH = int(_os.environ.get('KATTACH', '1'))
GP = _os.environ.get('KGP', 'pool')

C_RQ, C_RK, C_RV, C_RG = 0, 256, 512, 768
C_AQ, C_AK, C_AV, C_ALR, C_AG = 1024, 1152, 1280, 1536, 1552
C_HQ, C_HF, C_HI, C_HG = 1808, 2064, 2320, 2576
C_DQ, C_DK, C_DV, C_DB, C_DA, C_DG = 2832, 3088, 3344, 3600, 3604, 3608
WT = {}
_names = (["rq0", "rq1", "rk0", "rk1", "rv0", "rv1", "rg0", "rg1"] +
          ["aq0", "aq1", "ak0", "ak1", "av0", "av1", "ag0", "ag1", "alr"] +
          ["hq0", "hq1", "hf0", "hf1", "hi0", "hi1", "hg0", "hg1"] +
          ["dq0", "dq1", "dk0", "dk1", "dv0", "dv1", "dg0", "dg1", "db", "da"])
for _i, _n in enumerate(_names):
    WT[_n] = _i
NWT = len(_names)
PV_ADAB, PV_LNG, PV_LNB, PV_CONV, PV_BG, PV_NORM, PV_LB0, PV_LB1, PV_ALOG, PV_DTB, NPV = 0, 72, 96, 120, 144, 146, 149, 151, 153, 154, 160


def _esize(dt):
    return mybir.dt.size(dt)


class _Op:
    __slots__ = ("eng", "fn", "deps", "dmaq", "signal", "sigval", "dslot", "dval", "clock", "prio")

    def __init__(self, eng, fn, deps, dmaq):
        self.eng = eng
        self.fn = fn
        self.deps = deps
        self.dmaq = dmaq
        self.signal = False
        self.sigval = 0
        self.dslot = 0
        self.dval = 0
        self.clock = None
        self.prio = ()


class Sched:
    BUCK = 1024

    def __init__(self, nc, es):
        self.nc = nc
        self.engs = {"pe": nc.tensor, "act": nc.scalar, "dve": nc.vector, "pool": nc.gpsimd, "sp": nc.sync}
        self.ops = []
        self.buckets = {}
        self.mloc = {}
        self.sem = {e: es.enter_context(nc.semaphore("s_" + e)) for e in self.engs}
        self.dsem = {q: [es.enter_context(nc.semaphore("d_%s%d" % (q, i))) for i in range(NDS)]
                     for q in ("sp", "act", "pool")}
        self.out_dmas = []

    def _box(self, ap):
        sp = str(ap.space)
        if sp not in ("SB", "PSUM"):
            return None
        t = ap.tensor
        key = t.name
        info = self.mloc.get(key)
        if info is None:
            ml = self.nc.lookup_mloc(t)
            base = int(ml.addr)
            if sp == "PSUM":
                base += int(ml.bank) * 2048
            info = base
            self.mloc[key] = info
        shape = t.shape
        F = 1
        for s in shape[1:]:
            F *= int(s)
        off = int(ap.offset)
        p0 = off // F
        f0 = off % F
        dims = ap.ap
        pc = int(dims[0][1])
        lo = f0
        hi = f0
        for (st, cnt) in dims[1:]:
            ext = int(st) * (int(cnt) - 1)
            if ext < 0:
                lo += ext
            else:
                hi += ext
        hi += 1
        es_ = _esize(ap.dtype)
        if sp == "PSUM" and STRICT:
            b0 = ((info + lo * es_) // 2048) * 2048
            b1 = ((info + hi * es_ - 1) // 2048 + 1) * 2048
            return (sp, (p0 // 32) * 32, ((p0 + pc + 31) // 32) * 32, b0, b1)
        return (sp, p0, p0 + pc, info + lo * es_, info + hi * es_)

    @staticmethod
    def _ov(a, b):
        return a[1] < b[2] and b[1] < a[2] and a[3] < b[4] and b[3] < a[4]

    @staticmethod
    def _cov(a, b):
        return a[1] <= b[1] and a[2] >= b[2] and a[3] <= b[3] and a[4] >= b[4]

    def _keys(self, box):
        return [(box[0], k) for k in range(box[3] // self.BUCK, (box[4] - 1) // self.BUCK + 1)]

    def add(self, eng, fn, reads, writes, dmaq=None, prio=()):
        idx = len(self.ops)
        raw = set()
        praw = set()
        for ap in prio:
            b = self._box(ap)
            if b is not None:
                for k in self._keys(b):
                    for rec in self.buckets.get(k, ()):
                        if rec[2] and self._ov(b, rec[0]):
                            praw.add(rec[1])
        oth = set()
        rboxes = []
        wboxes = []
        for ap in reads:
            b = self._box(ap)
            if b is not None:
                rboxes.append(b)
        for ap in writes:
            b = self._box(ap)
            if b is not None:
                wboxes.append(b)
        for b in rboxes:
            for k in self._keys(b):
                for rec in self.buckets.get(k, ()):
                    if rec[2] and self._ov(b, rec[0]):
                        raw.add(rec[1])
        for b in wboxes:
            for k in self._keys(b):
                lst = self.buckets.get(k)
                if not lst:
                    continue
                keep = []
                for rec in lst:
                    if self._ov(b, rec[0]):
                        oth.add(rec[1])
                        if self._cov(b, rec[0]):
                            continue
                    keep.append(rec)
                self.buckets[k] = keep
        myeng = eng
        for b in rboxes:
            rec = (b, idx, False, myeng, dmaq is not None)
            for k in self._keys(b):
                lst = self.buckets.setdefault(k, [])
                for i2, r2 in enumerate(lst):
                    if (not r2[2]) and r2[0] == b and r2[3] == myeng and (not r2[4]) and dmaq is None:
                        lst[i2] = rec
                        break
                else:
                    lst.append(rec)
        for b in wboxes:
            rec = (b, idx, True, myeng, dmaq is not None)
            for k in self._keys(b):
                self.buckets.setdefault(k, []).append(rec)
        deps = []
        for d in raw | oth:
            dop = self.ops[d]
            if dop.dmaq is None and dmaq is None and dop.eng == eng and not STRICT:
                if eng == "pe":
                    continue
                if d not in raw:
                    continue
            deps.append(d)
            if dop.dmaq is None:
                dop.signal = True
        op = _Op(eng, fn, deps, dmaq)
        op.prio = praw
        self.ops.append(op)
        return idx

    def emit(self):
        nc = self.nc
        cnt = {e: 0 for e in self.engs}
        for op in self.ops:
            if op.dmaq is None and op.signal:
                cnt[op.eng] += 1
                op.sigval = cnt[op.eng]
        seen = {e: {} for e in self.engs}
        self.opidx = {id(o): i for i, o in enumerate(self.ops)}
        dcount = {q: 0 for q in self.dsem}
        dlast = {q: [None] * NDS for q in self.dsem}
        nwaits = 0
        for op in self.ops:
            e = op.eng
            eng = self.engs[e]
            sn = seen[e]
            needs = []
            for d in op.deps:
                dop = self.ops[d]
                if dop.dmaq is None:
                    needs.append((("c", dop.eng), dop.sigval, dop))
                else:
                    needs.append((("d", dop.dmaq, dop.dslot), dop.dval, dop))
            if op.dmaq is not None:
                q = op.dmaq
                slot = dcount[q] % NDS
                dcount[q] += 1
                prev = dlast[q][slot]
                op.dslot = slot
                op.dval = (prev.dval if prev is not None else 0) + 16
                if prev is not None:
                    needs.append((("d", q, slot), prev.dval, prev))
                dlast[q][slot] = op
            needs.sort(key=lambda x: -self.opidx[id(x[2])])
            pending = []
            for (key, val, dop) in needs:
                if sn.get(key, 0) >= val:
                    continue
                semh = self.sem[key[1]] if key[0] == "c" else self.dsem[key[1]][key[2]]
                pending.append((semh, val, self.opidx[id(dop)] in op.prio))
                nwaits += 1
                for k2, v2 in dop.clock.items():
                    if sn.get(k2, 0) < v2:
                        sn[k2] = v2
            attach = None
            if pending and ATTACH:
                pi = [i for i, x in enumerate(pending) if x[2]]
                attach = pending.pop(pi[-1] if pi else -1)
            for (semh, val, _) in pending:
                eng.wait_ge(semh, val)
            ins = op.fn(eng)
            if attach is not None:
                ins._wait_ge(attach[0], attach[1])
            clock = dict(sn)
            if op.dmaq is not None:
                ins.then_inc(self.dsem[op.dmaq][op.dslot], 16)
                clock[("d", op.dmaq, op.dslot)] = op.dval
            elif op.signal:
                ins.then_inc(self.sem[e], 1)
                clock[("c", e)] = op.sigval
            op.clock = clock
            op.fn = None
        sp = self.engs["sp"]
        sn = seen["sp"]
        for d in self.out_dmas:
            dop = self.ops[d]
            key = ("d", dop.dmaq, dop.dslot)
            if sn.get(key, 0) >= dop.dval:
                continue
            sp.wait_ge(self.dsem[dop.dmaq][dop.dslot], dop.dval)
            sn[key] = dop.dval
        return nwaits


class KB:
    def __init__(self, nc, es):
        self.nc = nc
        self.es = es
        self.S = Sched(nc, es)
        self.dbg_outs = []

    def sb(self, name, shape, dt):
        return self.es.enter_context(self.nc.sbuf_tensor("sb_" + name, shape, dt))

    def psum(self, name, shape, dt):
        return self.es.enter_context(self.nc.psum_tensor(name, shape, dt))

    def dram_in(self, name, shape):
        return self.nc.dram_tensor(name, list(shape), F32, kind="ExternalInput").ap()

    def dram_out(self, name, shape):
        return self.nc.dram_tensor(name, list(shape), F32, kind="ExternalOutput").ap()

    def mm(self, out, lhsT, rhs, start=True, stop=True):
        self.S.add("pe", lambda e: e.matmul(out, lhsT=lhsT, rhs=rhs, start=start, stop=stop), [lhsT, rhs], [out], prio=[lhsT])

    def tr(self, out, in_, ident):
        self.S.add("pe", lambda e: e.transpose(out, in_, ident), [in_, ident], [out], prio=[in_])

    def act(self, out, in_, func, bias=None, scale=None, eng="act"):
        kw = {}
        reads = [in_]
        if bias is not None:
            kw["bias"] = bias
            if not isinstance(bias, (int, float)):
                reads.append(bias)
        if scale is not None:
            kw["scale"] = scale
            if not isinstance(scale, (int, float)):
                reads.append(scale)
        self.S.add(eng, lambda e: e.activation(out=out, in_=in_, func=func, **kw), reads, [out])

    def tt(self, out, a, b, op, eng="dve"):
        self.S.add(eng, lambda e: e.tensor_tensor(out=out, in0=a, in1=b, op=op), [a, b], [out])

    def ts(self, out, a, s1, op0, s2=None, op1=None, eng="dve"):
        reads = [a]
        if not isinstance(s1, (int, float)):
            reads.append(s1)
        if s2 is not None and not isinstance(s2, (int, float)):
            reads.append(s2)
        if s2 is None:
            self.S.add(eng, lambda e: e.tensor_scalar(out=out, in0=a, scalar1=s1, scalar2=None, op0=op0), reads, [out])
        else:
            self.S.add(eng, lambda e: e.tensor_scalar(out=out, in0=a, scalar1=s1, scalar2=s2, op0=op0, op1=op1), reads, [out])

    def stt(self, out, a, s, b, op0, op1):
        reads = [a, b]
        if not isinstance(s, (int, float)):
            reads.append(s)
        self.S.add("dve", lambda e: e.scalar_tensor_tensor(out=out, in0=a, scalar=s, in1=b, op0=op0, op1=op1), reads, [out])

    def cp(self, out, in_, eng="dve"):
        if eng == "act":
            fn_ = AF.Identity if _os.environ.get('KVAR3', '') == 'ident' else AF.Copy
            self.S.add("act", lambda e: e.activation(out=out, in_=in_, func=fn_), [in_], [out])
        else:
            self.S.add(eng, lambda e: e.tensor_copy(out=out, in_=in_), [in_], [out])

    def ms(self, ap, val, eng="dve"):
        self.S.add(eng, lambda e: e.memset(ap, val), [], [ap])

    def scan(self, out, d0, d1, init, op0, op1):
        reads = [d0, d1]
        self.S.add("dve", lambda e: e.tensor_tensor_scan(out=out, data0=d0, data1=d1, initial=init, op0=op0, op1=op1), reads, [out])

    def recip(self, out, in_):
        self.S.add("dve", lambda e: e.reciprocal(out=out, in_=in_), [in_], [out])

    def dma(self, q, out, in_, is_out=False):
        idx = self.S.add(q, lambda e: e.dma_start(out=out, in_=in_), [in_], [out], dmaq=q)
        if is_out:
            self.S.out_dmas.append(idx)
        return idx

    def dbg(self, name, ap):
        shape = [int(s) for s in ap.shape]
        o = self.dram_out("dbg_" + name, shape)
        self.dma("sp", o, ap, is_out=True)
        self.dbg_outs.append(("dbg_" + name, shape))


class Prog:
    def __init__(self, debug=None):
        self.debug = debug or set()
        self.nc = bass.Bass("TRN2", target_bir_lowering=False)
        self.es = ExitStack()
        self.k = KB(self.nc, self.es)
        self.wcnt = 0
        self.pscnt = 0

    def alloc(self):
        k = self.k
        self.d_xT = k.dram_in("xT", [D, NTOK])
        self.d_cT = k.dram_in("cT", [D, 17])
        self.d_wi = [k.dram_in("wi1", [DEPTH, 2 * NJ, 128, 1024]), k.dram_in("wi2", [DEPTH, 2 * NJ, 128, 1024])]
        self.d_wo = [k.dram_in("wo1", [DEPTH, DFF, D]), k.dram_in("wo2", [DEPTH, DFF, D])]
        self.d_win = k.dram_in("win", [DEPTH, NWT, 128, 1024])
        self.d_wout = k.dram_in("wout", [DEPTH, 8, 128, 1024])
        self.d_adaw = k.dram_in("adaw", [DEPTH, 72, 128, 1024])
        self.d_pvec = k.dram_in("pvec", [DEPTH, 128, NPV])
        self.d_wg = k.dram_in("wgpad", [DEPTH, 16, 256])
        self.d_cos = k.dram_in("cosT", [128, NTOK])
        self.d_sin = k.dram_in("sinT", [128, NTOK])
        self.d_cst = k.dram_in("cst", [128, CST_N])
        self.d_gret = k.dram_in("gret", [2, 128, 2, MB])
        self.d_st = {}
        for nm in ("ret", "gla", "hg", "gdn"):
            self.d_st[nm] = k.dram_in("st_" + nm, [DEPTH, 128, 2, NSQ, 64])
        self.d_stconv = k.dram_in("st_conv", [DEPTH, 128, 6, NSQ, 3])
        self.o_yT = k.dram_out("yT", [D, NTOK])
        self.o_ps = {}
        self.o_ss = {}
        for nm in ("ret", "gla", "hg", "gdn"):
            self.o_ps[nm] = k.dram_out("ops_" + nm, [DEPTH, 128, 2, 64])
            self.o_ss[nm] = k.dram_out("oss_" + nm, [DEPTH, 128, 2, NSQ, 64])
        self.o_pconv = k.dram_out("opconv", [DEPTH, 128, 6, 3])
        self.o_sconv = k.dram_out("osconv", [DEPTH, 128, 6, NSQ, 3])
        self.xres = k.sb("xres", [128, KC, NTOK], F32)
        self.hT = k.sb("hT", [128, KC, NTOK], BF16)
        self.gbig = k.sb("gbig", [128, 8 * NTOK], BF16)
        self.wpool = [k.sb("w%d" % i, [128, 1024], BF16) for i in range(NW)]
        self.cst = k.sb("cst", [128, CST_N], F32)
        self.cstb = k.sb("cstb", [128, CSTB_N], BF16)
        self.pvec = [k.sb("pvec%d" % l, [128, NPV], F32) for l in range(DEPTH)]
        self.cTs = k.sb("cTs", [128, KC, 17], F32)
        self.csl = k.sb("csl", [128, KC, 17], BF16)
        self.modg = k.sb("modg", [128, 24, 17], F32)
        self.cF = [[k.sb("cF%d%d" % (l, i), [128, KC, 17], F32) for i in range(3)] for l in range(DEPTH)]
        self.hs = [[k.sb("hs%d%d" % (l, i), [128, KC, 17], F32) for i in range(3)] for l in range(DEPTH)]
        self.hb = [[k.sb("hb%d%d" % (l, i), [128, KC, 17], F32) for i in range(3)] for l in range(DEPTH)]
        self.xs = [k.sb("xs%d" % l, [128, 3, KC], F32) for l in range(DEPTH)]
        self.xb = [k.sb("xb%d" % l, [128, 3, KC], F32) for l in range(DEPTH)]
        self.wg = k.sb("wg", [16, 256], F32)
        self.wgb = k.sb("wgb", [16, 256], BF16)
        self.small = k.sb("small", [128, 64], F32)
        self.arena2 = k.sb("arena2", [128, A2_BYTES // 4], F32)
        self.psb = [k.psum("ps%d" % i, [128, 512], F32) for i in range(8)]

    def carve(self, region, off, nbytes, dt, parts=128):
        if region == "g":
            base = self.gbig
            es = 2
            tot = 8 * NTOK * 2
        else:
            base = self.arena2
            es = 4
            tot = A2_BYTES
        assert off % 4 == 0 and nbytes % 4 == 0 and off + nbytes <= tot, (region, off, nbytes, tot)
        ap = base[0:parts, off // es:(off + nbytes) // es]
        if dt == F32 and es == 2:
            ap = ap.bitcast(F32)
        elif dt == BF16 and es == 4:
            ap = ap.bitcast(BF16)
        return ap

    def ps(self):
        p = self.psb[self.pscnt % 6]
        self.pscnt += 1
        return p

    def wload(self, dram_ap):
        w = self.wpool[self.wcnt % NW]
        self.wcnt += 1
        self.k.dma("pool", w[:, :], dram_ap)
        return w

    def pv(self, l, col, n=1):
        return self.pvec[l][:, col:col + n]


CST_IDENT = 0
CST_MINCL = 128
CST_MSTR = 192
CST_MSTRT = 256
CST_ID2 = 320
CST_RESET = 384
CST_SEL = 384 + MB
CST_SELF = CST_SEL + 256
CST_ONES = CST_SELF + 256
CST_BONES = CST_ONES + 128
CST_PM = CST_BONES + 128
CST_SELHP = CST_PM + 128
CST_N = CST_SELHP + 4
CB_IDENT, CB_ONES, CB_BONES, CB_PM, CSTB_N = 0, 128, 256, 384, 512
A2_BYTES = 24 * 1024


def _bc(ap, shape):
    return ap.to_broadcast(list(shape))


class Prog2(Prog):
    def setup(self):
        k = self.k
        k.dma("sp", self.cst[:, :], self.d_cst)
        for l in range(DEPTH):
            k.dma("sp", self.pvec[l][:, :], self.d_pvec[l])
        k.dma("sp", self.cTs[:, :, :], self.d_cT.rearrange("(kc p) b -> p kc b", p=128))
        k.dma("sp", self.xres[:, :, :], self.d_xT.rearrange("(kc p) t -> p kc t", p=128))
        for (src, dst) in ((CST_IDENT, CB_IDENT), (CST_ONES, CB_ONES), (CST_BONES, CB_BONES), (CST_PM, CB_PM)):
            k.cp(self.cstb[:, dst:dst + 128], self.cst[:, src:src + 128])
        k.ms(self.small[:, 8:9], 1.0)
        k.ms(self.small[:, 9:10], RMS_EPS)
        k.ms(self.small[:, 10:11], 0.0)
        self.c_one = self.small[:, 8:9]
        self.c_eps = self.small[:, 9:10]
        k.act(self.csl[:, :, :], self.cTs[:, :, :], AF.Silu)
        for l in range(DEPTH):
            last = (l == DEPTH - 1)
            for i in range(3):
                a = 1.0 if (last and i == 2) else ALPHA
                k.ts(self.xs[l][:, i, :], self.pv(l, PV_LNG + i * 8, 8), a, ALU.mult)
                k.ts(self.xb[l][:, i, :], self.pv(l, PV_LNB + i * 8, 8), a, ALU.mult)

    def ident_b(self):
        return self.cstb[:, CB_IDENT:CB_IDENT + 128]

    def mod_group(self, l, i):
        k = self.k
        for f in range(24):
            ft = i * 24 + f
            w = self.wload(self.d_adaw[l, ft])
            p = self.ps()
            for kc in range(KC):
                k.mm(p[:, 0:17], w[:, kc * 128:(kc + 1) * 128], self.csl[:, kc, :], start=(kc == 0), stop=(kc == KC - 1))
            k.ts(self.modg[:, f, :], p[:, 0:17], self.pv(l, PV_ADAB + ft), ALU.add)
        sh = self.modg[:, 0:8, :]
        sc = self.modg[:, 8:16, :]
        gt = self.modg[:, 16:24, :]
        coef = 1.0 if i == 1 else 0.5
        k.ts(self.cF[l][i][:, :, :], gt, 1.0, ALU.add, coef, ALU.mult)
        if i == 0 and l == 0:
            k.ts(self.hs[l][i][:, :, :], sc, 1.0, ALU.add)
            k.cp(self.hb[l][i][:, :, :], sh)
        else:
            pl, pi = (l, i - 1) if i > 0 else (l - 1, 2)
            gp = _bc(self.pv(pl, PV_LNG + pi * 8, 8).unsqueeze(2), [128, 8, 17])
            bp = _bc(self.pv(pl, PV_LNB + pi * 8, 8).unsqueeze(2), [128, 8, 17])
            k.ts(self.hs[l][i][:, :, :], sc, 1.0, ALU.add)
            k.tt(self.hb[l][i][:, :, :], self.hs[l][i][:, :, :], bp, ALU.mult)
            k.tt(self.hb[l][i][:, :, :], self.hb[l][i][:, :, :], sh, ALU.add)
            k.tt(self.hs[l][i][:, :, :], self.hs[l][i][:, :, :], gp, ALU.mult)

    def affine_h(self, out, in_, sv, bv, kc, tbi, tmp):
        k = self.k
        if tbi < 4:
            k.act(out, in_, AF.Identity, bias=bv[:, kc, 0:1], scale=sv[:, kc, 0:1])
        else:
            s3 = _bc(sv[:, kc, 1:17].unsqueeze(2), [128, NSQ, TS])
            b3 = _bc(bv[:, kc, 1:17].unsqueeze(2), [128, NSQ, TS])
            i3 = in_.rearrange("p (s j) -> p s j", j=TS)
            o3 = out.rearrange("p (s j) -> p s j", j=TS)
            t3 = tmp.rearrange("p (s j) -> p s j", j=TS)
            k.tt(t3, i3, s3, ALU.mult)
            k.tt(o3, t3, b3, ALU.add)

    def initial_h(self):
        k = self.k
        tmpS = self.carve("a", 18432, 256, F32)
        for tbi, (t0, n) in enumerate(TBS):
            for kc in range(KC):
                self.affine_h(self.hT[:, kc, t0:t0 + n], self.xres[:, kc, t0:t0 + n], self.hs[0][0], self.hb[0][0], kc, tbi, tmpS[:, 0:64])
        for kc in range(KC):
            k.ts(self.xres[:, kc, :], self.xres[:, kc, :], ALPHA, ALU.mult)

    def resid_acc(self, ps_ap, l, i, kc, tbi, t0, n, s0=0, ns=NSQ):
        k = self.k
        xr = self.xres[:, kc, t0:t0 + n]
        cf = self.cF[l][i]
        if tbi < 4:
            k.stt(xr, ps_ap, cf[:, kc, 0:1], xr, ALU.mult, ALU.add)
        else:
            tmpS = self.carve("a", 18432, 256, F32)[:, 0:n]
            c3 = _bc(cf[:, kc, 1 + s0:1 + s0 + ns].unsqueeze(2), [128, ns, TS])
            k.tt(tmpS.rearrange("p (s j) -> p s j", j=TS), ps_ap.rearrange("p (s j) -> p s j", j=TS), c3, ALU.mult)
            k.tt(xr, xr, tmpS, ALU.add)

    def stats_acc(self, ps1, ps2, kc, t0, n):
        k = self.k
        rb = self.carve("a", 4096 + (kc % 2) * 1024, 1024, BF16)[:, 0:n]
        rsq = self.carve("a", 6144 + (kc % 2) * 1024, 1024, BF16)[:, 0:n]
        xr = self.xres[:, kc, t0:t0 + n]
        k.cp(rb, xr, eng="act")
        k.act(rsq, xr, AF.Square)
        ones = self.cstb[:, CB_ONES:CB_ONES + 128]
        k.mm(ps1[:, 0:n], ones, rb, start=(kc == 0), stop=(kc == KC - 1))
        k.mm(ps2[:, 0:n], ones, rsq, start=(kc == 0), stop=(kc == KC - 1))

    def ln_block(self, ps1, ps2, l, i, tbi, t0, n):
        k = self.k
        mean = self.carve("a", 8192, 2048, F32)[:, 0:n]
        rstd = self.carve("a", 10240, 2048, F32)[:, 0:n]
        nmr = self.carve("a", 12288, 2048, F32)[:, 0:n]
        ex2 = self.carve("a", 18688, 2048, F32)[:, 0:n]
        tmpS = self.carve("a", 18432, 256, F32)
        k.ts(mean, ps1[:, 0:n], 1.0 / D, ALU.mult)
        k.tt(ex2, mean, mean, ALU.mult)
        k.stt(ex2, ps2[:, 0:n], 1.0 / D, ex2, ALU.mult, ALU.subtract)
        k.ts(ex2, ex2, 0.0, ALU.max, LN_EPS, ALU.add)
        k.act(ex2, ex2, AF.Sqrt)
        k.recip(rstd, ex2)
        k.stt(nmr, mean, -1.0, rstd, ALU.mult, ALU.mult)
        last = (l == DEPTH - 1 and i == 2)
        if i < 2:
            nl, ni = l, i + 1
        else:
            nl, ni = l + 1, 0
        for kc in range(KC):
            xn = self.carve("a", 14336 + (kc % 2) * 2048, 2048, F32)[:, 0:n]
            xr = self.xres[:, kc, t0:t0 + n]
            k.tt(xn, xr, rstd, ALU.mult)
            k.tt(xn, xn, nmr, ALU.add)
            k.act(xr, xn, AF.Identity, bias=self.xb[l][:, i, kc:kc + 1], scale=self.xs[l][:, i, kc:kc + 1])
            if not last:
                self.affine_h(self.hT[:, kc, t0:t0 + n], xn, self.hs[nl][ni], self.hb[nl][ni], kc, tbi, tmpS[:, 0:64])

    def ffn(self, l, which, mid_hook=None):
        k = self.k
        i = 0 if which == 0 else 2
        d_wi = self.d_wi[which]
        d_wo = self.d_wo[which]
        gb = self.gbig
        for gi, js in enumerate(JG):
            for jj, j in enumerate(js):
                wa = self.wload(d_wi[l, j])
                wb = self.wload(d_wi[l, NJ + j])
                for tbi, (t0, n) in enumerate(TBS):
                    pa = self.ps()
                    pb = self.ps()
                    for kc in range(KC):
                        k.mm(pa[:, 0:n], wa[:, kc * 128:(kc + 1) * 128], self.hT[:, kc, t0:t0 + n], start=(kc == 0), stop=(kc == KC - 1))
                    for kc in range(KC):
                        k.mm(pb[:, 0:n], wb[:, kc * 128:(kc + 1) * 128], self.hT[:, kc, t0:t0 + n], start=(kc == 0), stop=(kc == KC - 1))
                    sA = self.carve("a", ((jj * 5 + tbi) % 2) * 2048, 2048, F32)[:, 0:n]
                    k.act(sA, pa[:, 0:n], AF.Silu)
                    k.tt(gb[:, jj * NTOK + t0: jj * NTOK + t0 + n], sA, pb[:, 0:n], ALU.mult)
            if gi == 0 and mid_hook is not None:
                mid_hook()
            wos = [self.wload(d_wo[l, j * 128:(j + 1) * 128, :]) for j in js]
            final = (gi == len(JG) - 1)
            for tbi, (t0, n) in enumerate(TBS):
                if final:
                    ps1 = self.psb[6]
                    ps2 = self.psb[7]
                for oc in range(KC):
                    po = self.ps()
                    for jj in range(len(js)):
                        k.mm(po[:, 0:n], wos[jj][:, oc * 128:(oc + 1) * 128], gb[:, jj * NTOK + t0: jj * NTOK + t0 + n],
                             start=(jj == 0), stop=(jj == len(js) - 1))
                    self.resid_acc(po[:, 0:n], l, i, oc, tbi, t0, n)
                    if final:
                        self.stats_acc(ps1, ps2, oc, t0, n)
                if final:
                    self.ln_block(ps1, ps2, l, i, tbi, t0, n)

    def ln_pass(self, l, i):
        for tbi, (t0, n) in enumerate(TBS):
            ps1 = self.psb[6]
            ps2 = self.psb[7]
            for kc in range(KC):
                self.stats_acc(ps1, ps2, kc, t0, n)
            self.ln_block(ps1, ps2, l, i, tbi, t0, n)

    def store_y(self):
        self.k.dma("sp", self.o_yT.rearrange("(kc p) t -> p kc t", p=128), self.xres[:, :, :], is_out=True)


class Prog3(Prog2):
    def blkinfo(self, blk):
        if blk < NPB:
            return False, blk * MB, MB, 0
        sb_ = blk - NPB
        return True, TP + sb_ * NCH * TS, NCH * TS, sb_ * NCH

    def cv(self, ap, is_s, N):
        a = ap[:, 0:N]
        if is_s:
            a = a.rearrange("p (s j) -> p s j", j=TS)
        return a

    def pw(self, ap, is_s):
        if is_s:
            return ap.rearrange("p (s t) -> p s t", t=64)[:, :, 0:TS]
        return ap

    def proj(self, w, t0, N, M=128):
        k = self.k
        p = self.ps()
        for kc in range(KC):
            k.mm(p[0:M, 0:N], w[:, kc * 128:kc * 128 + M], self.hT[:, kc, t0:t0 + N], start=(kc == 0), stop=(kc == KC - 1))
        return p

    def layer_small(self, l):
        k = self.k
        sm = self.small
        k.ts(sm[:, 0:2], self.pv(l, PV_BG, 2), -1.0, ALU.mult)
        if l == 0:
            k.ms(sm[:, 2:4], 0.0)
        else:
            k.tt(sm[:, 2:4], self.pv(l, PV_LB1, 2), self.pv(l, PV_LB0, 2), ALU.subtract)
            k.act(sm[:, 2:4], sm[:, 2:4], AF.Exp, scale=-1.0)
            k.ts(sm[:, 2:4], sm[:, 2:4], 1.0, ALU.add)
            k.recip(sm[:, 2:4], sm[:, 2:4])
        k.ts(sm[:, 4:6], sm[:, 2:4], -1.0, ALU.mult, 1.0, ALU.add)
        k.ts(sm[:, 6:8], sm[:, 4:6], -1.0, ALU.mult)
        k.act(sm[0:4, 11:12], self.pv(l, PV_ALOG)[0:4, :], AF.Exp)
        k.ts(sm[0:4, 11:12], sm[0:4, 11:12], -1.0, ALU.mult)
        k.dma("sp", self.wg[:, :], self.d_wg[l])
        k.cp(self.wgb[:, :], self.wg[:, :])

    def mix_layer(self, l, mid_hook=None):
        k = self.k
        self.layer_small(l)
        mixers = [("ret", ["rq0", "rq1", "rk0", "rk1", "rv0", "rv1", "rg0", "rg1"]),
                  ("gla", ["aq0", "aq1", "ak0", "ak1", "av0", "av1", "ag0", "ag1", "alr"]),
                  ("hg", ["hq0", "hq1", "hf0", "hf1", "hi0", "hi1", "hg0", "hg1"]),
                  ("gdn", ["dq0", "dq1", "dk0", "dk1", "dv0", "dv1", "dg0", "dg1", "db", "da"])]
        for m, (name, wn) in enumerate(mixers):
            if name in self.skip_mixers:
                continue
            W = {n: self.wload(self.d_win[l, WT[n]]) for n in wn}
            Wo = [self.wload(self.d_wout[l, 2 * m + tl]) for tl in range(2)]
            blks = list(range(NPB + NSB) if self.nblk is None else self.nblk)
            if name == "gdn":
                for blk in blks:
                    self.mix_gdn(l, m, blk, W, Wo)
            else:
                def run_to_prep(g_):
                    for v_ in g_:
                        if v_ == "PREP_DONE":
                            return
                cur = self.mix_gla(l, m, name, blks[0], W, Wo, 0)
                run_to_prep(cur)
                for bi in range(len(blks)):
                    nxt = self.mix_gla(l, m, name, blks[bi + 1], W, Wo, (bi + 1) % 2) if bi + 1 < len(blks) else None
                    cur_live, nxt_live = True, nxt is not None
                    while cur_live or nxt_live:
                        if cur_live:
                            try:
                                next(cur)
                            except StopIteration:
                                cur_live = False
                        if nxt_live:
                            try:
                                if next(nxt) == "PREP_DONE":
                                    nxt_live = False
                            except StopIteration:
                                nxt_live = False
                    cur = nxt
            if mid_hook is not None:
                mid_hook()
                mid_hook = None
        if mid_hook is not None:
            mid_hook()
        self.ln_pass(l, 1)

    def chk(self):
        self.chkc = getattr(self, "chkc", 0) + 1
        return self.chkc == CUTN

    def gbuf(self, off, nbytes, dt, parts=128):
        return self.carve("g", off, nbytes, dt, parts)

    def mix_gla(self, l, m, name, blk, W, Wo, st=0):
        k = self.k
        is_s, t0, N, s0 = self.blkinfo(blk)
        cv = lambda ap: self.cv(ap, is_s, N)
        pw = lambda ap: self.pw(ap, is_s)
        r3 = lambda ap: ap.rearrange("p (a t) -> p a t", a=2)
        Qp = r3(self.gbuf(0, 1024, BF16))
        Kp = r3(self.gbuf(1024, 1024, BF16))
        if st == 0:
            V = r3(self.gbuf(2048, 1024, BF16))
            GATE = r3(self.gbuf(31488, 2048, F32))
            QT = r3(self.gbuf(12288, 1024, BF16))
            KT = r3(self.gbuf(13312, 1024, BF16))
            QG = r3(self.gbuf(14336, 1024, BF16))
            av = self.gbuf(31232, 32, F32).rearrange("p (a c) -> p a c", a=2)
            bv = self.gbuf(31264, 32, F32).rearrange("p (a c) -> p a c", a=2)
        else:
            QT = r3(self.carve("a", 4096, 1024, BF16))
            KT = r3(self.carve("a", 5120, 1024, BF16))
            QG = r3(self.carve("a", 6144, 1024, BF16))
            V = r3(self.carve("a", 7168, 1024, BF16))
            GATE = r3(self.carve("a", 8192, 2048, F32))
            av = self.carve("a", 10240, 32, F32).rearrange("p (a c) -> p a c", a=2)
            bv = self.carve("a", 10272, 32, F32).rearrange("p (a c) -> p a c", a=2)
        T1r = self.carve("a", 12288, 2048, F32)
        T2r = self.carve("a", 14336, 2048, F32)
        Gs = r3(self.gbuf(4096, 2048, F32))
        T1 = self.gbuf(6144, 2048, F32)
        T2 = self.gbuf(8192, 2048, F32)
        G0 = r3(self.gbuf(10240, 2048, F32))
        KTM = self.gbuf(15360, 2048, BF16)
        VTM = self.gbuf(17408, 2048, BF16)
        ATT = self.gbuf(19456, 2048, BF16)
        UP = self.gbuf(21504, 2048, F32).rearrange("p (a c v) -> p a c v", a=2, c=NCH)
        SALL = self.gbuf(23552, 2560, F32).rearrange("p (a c v) -> p a c v", a=2, c=NCH + 1)
        SBF = self.gbuf(26112, 1024, BF16).rearrange("p (a c v) -> p a c v", a=2, c=NCH)
        SIN = self.gbuf(27136, 2048, F32).rearrange("p (a c v) -> p a c v", a=2, c=NCH)
        OSQ = self.gbuf(29184, 1024, BF16)
        OB = r3(self.gbuf(30208, 1024, BF16))
        cosb = self.carve("a", 0, 1024, F32)
        sinb = self.carve("a", 1024, 1024, F32)
        alrb = self.carve("a", 2048, 512, BF16)
        qb = self.carve("a", 2560, 512, BF16)
        t1c = T1[:, 0:MB]
        t2c = T2[:, 0:MB]
        identb = self.ident_b()
        sm = self.small

        if is_s:
            for tl in range(2):
                k.ms(Qp[:, tl, :], 0.0)
                k.ms(Kp[:, tl, :], 0.0)
                k.ms(V[:, tl, :], 0.0)
                if name != "ret":
                    k.ms(G0[:, tl, :], 0.0)

        def simple_v_gate(vn, gn, tl):
            p = self.proj(W[vn + str(tl)], t0, N)
            k.cp(pw(V[:, tl, :]), cv(p))
            p = self.proj(W[gn + str(tl)], t0, N)
            k.act(GATE[:, tl, 0:N], p[:, 0:N], AF.Silu)

        if name == "ret":
            if _os.environ.get('KVAR', '') != 'nodma':
                k.dma("sp", cosb[:, 0:N], self.d_cos[:, t0:t0 + N])
                k.dma("sp", sinb[:, 0:N], self.d_sin[:, t0:t0 + N])
                k.dma("sp", Gs, self.d_gret[1 if is_s else 0])
            pmb = self.cstb[:, CB_PM:CB_PM + 128]
            if CUT == 11:
                return
            for tl in range(2):
                for (wn_, dst, scale) in (("rq", Qp, 1.0), ("rk", Kp, 0.125)):
                    p = self.proj(W[wn_ + str(tl)], t0, N)
                    if self.chk():
                        return
                    k.cp(qb[:, 0:N], p[:, 0:N])
                    pp = self.ps()
                    k.mm(pp[:, 0:N], pmb, qb[:, 0:N])
                    if self.chk():
                        return
                    k.stt(t1c[:, 0:N], p[:, 0:N], scale, cosb[:, 0:N], ALU.mult, ALU.mult)
                    k.stt(t2c[:, 0:N], pp[:, 0:N], scale, sinb[:, 0:N], ALU.mult, ALU.mult)
                    if self.chk():
                        return
                    _v = _os.environ.get('KVAR', '')
                    if _v == 'A' and tl == 1:
                        k.tt(pw(dst[:, 0, :]), cv(t1c), cv(t2c), ALU.add)
                    elif _v == 'B' and tl == 1:
                        k.tt(pw(dst[:, tl, :]), cv(t1c), cv(t2c), ALU.add, eng="pool")
                    else:
                        k.tt(pw(dst[:, tl, :]), cv(t1c), cv(t2c), ALU.add, eng=GP)
                    if self.chk():
                        return
                yield
                simple_v_gate("rv", "rg", tl)
                yield
                if self.chk():
                    return
        elif name == "gla":
            p = self.proj(W["alr"], t0, N, M=16)
            k.cp(alrb[0:16, 0:N], p[0:16, 0:N])
            for tl in range(2):
                pg = self.ps()
                k.mm(pg[:, 0:N], self.wgb[0:16, tl * 128:(tl + 1) * 128], alrb[0:16, 0:N])
                k.act(t1c[:, 0:N], pg[:, 0:N], AF.Exp, bias=sm[:, tl:tl + 1], scale=-1.0)
                k.act(t1c[:, 0:N], t1c[:, 0:N], AF.Ln, bias=self.c_one)
                k.ts(pw(G0[:, tl, :]), cv(t1c), -1.0 / 16.0, ALU.mult)
                p = self.proj(W["aq" + str(tl)], t0, N)
                k.ts(pw(Qp[:, tl, :]), cv(p), float(32.0 ** -0.5), ALU.mult)
                p = self.proj(W["ak" + str(tl)], t0, N)
                k.cp(pw(Kp[:, tl, :]), cv(p))
                yield
                simple_v_gate("av", "ag", tl)
                yield
        else:
            for tl in range(2):
                p = self.proj(W["hq" + str(tl)], t0, N)
                k.act(t1c[:, 0:N], p[:, 0:N], AF.Silu)
                k.ts(pw(Qp[:, tl, :]), cv(t1c), 0.125, ALU.mult)
                p = self.proj(W["hf" + str(tl)], t0, N)
                k.act(t1c[:, 0:N], p[:, 0:N], AF.Exp, scale=-1.0)
                k.ts(t1c[:, 0:N], t1c[:, 0:N], 1.0, ALU.add)
                k.recip(t1c[:, 0:N], t1c[:, 0:N])
                k.act(t2c[:, 0:N], t1c[:, 0:N], AF.Ln, bias=sm[:, 2 + tl:3 + tl], scale=sm[:, 4 + tl:5 + tl])
                k.cp(pw(G0[:, tl, :]), cv(t2c))
                k.ts(pw(Kp[:, tl, :]), cv(t1c), sm[:, 6 + tl:7 + tl], ALU.mult, sm[:, 4 + tl:5 + tl], ALU.add)
                yield
                simple_v_gate("hi", "hg", tl)
                yield
        if name != "ret":
            reset = self.cst[:, CST_RESET:CST_RESET + MB]
            for tl in range(2):
                k.scan(Gs[:, tl, :], reset, G0[:, tl, :], 0.0, ALU.mult, ALU.add)

        if CUT == 1:
            return
        for tl in range(2):
            G3 = Gs[:, tl, :].rearrange("p (c t) -> p c t", t=64)
            D3 = t1c.rearrange("p (c t) -> p c t", t=64)
            k.tt(D3, G3, _bc(G3[:, :, 31:32], [128, NCH, 64]), ALU.subtract)
            k.act(t2c, t1c, AF.Exp)
            k.tt(QT[:, tl, :], Qp[:, tl, :], t2c, ALU.mult, eng=GP)
            k.act(t2c, t1c, AF.Exp, scale=-1.0)
            k.tt(KT[:, tl, :], Kp[:, tl, :], t2c, ALU.mult, eng=GP)
            k.act(t2c, Gs[:, tl, :], AF.Exp)
            k.tt(QG[:, tl, :], Qp[:, tl, :], t2c, ALU.mult, eng=GP)
            k.act(av[:, tl, :], G3[:, :, 63], AF.Exp)
            k.act(bv[:, tl, :], D3[:, :, 63], AF.Exp)
            yield
        yield "PREP_DONE"

        P2 = [slice(0, 64), slice(64, 128)]
        for (src, dstm) in ((KT, KTM), (V, VTM)):
            pT = [self.ps(), self.ps()]
            for p_ in range(2):
                pTb = pT[p_][:, :].bitcast(BF16)
                for c in range(NCH):
                    for tl in range(2):
                        k.tr(pTb[P2[p_], (c * 2 + tl) * 64:(c * 2 + tl + 1) * 64], src[P2[p_], tl, c * 64:(c + 1) * 64],
                             identb[P2[p_], 64 * p_:64 * p_ + 64])
                k.cp(dstm[P2[p_], 0:NCH * 128], pTb[P2[p_], 0:NCH * 128])
            yield

        mincl2 = self.cst[:, CST_MINCL:CST_MINCL + 64]
        pA = [self.ps(), self.ps()]
        for p_ in range(2):
            for c in range(NCH):
                for tl in range(2):
                    sl = slice((c * 2 + tl) * 64, (c * 2 + tl + 1) * 64)
                    k.mm(pA[p_][P2[p_], sl], KT[P2[p_], tl, c * 64:(c + 1) * 64], QT[P2[p_], tl, c * 64:(c + 1) * 64])
            k.tt(ATT[P2[p_], 0:NCH * 128].rearrange("p (a t) -> p a t", t=64),
                 pA[p_][P2[p_], 0:NCH * 128].rearrange("p (a t) -> p a t", t=64),
                 _bc(mincl2[P2[p_], :].unsqueeze(1), [64, NCH * 2, 64]), ALU.mult)

        yield
        pU = [self.ps(), self.ps()]
        for p_ in range(2):
            for c in range(NCH):
                for tl in range(2):
                    sl = slice((c * 2 + tl) * 64, (c * 2 + tl + 1) * 64)
                    k.mm(pU[p_][P2[p_], (tl * NCH + c) * 64:(tl * NCH + c + 1) * 64], KTM[P2[p_], sl], VTM[P2[p_], sl])
            k.tt(UP.rearrange("p a c v -> p (a c) v")[P2[p_]], pU[p_][P2[p_], :].rearrange("p (a v) -> p a v", v=64),
                 _bc(bv.rearrange("p a c -> p (a c)")[P2[p_]].unsqueeze(2), [64, 2 * NCH, 64]), ALU.mult)

        yield
        if not is_s:
            if blk == 0:
                k.ms(SALL[:, :, 0, :], 0.0)
            else:
                k.cp(SALL[:, :, 0, :], SALL[:, :, NCH, :])
            SBv = SALL
        else:
            k.dma("sp", SIN, self.d_st[name][l][:, :, s0:s0 + NCH, :])
            SBv = SIN
        for c in range(NCH):
            for tl in range(2):
                src = SALL[:, tl, c, :] if not is_s else SIN[:, tl, c, :]
                k.stt(SALL[:, tl, c + 1, :], src, av[:, tl, c:c + 1], UP[:, tl, c, :], ALU.mult, ALU.add)
        if is_s:
            k.dma("sp", self.o_ss[name][l][:, :, s0:s0 + NCH, :], SALL[:, :, 1:NCH + 1, :], is_out=True)
        elif blk == NPB - 1:
            k.dma("sp", self.o_ps[name][l], SALL[:, :, NCH, :], is_out=True)
        k.cp(SBF, SBv[:, :, 0:NCH, :], eng=GP)

        yield
        pO = [self.ps(), self.ps()]
        for p_ in range(2):
            for c in range(NCH):
                for tl in range(2):
                    sl = slice((c * 2 + tl) * 64, (c * 2 + tl + 1) * 64)
                    o_ap = pO[p_][P2[p_], (tl * NCH + c) * 64:(tl * NCH + c + 1) * 64]
                    k.mm(o_ap, VTM[P2[p_], sl], ATT[P2[p_], sl], start=True, stop=False)
                    k.mm(o_ap, SBF[P2[p_], tl, c, :], QG[P2[p_], tl, c * 64:(c + 1) * 64], start=False, stop=True)
        for p_ in range(2):
            k.cp(T2r[P2[p_], 0:2 * MB], pO[p_][P2[p_], 0:2 * MB], eng="act")
        yield
        normw = None if name == "ret" else self.pv(l, PV_NORM + {"gla": 0, "hg": 1}[name])
        self.o_post(l, m, T2r, GATE, OB, OSQ, T1r, T2r, normw, Wo, is_s, t0, N, s0)

    def o_post(self, l, m, pO, GATE, OB, OSQ, T1, T2, normw, Wo, is_s, t0, N, s0):
        k = self.k
        cv = lambda ap: self.cv(ap, is_s, N)
        pw = lambda ap: self.pw(ap, is_s)
        bones = self.cstb[:, CB_BONES:CB_BONES + 128]
        if str(pO.space) == "PSUM":
            k.cp(T2[:, 0:2 * MB], pO[:, 0:2 * MB], eng="act")
            pO = T2
        k.act(OSQ[:, 0:2 * MB], pO[:, 0:2 * MB], AF.Square)
        pS = self.ps()
        for tl in range(2):
            k.mm(pS[:, tl * MB:(tl + 1) * MB], bones, OSQ[:, tl * MB:(tl + 1) * MB])
        k.act(T1[:, 0:2 * MB], pS[:, 0:2 * MB], AF.Sqrt, bias=self.c_eps, scale=1.0 / 64.0)
        k.recip(T1[:, 0:2 * MB], T1[:, 0:2 * MB])
        k.tt(T2[:, 0:2 * MB], pO[:, 0:2 * MB], T1[:, 0:2 * MB], ALU.mult)
        for tl in range(2):
            src = pw(T2[:, tl * MB:(tl + 1) * MB])
            if normw is None:
                k.tt(cv(OB[:, tl, :]), src, cv(GATE[:, tl, :]), ALU.mult)
            else:
                k.stt(cv(OB[:, tl, :]), src, normw, cv(GATE[:, tl, :]), ALU.mult, ALU.mult)
        for oc in range(KC):
            po = self.ps()
            for tl in range(2):
                k.mm(po[:, 0:N], Wo[tl][:, oc * 128:(oc + 1) * 128], OB[:, tl, 0:N], start=(tl == 0), stop=(tl == 1))
            self.resid_acc(po[:, 0:N], l, 1, oc, 4 if is_s else 0, t0, N, s0, NCH)


class Prog4(Prog3):
    def mix_gdn(self, l, m, blk, W, Wo):
        k = self.k
        is_s, t0, N, s0 = self.blkinfo(blk)
        cv = lambda ap: self.cv(ap, is_s, N)
        pw = lambda ap: self.pw(ap, is_s)
        r3 = lambda ap: ap.rearrange("p (a t) -> p a t", a=2)
        g = self.gbuf
        P2 = [slice(0, 64), slice(64, 128)]
        Qp = r3(g(0, 2048, F32))
        Kp = r3(g(2048, 2048, F32))
        QGx = r3(g(4096, 2048, F32))
        QpT = r3(g(6144, 2048, F32))
        VTM = g(8192, 2048, F32)
        KTM = g(10240, 2048, F32)
        MT = g(12288, 2048, F32).rearrange("p (a c v) -> p a c v", a=2, c=NCH)
        BC = g(14336, 2048, F32).rearrange("p (a c v) -> p a c v", a=2, c=NCH)
        SALL = g(16384, 2560, F32).rearrange("p (a c v) -> p a c v", a=2, c=NCH + 1)
        SIN = g(18944, 2048, F32).rearrange("p (a c v) -> p a c v", a=2, c=NCH)
        OB = r3(g(20992, 1024, BF16))
        OSQ = g(22016, 1024, BF16)
        GD = g(23040, 1024, F32)
        GC = g(24064, 1024, F32)
        BET = g(25088, 1024, F32)
        c3v = lambda ap: ap[:, 0:NCH * 2].rearrange("p (c l) -> p c l", l=2)
        GCOL = g(26112, 64, F32)
        BCOL = g(26176, 64, F32)
        EGC = g(26240, 64, F32)
        KHS = g(26304, 64, F32)
        BEG = g(26368, 64, F32)
        eGL = g(26432, 32, F32).rearrange("p (a c) -> p a c", a=2)
        halo = g(26496, 72, F32).rearrange("p (i r) -> p i r", r=3)
        OL = r3(g(26624, 2048, F32))
        GATE = r3(g(28672, 2048, F32))
        A = lambda i: self.carve("a", i * 1024, 1024, F32)
        UQ = r3(self.carve("a", 9216, 2048, F32))
        UK = r3(self.carve("a", 11264, 2048, F32))
        VT = r3(self.carve("a", 13312, 2048, F32))
        SQ = self.carve("a", 15360, 1024, BF16)
        ubuf = self.carve("a", 16384, 1040, F32)
        acc = self.carve("a", 17424, 1024, F32)
        T1 = self.carve("a", 18448, 2048, F32)
        T2 = self.carve("a", 20496, 2048, F32)
        sm = self.small
        identF = self.cst[:, CST_IDENT:CST_IDENT + 128]
        bones = self.cstb[:, CB_BONES:CB_BONES + 128]
        selF = self.cst[0:4, CST_SELF:CST_SELF + 256]
        selHP = self.cst[0:4, CST_SELHP:CST_SELHP + 4]
        mincl = self.cst[:, CST_MINCL:CST_MINCL + 64]
        mstr = self.cst[:, CST_MSTR:CST_MSTR + 64]
        mstrT = self.cst[:, CST_MSTRT:CST_MSTRT + 64]
        ID2 = self.cst[:, CST_ID2:CST_ID2 + 64]

        if is_s:
            for tl in range(2):
                k.ms(Qp[:, tl, :], 0.0)
                k.ms(Kp[:, tl, :], 0.0)
                k.ms(VT[:, tl, :], 0.0)
            k.ms(GD[0:4, :], 0.0)
            k.ms(BET[0:4, :], 0.0)

        names = ["dq0", "dq1", "dk0", "dk1", "dv0", "dv1"]
        for idx, wn_ in enumerate(names):
            p = self.proj(W[wn_], t0, N)
            tl = idx % 2
            cw = lambda i: self.pv(l, PV_CONV + idx * 4 + i)
            if not is_s:
                if blk == 0:
                    k.ms(ubuf[:, 0:3], 0.0)
                else:
                    k.cp(ubuf[:, 0:3], halo[:, idx, :], eng=GP)
                k.cp(ubuf[:, 3:3 + N], p[:, 0:N], eng="act")
                k.ts(acc[:, 0:N], ubuf[:, 0:N], cw(0), ALU.mult)
                for i in range(1, 4):
                    k.stt(acc[:, 0:N], ubuf[:, i:i + N], cw(i), acc[:, 0:N], ALU.mult, ALU.add)
                k.cp(halo[:, idx, :], ubuf[:, N:N + 3], eng=GP)
                if blk == NPB - 1:
                    k.dma("sp", self.o_pconv[l][:, idx, :], halo[:, idx, :], is_out=True)
            else:
                ubs = ubuf[:, 0:NCH * 7].rearrange("p (s r) -> p s r", r=7)
                k.dma("sp", ubs[:, :, 0:3], self.d_stconv[l][:, idx, s0:s0 + NCH, :])
                k.cp(ubs[:, :, 3:7], p[:, 0:N].rearrange("p (s j) -> p s j", j=TS), eng="act")
                a3 = acc[:, 0:N].rearrange("p (s j) -> p s j", j=TS)
                k.ts(a3, ubs[:, :, 0:4], cw(0), ALU.mult)
                for i in range(1, 4):
                    k.stt(a3, ubs[:, :, i:i + 4], cw(i), a3, ALU.mult, ALU.add)
                k.dma("sp", self.o_sconv[l][:, idx, s0:s0 + NCH, :], ubs[:, :, 4:7], is_out=True)
            accv = acc[:, 0:N]
            if idx < 2:
                k.act(UQ[:, tl, 0:N], accv, AF.Silu)
            elif idx < 4:
                k.act(UK[:, tl, 0:N], accv, AF.Silu)
            else:
                k.act(T1[:, 0:N], accv, AF.Silu)
                k.cp(pw(VT[:, tl, :]), cv(T1[:, 0:MB]), eng=GP)
        if CUTG == 1:
            return
        for (X, dst, scale) in ((UQ, Qp, 0.125), (UK, Kp, 1.0)):
            pS = self.ps()
            for tl in range(2):
                k.act(SQ[:, tl * MB:tl * MB + N], X[:, tl, 0:N], AF.Square)
                k.mm(pS[:, tl * MB:tl * MB + N], bones, SQ[:, tl * MB:tl * MB + N])
                k.act(T1[:, tl * MB:tl * MB + N], pS[:, tl * MB:tl * MB + N], AF.Sqrt, bias=self.c_eps, scale=1.0)
                k.recip(T1[:, tl * MB:tl * MB + N], T1[:, tl * MB:tl * MB + N])
                k.stt(pw(dst[:, tl, :]), cv(X[:, tl, :]), scale, cv(T1[:, tl * MB:(tl + 1) * MB]), ALU.mult, ALU.mult)
        if CUTG == 2:
            return
        for tl in range(2):
            p = self.proj(W["dg" + str(tl)], t0, N)
            k.act(GATE[:, tl, 0:N], p[:, 0:N], AF.Silu)
        if CUTG == 3:
            return
        p = self.proj(W["db"], t0, N, M=4)
        k.act(T2[0:4, 0:N], p[0:4, 0:N], AF.Exp, scale=-1.0)
        k.ts(T2[0:4, 0:N], T2[0:4, 0:N], 1.0, ALU.add)
        k.recip(T2[0:4, 0:N], T2[0:4, 0:N])
        k.cp(pw(BET[0:4, 0:MB]), cv(T2[0:4, 0:MB]))
        p = self.proj(W["da"], t0, N, M=4)
        k.act(T2[0:4, 0:N], p[0:4, 0:N], AF.Exp, bias=self.pv(l, PV_DTB)[0:4, :], scale=1.0)
        k.act(T2[0:4, 0:N], T2[0:4, 0:N], AF.Ln, bias=self.c_one[0:4, :])
        k.ts(pw(GD[0:4, 0:MB]), cv(T2[0:4, 0:MB]), sm[0:4, 11:12], ALU.mult)
        k.scan(GC[0:4, 0:MB], self.cst[0:4, CST_RESET:CST_RESET + MB], GD[0:4, 0:MB], 0.0, ALU.mult, ALU.add)
        if CUTG == 4:
            return
        pX = self.ps()
        for tl in range(2):
            k.mm(pX[:, tl * MB:(tl + 1) * MB], selF[0:4, tl * 128:(tl + 1) * 128], GC[0:4, 0:MB])
        k.act(T1[:, 0:2 * MB], pX[:, 0:2 * MB], AF.Exp)
        for tl in range(2):
            k.tt(QGx[:, tl, :], Qp[:, tl, :], T1[:, tl * MB:(tl + 1) * MB], ALU.mult, eng=GP)
            k.cp(eGL[:, tl, :], T1[:, tl * MB:(tl + 1) * MB].rearrange("p (c t) -> p c t", t=64)[:, :, 63], eng=GP)
        pC = [self.ps(), self.ps()]
        for p_ in range(2):
            for c in range(NCH):
                k.mm(pC[p_][P2[p_], c * 2:(c + 1) * 2], GC[0:4, c * 64:(c + 1) * 64], selHP[0:4, p_ * 2:p_ * 2 + 2])
                k.mm(pC[p_][P2[p_], 64 + c * 2:64 + (c + 1) * 2], BET[0:4, c * 64:(c + 1) * 64], selHP[0:4, p_ * 2:p_ * 2 + 2])
            k.cp(GCOL[P2[p_], 0:NCH * 2], pC[p_][P2[p_], 0:NCH * 2])
            k.cp(BCOL[P2[p_], 0:NCH * 2], pC[p_][P2[p_], 64:64 + NCH * 2])
        k.act(EGC[:, 0:NCH * 2], GCOL[:, 0:NCH * 2], AF.Exp)
        k.tt(BEG[:, 0:NCH * 2], BCOL[:, 0:NCH * 2], EGC[:, 0:NCH * 2], ALU.mult)
        pL = self.ps()
        glast = GC[0:4, 0:MB].rearrange("p (c t) -> p c t", t=64)[:, :, 63]
        for tl in range(2):
            k.mm(pL[:, tl * NCH:(tl + 1) * NCH], selF[0:4, tl * 128:(tl + 1) * 128], glast)
        k.tt(c3v(KHS), pL[:, 0:NCH * 2].rearrange("p (l c) -> p c l", l=2), c3v(GCOL), ALU.subtract)
        k.act(KHS[:, 0:NCH * 2], KHS[:, 0:NCH * 2], AF.Exp)
        if CUTG == 5:
            return
        for (src, dstm) in ((Kp, KTM), (VT, VTM)):
            pT = [self.ps(), self.ps()]
            for p_ in range(2):
                for c in range(NCH):
                    for tl in range(2):
                        k.mm(pT[p_][P2[p_], (c * 2 + tl) * 64:(c * 2 + tl + 1) * 64], src[P2[p_], tl, c * 64:(c + 1) * 64],
                             identF[P2[p_], 64 * p_:64 * p_ + 64])
                k.cp(dstm[P2[p_], 0:NCH * 128], pT[p_][P2[p_], 0:NCH * 128], eng=("act" if p_ else "dve"))
        KTM4 = KTM[:, 0:NCH * 128].rearrange("p (c l d) -> p c l d", c=NCH, l=2)
        VTM4 = VTM[:, 0:NCH * 128].rearrange("p (c l d) -> p c l d", c=NCH, l=2)
        GCOL3 = c3v(GCOL)
        BCOL3 = c3v(BCOL)
        BEG3 = c3v(BEG)
        KHS3 = c3v(KHS)
        v4 = lambda ap: ap[:, 0:256].rearrange("p (c l t) -> p c l t", c=2, l=2)
        v3 = lambda ap: ap[:, 0:256].rearrange("p (a t) -> p a t", t=64)
        bc4 = lambda ap3: _bc(ap3.unsqueeze(3), [128, 2, 2, 64])
        bm = lambda mk: _bc(mk.unsqueeze(1), [128, 4, 64])

        def evac2(dst, pp, eng0="dve", eng1="act"):
            k.cp(dst[P2[0], 0:256], pp[0][P2[0], 0:256], eng=eng0)
            k.cp(dst[P2[1], 0:256], pp[1][P2[1], 0:256], eng=eng1)

        if CUTG == 6:
            return
        def solve(sbi):
            c0 = sbi * 2
            Dm, X1, X2, LT, Nk, Ak, P, U, Wm = [A(i + 11 * sbi) for i in range(9)]
            pGr = self.ps()
            pBr = self.ps()
            for tl in range(2):
                k.mm(pGr[:, tl * 128:(tl + 1) * 128], selF[0:4, tl * 128:(tl + 1) * 128], GC[0:4, c0 * 64:c0 * 64 + 128])
                k.mm(pBr[:, tl * 128:(tl + 1) * 128], selF[0:4, tl * 128:(tl + 1) * 128], BET[0:4, c0 * 64:c0 * 64 + 128])
            gr4 = pGr[:, 0:256].rearrange("p (l c t) -> p c l t", l=2, c=2)
            br4 = pBr[:, 0:256].rearrange("p (l c t) -> p c l t", l=2, c=2)
            k.tt(v4(Dm), gr4, bc4(GCOL3[:, c0:c0 + 2, :]), ALU.subtract)
            k.ts(X1[:, 0:256], Dm[:, 0:256], 0.0, ALU.min)
            k.act(X1[:, 0:256], X1[:, 0:256], AF.Exp)
            k.ts(X2[:, 0:256], Dm[:, 0:256], -1.0, ALU.mult, 0.0, ALU.min)
            k.act(X2[:, 0:256], X2[:, 0:256], AF.Exp)
            k.tt(v3(LT), v3(X1), bm(mincl), ALU.mult, eng=GP)
            k.tt(v3(X1), v3(X1), bm(mstr), ALU.mult, eng=GP)
            k.tt(v4(X1), v4(X1), br4, ALU.mult)
            k.tt(v3(X2), v3(X2), bm(mstrT), ALU.mult, eng=GP)
            k.tt(v4(X2), v4(X2), bc4(BCOL3[:, c0:c0 + 2, :]), ALU.mult, eng=GP)
            yield
            pKK = [self.ps(), self.ps()]
            pQK = [self.ps(), self.ps()]
            for cc in range(2):
                c = c0 + cc
                for tl in range(2):
                    for p_ in range(2):
                        sl = slice((cc * 2 + tl) * 64, (cc * 2 + tl + 1) * 64)
                        kk = Kp[P2[p_], tl, c * 64:(c + 1) * 64]
                        qq = Qp[P2[p_], tl, c * 64:(c + 1) * 64]
                        k.mm(pKK[p_][P2[p_], sl], kk, kk)
                        k.mm(pQK[p_][P2[p_], sl], kk, qq)
            for p_ in range(2):
                k.tt(Nk[P2[p_], 0:256], pKK[p_][P2[p_], 0:256], X1[P2[p_], 0:256], ALU.mult)
                k.tt(Ak[P2[p_], 0:256], pKK[p_][P2[p_], 0:256], X2[P2[p_], 0:256], ALU.mult)
                k.tt(LT[P2[p_], 0:256], pQK[p_][P2[p_], 0:256], LT[P2[p_], 0:256], ALU.mult)
            k.stt(v3(P), v3(Nk), -1.0, bm(ID2), ALU.mult, ALU.add)
            yield
            for lev in range(5):
                pA_ = [self.ps(), self.ps()]
                if lev < 4:
                    pN_ = [self.ps(), self.ps()]
                for j in range(4):
                    for p_ in range(2):
                        sl = slice(j * 64, (j + 1) * 64)
                        if lev < 4:
                            k.mm(pN_[p_][P2[p_], sl], Ak[P2[p_], sl], Nk[P2[p_], sl])
                        k.mm(pA_[p_][P2[p_], sl], Nk[P2[p_], sl], Ak[P2[p_], sl])
                if lev < 4:
                    evac2(Nk, pN_, "act", "act")
                evac2(Ak, pA_, "dve", "dve")
                yield
                pP = [self.ps(), self.ps()]
                for j in range(4):
                    for p_ in range(2):
                        sl = slice(j * 64, (j + 1) * 64)
                        k.mm(pP[p_][P2[p_], sl], Ak[P2[p_], sl], P[P2[p_], sl])
                for p_ in range(2):
                    k.tt(P[P2[p_], 0:256], P[P2[p_], 0:256], pP[p_][P2[p_], 0:256], ALU.add)
                yield
            k.tt(v4(X1), VTM4[:, c0:c0 + 2, :, :], bc4(BCOL3[:, c0:c0 + 2, :]), ALU.mult, eng=GP)
            k.tt(v4(X2), KTM4[:, c0:c0 + 2, :, :], bc4(BEG3[:, c0:c0 + 2, :]), ALU.mult, eng=GP)
            pu = [self.ps(), self.ps()]
            pw_ = [self.ps(), self.ps()]
            for j in range(4):
                for p_ in range(2):
                    sl = slice(j * 64, (j + 1) * 64)
                    k.mm(pu[p_][P2[p_], sl], P[P2[p_], sl], X1[P2[p_], sl])
                    k.mm(pw_[p_][P2[p_], sl], P[P2[p_], sl], X2[P2[p_], sl])
            evac2(U, pu, "act", "act")
            evac2(Wm, pw_, "dve", "dve")
            yield
            k.tt(v4(Dm), KTM4[:, c0:c0 + 2, :, :], bc4(KHS3[:, c0:c0 + 2, :]), ALU.mult, eng=GP)
            pM = [self.ps(), self.ps()]
            pB = [self.ps(), self.ps()]
            Mraw, Braw = A(9 + 11 * sbi), A(10 + 11 * sbi)
            for j in range(4):
                for p_ in range(2):
                    sl = slice(j * 64, (j + 1) * 64)
                    k.mm(pM[p_][P2[p_], sl], Wm[P2[p_], sl], Dm[P2[p_], sl])
                    k.mm(pB[p_][P2[p_], sl], Dm[P2[p_], sl], U[P2[p_], sl])
            evac2(Mraw, pM, "act", "act")
            evac2(Braw, pB, "dve", "dve")
            yield
            for tl in range(2):
                for cc in range(2):
                    sl = slice((cc * 2 + tl) * 64, (cc * 2 + tl + 1) * 64)
                    k.stt(MT[:, tl, c0 + cc, :], ID2, eGL[:, tl, c0 + cc:c0 + cc + 1], Mraw[:, sl], ALU.mult, ALU.subtract)
                k.cp(BC[:, tl, c0:c0 + 2, :], v4(Braw)[:, :, tl, :], eng=GP)
            pQ = [self.ps(), self.ps()]
            pOL = [self.ps(), self.ps()]
            for j in range(4):
                for p_ in range(2):
                    sl = slice(j * 64, (j + 1) * 64)
                    k.mm(pQ[p_][P2[p_], sl], Wm[P2[p_], sl], LT[P2[p_], sl])
                    k.mm(pOL[p_][P2[p_], sl], U[P2[p_], sl], LT[P2[p_], sl])
            evac2(Mraw, pQ, "act", "act")
            evac2(Braw, pOL, "dve", "dve")
            yield
            for tl in range(2):
                qv = QpT[:, tl, c0 * 64:c0 * 64 + 128].rearrange("p (c t) -> p c t", t=64)
                gv = QGx[:, tl, c0 * 64:c0 * 64 + 128].rearrange("p (c t) -> p c t", t=64)
                k.tt(qv, gv, v4(Mraw)[:, :, tl, :], ALU.subtract, eng=GP)
                k.cp(OL[:, tl, c0 * 64:c0 * 64 + 128].rearrange("p (c t) -> p c t", t=64), v4(Braw)[:, :, tl, :], eng=GP)
        gens = [solve(sbi) for sbi in range(NCH // 2)]
        while gens:
            for g_ in list(gens):
                try:
                    next(g_)
                except StopIteration:
                    gens.remove(g_)
        if CUTG == 7:
            return
        if not is_s:
            if blk == 0:
                k.ms(SALL[:, :, 0, :], 0.0)
            else:
                k.cp(SALL[:, :, 0, :], SALL[:, :, NCH, :])
        else:
            k.dma("sp", SIN, self.d_st["gdn"][l][:, :, s0:s0 + NCH, :])
        SB = (lambda tl, c: SIN[:, tl, c, :]) if is_s else (lambda tl, c: SALL[:, tl, c, :])
        for c in range(NCH):
            pS = [self.ps(), self.ps()]
            for p_ in range(2):
                for tl in range(2):
                    k.mm(pS[p_][P2[p_], tl * 64:(tl + 1) * 64], MT[P2[p_], tl, c, :], SB(tl, c)[P2[p_], :])
                k.tt(SALL[P2[p_], :, c + 1, :], pS[p_][P2[p_], 0:128].rearrange("p (a v) -> p a v", v=64), BC[P2[p_], :, c, :], ALU.add)
        if is_s:
            k.dma("sp", self.o_ss["gdn"][l][:, :, s0:s0 + NCH, :], SALL[:, :, 1:NCH + 1, :], is_out=True)
        elif blk == NPB - 1:
            k.dma("sp", self.o_ps["gdn"][l], SALL[:, :, NCH, :], is_out=True)
        if CUTG == 8:
            return
        pO = [self.ps(), self.ps()]
        for p_ in range(2):
            for c in range(NCH):
                for tl in range(2):
                    k.mm(pO[p_][P2[p_], (tl * NCH + c) * 64:(tl * NCH + c + 1) * 64], SB(tl, c)[P2[p_], :], QpT[P2[p_], tl, c * 64:(c + 1) * 64])
            k.tt(T2[P2[p_], 0:2 * MB], pO[p_][P2[p_], 0:2 * MB], OL.rearrange("p a t -> p (a t)")[P2[p_]], ALU.add)
        if _os.environ.get('KDBG', '') == 'gdn' and blk == 0 and l == 0:
            for nm_, ap_ in (("GC", GC[0:4, 0:MB]), ("BET", BET[0:4, 0:MB]), ("GCOL", GCOL[:, 0:8]), ("BCOL", BCOL[:, 0:8]),
                             ("KHS", KHS[:, 0:8]), ("Qp", Qp), ("Kp", Kp), ("KTM", KTM[:, 0:512]), ("VTM", VTM[:, 0:512]),
                             ("MT", MT), ("BC", BC), ("QpT", QpT), ("OL", OL), ("SALL", SALL), ("T2o", T2[:, 0:512]), ("QGx", QGx)):
                k.dbg(nm_, ap_)
        self.o_post(l, m, T2, GATE, OB, OSQ, T1, T2, self.pv(l, PV_NORM + 2), Wo, is_s, t0, N, s0)

    def build(self, skip_mixers=(), stop_after=None, no_ffn=False, nblk=None):
        self.skip_mixers = set(skip_mixers)
        self.nblk = nblk
        self.alloc()
        self.setup()
        self.mod_group(0, 0)
        self.initial_h()
        for l in range(DEPTH):
            if no_ffn:
                self.mod_group(l, 1)
                self.ln_pass(l, 0)
            else:
                self.ffn(l, 0, mid_hook=lambda: self.mod_group(l, 1))
            if stop_after == ("ffn1", l):
                break
            self.mix_layer(l, mid_hook=lambda: self.mod_group(l, 2))
            if stop_after == ("mix", l):
                break
            hook = (lambda: self.mod_group(l + 1, 0)) if l + 1 < DEPTH else None
            self.ffn(l, 1, mid_hook=hook)
        self.store_y()
        nw = self.k.S.emit()
        self.nwaits = nw
        return self.nc


def _const_pack():
    c = np.zeros((128, CST_N), np.float32)
    c[:, CST_IDENT:CST_IDENT + 128] = np.eye(128, dtype=np.float32)
    s = np.arange(64)[:, None]
    t = np.arange(64)[None, :]
    for hf in range(2):
        c[64 * hf:64 * hf + 64, CST_MINCL:CST_MINCL + 64] = (s <= t)
        c[64 * hf:64 * hf + 64, CST_MSTR:CST_MSTR + 64] = (s < t)
        c[64 * hf:64 * hf + 64, CST_MSTRT:CST_MSTRT + 64] = (t < s)
    c[:, CST_ID2:CST_ID2 + 64] = np.tile(np.eye(64, dtype=np.float32), (2, 1))
    r = np.ones((MB,), np.float32)
    r[0::64] = 0.0
    c[:, CST_RESET:CST_RESET + MB] = r[None, :]
    for h in range(4):
        c[h, CST_SEL + h * 64:CST_SEL + (h + 1) * 64] = 1.0
        tl, p = divmod(h, 2)
        c[h, CST_SELF + tl * 128 + p * 64:CST_SELF + tl * 128 + (p + 1) * 64] = 1.0
    c[:, CST_ONES:CST_ONES + 128] = 1.0
    bo = np.zeros((128, 128), np.float32)
    bo[0:64, 0:64] = 1.0
    bo[64:128, 64:128] = 1.0
    c[:, CST_BONES:CST_BONES + 128] = bo
    pm = np.zeros((128, 128), np.float32)
    for m_ in range(128):
        kk = (m_ // 64) * 64 + ((m_ % 64) + 32) % 64
        pm[kk, m_] = 1.0
    c[:, CST_PM:CST_PM + 128] = pm
    for p_ in range(2):
        for tl in range(2):
            c[2 * tl + p_, CST_SELHP + p_ * 2 + tl] = 1.0
    return c


def _rot_tables():
    half = 32
    inv = (np.float32(10000.0) ** (-np.arange(half, dtype=np.float32) / np.float32(half))).astype(np.float32)
    pos = np.concatenate([np.arange(TP, dtype=np.float32),
                          np.tile(np.float32(16384.0) + np.arange(TS, dtype=np.float32), NSQ)]).astype(np.float32)
    ang = (pos[:, None] * inv[None, :]).astype(np.float32)
    cos = np.cos(ang).astype(np.float32).T
    sin = np.sin(ang).astype(np.float32).T
    cosT = np.zeros((128, NTOK), np.float32)
    sinT = np.zeros((128, NTOK), np.float32)
    for p in range(128):
        d = p % 64
        i = d % 32
        cosT[p] = cos[i]
        sinT[p] = -sin[i] if d < 32 else sin[i]
    return cosT, sinT


def _gret_tables():
    heads = np.arange(4, dtype=np.float32)
    lg = np.log(np.float32(1.0) - np.float32(2.0) ** (np.float32(-5.0) - heads)).astype(np.float32)
    g = np.zeros((2, 128, 2, MB), np.float32)
    j = np.arange(MB) % 64
    for tl in range(2):
        for p in range(128):
            h = 2 * tl + p // 64
            g[0, p, tl, :] = (j + 1).astype(np.float32) * lg[h]
            g[1, p, tl, :] = (np.minimum(j, TS - 1) + 1).astype(np.float32) * lg[h]
    return g


def _tile_w(w, cols):
    out = np.zeros((128, KC, 128), np.float32)
    cols = np.asarray(cols)
    valid = cols >= 0
    sub = w[:, cols[valid]].reshape(KC, 128, -1)
    out[:, :, np.nonzero(valid)[0]] = np.transpose(sub, (1, 0, 2))
    return out.reshape(128, KC * 128)


def _win_tiles(w):
    tiles = []
    r = lambda a, n: list(range(a, a + n))
    pad = lambda lst: lst + [-1] * (128 - len(lst))
    for base in (C_RQ, C_RK, C_RV, C_RG):
        for tl in range(2):
            tiles.append(r(base + tl * 128, 128))
    for base in (C_AQ, C_AK):
        for tl in range(2):
            cols = []
            for p in range(2):
                h = 2 * tl + p
                cols += r(base + h * 32, 32) + [-1] * 32
            tiles.append(cols)
    for base in (C_AV, C_AG):
        for tl in range(2):
            tiles.append(r(base + tl * 128, 128))
    tiles.append(pad(r(C_ALR, 16)))
    for base in (C_HQ, C_HF, C_HI, C_HG):
        for tl in range(2):
            tiles.append(r(base + tl * 128, 128))
    for base in (C_DQ, C_DK, C_DV, C_DG):
        for tl in range(2):
            tiles.append(r(base + tl * 128, 128))
    tiles.append(pad(r(C_DB, 4)))
    tiles.append(pad(r(C_DA, 4)))
    assert len(tiles) == NWT
    return np.stack([_tile_w(w, c) for c in tiles], 0)


def _state_in(st, dk):
    out = np.zeros((DEPTH, 2, 64, 2, NSQ, 64), np.float32)
    x = st.reshape(DEPTH, NSQ, 2, 2, dk, 64)
    out[:, :, 0:dk] = np.transpose(x, (0, 3, 4, 2, 1, 5))
    return out.reshape(DEPTH, 128, 2, NSQ, 64)


def _state_out_s(o, dk):
    x = o.reshape(DEPTH, 2, 64, 2, NSQ, 64)[:, :, 0:dk]
    return np.ascontiguousarray(np.transpose(x, (0, 4, 3, 1, 2, 5)).reshape(DEPTH, NSQ, 4, dk, 64))


def _state_out_p(o, dk):
    x = o.reshape(DEPTH, 2, 64, 2, 64)[:, :, 0:dk]
    return np.ascontiguousarray(np.transpose(x, (0, 3, 1, 2, 4)).reshape(DEPTH, 4, dk, 64))


_NC_CACHE = {}


def _get_nc(key=(), **kw):
    if key not in _NC_CACHE:
        p = Prog4()
        nc = p.build(**kw)
        _NC_CACHE[key] = (nc, p)
    return _NC_CACHE[key]


def _prepare_inputs(x_prompt, x_sample, state_ret, state_gla, state_hgrn, state_gdn, state_gdn_conv,
                    c_prompt, c_sample, ada_w, ada_b, ln_g, ln_b, ffn1_wi, ffn1_wo, ffn2_wi, ffn2_wo,
                    w_in, gla_wg, gla_bg, hg_lb, gdn_conv, gdn_a_log, gdn_dt_bias,
                    gla_norm, hg_norm, gdn_norm, w_out):
    f = lambda a: np.ascontiguousarray(np.asarray(a, dtype=np.float32))
    shared = {}
    for nm, wi in (("wi1", ffn1_wi), ("wi2", ffn2_wi)):
        wi = f(wi)
        shared[nm] = np.stack([np.stack([_tile_w(wi[l], list(range(c * 128, (c + 1) * 128))) for c in range(2 * NJ)], 0)
                               for l in range(DEPTH)], 0)
    shared["wo1"] = f(ffn1_wo)
    shared["wo2"] = f(ffn2_wo)
    w_in = f(w_in)
    shared["win"] = np.stack([_win_tiles(w_in[l]) for l in range(DEPTH)], 0)
    shared["wout"] = f(w_out).reshape(DEPTH, 8, 128, 1024)
    ada_w = f(ada_w)
    shared["adaw"] = np.stack([np.stack([_tile_w(ada_w[l], list(range(c * 128, (c + 1) * 128))) for c in range(72)], 0)
                               for l in range(DEPTH)], 0)
    pv = np.zeros((DEPTH, 128, NPV), np.float32)
    ada_b = f(ada_b); ln_g = f(ln_g); ln_b = f(ln_b); gdn_conv = f(gdn_conv); gla_bg = f(gla_bg); hg_lb = f(hg_lb)
    gla_norm = f(gla_norm); hg_norm = f(hg_norm); gdn_norm = f(gdn_norm); gdn_a_log = f(gdn_a_log); gdn_dt_bias = f(gdn_dt_bias)
    for l in range(DEPTH):
        pv[l, :, PV_ADAB:PV_ADAB + 72] = ada_b[l].reshape(72, 128).T
        for i in range(3):
            pv[l, :, PV_LNG + i * 8:PV_LNG + (i + 1) * 8] = ln_g[l, i].reshape(8, 128).T
            pv[l, :, PV_LNB + i * 8:PV_LNB + (i + 1) * 8] = ln_b[l, i].reshape(8, 128).T
        cw = gdn_conv[l].reshape(4, 6, 128)
        pv[l, :, PV_CONV:PV_CONV + 24] = np.transpose(cw, (2, 1, 0)).reshape(128, 24)
        bg = np.zeros((2, 2, 64), np.float32)
        bg[:, :, 0:32] = gla_bg[l].reshape(2, 2, 32)
        pv[l, :, PV_BG:PV_BG + 2] = bg.reshape(2, 128).T
        pv[l, :, PV_NORM + 0] = np.tile(gla_norm[l], 2)
        pv[l, :, PV_NORM + 1] = np.tile(hg_norm[l], 2)
        pv[l, :, PV_NORM + 2] = np.tile(gdn_norm[l], 2)
        pv[l, :, PV_LB0:PV_LB0 + 2] = hg_lb[0].reshape(2, 128).T
        pv[l, :, PV_LB1:PV_LB1 + 2] = hg_lb[1].reshape(2, 128).T
        pv[l, 0:4, PV_ALOG] = gdn_a_log[l]
        pv[l, 0:4, PV_DTB] = gdn_dt_bias[l]
    shared["pvec"] = pv
    gla_wg = f(gla_wg)
    wgp = np.zeros((DEPTH, 16, 2, 2, 64), np.float32)
    wgp[:, :, :, :, 0:32] = gla_wg.reshape(DEPTH, 16, 2, 2, 32)
    shared["wgpad"] = wgp.reshape(DEPTH, 16, 256)
    cosT, sinT = _rot_tables()
    shared["cosT"] = cosT
    shared["sinT"] = sinT
    shared["cst"] = _const_pack()
    shared["gret"] = _gret_tables()
    x_prompt = f(x_prompt); x_sample = f(x_sample); c_prompt = f(c_prompt); c_sample = f(c_sample)
    sts = {"ret": (f(state_ret), 64), "gla": (f(state_gla), 32), "hg": (f(state_hgrn), 64), "gdn": (f(state_gdn), 64)}
    state_gdn_conv = f(state_gdn_conv)
    in_maps = []
    for c in range(NCORES):
        d = dict(shared)
        sq = slice(c * NSQ, (c + 1) * NSQ)
        xs = x_sample[sq].reshape(NSQ * TS, D)
        d["xT"] = np.ascontiguousarray(np.concatenate([x_prompt[c], xs], 0).T)
        d["cT"] = np.ascontiguousarray(np.concatenate([c_prompt[c:c + 1], c_sample[sq]], 0).T)
        for nm, (st, dk) in sts.items():
            d["st_" + nm] = _state_in(st[:, sq], dk)
        cvs = state_gdn_conv[:, sq].reshape(DEPTH, NSQ, 3, 6, 128)
        d["st_conv"] = np.ascontiguousarray(np.transpose(cvs, (0, 4, 3, 1, 2)))
        in_maps.append(d)
    return in_maps


def _assemble(results):
    y_p = np.zeros((NCORES, TP, D), np.float32)
    y_s = np.zeros((NCORES * NSQ, TS, D), np.float32)
    dks = {"ret": 64, "gla": 32, "hg": 64, "gdn": 64}
    p_st = {nm: np.zeros((DEPTH, NCORES, 4, dk, 64), np.float32) for nm, dk in dks.items()}
    s_st = {nm: np.zeros((DEPTH, NCORES * NSQ, 4, dk, 64), np.float32) for nm, dk in dks.items()}
    p_conv = np.zeros((DEPTH, NCORES, 3, 768), np.float32)
    s_conv = np.zeros((DEPTH, NCORES * NSQ, 3, 768), np.float32)
    for c, r in enumerate(results):
        yT = np.asarray(r["yT"])
        y_p[c] = yT[:, 0:TP].T
        y_s[c * NSQ:(c + 1) * NSQ] = yT[:, TP:].T.reshape(NSQ, TS, D)
        for nm, dk in dks.items():
            p_st[nm][:, c] = _state_out_p(np.asarray(r["ops_" + nm]), dk)
            s_st[nm][:, c * NSQ:(c + 1) * NSQ] = _state_out_s(np.asarray(r["oss_" + nm]), dk)
        pc = np.asarray(r["opconv"])
        p_conv[:, c] = np.transpose(pc, (0, 3, 2, 1)).reshape(DEPTH, 3, 768)
        sc = np.asarray(r["osconv"])
        s_conv[:, c * NSQ:(c + 1) * NSQ] = np.transpose(sc, (0, 3, 4, 2, 1)).reshape(DEPTH, NSQ, 3, 768)
    return (y_p, y_s, p_st["ret"], p_st["gla"], p_st["hg"], p_st["gdn"], p_conv,
            s_st["ret"], s_st["gla"], s_st["hg"], s_st["gdn"], s_conv)


def kernel(**inputs):
    in_maps = _prepare_inputs(**inputs)
    nc, _ = _get_nc()
    res = run_bass_kernel_spmd(nc, in_maps, core_ids=list(range(NCORES)))
    return _assemble(res.results)
```

```python
import numpy as np
from contextlib import ExitStack
import concourse.bass as bass
import concourse.mybir as mybir
from concourse.bass_utils import run_bass_kernel_spmd

F32 = mybir.dt.float32
BF16 = mybir.dt.bfloat16
AF = mybir.ActivationFunctionType
ALU = mybir.AluOpType

NCORES = 8
D = 1024
KC = 8
DFF = 2816
NJ = 22
TP = 2048
NSQ = 16
TS = 4
NTOK = TP + NSQ * TS
DEPTH = 2
ALPHA = float((2.0 * DEPTH) ** 0.25)
LN_EPS = 1e-5
RMS_EPS = 1e-6
TBS = [(0, 512), (512, 512), (1024, 512), (1536, 512), (2048, 64)]
JG = [list(range(0, 8)), list(range(8, 15)), list(range(15, 22))]
MB = 256
NCH = MB // 64
NPB = TP // MB
NSB = NSQ // NCH
NW = 12
NDS = 8
import os as _os
CUT = int(_os.environ.get('KCUT', '0'))
CUTN = int(_os.environ.get('KCUTN', '-1'))
STRICT = int(_os.environ.get('KSTRICT', '1'))
CUTG = int(_os.environ.get('KCUTG', '0'))
ATTACH = int(_os.environ.get('KATTACH', '1'))
GP = _os.environ.get('KGP', 'pool')

C_RQ, C_RK, C_RV, C_RG = 0, 256, 512, 768
C_AQ, C_AK, C_AV, C_ALR, C_AG = 1024, 1152, 1280, 1536, 1552
C_HQ, C_HF, C_HI, C_HG = 1808, 2064, 2320, 2576
C_DQ, C_DK, C_DV, C_DB, C_DA, C_DG = 2832, 3088, 3344, 3600, 3604, 3608
WT = {}
_names = (["rq0", "rq1", "rk0", "rk1", "rv0", "rv1", "rg0", "rg1"] +
          ["aq0", "aq1", "ak0", "ak1", "av0", "av1", "ag0", "ag1", "alr"] +
          ["hq0", "hq1", "hf0", "hf1", "hi0", "hi1", "hg0", "hg1"] +
          ["dq0", "dq1", "dk0", "dk1", "dv0", "dv1", "dg0", "dg1", "db", "da"])
for _i, _n in enumerate(_names):
    WT[_n] = _i
NWT = len(_names)
PV_ADAB, PV_LNG, PV_LNB, PV_CONV, PV_BG, PV_NORM, PV_LB0, PV_LB1, PV_ALOG, PV_DTB, NPV = 0, 72, 96, 120, 144, 146, 149, 151, 153, 154, 160


def _esize(dt):
    return mybir.dt.size(dt)


class _Op:
    __slots__ = ("eng", "fn", "deps", "dmaq", "signal", "sigval", "dslot", "dval", "clock", "prio")

    def __init__(self, eng, fn, deps, dmaq):
        self.eng = eng
        self.fn = fn
        self.deps = deps
        self.dmaq = dmaq
        self.signal = False
        self.sigval = 0
        self.dslot = 0
        self.dval = 0
        self.clock = None
        self.prio = ()


class Sched:
    BUCK = 1024

    def __init__(self, nc, es):
        self.nc = nc
        self.engs = {"pe": nc.tensor, "act": nc.scalar, "dve": nc.vector, "pool": nc.gpsimd, "sp": nc.sync}
        self.ops = []
        self.buckets = {}
        self.mloc = {}
        self.sem = {e: es.enter_context(nc.semaphore("s_" + e)) for e in self.engs}
        self.dsem = {q: [es.enter_context(nc.semaphore("d_%s%d" % (q, i))) for i in range(NDS)]
                     for q in ("sp", "act", "pool")}
        self.out_dmas = []

    def _box(self, ap):
        sp = str(ap.space)
        if sp not in ("SB", "PSUM"):
            return None
        t = ap.tensor
        key = t.name
        info = self.mloc.get(key)
        if info is None:
            ml = self.nc.lookup_mloc(t)
            base = int(ml.addr)
            if sp == "PSUM":
                base += int(ml.bank) * 2048
            info = base
            self.mloc[key] = info
        shape = t.shape
        F = 1
        for s in shape[1:]:
            F *= int(s)
        off = int(ap.offset)
        p0 = off // F
        f0 = off % F
        dims = ap.ap
        pc = int(dims[0][1])
        lo = f0
        hi = f0
        for (st, cnt) in dims[1:]:
            ext = int(st) * (int(cnt) - 1)
            if ext < 0:
                lo += ext
            else:
                hi += ext
        hi += 1
        es_ = _esize(ap.dtype)
        if sp == "PSUM" and STRICT:
            b0 = ((info + lo * es_) // 2048) * 2048
            b1 = ((info + hi * es_ - 1) // 2048 + 1) * 2048
            return (sp, (p0 // 32) * 32, ((p0 + pc + 31) // 32) * 32, b0, b1)
        return (sp, p0, p0 + pc, info + lo * es_, info + hi * es_)

    @staticmethod
    def _ov(a, b):
        return a[1] < b[2] and b[1] < a[2] and a[3] < b[4] and b[3] < a[4]

    @staticmethod
    def _cov(a, b):
        return a[1] <= b[1] and a[2] >= b[2] and a[3] <= b[3] and a[4] >= b[4]

    def _keys(self, box):
        return [(box[0], k) for k in range(box[3] // self.BUCK, (box[4] - 1) // self.BUCK + 1)]

    def add(self, eng, fn, reads, writes, dmaq=None, prio=()):
        idx = len(self.ops)
        raw = set()
        praw = set()
        for ap in prio:
            b = self._box(ap)
            if b is not None:
                for k in self._keys(b):
                    for rec in self.buckets.get(k, ()):
                        if rec[2] and self._ov(b, rec[0]):
                            praw.add(rec[1])
        oth = set()
        rboxes = []
        wboxes = []
        for ap in reads:
            b = self._box(ap)
            if b is not None:
                rboxes.append(b)
        for ap in writes:
            b = self._box(ap)
            if b is not None:
                wboxes.append(b)
        for b in rboxes:
            for k in self._keys(b):
                for rec in self.buckets.get(k, ()):
                    if rec[2] and self._ov(b, rec[0]):
                        raw.add(rec[1])
        for b in wboxes:
            for k in self._keys(b):
                lst = self.buckets.get(k)
                if not lst:
                    continue
                keep = []
                for rec in lst:
                    if self._ov(b, rec[0]):
                        oth.add(rec[1])
                        if self._cov(b, rec[0]):
                            continue
                    keep.append(rec)
                self.buckets[k] = keep
        myeng = eng
        for b in rboxes:
            rec = (b, idx, False, myeng, dmaq is not None)
            for k in self._keys(b):
                lst = self.buckets.setdefault(k, [])
                for i2, r2 in enumerate(lst):
                    if (not r2[2]) and r2[0] == b and r2[3] == myeng and (not r2[4]) and dmaq is None:
                        lst[i2] = rec
                        break
                else:
                    lst.append(rec)
        for b in wboxes:
            rec = (b, idx, True, myeng, dmaq is not None)
            for k in self._keys(b):
                self.buckets.setdefault(k, []).append(rec)
        deps = []
        for d in raw | oth:
            dop = self.ops[d]
            if dop.dmaq is None and dmaq is None and dop.eng == eng:
                if eng == "pe":
                    continue
                if d not in raw and not STRICT:
                    continue
            deps.append(d)
            if dop.dmaq is None:
                dop.signal = True
        op = _Op(eng, fn, deps, dmaq)
        op.prio = praw
        self.ops.append(op)
        return idx

    def emit(self):
        nc = self.nc
        cnt = {e: 0 for e in self.engs}
        for op in self.ops:
            if op.dmaq is None and op.signal:
                cnt[op.eng] += 1
                op.sigval = cnt[op.eng]
        seen = {e: {} for e in self.engs}
        self.opidx = {id(o): i for i, o in enumerate(self.ops)}
        dcount = {q: 0 for q in self.dsem}
        dlast = {q: [None] * NDS for q in self.dsem}
        nwaits = 0
        for op in self.ops:
            e = op.eng
            eng = self.engs[e]
            sn = seen[e]
            needs = []
            for d in op.deps:
                dop = self.ops[d]
                if dop.dmaq is None:
                    needs.append((("c", dop.eng), dop.sigval, dop))
                else:
                    needs.append((("d", dop.dmaq, dop.dslot), dop.dval, dop))
            if op.dmaq is not None:
                q = op.dmaq
                slot = dcount[q] % NDS
                dcount[q] += 1
                prev = dlast[q][slot]
                op.dslot = slot
                op.dval = (prev.dval if prev is not None else 0) + 16
                if prev is not None:
                    needs.append((("d", q, slot), prev.dval, prev))
                dlast[q][slot] = op
            needs.sort(key=lambda x: -self.opidx[id(x[2])])
            pending = []
            for (key, val, dop) in needs:
                if sn.get(key, 0) >= val:
                    continue
                semh = self.sem[key[1]] if key[0] == "c" else self.dsem[key[1]][key[2]]
                pending.append((semh, val, self.opidx[id(dop)] in op.prio))
                nwaits += 1
                for k2, v2 in dop.clock.items():
                    if sn.get(k2, 0) < v2:
                        sn[k2] = v2
            attach = None
            if pending and ATTACH:
                pi = [i for i, x in enumerate(pending) if x[2]]
                attach = pending.pop(pi[-1] if pi else -1)
            for (semh, val, _) in pending:
                eng.wait_ge(semh, val)
            ins = op.fn(eng)
            if attach is not None:
                ins._wait_ge(attach[0], attach[1])
            clock = dict(sn)
            if op.dmaq is not None:
                ins.then_inc(self.dsem[op.dmaq][op.dslot], 16)
                clock[("d", op.dmaq, op.dslot)] = op.dval
            elif op.signal:
                ins.then_inc(self.sem[e], 1)
                clock[("c", e)] = op.sigval
            op.clock = clock
            op.fn = None
        sp = self.engs["sp"]
        sn = seen["sp"]
        for d in self.out_dmas:
            dop = self.ops[d]
            key = ("d", dop.dmaq, dop.dslot)
            if sn.get(key, 0) >= dop.dval:
                continue
            sp.wait_ge(self.dsem[dop.dmaq][dop.dslot], dop.dval)
            sn[key] = dop.dval
        return nwaits


class KB:
    def __init__(self, nc, es):
        self.nc = nc
        self.es = es
        self.S = Sched(nc, es)
        self.dbg_outs = []

    def sb(self, name, shape, dt):
        return self.es.enter_context(self.nc.sbuf_tensor("sb_" + name, shape, dt))

    def psum(self, name, shape, dt):
        return self.es.enter_context(self.nc.psum_tensor(name, shape, dt))

    def dram_in(self, name, shape):
        return self.nc.dram_tensor(name, list(shape), F32, kind="ExternalInput").ap()

    def dram_out(self, name, shape):
        return self.nc.dram_tensor(name, list(shape), F32, kind="ExternalOutput").ap()

    def mm(self, out, lhsT, rhs, start=True, stop=True):
        self.S.add("pe", lambda e: e.matmul(out, lhsT=lhsT, rhs=rhs, start=start, stop=stop), [lhsT, rhs], [out], prio=[lhsT])

    def tr(self, out, in_, ident):
        self.S.add("pe", lambda e: e.transpose(out, in_, ident), [in_, ident], [out], prio=[in_])

    def act(self, out, in_, func, bias=None, scale=None, eng="act"):
        kw = {}
        reads = [in_]
        if bias is not None:
            kw["bias"] = bias
            if not isinstance(bias, (int, float)):
                reads.append(bias)
        if scale is not None:
            kw["scale"] = scale
            if not isinstance(scale, (int, float)):
                reads.append(scale)
        self.S.add(eng, lambda e: e.activation(out=out, in_=in_, func=func, **kw), reads, [out])

    def tt(self, out, a, b, op, eng="dve"):
        self.S.add(eng, lambda e: e.tensor_tensor(out=out, in0=a, in1=b, op=op), [a, b], [out])

    def ts(self, out, a, s1, op0, s2=None, op1=None, eng="dve"):
        reads = [a]
        if not isinstance(s1, (int, float)):
            reads.append(s1)
        if s2 is not None and not isinstance(s2, (int, float)):
            reads.append(s2)
        if s2 is None:
            self.S.add(eng, lambda e: e.tensor_scalar(out=out, in0=a, scalar1=s1, scalar2=None, op0=op0), reads, [out])
        else:
            self.S.add(eng, lambda e: e.tensor_scalar(out=out, in0=a, scalar1=s1, scalar2=s2, op0=op0, op1=op1), reads, [out])

    def stt(self, out, a, s, b, op0, op1):
        reads = [a, b]
        if not isinstance(s, (int, float)):
            reads.append(s)
        self.S.add("dve", lambda e: e.scalar_tensor_tensor(out=out, in0=a, scalar=s, in1=b, op0=op0, op1=op1), reads, [out])

    def cp(self, out, in_, eng="dve"):
        if eng == "act":
            fn_ = AF.Identity if _os.environ.get('KVAR3', '') == 'ident' else AF.Copy
            self.S.add("act", lambda e: e.activation(out=out, in_=in_, func=fn_), [in_], [out])
        else:
            self.S.add(eng, lambda e: e.tensor_copy(out=out, in_=in_), [in_], [out])

    def ms(self, ap, val, eng="dve"):
        self.S.add(eng, lambda e: e.memset(ap, val), [], [ap])

    def scan(self, out, d0, d1, init, op0, op1):
        reads = [d0, d1]
        self.S.add("dve", lambda e: e.tensor_tensor_scan(out=out, data0=d0, data1=d1, initial=init, op0=op0, op1=op1), reads, [out])

    def recip(self, out, in_):
        self.S.add("dve", lambda e: e.reciprocal(out=out, in_=in_), [in_], [out])

    def dma(self, q, out, in_, is_out=False):
        idx = self.S.add(q, lambda e: e.dma_start(out=out, in_=in_), [in_], [out], dmaq=q)
        if is_out:
            self.S.out_dmas.append(idx)
        return idx

    def dbg(self, name, ap):
        shape = [int(s) for s in ap.shape]
        o = self.dram_out("dbg_" + name, shape)
        self.dma("sp", o, ap, is_out=True)
        self.dbg_outs.append(("dbg_" + name, shape))


class Prog:
    def __init__(self, debug=None):
        self.debug = debug or set()
        self.nc = bass.Bass("TRN2", target_bir_lowering=False)
        self.es = ExitStack()
        self.k = KB(self.nc, self.es)
        self.wcnt = 0
        self.pscnt = 0

    def alloc(self):
        k = self.k
        self.d_xT = k.dram_in("xT", [D, NTOK])
        self.d_cT = k.dram_in("cT", [D, 17])
        self.d_wi = [k.dram_in("wi1", [DEPTH, 2 * NJ, 128, 1024]), k.dram_in("wi2", [DEPTH, 2 * NJ, 128, 1024])]
        self.d_wo = [k.dram_in("wo1", [DEPTH, DFF, D]), k.dram_in("wo2", [DEPTH, DFF, D])]
        self.d_win = k.dram_in("win", [DEPTH, NWT, 128, 1024])
        self.d_wout = k.dram_in("wout", [DEPTH, 8, 128, 1024])
        self.d_adaw = k.dram_in("adaw", [DEPTH, 72, 128, 1024])
        self.d_pvec = k.dram_in("pvec", [DEPTH, 128, NPV])
        self.d_wg = k.dram_in("wgpad", [DEPTH, 16, 256])
        self.d_cos = k.dram_in("cosT", [128, NTOK])
        self.d_sin = k.dram_in("sinT", [128, NTOK])
        self.d_cst = k.dram_in("cst", [128, CST_N])
        self.d_gret = k.dram_in("gret", [2, 128, 2, MB])
        self.d_st = {}
        for nm in ("ret", "gla", "hg", "gdn"):
            self.d_st[nm] = k.dram_in("st_" + nm, [DEPTH, 128, 2, NSQ, 64])
        self.d_stconv = k.dram_in("st_conv", [DEPTH, 128, 6, NSQ, 3])
        self.o_yT = k.dram_out("yT", [D, NTOK])
        self.o_ps = {}
        self.o_ss = {}
        for nm in ("ret", "gla", "hg", "gdn"):
            self.o_ps[nm] = k.dram_out("ops_" + nm, [DEPTH, 128, 2, 64])
            self.o_ss[nm] = k.dram_out("oss_" + nm, [DEPTH, 128, 2, NSQ, 64])
        self.o_pconv = k.dram_out("opconv", [DEPTH, 128, 6, 3])
        self.o_sconv = k.dram_out("osconv", [DEPTH, 128, 6, NSQ, 3])
        self.xres = k.sb("xres", [128, KC, NTOK], F32)
        self.hT = k.sb("hT", [128, KC, NTOK], BF16)
        self.gbig = k.sb("gbig", [128, 8 * NTOK], BF16)
        self.wpool = [k.sb("w%d" % i, [128, 1024], BF16) for i in range(NW)]
        self.cst = k.sb("cst", [128, CST_N], F32)
        self.cstb = k.sb("cstb", [128, CSTB_N], BF16)
        self.pvec = [k.sb("pvec%d" % l, [128, NPV], F32) for l in range(DEPTH)]
        self.cTs = k.sb("cTs", [128, KC, 17], F32)
        self.csl = k.sb("csl", [128, KC, 17], BF16)
        self.modg = k.sb("modg", [128, 24, 17], F32)
        self.cF = [[k.sb("cF%d%d" % (l, i), [128, KC, 17], F32) for i in range(3)] for l in range(DEPTH)]
        self.hs = [[k.sb("hs%d%d" % (l, i), [128, KC, 17], F32) for i in range(3)] for l in range(DEPTH)]
        self.hb = [[k.sb("hb%d%d" % (l, i), [128, KC, 17], F32) for i in range(3)] for l in range(DEPTH)]
        self.xs = [k.sb("xs%d" % l, [128, 3, KC], F32) for l in range(DEPTH)]
        self.xb = [k.sb("xb%d" % l, [128, 3, KC], F32) for l in range(DEPTH)]
        self.wg = k.sb("wg", [16, 256], F32)
        self.wgb = k.sb("wgb", [16, 256], BF16)
        self.small = k.sb("small", [128, 64], F32)
        self.arena2 = k.sb("arena2", [128, A2_BYTES // 4], F32)
        self.psb = [k.psum("ps%d" % i, [128, 512], F32) for i in range(8)]

    def carve(self, region, off, nbytes, dt, parts=128):
        if region == "g":
            base = self.gbig
            es = 2
            tot = 8 * NTOK * 2
        else:
            base = self.arena2
            es = 4
            tot = A2_BYTES
        assert off % 4 == 0 and nbytes % 4 == 0 and off + nbytes <= tot, (region, off, nbytes, tot)
        ap = base[0:parts, off // es:(off + nbytes) // es]
        if dt == F32 and es == 2:
            ap = ap.bitcast(F32)
        elif dt == BF16 and es == 4:
            ap = ap.bitcast(BF16)
        return ap

    def ps(self):
        p = self.psb[self.pscnt % 6]
        self.pscnt += 1
        return p

    def wload(self, dram_ap):
        w = self.wpool[self.wcnt % NW]
        self.wcnt += 1
        self.k.dma("pool", w[:, :], dram_ap)
        return w

    def pv(self, l, col, n=1):
        return self.pvec[l][:, col:col + n]


CST_IDENT = 0
CST_MINCL = 128
CST_MSTR = 192
CST_MSTRT = 256
CST_ID2 = 320
CST_RESET = 384
CST_SEL = 384 + MB
CST_SELF = CST_SEL + 256
CST_ONES = CST_SELF + 256
CST_BONES = CST_ONES + 128
CST_PM = CST_BONES + 128
CST_SELHP = CST_PM + 128
CST_N = CST_SELHP + 4
CB_IDENT, CB_ONES, CB_BONES, CB_PM, CSTB_N = 0, 128, 256, 384, 512
A2_BYTES = 24 * 1024


def _bc(ap, shape):
    return ap.to_broadcast(list(shape))


class Prog2(Prog):
    def setup(self):
        k = self.k
        k.dma("sp", self.cst[:, :], self.d_cst)
        for l in range(DEPTH):
            k.dma("sp", self.pvec[l][:, :], self.d_pvec[l])
        k.dma("sp", self.cTs[:, :, :], self.d_cT.rearrange("(kc p) b -> p kc b", p=128))
        k.dma("sp", self.xres[:, :, :], self.d_xT.rearrange("(kc p) t -> p kc t", p=128))
        for (src, dst) in ((CST_IDENT, CB_IDENT), (CST_ONES, CB_ONES), (CST_BONES, CB_BONES), (CST_PM, CB_PM)):
            k.cp(self.cstb[:, dst:dst + 128], self.cst[:, src:src + 128])
        k.ms(self.small[:, 8:9], 1.0)
        k.ms(self.small[:, 9:10], RMS_EPS)
        k.ms(self.small[:, 10:11], 0.0)
        self.c_one = self.small[:, 8:9]
        self.c_eps = self.small[:, 9:10]
        k.act(self.csl[:, :, :], self.cTs[:, :, :], AF.Silu)
        for l in range(DEPTH):
            last = (l == DEPTH - 1)
            for i in range(3):
                a = 1.0 if (last and i == 2) else ALPHA
                k.ts(self.xs[l][:, i, :], self.pv(l, PV_LNG + i * 8, 8), a, ALU.mult)
                k.ts(self.xb[l][:, i, :], self.pv(l, PV_LNB + i * 8, 8), a, ALU.mult)

    def ident_b(self):
        return self.cstb[:, CB_IDENT:CB_IDENT + 128]

    def mod_group(self, l, i):
        k = self.k
        for f in range(24):
            ft = i * 24 + f
            w = self.wload(self.d_adaw[l, ft])
            p = self.ps()
            for kc in range(KC):
                k.mm(p[:, 0:17], w[:, kc * 128:(kc + 1) * 128], self.csl[:, kc, :], start=(kc == 0), stop=(kc == KC - 1))
            k.ts(self.modg[:, f, :], p[:, 0:17], self.pv(l, PV_ADAB + ft), ALU.add)
        sh = self.modg[:, 0:8, :]
        sc = self.modg[:, 8:16, :]
        gt = self.modg[:, 16:24, :]
        coef = 1.0 if i == 1 else 0.5
        k.ts(self.cF[l][i][:, :, :], gt, 1.0, ALU.add, coef, ALU.mult)
        if i == 0 and l == 0:
            k.ts(self.hs[l][i][:, :, :], sc, 1.0, ALU.add)
            k.cp(self.hb[l][i][:, :, :], sh)
        else:
            pl, pi = (l, i - 1) if i > 0 else (l - 1, 2)
            gp = _bc(self.pv(pl, PV_LNG + pi * 8, 8).unsqueeze(2), [128, 8, 17])
            bp = _bc(self.pv(pl, PV_LNB + pi * 8, 8).unsqueeze(2), [128, 8, 17])
            k.ts(self.hs[l][i][:, :, :], sc, 1.0, ALU.add)
            k.tt(self.hb[l][i][:, :, :], self.hs[l][i][:, :, :], bp, ALU.mult)
            k.tt(self.hb[l][i][:, :, :], self.hb[l][i][:, :, :], sh, ALU.add)
            k.tt(self.hs[l][i][:, :, :], self.hs[l][i][:, :, :], gp, ALU.mult)

    def affine_h(self, out, in_, sv, bv, kc, tbi, tmp):
        k = self.k
        if tbi < 4:
            k.act(out, in_, AF.Identity, bias=bv[:, kc, 0:1], scale=sv[:, kc, 0:1])
        else:
            s3 = _bc(sv[:, kc, 1:17].unsqueeze(2), [128, NSQ, TS])
            b3 = _bc(bv[:, kc, 1:17].unsqueeze(2), [128, NSQ, TS])
            i3 = in_.rearrange("p (s j) -> p s j", j=TS)
            o3 = out.rearrange("p (s j) -> p s j", j=TS)
            t3 = tmp.rearrange("p (s j) -> p s j", j=TS)
            k.tt(t3, i3, s3, ALU.mult)
            k.tt(o3, t3, b3, ALU.add)

    def initial_h(self):
        k = self.k
        tmpS = self.carve("a", 18432, 256, F32)
        for tbi, (t0, n) in enumerate(TBS):
            for kc in range(KC):
                self.affine_h(self.hT[:, kc, t0:t0 + n], self.xres[:, kc, t0:t0 + n], self.hs[0][0], self.hb[0][0], kc, tbi, tmpS[:, 0:64])
        for kc in range(KC):
            k.ts(self.xres[:, kc, :], self.xres[:, kc, :], ALPHA, ALU.mult)

    def resid_acc(self, ps_ap, l, i, kc, tbi, t0, n, s0=0, ns=NSQ):
        k = self.k
        xr = self.xres[:, kc, t0:t0 + n]
        cf = self.cF[l][i]
        if tbi < 4:
            k.stt(xr, ps_ap, cf[:, kc, 0:1], xr, ALU.mult, ALU.add)
        else:
            tmpS = self.carve("a", 18432, 256, F32)[:, 0:n]
            c3 = _bc(cf[:, kc, 1 + s0:1 + s0 + ns].unsqueeze(2), [128, ns, TS])
            k.tt(tmpS.rearrange("p (s j) -> p s j", j=TS), ps_ap.rearrange("p (s j) -> p s j", j=TS), c3, ALU.mult)
            k.tt(xr, xr, tmpS, ALU.add)

    def stats_acc(self, ps1, ps2, kc, t0, n):
        k = self.k
        rb = self.carve("a", 4096 + (kc % 2) * 1024, 1024, BF16)[:, 0:n]
        rsq = self.carve("a", 6144 + (kc % 2) * 1024, 1024, BF16)[:, 0:n]
        xr = self.xres[:, kc, t0:t0 + n]
        k.cp(rb, xr, eng="act")
        k.act(rsq, xr, AF.Square)
        ones = self.cstb[:, CB_ONES:CB_ONES + 128]
        k.mm(ps1[:, 0:n], ones, rb, start=(kc == 0), stop=(kc == KC - 1))
        k.mm(ps2[:, 0:n], ones, rsq, start=(kc == 0), stop=(kc == KC - 1))

    def ln_block(self, ps1, ps2, l, i, tbi, t0, n):
        k = self.k
        mean = self.carve("a", 8192, 2048, F32)[:, 0:n]
        rstd = self.carve("a", 10240, 2048, F32)[:, 0:n]
        nmr = self.carve("a", 12288, 2048, F32)[:, 0:n]
        ex2 = self.carve("a", 18688, 2048, F32)[:, 0:n]
        tmpS = self.carve("a", 18432, 256, F32)
        k.ts(mean, ps1[:, 0:n], 1.0 / D, ALU.mult)
        k.tt(ex2, mean, mean, ALU.mult)
        k.stt(ex2, ps2[:, 0:n], 1.0 / D, ex2, ALU.mult, ALU.subtract)
        k.ts(ex2, ex2, 0.0, ALU.max, LN_EPS, ALU.add)
        k.act(ex2, ex2, AF.Sqrt)
        k.recip(rstd, ex2)
        k.stt(nmr, mean, -1.0, rstd, ALU.mult, ALU.mult)
        last = (l == DEPTH - 1 and i == 2)
        if i < 2:
            nl, ni = l, i + 1
        else:
            nl, ni = l + 1, 0
        for kc in range(KC):
            xn = self.carve("a", 14336 + (kc % 2) * 2048, 2048, F32)[:, 0:n]
            xr = self.xres[:, kc, t0:t0 + n]
            k.tt(xn, xr, rstd, ALU.mult)
            k.tt(xn, xn, nmr, ALU.add)
            k.act(xr, xn, AF.Identity, bias=self.xb[l][:, i, kc:kc + 1], scale=self.xs[l][:, i, kc:kc + 1])
            if not last:
                self.affine_h(self.hT[:, kc, t0:t0 + n], xn, self.hs[nl][ni], self.hb[nl][ni], kc, tbi, tmpS[:, 0:64])

    def ffn(self, l, which, mid_hook=None):
        k = self.k
        i = 0 if which == 0 else 2
        d_wi = self.d_wi[which]
        d_wo = self.d_wo[which]
        gb = self.gbig
        for gi, js in enumerate(JG):
            for jj, j in enumerate(js):
                wa = self.wload(d_wi[l, j])
                wb = self.wload(d_wi[l, NJ + j])
                for tbi, (t0, n) in enumerate(TBS):
                    pa = self.ps()
                    pb = self.ps()
                    for kc in range(KC):
                        k.mm(pa[:, 0:n], wa[:, kc * 128:(kc + 1) * 128], self.hT[:, kc, t0:t0 + n], start=(kc == 0), stop=(kc == KC - 1))
                    for kc in range(KC):
                        k.mm(pb[:, 0:n], wb[:, kc * 128:(kc + 1) * 128], self.hT[:, kc, t0:t0 + n], start=(kc == 0), stop=(kc == KC - 1))
                    sA = self.carve("a", ((jj * 5 + tbi) % 2) * 2048, 2048, F32)[:, 0:n]
                    k.act(sA, pa[:, 0:n], AF.Silu)
                    k.tt(gb[:, jj * NTOK + t0: jj * NTOK + t0 + n], sA, pb[:, 0:n], ALU.mult)
            if gi == 0 and mid_hook is not None:
                mid_hook()
            wos = [self.wload(d_wo[l, j * 128:(j + 1) * 128, :]) for j in js]
            final = (gi == len(JG) - 1)
            for tbi, (t0, n) in enumerate(TBS):
                if final:
                    ps1 = self.psb[6]
                    ps2 = self.psb[7]
                for oc in range(KC):
                    po = self.ps()
                    for jj in range(len(js)):
                        k.mm(po[:, 0:n], wos[jj][:, oc * 128:(oc + 1) * 128], gb[:, jj * NTOK + t0: jj * NTOK + t0 + n],
                             start=(jj == 0), stop=(jj == len(js) - 1))
                    self.resid_acc(po[:, 0:n], l, i, oc, tbi, t0, n)
                    if final:
                        self.stats_acc(ps1, ps2, oc, t0, n)
                if final:
                    self.ln_block(ps1, ps2, l, i, tbi, t0, n)

    def ln_pass(self, l, i):
        for tbi, (t0, n) in enumerate(TBS):
            ps1 = self.psb[6]
            ps2 = self.psb[7]
            for kc in range(KC):
                self.stats_acc(ps1, ps2, kc, t0, n)
            self.ln_block(ps1, ps2, l, i, tbi, t0, n)

    def store_y(self):
        self.k.dma("sp", self.o_yT.rearrange("(kc p) t -> p kc t", p=128), self.xres[:, :, :], is_out=True)


class Prog3(Prog2):
    def blkinfo(self, blk):
        if blk < NPB:
            return False, blk * MB, MB, 0
        sb_ = blk - NPB
        return True, TP + sb_ * NCH * TS, NCH * TS, sb_ * NCH

    def cv(self, ap, is_s, N):
        a = ap[:, 0:N]
        if is_s:
            a = a.rearrange("p (s j) -> p s j", j=TS)
        return a

    def pw(self, ap, is_s):
        if is_s:
            return ap.rearrange("p (s t) -> p s t", t=64)[:, :, 0:TS]
        return ap

    def proj(self, w, t0, N, M=128):
        k = self.k
        p = self.ps()
        for kc in range(KC):
            k.mm(p[0:M, 0:N], w[:, kc * 128:kc * 128 + M], self.hT[:, kc, t0:t0 + N], start=(kc == 0), stop=(kc == KC - 1))
        return p

    def layer_small(self, l):
        k = self.k
        sm = self.small
        k.ts(sm[:, 0:2], self.pv(l, PV_BG, 2), -1.0, ALU.mult)
        if l == 0:
            k.ms(sm[:, 2:4], 0.0)
        else:
            k.tt(sm[:, 2:4], self.pv(l, PV_LB1, 2), self.pv(l, PV_LB0, 2), ALU.subtract)
            k.act(sm[:, 2:4], sm[:, 2:4], AF.Exp, scale=-1.0)
            k.ts(sm[:, 2:4], sm[:, 2:4], 1.0, ALU.add)
            k.recip(sm[:, 2:4], sm[:, 2:4])
        k.ts(sm[:, 4:6], sm[:, 2:4], -1.0, ALU.mult, 1.0, ALU.add)
        k.ts(sm[:, 6:8], sm[:, 4:6], -1.0, ALU.mult)
        k.act(sm[0:4, 11:12], self.pv(l, PV_ALOG)[0:4, :], AF.Exp)
        k.ts(sm[0:4, 11:12], sm[0:4, 11:12], -1.0, ALU.mult)
        k.dma("sp", self.wg[:, :], self.d_wg[l])
        k.cp(self.wgb[:, :], self.wg[:, :])

    def mix_layer(self, l, mid_hook=None):
        k = self.k
        self.layer_small(l)
        mixers = [("ret", ["rq0", "rq1", "rk0", "rk1", "rv0", "rv1", "rg0", "rg1"]),
                  ("gla", ["aq0", "aq1", "ak0", "ak1", "av0", "av1", "ag0", "ag1", "alr"]),
                  ("hg", ["hq0", "hq1", "hf0", "hf1", "hi0", "hi1", "hg0", "hg1"]),
                  ("gdn", ["dq0", "dq1", "dk0", "dk1", "dv0", "dv1", "dg0", "dg1", "db", "da"])]
        for m, (name, wn) in enumerate(mixers):
            if name in self.skip_mixers:
                continue
            W = {n: self.wload(self.d_win[l, WT[n]]) for n in wn}
            Wo = [self.wload(self.d_wout[l, 2 * m + tl]) for tl in range(2)]
            blks = list(range(NPB + NSB) if self.nblk is None else self.nblk)
            if name == "gdn":
                for blk in blks:
                    self.mix_gdn(l, m, blk, W, Wo)
            else:
                def run_to_prep(g_):
                    for v_ in g_:
                        if v_ == "PREP_DONE":
                            return
                cur = self.mix_gla(l, m, name, blks[0], W, Wo, 0)
                run_to_prep(cur)
                for bi in range(len(blks)):
                    nxt = self.mix_gla(l, m, name, blks[bi + 1], W, Wo, (bi + 1) % 2) if bi + 1 < len(blks) else None
                    cur_live, nxt_live = True, nxt is not None
                    while cur_live or nxt_live:
                        if cur_live:
                            try:
                                next(cur)
                            except StopIteration:
                                cur_live = False
                        if nxt_live:
                            try:
                                if next(nxt) == "PREP_DONE":
                                    nxt_live = False
                            except StopIteration:
                                nxt_live = False
                    cur = nxt
            if mid_hook is not None:
                mid_hook()
                mid_hook = None
        if mid_hook is not None:
            mid_hook()
        self.ln_pass(l, 1)

    def chk(self):
        self.chkc = getattr(self, "chkc", 0) + 1
        return self.chkc == CUTN

    def gbuf(self, off, nbytes, dt, parts=128):
        return self.carve("g", off, nbytes, dt, parts)

    def mix_gla(self, l, m, name, blk, W, Wo, st=0):
        k = self.k
        is_s, t0, N, s0 = self.blkinfo(blk)
        cv = lambda ap: self.cv(ap, is_s, N)
        pw = lambda ap: self.pw(ap, is_s)
        r3 = lambda ap: ap.rearrange("p (a t) -> p a t", a=2)
        Qp = r3(self.gbuf(0, 1024, BF16))
        Kp = r3(self.gbuf(1024, 1024, BF16))
        if st == 0:
            V = r3(self.gbuf(2048, 1024, BF16))
            GATE = r3(self.gbuf(31488, 2048, F32))
            QT = r3(self.gbuf(12288, 1024, BF16))
            KT = r3(self.gbuf(13312, 1024, BF16))
            QG = r3(self.gbuf(14336, 1024, BF16))
            av = self.gbuf(31232, 32, F32).rearrange("p (a c) -> p a c", a=2)
            bv = self.gbuf(31264, 32, F32).rearrange("p (a c) -> p a c", a=2)
        else:
            QT = r3(self.carve("a", 4096, 1024, BF16))
            KT = r3(self.carve("a", 5120, 1024, BF16))
            QG = r3(self.carve("a", 6144, 1024, BF16))
            V = r3(self.carve("a", 7168, 1024, BF16))
            GATE = r3(self.carve("a", 8192, 2048, F32))
            av = self.carve("a", 10240, 32, F32).rearrange("p (a c) -> p a c", a=2)
            bv = self.carve("a", 10272, 32, F32).rearrange("p (a c) -> p a c", a=2)
        T1r = self.carve("a", 12288, 2048, F32)
        T2r = self.carve("a", 14336, 2048, F32)
        Gs = r3(self.gbuf(4096, 2048, F32))
        T1 = self.gbuf(6144, 2048, F32)
        T2 = self.gbuf(8192, 2048, F32)
        G0 = r3(self.gbuf(10240, 2048, F32))
        KTM = self.gbuf(15360, 2048, BF16)
        VTM = self.gbuf(17408, 2048, BF16)
        ATT = self.gbuf(19456, 2048, BF16)
        UP = self.gbuf(21504, 2048, F32).rearrange("p (a c v) -> p a c v", a=2, c=NCH)
        SALL = self.gbuf(23552, 2560, F32).rearrange("p (a c v) -> p a c v", a=2, c=NCH + 1)
        SBF = self.gbuf(26112, 1024, BF16).rearrange("p (a c v) -> p a c v", a=2, c=NCH)
        SIN = self.gbuf(27136, 2048, F32).rearrange("p (a c v) -> p a c v", a=2, c=NCH)
        OSQ = self.gbuf(29184, 1024, BF16)
        OB = r3(self.gbuf(30208, 1024, BF16))
        cosb = self.carve("a", 0, 1024, F32)
        sinb = self.carve("a", 1024, 1024, F32)
        alrb = self.carve("a", 2048, 512, BF16)
        qb = self.carve("a", 2560, 512, BF16)
        t1c = T1[:, 0:MB]
        t2c = T2[:, 0:MB]
        identb = self.ident_b()
        sm = self.small

        if is_s:
            for tl in range(2):
                k.ms(Qp[:, tl, :], 0.0)
                k.ms(Kp[:, tl, :], 0.0)
                k.ms(V[:, tl, :], 0.0)
                if name != "ret":
                    k.ms(G0[:, tl, :], 0.0)

        def simple_v_gate(vn, gn, tl):
            p = self.proj(W[vn + str(tl)], t0, N)
            k.cp(pw(V[:, tl, :]), cv(p))
            p = self.proj(W[gn + str(tl)], t0, N)
            k.act(GATE[:, tl, 0:N], p[:, 0:N], AF.Silu)

        if name == "ret":
            if _os.environ.get('KVAR', '') != 'nodma':
                k.dma("sp", cosb[:, 0:N], self.d_cos[:, t0:t0 + N])
                k.dma("sp", sinb[:, 0:N], self.d_sin[:, t0:t0 + N])
                k.dma("sp", Gs, self.d_gret[1 if is_s else 0])
            pmb = self.cstb[:, CB_PM:CB_PM + 128]
            if CUT == 11:
                return
            for tl in range(2):
                for (wn_, dst, scale) in (("rq", Qp, 1.0), ("rk", Kp, 0.125)):
                    p = self.proj(W[wn_ + str(tl)], t0, N)
                    if self.chk():
                        return
                    k.cp(qb[:, 0:N], p[:, 0:N])
                    pp = self.ps()
                    k.mm(pp[:, 0:N], pmb, qb[:, 0:N])
                    if self.chk():
                        return
                    k.stt(t1c[:, 0:N], p[:, 0:N], scale, cosb[:, 0:N], ALU.mult, ALU.mult)
                    k.stt(t2c[:, 0:N], pp[:, 0:N], scale, sinb[:, 0:N], ALU.mult, ALU.mult)
                    if self.chk():
                        return
                    _v = _os.environ.get('KVAR', '')
                    if _v == 'A' and tl == 1:
                        k.tt(pw(dst[:, 0, :]), cv(t1c), cv(t2c), ALU.add)
                    elif _v == 'B' and tl == 1:
                        k.tt(pw(dst[:, tl, :]), cv(t1c), cv(t2c), ALU.add, eng="pool")
                    else:
                        k.tt(pw(dst[:, tl, :]), cv(t1c), cv(t2c), ALU.add, eng=GP)
                    if self.chk():
                        return
                yield
                simple_v_gate("rv", "rg", tl)
                yield
                if self.chk():
                    return
        elif name == "gla":
            p = self.proj(W["alr"], t0, N, M=16)
            k.cp(alrb[0:16, 0:N], p[0:16, 0:N])
            for tl in range(2):
                pg = self.ps()
                k.mm(pg[:, 0:N], self.wgb[0:16, tl * 128:(tl + 1) * 128], alrb[0:16, 0:N])
                k.act(t1c[:, 0:N], pg[:, 0:N], AF.Exp, bias=sm[:, tl:tl + 1], scale=-1.0)
                k.act(t1c[:, 0:N], t1c[:, 0:N], AF.Ln, bias=self.c_one)
                k.ts(pw(G0[:, tl, :]), cv(t1c), -1.0 / 16.0, ALU.mult)
                p = self.proj(W["aq" + str(tl)], t0, N)
                k.ts(pw(Qp[:, tl, :]), cv(p), float(32.0 ** -0.5), ALU.mult)
                p = self.proj(W["ak" + str(tl)], t0, N)
                k.cp(pw(Kp[:, tl, :]), cv(p))
                yield
                simple_v_gate("av", "ag", tl)
                yield
        else:
            for tl in range(2):
                p = self.proj(W["hq" + str(tl)], t0, N)
                k.act(t1c[:, 0:N], p[:, 0:N], AF.Silu)
                k.ts(pw(Qp[:, tl, :]), cv(t1c), 0.125, ALU.mult)
                p = self.proj(W["hf" + str(tl)], t0, N)
                k.act(t1c[:, 0:N], p[:, 0:N], AF.Exp, scale=-1.0)
                k.ts(t1c[:, 0:N], t1c[:, 0:N], 1.0, ALU.add)
                k.recip(t1c[:, 0:N], t1c[:, 0:N])
                k.act(t2c[:, 0:N], t1c[:, 0:N], AF.Ln, bias=sm[:, 2 + tl:3 + tl], scale=sm[:, 4 + tl:5 + tl])
                k.cp(pw(G0[:, tl, :]), cv(t2c))
                k.ts(pw(Kp[:, tl, :]), cv(t1c), sm[:, 6 + tl:7 + tl], ALU.mult, sm[:, 4 + tl:5 + tl], ALU.add)
                yield
                simple_v_gate("hi", "hg", tl)
                yield
        if name != "ret":
            reset = self.cst[:, CST_RESET:CST_RESET + MB]
            for tl in range(2):
                k.scan(Gs[:, tl, :], reset, G0[:, tl, :], 0.0, ALU.mult, ALU.add)

        if CUT == 1:
            return
        for tl in range(2):
            G3 = Gs[:, tl, :].rearrange("p (c t) -> p c t", t=64)
            D3 = t1c.rearrange("p (c t) -> p c t", t=64)
            k.tt(D3, G3, _bc(G3[:, :, 31:32], [128, NCH, 64]), ALU.subtract)
            k.act(t2c, t1c, AF.Exp)
            k.tt(QT[:, tl, :], Qp[:, tl, :], t2c, ALU.mult, eng=GP)
            k.act(t2c, t1c, AF.Exp, scale=-1.0)
            k.tt(KT[:, tl, :], Kp[:, tl, :], t2c, ALU.mult, eng=GP)
            k.act(t2c, Gs[:, tl, :], AF.Exp)
            k.tt(QG[:, tl, :], Qp[:, tl, :], t2c, ALU.mult, eng=GP)
            k.act(av[:, tl, :], G3[:, :, 63], AF.Exp)
            k.act(bv[:, tl, :], D3[:, :, 63], AF.Exp)
            yield
        yield "PREP_DONE"

        P2 = [slice(0, 64), slice(64, 128)]
        for (src, dstm) in ((KT, KTM), (V, VTM)):
            pT = [self.ps(), self.ps()]
            for p_ in range(2):
                pTb = pT[p_][:, :].bitcast(BF16)
                for c in range(NCH):
                    for tl in range(2):
                        k.tr(pTb[P2[p_], (c * 2 + tl) * 64:(c * 2 + tl + 1) * 64], src[P2[p_], tl, c * 64:(c + 1) * 64],
                             identb[P2[p_], 64 * p_:64 * p_ + 64])
                k.cp(dstm[P2[p_], 0:NCH * 128], pTb[P2[p_], 0:NCH * 128])
            yield

        mincl2 = self.cst[:, CST_MINCL:CST_MINCL + 64]
        pA = [self.ps(), self.ps()]
        for p_ in range(2):
            for c in range(NCH):
                for tl in range(2):
                    sl = slice((c * 2 + tl) * 64, (c * 2 + tl + 1) * 64)
                    k.mm(pA[p_][P2[p_], sl], KT[P2[p_], tl, c * 64:(c + 1) * 64], QT[P2[p_], tl, c * 64:(c + 1) * 64])
            k.tt(ATT[P2[p_], 0:NCH * 128].rearrange("p (a t) -> p a t", t=64),
                 pA[p_][P2[p_], 0:NCH * 128].rearrange("p (a t) -> p a t", t=64),
                 _bc(mincl2[P2[p_], :].unsqueeze(1), [64, NCH * 2, 64]), ALU.mult)

        yield
        pU = [self.ps(), self.ps()]
        for p_ in range(2):
            for c in range(NCH):
                for tl in range(2):
                    sl = slice((c * 2 + tl) * 64, (c * 2 + tl + 1) * 64)
                    k.mm(pU[p_][P2[p_], (tl * NCH + c) * 64:(tl * NCH + c + 1) * 64], KTM[P2[p_], sl], VTM[P2[p_], sl])
            k.tt(UP.rearrange("p a c v -> p (a c) v")[P2[p_]], pU[p_][P2[p_], :].rearrange("p (a v) -> p a v", v=64),
                 _bc(bv.rearrange("p a c -> p (a c)")[P2[p_]].unsqueeze(2), [64, 2 * NCH, 64]), ALU.mult)

        yield
        if not is_s:
            if blk == 0:
                k.ms(SALL[:, :, 0, :], 0.0)
            else:
                k.cp(SALL[:, :, 0, :], SALL[:, :, NCH, :])
            SBv = SALL
        else:
            k.dma("sp", SIN, self.d_st[name][l][:, :, s0:s0 + NCH, :])
            SBv = SIN
        for c in range(NCH):
            for tl in range(2):
                src = SALL[:, tl, c, :] if not is_s else SIN[:, tl, c, :]
                k.stt(SALL[:, tl, c + 1, :], src, av[:, tl, c:c + 1], UP[:, tl, c, :], ALU.mult, ALU.add)
        if is_s:
            k.dma("sp", self.o_ss[name][l][:, :, s0:s0 + NCH, :], SALL[:, :, 1:NCH + 1, :], is_out=True)
        elif blk == NPB - 1:
            k.dma("sp", self.o_ps[name][l], SALL[:, :, NCH, :], is_out=True)
        k.cp(SBF, SBv[:, :, 0:NCH, :], eng=GP)

        yield
        pO = [self.ps(), self.ps()]
        for p_ in range(2):
            for c in range(NCH):
                for tl in range(2):
                    sl = slice((c * 2 + tl) * 64, (c * 2 + tl + 1) * 64)
                    o_ap = pO[p_][P2[p_], (tl * NCH + c) * 64:(tl * NCH + c + 1) * 64]
                    k.mm(o_ap, VTM[P2[p_], sl], ATT[P2[p_], sl], start=True, stop=False)
                    k.mm(o_ap, SBF[P2[p_], tl, c, :], QG[P2[p_], tl, c * 64:(c + 1) * 64], start=False, stop=True)
        for p_ in range(2):
            k.cp(T2r[P2[p_], 0:2 * MB], pO[p_][P2[p_], 0:2 * MB], eng="act")
        yield
        normw = None if name == "ret" else self.pv(l, PV_NORM + {"gla": 0, "hg": 1}[name])
        self.o_post(l, m, T2r, GATE, OB, OSQ, T1r, T2r, normw, Wo, is_s, t0, N, s0)

    def o_post(self, l, m, pO, GATE, OB, OSQ, T1, T2, normw, Wo, is_s, t0, N, s0):
        k = self.k
        cv = lambda ap: self.cv(ap, is_s, N)
        pw = lambda ap: self.pw(ap, is_s)
        bones = self.cstb[:, CB_BONES:CB_BONES + 128]
        if str(pO.space) == "PSUM":
            k.cp(T2[:, 0:2 * MB], pO[:, 0:2 * MB], eng="act")
            pO = T2
        k.act(OSQ[:, 0:2 * MB], pO[:, 0:2 * MB], AF.Square)
        pS = self.ps()
        for tl in range(2):
            k.mm(pS[:, tl * MB:(tl + 1) * MB], bones, OSQ[:, tl * MB:(tl + 1) * MB])
        k.act(T1[:, 0:2 * MB], pS[:, 0:2 * MB], AF.Sqrt, bias=self.c_eps, scale=1.0 / 64.0)
        k.recip(T1[:, 0:2 * MB], T1[:, 0:2 * MB])
        k.tt(T2[:, 0:2 * MB], pO[:, 0:2 * MB], T1[:, 0:2 * MB], ALU.mult)
        for tl in range(2):
            src = pw(T2[:, tl * MB:(tl + 1) * MB])
            if normw is None:
                k.tt(cv(OB[:, tl, :]), src, cv(GATE[:, tl, :]), ALU.mult)
            else:
                k.stt(cv(OB[:, tl, :]), src, normw, cv(GATE[:, tl, :]), ALU.mult, ALU.mult)
        for oc in range(KC):
            po = self.ps()
            for tl in range(2):
                k.mm(po[:, 0:N], Wo[tl][:, oc * 128:(oc + 1) * 128], OB[:, tl, 0:N], start=(tl == 0), stop=(tl == 1))
            self.resid_acc(po[:, 0:N], l, 1, oc, 4 if is_s else 0, t0, N, s0, NCH)


class Prog4(Prog3):
    def mix_gdn(self, l, m, blk, W, Wo):
        k = self.k
        is_s, t0, N, s0 = self.blkinfo(blk)
        cv = lambda ap: self.cv(ap, is_s, N)
        pw = lambda ap: self.pw(ap, is_s)
        r3 = lambda ap: ap.rearrange("p (a t) -> p a t", a=2)
        g = self.gbuf
        P2 = [slice(0, 64), slice(64, 128)]
        Qp = r3(g(0, 2048, F32))
        Kp = r3(g(2048, 2048, F32))
        QGx = r3(g(4096, 2048, F32))
        QpT = r3(g(6144, 2048, F32))
        VTM = g(8192, 2048, F32)
        KTM = g(10240, 2048, F32)
        MT = g(12288, 2048, F32).rearrange("p (a c v) -> p a c v", a=2, c=NCH)
        BC = g(14336, 2048, F32).rearrange("p (a c v) -> p a c v", a=2, c=NCH)
        SALL = g(16384, 2560, F32).rearrange("p (a c v) -> p a c v", a=2, c=NCH + 1)
        SIN = g(18944, 2048, F32).rearrange("p (a c v) -> p a c v", a=2, c=NCH)
        OB = r3(g(20992, 1024, BF16))
        OSQ = g(22016, 1024, BF16)
        GD = g(23040, 1024, F32)
        GC = g(24064, 1024, F32)
        BET = g(25088, 1024, F32)
        c3v = lambda ap: ap[:, 0:NCH * 2].rearrange("p (c l) -> p c l", l=2)
        GCOL = g(26112, 64, F32)
        BCOL = g(26176, 64, F32)
        EGC = g(26240, 64, F32)
        KHS = g(26304, 64, F32)
        BEG = g(26368, 64, F32)
        eGL = g(26432, 32, F32).rearrange("p (a c) -> p a c", a=2)
        halo = g(26496, 72, F32).rearrange("p (i r) -> p i r", r=3)
        OL = r3(g(26624, 2048, F32))
        GATE = r3(g(28672, 2048, F32))
        A = lambda i: self.carve("a", i * 1024, 1024, F32)
        UQ = r3(self.carve("a", 9216, 2048, F32))
        UK = r3(self.carve("a", 11264, 2048, F32))
        VT = r3(self.carve("a", 13312, 2048, F32))
        SQ = self.carve("a", 15360, 1024, BF16)
        ubuf = self.carve("a", 16384, 1040, F32)
        acc = self.carve("a", 17424, 1024, F32)
        T1 = self.carve("a", 18448, 2048, F32)
        T2 = self.carve("a", 20496, 2048, F32)
        sm = self.small
        identF = self.cst[:, CST_IDENT:CST_IDENT + 128]
        bones = self.cstb[:, CB_BONES:CB_BONES + 128]
        selF = self.cst[0:4, CST_SELF:CST_SELF + 256]
        selHP = self.cst[0:4, CST_SELHP:CST_SELHP + 4]
        mincl = self.cst[:, CST_MINCL:CST_MINCL + 64]
        mstr = self.cst[:, CST_MSTR:CST_MSTR + 64]
        mstrT = self.cst[:, CST_MSTRT:CST_MSTRT + 64]
        ID2 = self.cst[:, CST_ID2:CST_ID2 + 64]

        if is_s:
            for tl in range(2):
                k.ms(Qp[:, tl, :], 0.0)
                k.ms(Kp[:, tl, :], 0.0)
                k.ms(VT[:, tl, :], 0.0)
            k.ms(GD[0:4, :], 0.0)
            k.ms(BET[0:4, :], 0.0)

        names = ["dq0", "dq1", "dk0", "dk1", "dv0", "dv1"]
        for idx, wn_ in enumerate(names):
            p = self.proj(W[wn_], t0, N)
            tl = idx % 2
            cw = lambda i: self.pv(l, PV_CONV + idx * 4 + i)
            if not is_s:
                if blk == 0:
                    k.ms(ubuf[:, 0:3], 0.0)
                else:
                    k.cp(ubuf[:, 0:3], halo[:, idx, :], eng=GP)
                k.cp(ubuf[:, 3:3 + N], p[:, 0:N], eng="act")
                k.ts(acc[:, 0:N], ubuf[:, 0:N], cw(0), ALU.mult)
                for i in range(1, 4):
                    k.stt(acc[:, 0:N], ubuf[:, i:i + N], cw(i), acc[:, 0:N], ALU.mult, ALU.add)
                k.cp(halo[:, idx, :], ubuf[:, N:N + 3], eng=GP)
                if blk == NPB - 1:
                    k.dma("sp", self.o_pconv[l][:, idx, :], halo[:, idx, :], is_out=True)
            else:
                ubs = ubuf[:, 0:NCH * 7].rearrange("p (s r) -> p s r", r=7)
                k.dma("sp", ubs[:, :, 0:3], self.d_stconv[l][:, idx, s0:s0 + NCH, :])
                k.cp(ubs[:, :, 3:7], p[:, 0:N].rearrange("p (s j) -> p s j", j=TS), eng="act")
                a3 = acc[:, 0:N].rearrange("p (s j) -> p s j", j=TS)
                k.ts(a3, ubs[:, :, 0:4], cw(0), ALU.mult)
                for i in range(1, 4):
                    k.stt(a3, ubs[:, :, i:i + 4], cw(i), a3, ALU.mult, ALU.add)
                k.dma("sp", self.o_sconv[l][:, idx, s0:s0 + NCH, :], ubs[:, :, 4:7], is_out=True)
            accv = acc[:, 0:N]
            if idx < 2:
                k.act(UQ[:, tl, 0:N], accv, AF.Silu)
            elif idx < 4:
                k.act(UK[:, tl, 0:N], accv, AF.Silu)
            else:
                k.act(T1[:, 0:N], accv, AF.Silu)
                k.cp(pw(VT[:, tl, :]), cv(T1[:, 0:MB]), eng=GP)
        if CUTG == 1:
            return
        for (X, dst, scale) in ((UQ, Qp, 0.125), (UK, Kp, 1.0)):
            pS = self.ps()
            for tl in range(2):
                k.act(SQ[:, tl * MB:tl * MB + N], X[:, tl, 0:N], AF.Square)
                k.mm(pS[:, tl * MB:tl * MB + N], bones, SQ[:, tl * MB:tl * MB + N])
                k.act(T1[:, tl * MB:tl * MB + N], pS[:, tl * MB:tl * MB + N], AF.Sqrt, bias=self.c_eps, scale=1.0)
                k.recip(T1[:, tl * MB:tl * MB + N], T1[:, tl * MB:tl * MB + N])
                k.stt(pw(dst[:, tl, :]), cv(X[:, tl, :]), scale, cv(T1[:, tl * MB:(tl + 1) * MB]), ALU.mult, ALU.mult)
        if CUTG == 2:
            return
        for tl in range(2):
            p = self.proj(W["dg" + str(tl)], t0, N)
            k.act(GATE[:, tl, 0:N], p[:, 0:N], AF.Silu)
        if CUTG == 3:
            return
        p = self.proj(W["db"], t0, N, M=4)
        k.act(T2[0:4, 0:N], p[0:4, 0:N], AF.Exp, scale=-1.0)
        k.ts(T2[0:4, 0:N], T2[0:4, 0:N], 1.0, ALU.add)
        k.recip(T2[0:4, 0:N], T2[0:4, 0:N])
        k.cp(pw(BET[0:4, 0:MB]), cv(T2[0:4, 0:MB]))
        p = self.proj(W["da"], t0, N, M=4)
        k.act(T2[0:4, 0:N], p[0:4, 0:N], AF.Exp, bias=self.pv(l, PV_DTB)[0:4, :], scale=1.0)
        k.act(T2[0:4, 0:N], T2[0:4, 0:N], AF.Ln, bias=self.c_one[0:4, :])
        k.ts(pw(GD[0:4, 0:MB]), cv(T2[0:4, 0:MB]), sm[0:4, 11:12], ALU.mult)
        k.scan(GC[0:4, 0:MB], self.cst[0:4, CST_RESET:CST_RESET + MB], GD[0:4, 0:MB], 0.0, ALU.mult, ALU.add)
        if CUTG == 4:
            return
        pX = self.ps()
        for tl in range(2):
            k.mm(pX[:, tl * MB:(tl + 1) * MB], selF[0:4, tl * 128:(tl + 1) * 128], GC[0:4, 0:MB])
        k.act(T1[:, 0:2 * MB], pX[:, 0:2 * MB], AF.Exp)
        for tl in range(2):
            k.tt(QGx[:, tl, :], Qp[:, tl, :], T1[:, tl * MB:(tl + 1) * MB], ALU.mult, eng=GP)
            k.cp(eGL[:, tl, :], T1[:, tl * MB:(tl + 1) * MB].rearrange("p (c t) -> p c t", t=64)[:, :, 63], eng=GP)
        pC = [self.ps(), self.ps()]
        for p_ in range(2):
            for c in range(NCH):
                k.mm(pC[p_][P2[p_], c * 2:(c + 1) * 2], GC[0:4, c * 64:(c + 1) * 64], selHP[0:4, p_ * 2:p_ * 2 + 2])
                k.mm(pC[p_][P2[p_], 64 + c * 2:64 + (c + 1) * 2], BET[0:4, c * 64:(c + 1) * 64], selHP[0:4, p_ * 2:p_ * 2 + 2])
            k.cp(GCOL[P2[p_], 0:NCH * 2], pC[p_][P2[p_], 0:NCH * 2])
            k.cp(BCOL[P2[p_], 0:NCH * 2], pC[p_][P2[p_], 64:64 + NCH * 2])
        k.act(EGC[:, 0:NCH * 2], GCOL[:, 0:NCH * 2], AF.Exp)
        k.tt(BEG[:, 0:NCH * 2], BCOL[:, 0:NCH * 2], EGC[:, 0:NCH * 2], ALU.mult)
        pL = self.ps()
        glast = GC[0:4, 0:MB].rearrange("p (c t) -> p c t", t=64)[:, :, 63]
        for tl in range(2):
            k.mm(pL[:, tl * NCH:(tl + 1) * NCH], selF[0:4, tl * 128:(tl + 1) * 128], glast)
        k.tt(c3v(KHS), pL[:, 0:NCH * 2].rearrange("p (l c) -> p c l", l=2), c3v(GCOL), ALU.subtract)
        k.act(KHS[:, 0:NCH * 2], KHS[:, 0:NCH * 2], AF.Exp)
        if CUTG == 5:
            return
        for (src, dstm) in ((Kp, KTM), (VT, VTM)):
            pT = [self.ps(), self.ps()]
            for p_ in range(2):
                for c in range(NCH):
                    for tl in range(2):
                        k.mm(pT[p_][P2[p_], (c * 2 + tl) * 64:(c * 2 + tl + 1) * 64], src[P2[p_], tl, c * 64:(c + 1) * 64],
                             identF[P2[p_], 64 * p_:64 * p_ + 64])
                k.cp(dstm[P2[p_], 0:NCH * 128], pT[p_][P2[p_], 0:NCH * 128], eng=("act" if p_ else "dve"))
        KTM4 = KTM[:, 0:NCH * 128].rearrange("p (c l d) -> p c l d", c=NCH, l=2)
        VTM4 = VTM[:, 0:NCH * 128].rearrange("p (c l d) -> p c l d", c=NCH, l=2)
        GCOL3 = c3v(GCOL)
        BCOL3 = c3v(BCOL)
        BEG3 = c3v(BEG)
        KHS3 = c3v(KHS)
        v4 = lambda ap: ap[:, 0:256].rearrange("p (c l t) -> p c l t", c=2, l=2)
        v3 = lambda ap: ap[:, 0:256].rearrange("p (a t) -> p a t", t=64)
        bc4 = lambda ap3: _bc(ap3.unsqueeze(3), [128, 2, 2, 64])
        bm = lambda mk: _bc(mk.unsqueeze(1), [128, 4, 64])

        def evac2(dst, pp, eng0="dve", eng1="act"):
            k.cp(dst[P2[0], 0:256], pp[0][P2[0], 0:256], eng=eng0)
            k.cp(dst[P2[1], 0:256], pp[1][P2[1], 0:256], eng=eng1)

        if CUTG == 6:
            return
        def solve(sbi):
            c0 = sbi * 2
            Dm, X1, X2, LT, Nk, Ak, P, U, Wm = [A(i + 11 * sbi) for i in range(9)]
            pGr = self.ps()
            pBr = self.ps()
            for tl in range(2):
                k.mm(pGr[:, tl * 128:(tl + 1) * 128], selF[0:4, tl * 128:(tl + 1) * 128], GC[0:4, c0 * 64:c0 * 64 + 128])
                k.mm(pBr[:, tl * 128:(tl + 1) * 128], selF[0:4, tl * 128:(tl + 1) * 128], BET[0:4, c0 * 64:c0 * 64 + 128])
            gr4 = pGr[:, 0:256].rearrange("p (l c t) -> p c l t", l=2, c=2)
            br4 = pBr[:, 0:256].rearrange("p (l c t) -> p c l t", l=2, c=2)
            k.tt(v4(Dm), gr4, bc4(GCOL3[:, c0:c0 + 2, :]), ALU.subtract)
            k.ts(X1[:, 0:256], Dm[:, 0:256], 0.0, ALU.min)
            k.act(X1[:, 0:256], X1[:, 0:256], AF.Exp)
            k.ts(X2[:, 0:256], Dm[:, 0:256], -1.0, ALU.mult, 0.0, ALU.min)
            k.act(X2[:, 0:256], X2[:, 0:256], AF.Exp)
            k.tt(v3(LT), v3(X1), bm(mincl), ALU.mult, eng=GP)
            k.tt(v3(X1), v3(X1), bm(mstr), ALU.mult, eng=GP)
            k.tt(v4(X1), v4(X1), br4, ALU.mult)
            k.tt(v3(X2), v3(X2), bm(mstrT), ALU.mult, eng=GP)
            k.tt(v4(X2), v4(X2), bc4(BCOL3[:, c0:c0 + 2, :]), ALU.mult, eng=GP)
            yield
            pKK = [self.ps(), self.ps()]
            pQK = [self.ps(), self.ps()]
            for cc in range(2):
                c = c0 + cc
                for tl in range(2):
                    for p_ in range(2):
                        sl = slice((cc * 2 + tl) * 64, (cc * 2 + tl + 1) * 64)
                        kk = Kp[P2[p_], tl, c * 64:(c + 1) * 64]
                        qq = Qp[P2[p_], tl, c * 64:(c + 1) * 64]
                        k.mm(pKK[p_][P2[p_], sl], kk, kk)
                        k.mm(pQK[p_][P2[p_], sl], kk, qq)
            for p_ in range(2):
                k.tt(Nk[P2[p_], 0:256], pKK[p_][P2[p_], 0:256], X1[P2[p_], 0:256], ALU.mult)
                k.tt(Ak[P2[p_], 0:256], pKK[p_][P2[p_], 0:256], X2[P2[p_], 0:256], ALU.mult)
                k.tt(LT[P2[p_], 0:256], pQK[p_][P2[p_], 0:256], LT[P2[p_], 0:256], ALU.mult)
            k.stt(v3(P), v3(Nk), -1.0, bm(ID2), ALU.mult, ALU.add)
            yield
            for lev in range(5):
                pA_ = [self.ps(), self.ps()]
                if lev < 4:
                    pN_ = [self.ps(), self.ps()]
                for j in range(4):
                    for p_ in range(2):
                        sl = slice(j * 64, (j + 1) * 64)
                        if lev < 4:
                            k.mm(pN_[p_][P2[p_], sl], Ak[P2[p_], sl], Nk[P2[p_], sl])
                        k.mm(pA_[p_][P2[p_], sl], Nk[P2[p_], sl], Ak[P2[p_], sl])
                if lev < 4:
                    evac2(Nk, pN_, "act", "act")
                evac2(Ak, pA_, "dve", "dve")
                yield
                pP = [self.ps(), self.ps()]
                for j in range(4):
                    for p_ in range(2):
                        sl = slice(j * 64, (j + 1) * 64)
                        k.mm(pP[p_][P2[p_], sl], Ak[P2[p_], sl], P[P2[p_], sl])
                for p_ in range(2):
                    k.tt(P[P2[p_], 0:256], P[P2[p_], 0:256], pP[p_][P2[p_], 0:256], ALU.add)
                yield
            k.tt(v4(X1), VTM4[:, c0:c0 + 2, :, :], bc4(BCOL3[:, c0:c0 + 2, :]), ALU.mult, eng=GP)
            k.tt(v4(X2), KTM4[:, c0:c0 + 2, :, :], bc4(BEG3[:, c0:c0 + 2, :]), ALU.mult, eng=GP)
            pu = [self.ps(), self.ps()]
            pw_ = [self.ps(), self.ps()]
            for j in range(4):
                for p_ in range(2):
                    sl = slice(j * 64, (j + 1) * 64)
                    k.mm(pu[p_][P2[p_], sl], P[P2[p_], sl], X1[P2[p_], sl])
                    k.mm(pw_[p_][P2[p_], sl], P[P2[p_], sl], X2[P2[p_], sl])
            evac2(U, pu, "act", "act")
            evac2(Wm, pw_, "dve", "dve")
            yield
            k.tt(v4(Dm), KTM4[:, c0:c0 + 2, :, :], bc4(KHS3[:, c0:c0 + 2, :]), ALU.mult, eng=GP)
            pM = [self.ps(), self.ps()]
            pB = [self.ps(), self.ps()]
            Mraw, Braw = A(9 + 11 * sbi), A(10 + 11 * sbi)
            for j in range(4):
                for p_ in range(2):
                    sl = slice(j * 64, (j + 1) * 64)
                    k.mm(pM[p_][P2[p_], sl], Wm[P2[p_], sl], Dm[P2[p_], sl])
                    k.mm(pB[p_][P2[p_], sl], Dm[P2[p_], sl], U[P2[p_], sl])
            evac2(Mraw, pM, "act", "act")
            evac2(Braw, pB, "dve", "dve")
            yield
            for tl in range(2):
                for cc in range(2):
                    sl = slice((cc * 2 + tl) * 64, (cc * 2 + tl + 1) * 64)
                    k.stt(MT[:, tl, c0 + cc, :], ID2, eGL[:, tl, c0 + cc:c0 + cc + 1], Mraw[:, sl], ALU.mult, ALU.subtract)
                k.cp(BC[:, tl, c0:c0 + 2, :], v4(Braw)[:, :, tl, :], eng=GP)
            pQ = [self.ps(), self.ps()]
            pOL = [self.ps(), self.ps()]
            for j in range(4):
                for p_ in range(2):
                    sl = slice(j * 64, (j + 1) * 64)
                    k.mm(pQ[p_][P2[p_], sl], Wm[P2[p_], sl], LT[P2[p_], sl])
                    k.mm(pOL[p_][P2[p_], sl], U[P2[p_], sl], LT[P2[p_], sl])
            evac2(Mraw, pQ, "act", "act")
            evac2(Braw, pOL, "dve", "dve")
            yield
            for tl in range(2):
                qv = QpT[:, tl, c0 * 64:c0 * 64 + 128].rearrange("p (c t) -> p c t", t=64)
                gv = QGx[:, tl, c0 * 64:c0 * 64 + 128].rearrange("p (c t) -> p c t", t=64)
                k.tt(qv, gv, v4(Mraw)[:, :, tl, :], ALU.subtract, eng=GP)
                k.cp(OL[:, tl, c0 * 64:c0 * 64 + 128].rearrange("p (c t) -> p c t", t=64), v4(Braw)[:, :, tl, :], eng=GP)
        gens = [solve(sbi) for sbi in range(NCH // 2)]
        while gens:
            for g_ in list(gens):
                try:
                    next(g_)
                except StopIteration:
                    gens.remove(g_)
        if CUTG == 7:
            return
        if not is_s:
            if blk == 0:
                k.ms(SALL[:, :, 0, :], 0.0)
            else:
                k.cp(SALL[:, :, 0, :], SALL[:, :, NCH, :])
        else:
            k.dma("sp", SIN, self.d_st["gdn"][l][:, :, s0:s0 + NCH, :])
        SB = (lambda tl, c: SIN[:, tl, c, :]) if is_s else (lambda tl, c: SALL[:, tl, c, :])
        for c in range(NCH):
            pS = [self.ps(), self.ps()]
            for p_ in range(2):
                for tl in range(2):
                    k.mm(pS[p_][P2[p_], tl * 64:(tl + 1) * 64], MT[P2[p_], tl, c, :], SB(tl, c)[P2[p_], :])
                k.tt(SALL[P2[p_], :, c + 1, :], pS[p_][P2[p_], 0:128].rearrange("p (a v) -> p a v", v=64), BC[P2[p_], :, c, :], ALU.add)
        if is_s:
            k.dma("sp", self.o_ss["gdn"][l][:, :, s0:s0 + NCH, :], SALL[:, :, 1:NCH + 1, :], is_out=True)
        elif blk == NPB - 1:
            k.dma("sp", self.o_ps["gdn"][l], SALL[:, :, NCH, :], is_out=True)
        if CUTG == 8:
            return
        pO = [self.ps(), self.ps()]
        for p_ in range(2):
            for c in range(NCH):
                for tl in range(2):
                    k.mm(pO[p_][P2[p_], (tl * NCH + c) * 64:(tl * NCH + c + 1) * 64], SB(tl, c)[P2[p_], :], QpT[P2[p_], tl, c * 64:(c + 1) * 64])
            k.tt(T2[P2[p_], 0:2 * MB], pO[p_][P2[p_], 0:2 * MB], OL.rearrange("p a t -> p (a t)")[P2[p_]], ALU.add)
        if _os.environ.get('KDBG', '') == 'gdn' and blk == 0 and l == 0:
            for nm_, ap_ in (("GC", GC[0:4, 0:MB]), ("BET", BET[0:4, 0:MB]), ("GCOL", GCOL[:, 0:8]), ("BCOL", BCOL[:, 0:8]),
                             ("KHS", KHS[:, 0:8]), ("Qp", Qp), ("Kp", Kp), ("KTM", KTM[:, 0:512]), ("VTM", VTM[:, 0:512]),
                             ("MT", MT), ("BC", BC), ("QpT", QpT), ("OL", OL), ("SALL", SALL), ("T2o", T2[:, 0:512]), ("QGx", QGx)):
                k.dbg(nm_, ap_)
        self.o_post(l, m, T2, GATE, OB, OSQ, T1, T2, self.pv(l, PV_NORM + 2), Wo, is_s, t0, N, s0)

    def build(self, skip_mixers=(), stop_after=None, no_ffn=False, nblk=None):
        self.skip_mixers = set(skip_mixers)
        self.nblk = nblk
        self.alloc()
        self.setup()
        self.mod_group(0, 0)
        self.initial_h()
        for l in range(DEPTH):
            if no_ffn:
                self.mod_group(l, 1)
                self.ln_pass(l, 0)
            else:
                self.ffn(l, 0, mid_hook=lambda: self.mod_group(l, 1))
            if stop_after == ("ffn1", l):
                break
            self.mix_layer(l, mid_hook=lambda: self.mod_group(l, 2))
            if stop_after == ("mix", l):
                break
            hook = (lambda: self.mod_group(l + 1, 0)) if l + 1 < DEPTH else None
            self.ffn(l, 1, mid_hook=hook)
        self.store_y()
        nw = self.k.S.emit()
        self.nwaits = nw
        return self.nc


def _const_pack():
    c = np.zeros((128, CST_N), np.float32)
    c[:, CST_IDENT:CST_IDENT + 128] = np.eye(128, dtype=np.float32)
    s = np.arange(64)[:, None]
    t = np.arange(64)[None, :]
    for hf in range(2):
        c[64 * hf:64 * hf + 64, CST_MINCL:CST_MINCL + 64] = (s <= t)
        c[64 * hf:64 * hf + 64, CST_MSTR:CST_MSTR + 64] = (s < t)
        c[64 * hf:64 * hf + 64, CST_MSTRT:CST_MSTRT + 64] = (t < s)
    c[:, CST_ID2:CST_ID2 + 64] = np.tile(np.eye(64, dtype=np.float32), (2, 1))
    r = np.ones((MB,), np.float32)
    r[0::64] = 0.0
    c[:, CST_RESET:CST_RESET + MB] = r[None, :]
    for h in range(4):
        c[h, CST_SEL + h * 64:CST_SEL + (h + 1) * 64] = 1.0
        tl, p = divmod(h, 2)
        c[h, CST_SELF + tl * 128 + p * 64:CST_SELF + tl * 128 + (p + 1) * 64] = 1.0
    c[:, CST_ONES:CST_ONES + 128] = 1.0
    bo = np.zeros((128, 128), np.float32)
    bo[0:64, 0:64] = 1.0
    bo[64:128, 64:128] = 1.0
    c[:, CST_BONES:CST_BONES + 128] = bo
    pm = np.zeros((128, 128), np.float32)
    for m_ in range(128):
        kk = (m_ // 64) * 64 + ((m_ % 64) + 32) % 64
        pm[kk, m_] = 1.0
    c[:, CST_PM:CST_PM + 128] = pm
    for p_ in range(2):
        for tl in range(2):
            c[2 * tl + p_, CST_SELHP + p_ * 2 + tl] = 1.0
    return c


def _rot_tables():
    half = 32
    inv = (np.float32(10000.0) ** (-np.arange(half, dtype=np.float32) / np.float32(half))).astype(np.float32)
    pos = np.concatenate([np.arange(TP, dtype=np.float32),
                          np.tile(np.float32(16384.0) + np.arange(TS, dtype=np.float32), NSQ)]).astype(np.float32)
    ang = (pos[:, None] * inv[None, :]).astype(np.float32)
    cos = np.cos(ang).astype(np.float32).T
    sin = np.sin(ang).astype(np.float32).T
    cosT = np.zeros((128, NTOK), np.float32)
    sinT = np.zeros((128, NTOK), np.float32)
    for p in range(128):
        d = p % 64
        i = d % 32
        cosT[p] = cos[i]
        sinT[p] = -sin[i] if d < 32 else sin[i]
    return cosT, sinT


def _gret_tables():
    heads = np.arange(4, dtype=np.float32)
    lg = np.log(np.float32(1.0) - np.float32(2.0) ** (np.float32(-5.0) - heads)).astype(np.float32)
    g = np.zeros((2, 128, 2, MB), np.float32)
    j = np.arange(MB) % 64
    for tl in range(2):
        for p in range(128):
            h = 2 * tl + p // 64
            g[0, p, tl, :] = (j + 1).astype(np.float32) * lg[h]
            g[1, p, tl, :] = (np.minimum(j, TS - 1) + 1).astype(np.float32) * lg[h]
    return g


def _tile_w(w, cols):
    out = np.zeros((128, KC, 128), np.float32)
    cols = np.asarray(cols)
    valid = cols >= 0
    sub = w[:, cols[valid]].reshape(KC, 128, -1)
    out[:, :, np.nonzero(valid)[0]] = np.transpose(sub, (1, 0, 2))
    return out.reshape(128, KC * 128)


def _win_tiles(w):
    tiles = []
    r = lambda a, n: list(range(a, a + n))
    pad = lambda lst: lst + [-1] * (128 - len(lst))
    for base in (C_RQ, C_RK, C_RV, C_RG):
        for tl in range(2):
            tiles.append(r(base + tl * 128, 128))
    for base in (C_AQ, C_AK):
        for tl in range(2):
            cols = []
            for p in range(2):
                h = 2 * tl + p
                cols += r(base + h * 32, 32) + [-1] * 32
            tiles.append(cols)
    for base in (C_AV, C_AG):
        for tl in range(2):
            tiles.append(r(base + tl * 128, 128))
    tiles.append(pad(r(C_ALR, 16)))
    for base in (C_HQ, C_HF, C_HI, C_HG):
        for tl in range(2):
            tiles.append(r(base + tl * 128, 128))
    for base in (C_DQ, C_DK, C_DV, C_DG):
        for tl in range(2):
            tiles.append(r(base + tl * 128, 128))
    tiles.append(pad(r(C_DB, 4)))
    tiles.append(pad(r(C_DA, 4)))
    assert len(tiles) == NWT
    return np.stack([_tile_w(w, c) for c in tiles], 0)


def _state_in(st, dk):
    out = np.zeros((DEPTH, 2, 64, 2, NSQ, 64), np.float32)
    x = st.reshape(DEPTH, NSQ, 2, 2, dk, 64)
    out[:, :, 0:dk] = np.transpose(x, (0, 3, 4, 2, 1, 5))
    return out.reshape(DEPTH, 128, 2, NSQ, 64)


def _state_out_s(o, dk):
    x = o.reshape(DEPTH, 2, 64, 2, NSQ, 64)[:, :, 0:dk]
    return np.ascontiguousarray(np.transpose(x, (0, 4, 3, 1, 2, 5)).reshape(DEPTH, NSQ, 4, dk, 64))


def _state_out_p(o, dk):
    x = o.reshape(DEPTH, 2, 64, 2, 64)[:, :, 0:dk]
    return np.ascontiguousarray(np.transpose(x, (0, 3, 1, 2, 4)).reshape(DEPTH, 4, dk, 64))


_NC_CACHE = {}


def _get_nc(key=(), **kw):
    if key not in _NC_CACHE:
        p = Prog4()
        nc = p.build(**kw)
        _NC_CACHE[key] = (nc, p)
    return _NC_CACHE[key]


def _prepare_inputs(x_prompt, x_sample, state_ret, state_gla, state_hgrn, state_gdn, state_gdn_conv,
                    c_prompt, c_sample, ada_w, ada_b, ln_g, ln_b, ffn1_wi, ffn1_wo, ffn2_wi, ffn2_wo,
                    w_in, gla_wg, gla_bg, hg_lb, gdn_conv, gdn_a_log, gdn_dt_bias,
                    gla_norm, hg_norm, gdn_norm, w_out):
    f = lambda a: np.ascontiguousarray(np.asarray(a, dtype=np.float32))
    shared = {}
    for nm, wi in (("wi1", ffn1_wi), ("wi2", ffn2_wi)):
        wi = f(wi)
        shared[nm] = np.stack([np.stack([_tile_w(wi[l], list(range(c * 128, (c + 1) * 128))) for c in range(2 * NJ)], 0)
                               for l in range(DEPTH)], 0)
    shared["wo1"] = f(ffn1_wo)
    shared["wo2"] = f(ffn2_wo)
    w_in = f(w_in)
    shared["win"] = np.stack([_win_tiles(w_in[l]) for l in range(DEPTH)], 0)
    shared["wout"] = f(w_out).reshape(DEPTH, 8, 128, 1024)
    ada_w = f(ada_w)
    shared["adaw"] = np.stack([np.stack([_tile_w(ada_w[l], list(range(c * 128, (c + 1) * 128))) for c in range(72)], 0)
                               for l in range(DEPTH)], 0)
    pv = np.zeros((DEPTH, 128, NPV), np.float32)
    ada_b = f(ada_b); ln_g = f(ln_g); ln_b = f(ln_b); gdn_conv = f(gdn_conv); gla_bg = f(gla_bg); hg_lb = f(hg_lb)
    gla_norm = f(gla_norm); hg_norm = f(hg_norm); gdn_norm = f(gdn_norm); gdn_a_log = f(gdn_a_log); gdn_dt_bias = f(gdn_dt_bias)
    for l in range(DEPTH):
        pv[l, :, PV_ADAB:PV_ADAB + 72] = ada_b[l].reshape(72, 128).T
        for i in range(3):
            pv[l, :, PV_LNG + i * 8:PV_LNG + (i + 1) * 8] = ln_g[l, i].reshape(8, 128).T
            pv[l, :, PV_LNB + i * 8:PV_LNB + (i + 1) * 8] = ln_b[l, i].reshape(8, 128).T
        cw = gdn_conv[l].reshape(4, 6, 128)
        pv[l, :, PV_CONV:PV_CONV + 24] = np.transpose(cw, (2, 1, 0)).reshape(128, 24)
        bg = np.zeros((2, 2, 64), np.float32)
        bg[:, :, 0:32] = gla_bg[l].reshape(2, 2, 32)
        pv[l, :, PV_BG:PV_BG + 2] = bg.reshape(2, 128).T
        pv[l, :, PV_NORM + 0] = np.tile(gla_norm[l], 2)
        pv[l, :, PV_NORM + 1] = np.tile(hg_norm[l], 2)
        pv[l, :, PV_NORM + 2] = np.tile(gdn_norm[l], 2)
        pv[l, :, PV_LB0:PV_LB0 + 2] = hg_lb[0].reshape(2, 128).T
        pv[l, :, PV_LB1:PV_LB1 + 2] = hg_lb[1].reshape(2, 128).T
        pv[l, 0:4, PV_ALOG] = gdn_a_log[l]
        pv[l, 0:4, PV_DTB] = gdn_dt_bias[l]
    shared["pvec"] = pv
    gla_wg = f(gla_wg)
    wgp = np.zeros((DEPTH, 16, 2, 2, 64), np.float32)
    wgp[:, :, :, :, 0:32] = gla_wg.reshape(DEPTH, 16, 2, 2, 32)
    shared["wgpad"] = wgp.reshape(DEPTH, 16, 256)
    cosT, sinT = _rot_tables()
    shared["cosT"] = cosT
    shared["sinT"] = sinT
    shared["cst"] = _const_pack()
    shared["gret"] = _gret_tables()
    x_prompt = f(x_prompt); x_sample = f(x_sample); c_prompt = f(c_prompt); c_sample = f(c_sample)
    sts = {"ret": (f(state_ret), 64), "gla": (f(state_gla), 32), "hg": (f(state_hgrn), 64), "gdn": (f(state_gdn), 64)}
    state_gdn_conv = f(state_gdn_conv)
    in_maps = []
    for c in range(NCORES):
        d = dict(shared)
        sq = slice(c * NSQ, (c + 1) * NSQ)
        xs = x_sample[sq].reshape(NSQ * TS, D)
        d["xT"] = np.ascontiguousarray(np.concatenate([x_prompt[c], xs], 0).T)
        d["cT"] = np.ascontiguousarray(np.concatenate([c_prompt[c:c + 1], c_sample[sq]], 0).T)
        for nm, (st, dk) in sts.items():
            d["st_" + nm] = _state_in(st[:, sq], dk)
        cvs = state_gdn_conv[:, sq].reshape(DEPTH, NSQ, 3, 6, 128)
        d["st_conv"] = np.ascontiguousarray(np.transpose(cvs, (0, 4, 3, 1, 2)))
        in_maps.append(d)
    return in_maps


def _assemble(results):
    y_p = np.zeros((NCORES, TP, D), np.float32)
    y_s = np.zeros((NCORES * NSQ, TS, D), np.float32)
    dks = {"ret": 64, "gla": 32, "hg": 64, "gdn": 64}
    p_st = {nm: np.zeros((DEPTH, NCORES, 4, dk, 64), np.float32) for nm, dk in dks.items()}
    s_st = {nm: np.zeros((DEPTH, NCORES * NSQ, 4, dk, 64), np.float32) for nm, dk in dks.items()}
    p_conv = np.zeros((DEPTH, NCORES, 3, 768), np.float32)
    s_conv = np.zeros((DEPTH, NCORES * NSQ, 3, 768), np.float32)
    for c, r in enumerate(results):
        yT = np.asarray(r["yT"])
        y_p[c] = yT[:, 0:TP].T
        y_s[c * NSQ:(c + 1) * NSQ] = yT[:, TP:].T.reshape(NSQ, TS, D)
        for nm, dk in dks.items():
            p_st[nm][:, c] = _state_out_p(np.asarray(r["ops_" + nm]), dk)
            s_st[nm][:, c * NSQ:(c + 1) * NSQ] = _state_out_s(np.asarray(r["oss_" + nm]), dk)
        pc = np.asarray(r["opconv"])
        p_conv[:, c] = np.transpose(pc, (0, 3, 2, 1)).reshape(DEPTH, 3, 768)
        sc = np.asarray(r["osconv"])
        s_conv[:, c * NSQ:(c + 1) * NSQ] = np.transpose(sc, (0, 3, 4, 2, 1)).reshape(DEPTH, NSQ, 3, 768)
    return (y_p, y_s, p_st["ret"], p_st["gla"], p_st["hg"], p_st["gdn"], p_conv,
            s_st["ret"], s_st["gla"], s_st["hg"], s_st["gdn"], s_conv)


def kernel(**inputs):
    in_maps = _prepare_inputs(**inputs)
    nc, _ = _get_nc()
    res = run_bass_kernel_spmd(nc, in_maps, core_ids=list(range(NCORES)))
    return _assemble(res.results)
```

```python
import numpy as np
from contextlib import ExitStack
import concourse.bass as bass
import concourse.mybir as mybir
from concourse.bass_utils import run_bass_kernel_spmd

F32 = mybir.dt.float32
BF16 = mybir.dt.bfloat16
AF = mybir.ActivationFunctionType
ALU = mybir.AluOpType

NCORES = 8
D = 1024
KC = 8
DFF = 2816
NJ = 22
TP = 2048
NSQ = 16
TS = 4
NTOK = TP + NSQ * TS
DEPTH = 2
ALPHA = float((2.0 * DEPTH) ** 0.25)
LN_EPS = 1e-5
RMS_EPS = 1e-6
TBS = [(0, 512), (512, 512), (1024, 512), (1536, 512), (2048, 64)]
JG = [list(range(0, 8)), list(range(8, 15)), list(range(15, 22))]
MB = 256
NCH = MB // 64
NPB = TP // MB
NSB = NSQ // NCH
NW = 12
NDS = 8
import os as _os
CUT = int(_os.environ.get('KCUT', '0'))
CUTN = int(_os.environ.get('KCUTN', '-1'))
STRICT = int(_os.environ.get('KSTRICT', '1'))
CUTG = int(_os.environ.get('KCUTG', '0'))
ATTACH = int(_os.environ.get('KATTACH', '1'))
GP = _os.environ.get('KGP', 'pool')

C_RQ, C_RK, C_RV, C_RG = 0, 256, 512, 768
C_AQ, C_AK, C_AV, C_ALR, C_AG = 1024, 1152, 1280, 1536, 1552
C_HQ, C_HF, C_HI, C_HG = 1808, 2064, 2320, 2576
C_DQ, C_DK, C_DV, C_DB, C_DA, C_DG = 2832, 3088, 3344, 3600, 3604, 3608
WT = {}
_names = (["rq0", "rq1", "rk0", "rk1", "rv0", "rv1", "rg0", "rg1"] +
          ["aq0", "aq1", "ak0", "ak1", "av0", "av1", "ag0", "ag1", "alr"] +
          ["hq0", "hq1", "hf0", "hf1", "hi0", "hi1", "hg0", "hg1"] +
          ["dq0", "dq1", "dk0", "dk1", "dv0", "dv1", "dg0", "dg1", "db", "da"])
for _i, _n in enumerate(_names):
    WT[_n] = _i
NWT = len(_names)
PV_ADAB, PV_LNG, PV_LNB, PV_CONV, PV_BG, PV_NORM, PV_LB0, PV_LB1, PV_ALOG, PV_DTB, NPV = 0, 72, 96, 120, 144, 146, 149, 151, 153, 154, 160


def _esize(dt):
    return mybir.dt.size(dt)


class _Op:
    __slots__ = ("eng", "fn", "deps", "dmaq", "signal", "sigval", "dslot", "dval", "clock", "prio")

    def __init__(self, eng, fn, deps, dmaq):
        self.eng = eng
        self.fn = fn
        self.deps = deps
        self.dmaq = dmaq
        self.signal = False
        self.sigval = 0
        self.dslot = 0
        self.dval = 0
        self.clock = None
        self.prio = ()


class Sched:
    BUCK = 1024

    def __init__(self, nc, es):
        self.nc = nc
        self.engs = {"pe": nc.tensor, "act": nc.scalar, "dve": nc.vector, "pool": nc.gpsimd, "sp": nc.sync}
        self.ops = []
        self.buckets = {}
        self.mloc = {}
        self.sem = {e: es.enter_context(nc.semaphore("s_" + e)) for e in self.engs}
        self.dsem = {q: [es.enter_context(nc.semaphore("d_%s%d" % (q, i))) for i in range(NDS)]
                     for q in ("sp", "act", "pool")}
        self.out_dmas = []

    def _box(self, ap):
        sp = str(ap.space)
        if sp not in ("SB", "PSUM"):
            return None
        t = ap.tensor
        key = t.name
        info = self.mloc.get(key)
        if info is None:
            ml = self.nc.lookup_mloc(t)
            base = int(ml.addr)
            if sp == "PSUM":
                base += int(ml.bank) * 2048
            info = base
            self.mloc[key] = info
        shape = t.shape
        F = 1
        for s in shape[1:]:
            F *= int(s)
        off = int(ap.offset)
        p0 = off // F
        f0 = off % F
        dims = ap.ap
        pc = int(dims[0][1])
        lo = f0
        hi = f0
        for (st, cnt) in dims[1:]:
            ext = int(st) * (int(cnt) - 1)
            if ext < 0:
                lo += ext
            else:
                hi += ext
        hi += 1
        es_ = _esize(ap.dtype)
        if sp == "PSUM" and STRICT:
            b0 = ((info + lo * es_) // 2048) * 2048
            b1 = ((info + hi * es_ - 1) // 2048 + 1) * 2048
            return (sp, (p0 // 32) * 32, ((p0 + pc + 31) // 32) * 32, b0, b1)
        return (sp, p0, p0 + pc, info + lo * es_, info + hi * es_)

    @staticmethod
    def _ov(a, b):
        return a[1] < b[2] and b[1] < a[2] and a[3] < b[4] and b[3] < a[4]

    @staticmethod
    def _cov(a, b):
        return a[1] <= b[1] and a[2] >= b[2] and a[3] <= b[3] and a[4] >= b[4]

    def _keys(self, box):
        return [(box[0], k) for k in range(box[3] // self.BUCK, (box[4] - 1) // self.BUCK + 1)]

    def add(self, eng, fn, reads, writes, dmaq=None, prio=()):
        idx = len(self.ops)
        raw = set()
        praw = set()
        for ap in prio:
            b = self._box(ap)
            if b is not None:
                for k in self._keys(b):
                    for rec in self.buckets.get(k, ()):
                        if rec[2] and self._ov(b, rec[0]):
                            praw.add(rec[1])
        oth = set()
        rboxes = []
        wboxes = []
        for ap in reads:
            b = self._box(ap)
            if b is not None:
                rboxes.append(b)
        for ap in writes:
            b = self._box(ap)
            if b is not None:
                wboxes.append(b)
        for b in rboxes:
            for k in self._keys(b):
                for rec in self.buckets.get(k, ()):
                    if rec[2] and self._ov(b, rec[0]):
                        raw.add(rec[1])
        for b in wboxes:
            for k in self._keys(b):
                lst = self.buckets.get(k)
                if not lst:
                    continue
                keep = []
                for rec in lst:
                    if self._ov(b, rec[0]):
                        oth.add(rec[1])
                        if self._cov(b, rec[0]):
                            continue
                    keep.append(rec)
                self.buckets[k] = keep
        myeng = eng
        for b in rboxes:
            rec = (b, idx, False, myeng, dmaq is not None)
            for k in self._keys(b):
                lst = self.buckets.setdefault(k, [])
                for i2, r2 in enumerate(lst):
                    if (not r2[2]) and r2[0] == b and r2[3] == myeng and (not r2[4]) and dmaq is None:
                        lst[i2] = rec
                        break
                else:
                    lst.append(rec)
        for b in wboxes:
            rec = (b, idx, True, myeng, dmaq is not None)
            for k in self._keys(b):
                self.buckets.setdefault(k, []).append(rec)
        deps = []
        for d in raw | oth:
            dop = self.ops[d]
            if dop.dmaq is None and dmaq is None and dop.eng == eng:
                if eng == "pe":
                    continue
                if d not in raw and not STRICT:
                    continue
            deps.append(d)
            if dop.dmaq is None:
                dop.signal = True
        op = _Op(eng, fn, deps, dmaq)
        op.prio = praw
        self.ops.append(op)
        return idx

    def emit(self):
        nc = self.nc
        cnt = {e: 0 for e in self.engs}
        for op in self.ops:
            if op.dmaq is None and op.signal:
                cnt[op.eng] += 1
                op.sigval = cnt[op.eng]
        seen = {e: {} for e in self.engs}
        self.opidx = {id(o): i for i, o in enumerate(self.ops)}
        dcount = {q: 0 for q in self.dsem}
        dlast = {q: [None] * NDS for q in self.dsem}
        nwaits = 0
        for op in self.ops:
            e = op.eng
            eng = self.engs[e]
            sn = seen[e]
            needs = []
            for d in op.deps:
                dop = self.ops[d]
                if dop.dmaq is None:
                    needs.append((("c", dop.eng), dop.sigval, dop))
                else:
                    needs.append((("d", dop.dmaq, dop.dslot), dop.dval, dop))
            if op.dmaq is not None:
                q = op.dmaq
                slot = dcount[q] % NDS
                dcount[q] += 1
                prev = dlast[q][slot]
                op.dslot = slot
                op.dval = (prev.dval if prev is not None else 0) + 16
                if prev is not None:
                    needs.append((("d", q, slot), prev.dval, prev))
                dlast[q][slot] = op
            needs.sort(key=lambda x: -self.opidx[id(x[2])])
            pending = []
            for (key, val, dop) in needs:
                if sn.get(key, 0) >= val:
                    continue
                semh = self.sem[key[1]] if key[0] == "c" else self.dsem[key[1]][key[2]]
                pending.append((semh, val, self.opidx[id(dop)] in op.prio))
                nwaits += 1
                for k2, v2 in dop.clock.items():
                    if sn.get(k2, 0) < v2:
                        sn[k2] = v2
            attach = None
            if pending and ATTACH:
                pi = [i for i, x in enumerate(pending) if x[2]]
                attach = pending.pop(pi[-1] if pi else -1)
            for (semh, val, _) in pending:
                eng.wait_ge(semh, val)
            ins = op.fn(eng)
            if attach is not None:
                ins._wait_ge(attach[0], attach[1])
            clock = dict(sn)
            if op.dmaq is not None:
                ins.then_inc(self.dsem[op.dmaq][op.dslot], 16)
                clock[("d", op.dmaq, op.dslot)] = op.dval
            elif op.signal:
                ins.then_inc(self.sem[e], 1)
                clock[("c", e)] = op.sigval
            op.clock = clock
            op.fn = None
        sp = self.engs["sp"]
        sn = seen["sp"]
        for d in self.out_dmas:
            dop = self.ops[d]
            key = ("d", dop.dmaq, dop.dslot)
            if sn.get(key, 0) >= dop.dval:
                continue
            sp.wait_ge(self.dsem[dop.dmaq][dop.dslot], dop.dval)
            sn[key] = dop.dval
        return nwaits


class KB:
    def __init__(self, nc, es):
        self.nc = nc
        self.es = es
        self.S = Sched(nc, es)
        self.dbg_outs = []

    def sb(self, name, shape, dt):
        return self.es.enter_context(self.nc.sbuf_tensor("sb_" + name, shape, dt))

    def psum(self, name, shape, dt):
        return self.es.enter_context(self.nc.psum_tensor(name, shape, dt))

    def dram_in(self, name, shape):
        return self.nc.dram_tensor(name, list(shape), F32, kind="ExternalInput").ap()

    def dram_out(self, name, shape):
        return self.nc.dram_tensor(name, list(shape), F32, kind="ExternalOutput").ap()

    def mm(self, out, lhsT, rhs, start=True, stop=True):
        self.S.add("pe", lambda e: e.matmul(out, lhsT=lhsT, rhs=rhs, start=start, stop=stop), [lhsT, rhs], [out], prio=[lhsT])

    def tr(self, out, in_, ident):
        self.S.add("pe", lambda e: e.transpose(out, in_, ident), [in_, ident], [out], prio=[in_])

    def act(self, out, in_, func, bias=None, scale=None, eng="act"):
        kw = {}
        reads = [in_]
        if bias is not None:
            kw["bias"] = bias
            if not isinstance(bias, (int, float)):
                reads.append(bias)
        if scale is not None:
            kw["scale"] = scale
            if not isinstance(scale, (int, float)):
                reads.append(scale)
        self.S.add(eng, lambda e: e.activation(out=out, in_=in_, func=func, **kw), reads, [out])

    def tt(self, out, a, b, op, eng="dve"):
        self.S.add(eng, lambda e: e.tensor_tensor(out=out, in0=a, in1=b, op=op), [a, b], [out])

    def ts(self, out, a, s1, op0, s2=None, op1=None, eng="dve"):
        reads = [a]
        if not isinstance(s1, (int, float)):
            reads.append(s1)
        if s2 is not None and not isinstance(s2, (int, float)):
            reads.append(s2)
        if s2 is None:
            self.S.add(eng, lambda e: e.tensor_scalar(out=out, in0=a, scalar1=s1, scalar2=None, op0=op0), reads, [out])
        else:
            self.S.add(eng, lambda e: e.tensor_scalar(out=out, in0=a, scalar1=s1, scalar2=s2, op0=op0, op1=op1), reads, [out])

    def stt(self, out, a, s, b, op0, op1):
        reads = [a, b]
        if not isinstance(s, (int, float)):
            reads.append(s)
        self.S.add("dve", lambda e: e.scalar_tensor_tensor(out=out, in0=a, scalar=s, in1=b, op0=op0, op1=op1), reads, [out])

    def cp(self, out, in_, eng="dve"):
        if eng == "act":
            fn_ = AF.Identity if _os.environ.get('KVAR3', '') == 'ident' else AF.Copy
            self.S.add("act", lambda e: e.activation(out=out, in_=in_, func=fn_), [in_], [out])
        else:
            self.S.add(eng, lambda e: e.tensor_copy(out=out, in_=in_), [in_], [out])

    def ms(self, ap, val, eng="dve"):
        self.S.add(eng, lambda e: e.memset(ap, val), [], [ap])

    def scan(self, out, d0, d1, init, op0, op1):
        reads = [d0, d1]
        self.S.add("dve", lambda e: e.tensor_tensor_scan(out=out, data0=d0, data1=d1, initial=init, op0=op0, op1=op1), reads, [out])

    def recip(self, out, in_):
        self.S.add("dve", lambda e: e.reciprocal(out=out, in_=in_), [in_], [out])

    def dma(self, q, out, in_, is_out=False):
        idx = self.S.add(q, lambda e: e.dma_start(out=out, in_=in_), [in_], [out], dmaq=q)
        if is_out:
            self.S.out_dmas.append(idx)
        return idx

    def dbg(self, name, ap):
        shape = [int(s) for s in ap.shape]
        o = self.dram_out("dbg_" + name, shape)
        self.dma("sp", o, ap, is_out=True)
        self.dbg_outs.append(("dbg_" + name, shape))


class Prog:
    def __init__(self, debug=None):
        self.debug = debug or set()
        self.nc = bass.Bass("TRN2", target_bir_lowering=False)
        self.es = ExitStack()
        self.k = KB(self.nc, self.es)
        self.wcnt = 0
        self.pscnt = 0

    def alloc(self):
        k = self.k
        self.d_xT = k.dram_in("xT", [D, NTOK])
        self.d_cT = k.dram_in("cT", [D, 17])
        self.d_wi = [k.dram_in("wi1", [DEPTH, 2 * NJ, 128, 1024]), k.dram_in("wi2", [DEPTH, 2 * NJ, 128, 1024])]
        self.d_wo = [k.dram_in("wo1", [DEPTH, DFF, D]), k.dram_in("wo2", [DEPTH, DFF, D])]
        self.d_win = k.dram_in("win", [DEPTH, NWT, 128, 1024])
        self.d_wout = k.dram_in("wout", [DEPTH, 8, 128, 1024])
        self.d_adaw = k.dram_in("adaw", [DEPTH, 72, 128, 1024])
        self.d_pvec = k.dram_in("pvec", [DEPTH, 128, NPV])
        self.d_wg = k.dram_in("wgpad", [DEPTH, 16, 256])
        self.d_cos = k.dram_in("cosT", [128, NTOK])
        self.d_sin = k.dram_in("sinT", [128, NTOK])
        self.d_cst = k.dram_in("cst", [128, CST_N])
        self.d_gret = k.dram_in("gret", [2, 128, 2, MB])
        self.d_st = {}
        for nm in ("ret", "gla", "hg", "gdn"):
            self.d_st[nm] = k.dram_in("st_" + nm, [DEPTH, 128, 2, NSQ, 64])
        self.d_stconv = k.dram_in("st_conv", [DEPTH, 128, 6, NSQ, 3])
        self.o_yT = k.dram_out("yT", [D, NTOK])
        self.o_ps = {}
        self.o_ss = {}
        for nm in ("ret", "gla", "hg", "gdn"):
            self.o_ps[nm] = k.dram_out("ops_" + nm, [DEPTH, 128, 2, 64])
            self.o_ss[nm] = k.dram_out("oss_" + nm, [DEPTH, 128, 2, NSQ, 64])
        self.o_pconv = k.dram_out("opconv", [DEPTH, 128, 6, 3])
        self.o_sconv = k.dram_out("osconv", [DEPTH, 128, 6, NSQ, 3])
        self.xres = k.sb("xres", [128, KC, NTOK], F32)
        self.hT = k.sb("hT", [128, KC, NTOK], BF16)
        self.gbig = k.sb("gbig", [128, 8 * NTOK], BF16)
        self.wpool = [k.sb("w%d" % i, [128, 1024], BF16) for i in range(NW)]
        self.cst = k.sb("cst", [128, CST_N], F32)
        self.cstb = k.sb("cstb", [128, CSTB_N], BF16)
        self.pvec = [k.sb("pvec%d" % l, [128, NPV], F32) for l in range(DEPTH)]
        self.cTs = k.sb("cTs", [128, KC, 17], F32)
        self.csl = k.sb("csl", [128, KC, 17], BF16)
        self.modg = k.sb("modg", [128, 24, 17], F32)
        self.cF = [[k.sb("cF%d%d" % (l, i), [128, KC, 17], F32) for i in range(3)] for l in range(DEPTH)]
        self.hs = [[k.sb("hs%d%d" % (l, i), [128, KC, 17], F32) for i in range(3)] for l in range(DEPTH)]
        self.hb = [[k.sb("hb%d%d" % (l, i), [128, KC, 17], F32) for i in range(3)] for l in range(DEPTH)]
        self.xs = [k.sb("xs%d" % l, [128, 3, KC], F32) for l in range(DEPTH)]
        self.xb = [k.sb("xb%d" % l, [128, 3, KC], F32) for l in range(DEPTH)]
        self.wg = k.sb("wg", [16, 256], F32)
        self.wgb = k.sb("wgb", [16, 256], BF16)
        self.small = k.sb("small", [128, 64], F32)
        self.arena2 = k.sb("arena2", [128, A2_BYTES // 4], F32)
        self.psb = [k.psum("ps%d" % i, [128, 512], F32) for i in range(8)]

    def carve(self, region, off, nbytes, dt, parts=128):
        if region == "g":
            base = self.gbig
            es = 2
            tot = 8 * NTOK * 2
        else:
            base = self.arena2
            es = 4
            tot = A2_BYTES
        assert off % 4 == 0 and nbytes % 4 == 0 and off + nbytes <= tot, (region, off, nbytes, tot)
        ap = base[0:parts, off // es:(off + nbytes) // es]
        if dt == F32 and es == 2:
            ap = ap.bitcast(F32)
        elif dt == BF16 and es == 4:
            ap = ap.bitcast(BF16)
        return ap

    def ps(self):
        p = self.psb[self.pscnt % 6]
        self.pscnt += 1
        return p

    def wload(self, dram_ap):
        w = self.wpool[self.wcnt % NW]
        self.wcnt += 1
        self.k.dma("pool", w[:, :], dram_ap)
        return w

    def pv(self, l, col, n=1):
        return self.pvec[l][:, col:col + n]


CST_IDENT = 0
CST_MINCL = 128
CST_MSTR = 192
CST_MSTRT = 256
CST_ID2 = 320
CST_RESET = 384
CST_SEL = 384 + MB
CST_SELF = CST_SEL + 256
CST_ONES = CST_SELF + 256
CST_BONES = CST_ONES + 128
CST_PM = CST_BONES + 128
CST_SELHP = CST_PM + 128
CST_N = CST_SELHP + 4
CB_IDENT, CB_ONES, CB_BONES, CB_PM, CSTB_N = 0, 128, 256, 384, 512
A2_BYTES = 24 * 1024


def _bc(ap, shape):
    return ap.to_broadcast(list(shape))


class Prog2(Prog):
    def setup(self):
        k = self.k
        k.dma("sp", self.cst[:, :], self.d_cst)
        for l in range(DEPTH):
            k.dma("sp", self.pvec[l][:, :], self.d_pvec[l])
        k.dma("sp", self.cTs[:, :, :], self.d_cT.rearrange("(kc p) b -> p kc b", p=128))
        k.dma("sp", self.xres[:, :, :], self.d_xT.rearrange("(kc p) t -> p kc t", p=128))
        for (src, dst) in ((CST_IDENT, CB_IDENT), (CST_ONES, CB_ONES), (CST_BONES, CB_BONES), (CST_PM, CB_PM)):
            k.cp(self.cstb[:, dst:dst + 128], self.cst[:, src:src + 128])
        k.ms(self.small[:, 8:9], 1.0)
        k.ms(self.small[:, 9:10], RMS_EPS)
        k.ms(self.small[:, 10:11], 0.0)
        self.c_one = self.small[:, 8:9]
        self.c_eps = self.small[:, 9:10]
        k.act(self.csl[:, :, :], self.cTs[:, :, :], AF.Silu)
        for l in range(DEPTH):
            last = (l == DEPTH - 1)
            for i in range(3):
                a = 1.0 if (last and i == 2) else ALPHA
                k.ts(self.xs[l][:, i, :], self.pv(l, PV_LNG + i * 8, 8), a, ALU.mult)
                k.ts(self.xb[l][:, i, :], self.pv(l, PV_LNB + i * 8, 8), a, ALU.mult)

    def ident_b(self):
        return self.cstb[:, CB_IDENT:CB_IDENT + 128]

    def mod_group(self, l, i):
        for _ in self.mod_group_gen(l, i):
            pass

    def mod_group_gen(self, l, i):
        k = self.k
        for f in range(24):
            ft = i * 24 + f
            w = self.wload(self.d_adaw[l, ft])
            p = self.ps()
            for kc in range(KC):
                k.mm(p[:, 0:17], w[:, kc * 128:(kc + 1) * 128], self.csl[:, kc, :], start=(kc == 0), stop=(kc == KC - 1))
            k.ts(self.modg[:, f, :], p[:, 0:17], self.pv(l, PV_ADAB + ft), ALU.add)
            yield
        sh = self.modg[:, 0:8, :]
        sc = self.modg[:, 8:16, :]
        gt = self.modg[:, 16:24, :]
        coef = 1.0 if i == 1 else 0.5
        k.ts(self.cF[l][i][:, :, :], gt, 1.0, ALU.add, coef, ALU.mult)
        if i == 0 and l == 0:
            k.ts(self.hs[l][i][:, :, :], sc, 1.0, ALU.add)
            k.cp(self.hb[l][i][:, :, :], sh)
        else:
            pl, pi = (l, i - 1) if i > 0 else (l - 1, 2)
            gp = _bc(self.pv(pl, PV_LNG + pi * 8, 8).unsqueeze(2), [128, 8, 17])
            bp = _bc(self.pv(pl, PV_LNB + pi * 8, 8).unsqueeze(2), [128, 8, 17])
            k.ts(self.hs[l][i][:, :, :], sc, 1.0, ALU.add)
            k.tt(self.hb[l][i][:, :, :], self.hs[l][i][:, :, :], bp, ALU.mult)
            k.tt(self.hb[l][i][:, :, :], self.hb[l][i][:, :, :], sh, ALU.add)
            k.tt(self.hs[l][i][:, :, :], self.hs[l][i][:, :, :], gp, ALU.mult)

    def affine_h(self, out, in_, sv, bv, kc, tbi, tmp):
        k = self.k
        if tbi < 4:
            k.act(out, in_, AF.Identity, bias=bv[:, kc, 0:1], scale=sv[:, kc, 0:1])
        else:
            s3 = _bc(sv[:, kc, 1:17].unsqueeze(2), [128, NSQ, TS])
            b3 = _bc(bv[:, kc, 1:17].unsqueeze(2), [128, NSQ, TS])
            i3 = in_.rearrange("p (s j) -> p s j", j=TS)
            o3 = out.rearrange("p (s j) -> p s j", j=TS)
            t3 = tmp.rearrange("p (s j) -> p s j", j=TS)
            k.tt(t3, i3, s3, ALU.mult)
            k.tt(o3, t3, b3, ALU.add)

    def initial_h(self):
        k = self.k
        tmpS = self.carve("a", 18432, 256, F32)
        for tbi, (t0, n) in enumerate(TBS):
            for kc in range(KC):
                self.affine_h(self.hT[:, kc, t0:t0 + n], self.xres[:, kc, t0:t0 + n], self.hs[0][0], self.hb[0][0], kc, tbi, tmpS[:, 0:64])
        for kc in range(KC):
            k.ts(self.xres[:, kc, :], self.xres[:, kc, :], ALPHA, ALU.mult)

    def resid_acc(self, ps_ap, l, i, kc, tbi, t0, n, s0=0, ns=NSQ):
        k = self.k
        xr = self.xres[:, kc, t0:t0 + n]
        cf = self.cF[l][i]
        if tbi < 4:
            k.stt(xr, ps_ap, cf[:, kc, 0:1], xr, ALU.mult, ALU.add)
        else:
            tmpS = self.carve("a", 18432, 256, F32)[:, 0:n]
            c3 = _bc(cf[:, kc, 1 + s0:1 + s0 + ns].unsqueeze(2), [128, ns, TS])
            k.tt(tmpS.rearrange("p (s j) -> p s j", j=TS), ps_ap.rearrange("p (s j) -> p s j", j=TS), c3, ALU.mult)
            k.tt(xr, xr, tmpS, ALU.add)

    def stats_acc(self, ps1, ps2, kc, t0, n):
        k = self.k
        rb = self.carve("a", 4096 + (kc % 2) * 1024, 1024, BF16)[:, 0:n]
        rsq = self.carve("a", 6144 + (kc % 2) * 1024, 1024, BF16)[:, 0:n]
        xr = self.xres[:, kc, t0:t0 + n]
        k.cp(rb, xr, eng="act")
        k.act(rsq, xr, AF.Square)
        ones = self.cstb[:, CB_ONES:CB_ONES + 128]
        k.mm(ps1[:, 0:n], ones, rb, start=(kc == 0), stop=(kc == KC - 1))
        k.mm(ps2[:, 0:n], ones, rsq, start=(kc == 0), stop=(kc == KC - 1))

    def ln_block(self, ps1, ps2, l, i, tbi, t0, n):
        k = self.k
        mean = self.carve("a", 8192, 2048, F32)[:, 0:n]
        rstd = self.carve("a", 10240, 2048, F32)[:, 0:n]
        nmr = self.carve("a", 12288, 2048, F32)[:, 0:n]
        ex2 = self.carve("a", 18688, 2048, F32)[:, 0:n]
        tmpS = self.carve("a", 18432, 256, F32)
        k.ts(mean, ps1[:, 0:n], 1.0 / D, ALU.mult)
        k.tt(ex2, mean, mean, ALU.mult)
        k.stt(ex2, ps2[:, 0:n], 1.0 / D, ex2, ALU.mult, ALU.subtract)
        k.ts(ex2, ex2, 0.0, ALU.max, LN_EPS, ALU.add)
        k.act(ex2, ex2, AF.Sqrt)
        k.recip(rstd, ex2)
        k.stt(nmr, mean, -1.0, rstd, ALU.mult, ALU.mult)
        last = (l == DEPTH - 1 and i == 2)
        if i < 2:
            nl, ni = l, i + 1
        else:
            nl, ni = l + 1, 0
        for kc in range(KC):
            xn = self.carve("a", 14336 + (kc % 2) * 2048, 2048, F32)[:, 0:n]
            xr = self.xres[:, kc, t0:t0 + n]
            k.tt(xn, xr, rstd, ALU.mult)
            k.tt(xn, xn, nmr, ALU.add)
            k.act(xr, xn, AF.Identity, bias=self.xb[l][:, i, kc:kc + 1], scale=self.xs[l][:, i, kc:kc + 1])
            if not last:
                self.affine_h(self.hT[:, kc, t0:t0 + n], xn, self.hs[nl][ni], self.hb[nl][ni], kc, tbi, tmpS[:, 0:64])

    def ffn(self, l, which, mid_hook=None):
        k = self.k
        i = 0 if which == 0 else 2
        d_wi = self.d_wi[which]
        d_wo = self.d_wo[which]
        gb = self.gbig
        mid_gen = mid_hook() if mid_hook is not None else None
        for gi, js in enumerate(JG):
            for jj, j in enumerate(js):
                wa = self.wload(d_wi[l, j])
                wb = self.wload(d_wi[l, NJ + j])
                for tbi, (t0, n) in enumerate(TBS):
                    pa = self.ps()
                    pb = self.ps()
                    for kc in range(KC):
                        k.mm(pa[:, 0:n], wa[:, kc * 128:(kc + 1) * 128], self.hT[:, kc, t0:t0 + n], start=(kc == 0), stop=(kc == KC - 1))
                    for kc in range(KC):
                        k.mm(pb[:, 0:n], wb[:, kc * 128:(kc + 1) * 128], self.hT[:, kc, t0:t0 + n], start=(kc == 0), stop=(kc == KC - 1))
                    sA = self.carve("a", ((jj * 5 + tbi) % 2) * 2048, 2048, F32)[:, 0:n]
                    k.act(sA, pa[:, 0:n], AF.Silu)
                    k.tt(gb[:, jj * NTOK + t0: jj * NTOK + t0 + n], sA, pb[:, 0:n], ALU.mult)
                if gi == 0 and mid_gen is not None:
                    for _ in range(3):
                        next(mid_gen, None)
            if gi == 0 and mid_gen is not None:
                for _ in mid_gen:
                    pass
            wos = [self.wload(d_wo[l, j * 128:(j + 1) * 128, :]) for j in js]
            final = (gi == len(JG) - 1)
            for tbi, (t0, n) in enumerate(TBS):
                if final:
                    ps1 = self.psb[6]
                    ps2 = self.psb[7]
                for oc in range(KC):
                    po = self.ps()
                    for jj in range(len(js)):
                        k.mm(po[:, 0:n], wos[jj][:, oc * 128:(oc + 1) * 128], gb[:, jj * NTOK + t0: jj * NTOK + t0 + n],
                             start=(jj == 0), stop=(jj == len(js) - 1))
                    self.resid_acc(po[:, 0:n], l, i, oc, tbi, t0, n)
                    if final:
                        self.stats_acc(ps1, ps2, oc, t0, n)
                if final:
                    self.ln_block(ps1, ps2, l, i, tbi, t0, n)

    def ln_pass(self, l, i):
        for tbi, (t0, n) in enumerate(TBS):
            ps1 = self.psb[6]
            ps2 = self.psb[7]
            for kc in range(KC):
                self.stats_acc(ps1, ps2, kc, t0, n)
            self.ln_block(ps1, ps2, l, i, tbi, t0, n)

    def store_y(self):
        self.k.dma("sp", self.o_yT.rearrange("(kc p) t -> p kc t", p=128), self.xres[:, :, :], is_out=True)


class Prog3(Prog2):
    def blkinfo(self, blk):
        if blk < NPB:
            return False, blk * MB, MB, 0
        sb_ = blk - NPB
        return True, TP + sb_ * NCH * TS, NCH * TS, sb_ * NCH

    def cv(self, ap, is_s, N):
        a = ap[:, 0:N]
        if is_s:
            a = a.rearrange("p (s j) -> p s j", j=TS)
        return a

    def pw(self, ap, is_s):
        if is_s:
            return ap.rearrange("p (s t) -> p s t", t=64)[:, :, 0:TS]
        return ap

    def proj(self, w, t0, N, M=128):
        k = self.k
        p = self.ps()
        for kc in range(KC):
            k.mm(p[0:M, 0:N], w[:, kc * 128:kc * 128 + M], self.hT[:, kc, t0:t0 + N], start=(kc == 0), stop=(kc == KC - 1))
        return p

    def layer_small(self, l):
        k = self.k
        sm = self.small
        k.ts(sm[:, 0:2], self.pv(l, PV_BG, 2), -1.0, ALU.mult)
        if l == 0:
            k.ms(sm[:, 2:4], 0.0)
        else:
            k.tt(sm[:, 2:4], self.pv(l, PV_LB1, 2), self.pv(l, PV_LB0, 2), ALU.subtract)
            k.act(sm[:, 2:4], sm[:, 2:4], AF.Exp, scale=-1.0)
            k.ts(sm[:, 2:4], sm[:, 2:4], 1.0, ALU.add)
            k.recip(sm[:, 2:4], sm[:, 2:4])
        k.ts(sm[:, 4:6], sm[:, 2:4], -1.0, ALU.mult, 1.0, ALU.add)
        k.ts(sm[:, 6:8], sm[:, 4:6], -1.0, ALU.mult)
        k.act(sm[0:4, 11:12], self.pv(l, PV_ALOG)[0:4, :], AF.Exp)
        k.ts(sm[0:4, 11:12], sm[0:4, 11:12], -1.0, ALU.mult)
        k.dma("sp", self.wg[:, :], self.d_wg[l])
        k.cp(self.wgb[:, :], self.wg[:, :])

    def mix_layer(self, l, mid_hook=None):
        k = self.k
        self.layer_small(l)
        mixers = [("ret", ["rq0", "rq1", "rk0", "rk1", "rv0", "rv1", "rg0", "rg1"]),
                  ("gla", ["aq0", "aq1", "ak0", "ak1", "av0", "av1", "ag0", "ag1", "alr"]),
                  ("hg", ["hq0", "hq1", "hf0", "hf1", "hi0", "hi1", "hg0", "hg1"]),
                  ("gdn", ["dq0", "dq1", "dk0", "dk1", "dv0", "dv1", "dg0", "dg1", "db", "da"])]
        for m, (name, wn) in enumerate(mixers):
            if name in self.skip_mixers:
                continue
            W = {n: self.wload(self.d_win[l, WT[n]]) for n in wn}
            Wo = [self.wload(self.d_wout[l, 2 * m + tl]) for tl in range(2)]
            blks = list(range(NPB + NSB) if self.nblk is None else self.nblk)
            if name == "gdn":
                for blk in blks:
                    self.mix_gdn(l, m, blk, W, Wo)
            else:
                def run_to_prep(g_):
                    for v_ in g_:
                        if v_ == "PREP_DONE":
                            return
                cur = self.mix_gla(l, m, name, blks[0], W, Wo, 0)
                run_to_prep(cur)
                for bi in range(len(blks)):
                    nxt = self.mix_gla(l, m, name, blks[bi + 1], W, Wo, (bi + 1) % 2) if bi + 1 < len(blks) else None
                    cur_live, nxt_live = True, nxt is not None
                    while cur_live or nxt_live:
                        if cur_live:
                            try:
                                next(cur)
                            except StopIteration:
                                cur_live = False
                        if nxt_live:
                            try:
                                if next(nxt) == "PREP_DONE":
                                    nxt_live = False
                            except StopIteration:
                                nxt_live = False
                    cur = nxt
            if mid_hook is not None:
                mid_hook()
                mid_hook = None
        if mid_hook is not None:
            mid_hook()
        self.ln_pass(l, 1)

    def chk(self):
        self.chkc = getattr(self, "chkc", 0) + 1
        return self.chkc == CUTN

    def gbuf(self, off, nbytes, dt, parts=128):
        return self.carve("g", off, nbytes, dt, parts)

    def mix_gla(self, l, m, name, blk, W, Wo, st=0):
        k = self.k
        is_s, t0, N, s0 = self.blkinfo(blk)
        cv = lambda ap: self.cv(ap, is_s, N)
        pw = lambda ap: self.pw(ap, is_s)
        r3 = lambda ap: ap.rearrange("p (a t) -> p a t", a=2)
        Qp = r3(self.gbuf(0, 1024, BF16))
        Kp = r3(self.gbuf(1024, 1024, BF16))
        if st == 0:
            V = r3(self.gbuf(2048, 1024, BF16))
            GATE = r3(self.gbuf(31488, 2048, F32))
            QT = r3(self.gbuf(12288, 1024, BF16))
            KT = r3(self.gbuf(13312, 1024, BF16))
            QG = r3(self.gbuf(14336, 1024, BF16))
            av = self.gbuf(31232, 32, F32).rearrange("p (a c) -> p a c", a=2)
            bv = self.gbuf(31264, 32, F32).rearrange("p (a c) -> p a c", a=2)
        else:
            QT = r3(self.carve("a", 4096, 1024, BF16))
            KT = r3(self.carve("a", 5120, 1024, BF16))
            QG = r3(self.carve("a", 6144, 1024, BF16))
            V = r3(self.carve("a", 7168, 1024, BF16))
            GATE = r3(self.carve("a", 8192, 2048, F32))
            av = self.carve("a", 10240, 32, F32).rearrange("p (a c) -> p a c", a=2)
            bv = self.carve("a", 10272, 32, F32).rearrange("p (a c) -> p a c", a=2)
        T1r = self.carve("a", 12288, 2048, F32)
        T2r = self.carve("a", 14336, 2048, F32)
        Gs = r3(self.gbuf(4096, 2048, F32))
        T1 = self.gbuf(6144, 2048, F32)
        T2 = self.gbuf(8192, 2048, F32)
        G0 = r3(self.gbuf(10240, 2048, F32))
        KTM = self.gbuf(15360, 2048, BF16)
        VTM = self.gbuf(17408, 2048, BF16)
        ATT = self.gbuf(19456, 2048, BF16)
        UP = self.gbuf(21504, 2048, F32).rearrange("p (a c v) -> p a c v", a=2, c=NCH)
        SALL = self.gbuf(23552, 2560, F32).rearrange("p (a c v) -> p a c v", a=2, c=NCH + 1)
        SBF = self.gbuf(26112, 1024, BF16).rearrange("p (a c v) -> p a c v", a=2, c=NCH)
        SIN = self.gbuf(27136, 2048, F32).rearrange("p (a c v) -> p a c v", a=2, c=NCH)
        OSQ = self.gbuf(29184, 1024, BF16)
        OB = r3(self.gbuf(30208, 1024, BF16))
        cosb = self.carve("a", 0, 1024, F32)
        sinb = self.carve("a", 1024, 1024, F32)
        alrb = self.carve("a", 2048, 512, BF16)
        qb = self.carve("a", 2560, 512, BF16)
        t1c = T1[:, 0:MB]
        t2c = T2[:, 0:MB]
        identb = self.ident_b()
        sm = self.small

        if is_s:
            for tl in range(2):
                k.ms(Qp[:, tl, :], 0.0)
                k.ms(Kp[:, tl, :], 0.0)
                k.ms(V[:, tl, :], 0.0)
                if name != "ret":
                    k.ms(G0[:, tl, :], 0.0)

        def simple_v_gate(vn, gn, tl):
            p = self.proj(W[vn + str(tl)], t0, N)
            k.cp(pw(V[:, tl, :]), cv(p))
            p = self.proj(W[gn + str(tl)], t0, N)
            k.act(GATE[:, tl, 0:N], p[:, 0:N], AF.Silu)

        if name == "ret":
            if _os.environ.get('KVAR', '') != 'nodma':
                k.dma("sp", cosb[:, 0:N], self.d_cos[:, t0:t0 + N])
                k.dma("sp", sinb[:, 0:N], self.d_sin[:, t0:t0 + N])
                k.dma("sp", Gs, self.d_gret[1 if is_s else 0])
            pmb = self.cstb[:, CB_PM:CB_PM + 128]
            if CUT == 11:
                return
            for tl in range(2):
                for (wn_, dst, scale) in (("rq", Qp, 1.0), ("rk", Kp, 0.125)):
                    p = self.proj(W[wn_ + str(tl)], t0, N)
                    if self.chk():
                        return
                    k.cp(qb[:, 0:N], p[:, 0:N])
                    pp = self.ps()
                    k.mm(pp[:, 0:N], pmb, qb[:, 0:N])
                    if self.chk():
                        return
                    k.stt(t1c[:, 0:N], p[:, 0:N], scale, cosb[:, 0:N], ALU.mult, ALU.mult)
                    k.stt(t2c[:, 0:N], pp[:, 0:N], scale, sinb[:, 0:N], ALU.mult, ALU.mult)
                    if self.chk():
                        return
                    _v = _os.environ.get('KVAR', '')
                    if _v == 'A' and tl == 1:
                        k.tt(pw(dst[:, 0, :]), cv(t1c), cv(t2c), ALU.add)
                    elif _v == 'B' and tl == 1:
                        k.tt(pw(dst[:, tl, :]), cv(t1c), cv(t2c), ALU.add, eng="pool")
                    else:
                        k.tt(pw(dst[:, tl, :]), cv(t1c), cv(t2c), ALU.add, eng=GP)
                    if self.chk():
                        return
                yield
                simple_v_gate("rv", "rg", tl)
                yield
                if self.chk():
                    return
        elif name == "gla":
            p = self.proj(W["alr"], t0, N, M=16)
            k.cp(alrb[0:16, 0:N], p[0:16, 0:N])
            for tl in range(2):
                pg = self.ps()
                k.mm(pg[:, 0:N], self.wgb[0:16, tl * 128:(tl + 1) * 128], alrb[0:16, 0:N])
                k.act(t1c[:, 0:N], pg[:, 0:N], AF.Exp, bias=sm[:, tl:tl + 1], scale=-1.0)
                k.act(t1c[:, 0:N], t1c[:, 0:N], AF.Ln, bias=self.c_one)
                k.ts(pw(G0[:, tl, :]), cv(t1c), -1.0 / 16.0, ALU.mult)
                p = self.proj(W["aq" + str(tl)], t0, N)
                k.ts(pw(Qp[:, tl, :]), cv(p), float(32.0 ** -0.5), ALU.mult)
                p = self.proj(W["ak" + str(tl)], t0, N)
                k.cp(pw(Kp[:, tl, :]), cv(p))
                yield
                simple_v_gate("av", "ag", tl)
                yield
        else:
            for tl in range(2):
                p = self.proj(W["hq" + str(tl)], t0, N)
                k.act(t1c[:, 0:N], p[:, 0:N], AF.Silu)
                k.ts(pw(Qp[:, tl, :]), cv(t1c), 0.125, ALU.mult)
                p = self.proj(W["hf" + str(tl)], t0, N)
                k.act(t1c[:, 0:N], p[:, 0:N], AF.Exp, scale=-1.0)
                k.ts(t1c[:, 0:N], t1c[:, 0:N], 1.0, ALU.add)
                k.recip(t1c[:, 0:N], t1c[:, 0:N])
                k.act(t2c[:, 0:N], t1c[:, 0:N], AF.Ln, bias=sm[:, 2 + tl:3 + tl], scale=sm[:, 4 + tl:5 + tl])
                k.cp(pw(G0[:, tl, :]), cv(t2c))
                k.ts(pw(Kp[:, tl, :]), cv(t1c), sm[:, 6 + tl:7 + tl], ALU.mult, sm[:, 4 + tl:5 + tl], ALU.add)
                yield
                simple_v_gate("hi", "hg", tl)
                yield
        if name != "ret":
            reset = self.cst[:, CST_RESET:CST_RESET + MB]
            for tl in range(2):
                k.scan(Gs[:, tl, :], reset, G0[:, tl, :], 0.0, ALU.mult, ALU.add)

        if CUT == 1:
            return
        for tl in range(2):
            G3 = Gs[:, tl, :].rearrange("p (c t) -> p c t", t=64)
            D3 = t1c.rearrange("p (c t) -> p c t", t=64)
            k.tt(D3, G3, _bc(G3[:, :, 31:32], [128, NCH, 64]), ALU.subtract)
            k.act(t2c, t1c, AF.Exp)
            k.tt(QT[:, tl, :], Qp[:, tl, :], t2c, ALU.mult, eng=GP)
            k.act(t2c, t1c, AF.Exp, scale=-1.0)
            k.tt(KT[:, tl, :], Kp[:, tl, :], t2c, ALU.mult, eng=GP)
            k.act(t2c, Gs[:, tl, :], AF.Exp)
            k.tt(QG[:, tl, :], Qp[:, tl, :], t2c, ALU.mult, eng=GP)
            k.act(av[:, tl, :], G3[:, :, 63], AF.Exp)
            k.act(bv[:, tl, :], D3[:, :, 63], AF.Exp)
            yield
        yield "PREP_DONE"

        P2 = [slice(0, 64), slice(64, 128)]
        for (src, dstm) in ((KT, KTM), (V, VTM)):
            pT = [self.ps(), self.ps()]
            for p_ in range(2):
                pTb = pT[p_][:, :].bitcast(BF16)
                for c in range(NCH):
                    for tl in range(2):
                        k.tr(pTb[P2[p_], (c * 2 + tl) * 64:(c * 2 + tl + 1) * 64], src[P2[p_], tl, c * 64:(c + 1) * 64],
                             identb[P2[p_], 64 * p_:64 * p_ + 64])
                k.cp(dstm[P2[p_], 0:NCH * 128], pTb[P2[p_], 0:NCH * 128])
            yield

        mincl2 = self.cst[:, CST_MINCL:CST_MINCL + 64]
        pA = [self.ps(), self.ps()]
        for p_ in range(2):
            for c in range(NCH):
                for tl in range(2):
                    sl = slice((c * 2 + tl) * 64, (c * 2 + tl + 1) * 64)
                    k.mm(pA[p_][P2[p_], sl], KT[P2[p_], tl, c * 64:(c + 1) * 64], QT[P2[p_], tl, c * 64:(c + 1) * 64])
            k.tt(ATT[P2[p_], 0:NCH * 128].rearrange("p (a t) -> p a t", t=64),
                 pA[p_][P2[p_], 0:NCH * 128].rearrange("p (a t) -> p a t", t=64),
                 _bc(mincl2[P2[p_], :].unsqueeze(1), [64, NCH * 2, 64]), ALU.mult)

        yield
        pU = [self.ps(), self.ps()]
        for p_ in range(2):
            for c in range(NCH):
                for tl in range(2):
                    sl = slice((c * 2 + tl) * 64, (c * 2 + tl + 1) * 64)
                    k.mm(pU[p_][P2[p_], (tl * NCH + c) * 64:(tl * NCH + c + 1) * 64], KTM[P2[p_], sl], VTM[P2[p_], sl])
            k.tt(UP.rearrange("p a c v -> p (a c) v")[P2[p_]], pU[p_][P2[p_], :].rearrange("p (a v) -> p a v", v=64),
                 _bc(bv.rearrange("p a c -> p (a c)")[P2[p_]].unsqueeze(2), [64, 2 * NCH, 64]), ALU.mult)

        yield
        if not is_s:
            if blk == 0:
                k.ms(SALL[:, :, 0, :], 0.0)
            else:
                k.cp(SALL[:, :, 0, :], SALL[:, :, NCH, :])
            SBv = SALL
        else:
            k.dma("sp", SIN, self.d_st[name][l][:, :, s0:s0 + NCH, :])
            SBv = SIN
        for c in range(NCH):
            for tl in range(2):
                src = SALL[:, tl, c, :] if not is_s else SIN[:, tl, c, :]
                k.stt(SALL[:, tl, c + 1, :], src, av[:, tl, c:c + 1], UP[:, tl, c, :], ALU.mult, ALU.add)
        if is_s:
            k.dma("sp", self.o_ss[name][l][:, :, s0:s0 + NCH, :], SALL[:, :, 1:NCH + 1, :], is_out=True)
        elif blk == NPB - 1:
            k.dma("sp", self.o_ps[name][l], SALL[:, :, NCH, :], is_out=True)
        k.cp(SBF, SBv[:, :, 0:NCH, :], eng=GP)

        yield
        pO = [self.ps(), self.ps()]
        for p_ in range(2):
            for c in range(NCH):
                for tl in range(2):
                    sl = slice((c * 2 + tl) * 64, (c * 2 + tl + 1) * 64)
                    o_ap = pO[p_][P2[p_], (tl * NCH + c) * 64:(tl * NCH + c + 1) * 64]
                    k.mm(o_ap, VTM[P2[p_], sl], ATT[P2[p_], sl], start=True, stop=False)
                    k.mm(o_ap, SBF[P2[p_], tl, c, :], QG[P2[p_], tl, c * 64:(c + 1) * 64], start=False, stop=True)
        for p_ in range(2):
            k.cp(T2r[P2[p_], 0:2 * MB], pO[p_][P2[p_], 0:2 * MB], eng="act")
        yield
        normw = None if name == "ret" else self.pv(l, PV_NORM + {"gla": 0, "hg": 1}[name])
        self.o_post(l, m, T2r, GATE, OB, OSQ, T1r, T2r, normw, Wo, is_s, t0, N, s0)

    def o_post(self, l, m, pO, GATE, OB, OSQ, T1, T2, normw, Wo, is_s, t0, N, s0):
        k = self.k
        cv = lambda ap: self.cv(ap, is_s, N)
        pw = lambda ap: self.pw(ap, is_s)
        bones = self.cstb[:, CB_BONES:CB_BONES + 128]
        if str(pO.space) == "PSUM":
            k.cp(T2[:, 0:2 * MB], pO[:, 0:2 * MB], eng="act")
            pO = T2
        k.act(OSQ[:, 0:2 * MB], pO[:, 0:2 * MB], AF.Square)
        pS = self.ps()
        for tl in range(2):
            k.mm(pS[:, tl * MB:(tl + 1) * MB], bones, OSQ[:, tl * MB:(tl + 1) * MB])
        k.act(T1[:, 0:2 * MB], pS[:, 0:2 * MB], AF.Sqrt, bias=self.c_eps, scale=1.0 / 64.0)
        k.recip(T1[:, 0:2 * MB], T1[:, 0:2 * MB])
        k.tt(T2[:, 0:2 * MB], pO[:, 0:2 * MB], T1[:, 0:2 * MB], ALU.mult)
        for tl in range(2):
            src = pw(T2[:, tl * MB:(tl + 1) * MB])
            if normw is None:
                k.tt(cv(OB[:, tl, :]), src, cv(GATE[:, tl, :]), ALU.mult)
            else:
                k.stt(cv(OB[:, tl, :]), src, normw, cv(GATE[:, tl, :]), ALU.mult, ALU.mult)
        for oc in range(KC):
            po = self.ps()
            for tl in range(2):
                k.mm(po[:, 0:N], Wo[tl][:, oc * 128:(oc + 1) * 128], OB[:, tl, 0:N], start=(tl == 0), stop=(tl == 1))
            self.resid_acc(po[:, 0:N], l, 1, oc, 4 if is_s else 0, t0, N, s0, NCH)


class Prog4(Prog3):
    def mix_gdn(self, l, m, blk, W, Wo):
        k = self.k
        is_s, t0, N, s0 = self.blkinfo(blk)
        cv = lambda ap: self.cv(ap, is_s, N)
        pw = lambda ap: self.pw(ap, is_s)
        r3 = lambda ap: ap.rearrange("p (a t) -> p a t", a=2)
        g = self.gbuf
        P2 = [slice(0, 64), slice(64, 128)]
        Qp = r3(g(0, 2048, F32))
        Kp = r3(g(2048, 2048, F32))
        QGx = r3(g(4096, 2048, F32))
        QpT = r3(g(6144, 2048, F32))
        VTM = g(8192, 2048, F32)
        KTM = g(10240, 2048, F32)
        MT = g(12288, 2048, F32).rearrange("p (a c v) -> p a c v", a=2, c=NCH)
        BC = g(14336, 2048, F32).rearrange("p (a c v) -> p a c v", a=2, c=NCH)
        SALL = g(16384, 2560, F32).rearrange("p (a c v) -> p a c v", a=2, c=NCH + 1)
        SIN = g(18944, 2048, F32).rearrange("p (a c v) -> p a c v", a=2, c=NCH)
        OB = r3(g(20992, 1024, BF16))
        OSQ = g(22016, 1024, BF16)
        GD = g(23040, 1024, F32)
        GC = g(24064, 1024, F32)
        BET = g(25088, 1024, F32)
        c3v = lambda ap: ap[:, 0:NCH * 2].rearrange("p (c l) -> p c l", l=2)
        GCOL = g(26112, 64, F32)
        BCOL = g(26176, 64, F32)
        EGC = g(26240, 64, F32)
        KHS = g(26304, 64, F32)
        BEG = g(26368, 64, F32)
        eGL = g(26432, 32, F32).rearrange("p (a c) -> p a c", a=2)
        halo = g(26496, 72, F32).rearrange("p (i r) -> p i r", r=3)
        OL = r3(g(26624, 2048, F32))
        GATE = r3(g(28672, 2048, F32))
        A = lambda i: self.carve("a", i * 1024, 1024, F32)
        UQ = r3(self.carve("a", 9216, 2048, F32))
        UK = r3(self.carve("a", 11264, 2048, F32))
        VT = r3(self.carve("a", 13312, 2048, F32))
        SQ = self.carve("a", 15360, 1024, BF16)
        ubuf = self.carve("a", 16384, 1040, F32)
        acc = self.carve("a", 17424, 1024, F32)
        T1 = self.carve("a", 18448, 2048, F32)
        T2 = self.carve("a", 20496, 2048, F32)
        sm = self.small
        identF = self.cst[:, CST_IDENT:CST_IDENT + 128]
        bones = self.cstb[:, CB_BONES:CB_BONES + 128]
        selF = self.cst[0:4, CST_SELF:CST_SELF + 256]
        selHP = self.cst[0:4, CST_SELHP:CST_SELHP + 4]
        mincl = self.cst[:, CST_MINCL:CST_MINCL + 64]
        mstr = self.cst[:, CST_MSTR:CST_MSTR + 64]
        mstrT = self.cst[:, CST_MSTRT:CST_MSTRT + 64]
        ID2 = self.cst[:, CST_ID2:CST_ID2 + 64]

        if is_s:
            for tl in range(2):
                k.ms(Qp[:, tl, :], 0.0)
                k.ms(Kp[:, tl, :], 0.0)
                k.ms(VT[:, tl, :], 0.0)
            k.ms(GD[0:4, :], 0.0)
            k.ms(BET[0:4, :], 0.0)

        names = ["dq0", "dq1", "dk0", "dk1", "dv0", "dv1"]
        for idx, wn_ in enumerate(names):
            p = self.proj(W[wn_], t0, N)
            tl = idx % 2
            cw = lambda i: self.pv(l, PV_CONV + idx * 4 + i)
            if not is_s:
                if blk == 0:
                    k.ms(ubuf[:, 0:3], 0.0)
                else:
                    k.cp(ubuf[:, 0:3], halo[:, idx, :], eng=GP)
                k.cp(ubuf[:, 3:3 + N], p[:, 0:N], eng="act")
                k.ts(acc[:, 0:N], ubuf[:, 0:N], cw(0), ALU.mult)
                for i in range(1, 4):
                    k.stt(acc[:, 0:N], ubuf[:, i:i + N], cw(i), acc[:, 0:N], ALU.mult, ALU.add)
                k.cp(halo[:, idx, :], ubuf[:, N:N + 3], eng=GP)
                if blk == NPB - 1:
                    k.dma("sp", self.o_pconv[l][:, idx, :], halo[:, idx, :], is_out=True)
            else:
                ubs = ubuf[:, 0:NCH * 7].rearrange("p (s r) -> p s r", r=7)
                k.dma("sp", ubs[:, :, 0:3], self.d_stconv[l][:, idx, s0:s0 + NCH, :])
                k.cp(ubs[:, :, 3:7], p[:, 0:N].rearrange("p (s j) -> p s j", j=TS), eng="act")
                a3 = acc[:, 0:N].rearrange("p (s j) -> p s j", j=TS)
                k.ts(a3, ubs[:, :, 0:4], cw(0), ALU.mult)
                for i in range(1, 4):
                    k.stt(a3, ubs[:, :, i:i + 4], cw(i), a3, ALU.mult, ALU.add)
                k.dma("sp", self.o_sconv[l][:, idx, s0:s0 + NCH, :], ubs[:, :, 4:7], is_out=True)
            accv = acc[:, 0:N]
            if idx < 2:
                k.act(UQ[:, tl, 0:N], accv, AF.Silu)
            elif idx < 4:
                k.act(UK[:, tl, 0:N], accv, AF.Silu)
            else:
                k.act(T1[:, 0:N], accv, AF.Silu)
                k.cp(pw(VT[:, tl, :]), cv(T1[:, 0:MB]), eng=GP)
        if CUTG == 1:
            return
        for (X, dst, scale) in ((UQ, Qp, 0.125), (UK, Kp, 1.0)):
            pS = self.ps()
            for tl in range(2):
                k.act(SQ[:, tl * MB:tl * MB + N], X[:, tl, 0:N], AF.Square)
                k.mm(pS[:, tl * MB:tl * MB + N], bones, SQ[:, tl * MB:tl * MB + N])
                k.act(T1[:, tl * MB:tl * MB + N], pS[:, tl * MB:tl * MB + N], AF.Sqrt, bias=self.c_eps, scale=1.0)
                k.recip(T1[:, tl * MB:tl * MB + N], T1[:, tl * MB:tl * MB + N])
                k.stt(pw(dst[:, tl, :]), cv(X[:, tl, :]), scale, cv(T1[:, tl * MB:(tl + 1) * MB]), ALU.mult, ALU.mult)
        if CUTG == 2:
            return
        for tl in range(2):
            p = self.proj(W["dg" + str(tl)], t0, N)
            k.act(GATE[:, tl, 0:N], p[:, 0:N], AF.Silu)
        if CUTG == 3:
            return
        p = self.proj(W["db"], t0, N, M=4)
        k.act(T2[0:4, 0:N], p[0:4, 0:N], AF.Exp, scale=-1.0)
        k.ts(T2[0:4, 0:N], T2[0:4, 0:N], 1.0, ALU.add)
        k.recip(T2[0:4, 0:N], T2[0:4, 0:N])
        k.cp(pw(BET[0:4, 0:MB]), cv(T2[0:4, 0:MB]))
        p = self.proj(W["da"], t0, N, M=4)
        k.act(T2[0:4, 0:N], p[0:4, 0:N], AF.Exp, bias=self.pv(l, PV_DTB)[0:4, :], scale=1.0)
        k.act(T2[0:4, 0:N], T2[0:4, 0:N], AF.Ln, bias=self.c_one[0:4, :])
        k.ts(pw(GD[0:4, 0:MB]), cv(T2[0:4, 0:MB]), sm[0:4, 11:12], ALU.mult)
        k.scan(GC[0:4, 0:MB], self.cst[0:4, CST_RESET:CST_RESET + MB], GD[0:4, 0:MB], 0.0, ALU.mult, ALU.add)
        if CUTG == 4:
            return
        pX = self.ps()
        for tl in range(2):
            k.mm(pX[:, tl * MB:(tl + 1) * MB], selF[0:4, tl * 128:(tl + 1) * 128], GC[0:4, 0:MB])
        k.act(T1[:, 0:2 * MB], pX[:, 0:2 * MB], AF.Exp)
        for tl in range(2):
            k.tt(QGx[:, tl, :], Qp[:, tl, :], T1[:, tl * MB:(tl + 1) * MB], ALU.mult, eng=GP)
            k.cp(eGL[:, tl, :], T1[:, tl * MB:(tl + 1) * MB].rearrange("p (c t) -> p c t", t=64)[:, :, 63], eng=GP)
        pC = [self.ps(), self.ps()]
        for p_ in range(2):
            for c in range(NCH):
                k.mm(pC[p_][P2[p_], c * 2:(c + 1) * 2], GC[0:4, c * 64:(c + 1) * 64], selHP[0:4, p_ * 2:p_ * 2 + 2])
                k.mm(pC[p_][P2[p_], 64 + c * 2:64 + (c + 1) * 2], BET[0:4, c * 64:(c + 1) * 64], selHP[0:4, p_ * 2:p_ * 2 + 2])
            k.cp(GCOL[P2[p_], 0:NCH * 2], pC[p_][P2[p_], 0:NCH * 2])
            k.cp(BCOL[P2[p_], 0:NCH * 2], pC[p_][P2[p_], 64:64 + NCH * 2])
        k.act(EGC[:, 0:NCH * 2], GCOL[:, 0:NCH * 2], AF.Exp)
        k.tt(BEG[:, 0:NCH * 2], BCOL[:, 0:NCH * 2], EGC[:, 0:NCH * 2], ALU.mult)
        pL = self.ps()
        glast = GC[0:4, 0:MB].rearrange("p (c t) -> p c t", t=64)[:, :, 63]
        for tl in range(2):
            k.mm(pL[:, tl * NCH:(tl + 1) * NCH], selF[0:4, tl * 128:(tl + 1) * 128], glast)
        k.tt(c3v(KHS), pL[:, 0:NCH * 2].rearrange("p (l c) -> p c l", l=2), c3v(GCOL), ALU.subtract)
        k.act(KHS[:, 0:NCH * 2], KHS[:, 0:NCH * 2], AF.Exp)
        if CUTG == 5:
            return
        for (src, dstm) in ((Kp, KTM), (VT, VTM)):
            pT = [self.ps(), self.ps()]
            for p_ in range(2):
                for c in range(NCH):
                    for tl in range(2):
                        k.mm(pT[p_][P2[p_], (c * 2 + tl) * 64:(c * 2 + tl + 1) * 64], src[P2[p_], tl, c * 64:(c + 1) * 64],
                             identF[P2[p_], 64 * p_:64 * p_ + 64])
                k.cp(dstm[P2[p_], 0:NCH * 128], pT[p_][P2[p_], 0:NCH * 128], eng=("act" if p_ else "dve"))
        KTM4 = KTM[:, 0:NCH * 128].rearrange("p (c l d) -> p c l d", c=NCH, l=2)
        VTM4 = VTM[:, 0:NCH * 128].rearrange("p (c l d) -> p c l d", c=NCH, l=2)
        GCOL3 = c3v(GCOL)
        BCOL3 = c3v(BCOL)
        BEG3 = c3v(BEG)
        KHS3 = c3v(KHS)
        v4 = lambda ap: ap[:, 0:256].rearrange("p (c l t) -> p c l t", c=2, l=2)
        v3 = lambda ap: ap[:, 0:256].rearrange("p (a t) -> p a t", t=64)
        bc4 = lambda ap3: _bc(ap3.unsqueeze(3), [128, 2, 2, 64])
        bm = lambda mk: _bc(mk.unsqueeze(1), [128, 4, 64])

        def evac2(dst, pp, eng0="dve", eng1="act"):
            k.cp(dst[P2[0], 0:256], pp[0][P2[0], 0:256], eng=eng0)
            k.cp(dst[P2[1], 0:256], pp[1][P2[1], 0:256], eng=eng1)

        if CUTG == 6:
            return
        def solve(sbi):
            c0 = sbi * 2
            Dm, X1, X2, LT, Nk, Ak, P, U, Wm = [A(i + 11 * sbi) for i in range(9)]
            pGr = self.ps()
            pBr = self.ps()
            for tl in range(2):
                k.mm(pGr[:, tl * 128:(tl + 1) * 128], selF[0:4, tl * 128:(tl + 1) * 128], GC[0:4, c0 * 64:c0 * 64 + 128])
                k.mm(pBr[:, tl * 128:(tl + 1) * 128], selF[0:4, tl * 128:(tl + 1) * 128], BET[0:4, c0 * 64:c0 * 64 + 128])
            gr4 = pGr[:, 0:256].rearrange("p (l c t) -> p c l t", l=2, c=2)
            br4 = pBr[:, 0:256].rearrange("p (l c t) -> p c l t", l=2, c=2)
            k.tt(v4(Dm), gr4, bc4(GCOL3[:, c0:c0 + 2, :]), ALU.subtract)
            k.ts(X1[:, 0:256], Dm[:, 0:256], 0.0, ALU.min)
            k.act(X1[:, 0:256], X1[:, 0:256], AF.Exp)
            k.ts(X2[:, 0:256], Dm[:, 0:256], -1.0, ALU.mult, 0.0, ALU.min)
            k.act(X2[:, 0:256], X2[:, 0:256], AF.Exp)
            k.tt(v3(LT), v3(X1), bm(mincl), ALU.mult, eng=GP)
            k.tt(v3(X1), v3(X1), bm(mstr), ALU.mult, eng=GP)
            k.tt(v4(X1), v4(X1), br4, ALU.mult)
            k.tt(v3(X2), v3(X2), bm(mstrT), ALU.mult, eng=GP)
            k.tt(v4(X2), v4(X2), bc4(BCOL3[:, c0:c0 + 2, :]), ALU.mult, eng=GP)
            yield
            pKK = [self.ps(), self.ps()]
            pQK = [self.ps(), self.ps()]
            for cc in range(2):
                c = c0 + cc
                for tl in range(2):
                    for p_ in range(2):
                        sl = slice((cc * 2 + tl) * 64, (cc * 2 + tl + 1) * 64)
                        kk = Kp[P2[p_], tl, c * 64:(c + 1) * 64]
                        qq = Qp[P2[p_], tl, c * 64:(c + 1) * 64]
                        k.mm(pKK[p_][P2[p_], sl], kk, kk)
                        k.mm(pQK[p_][P2[p_], sl], kk, qq)
            for p_ in range(2):
                k.tt(Nk[P2[p_], 0:256], pKK[p_][P2[p_], 0:256], X1[P2[p_], 0:256], ALU.mult)
                k.tt(Ak[P2[p_], 0:256], pKK[p_][P2[p_], 0:256], X2[P2[p_], 0:256], ALU.mult)
                k.tt(LT[P2[p_], 0:256], pQK[p_][P2[p_], 0:256], LT[P2[p_], 0:256], ALU.mult)
            k.stt(v3(P), v3(Nk), -1.0, bm(ID2), ALU.mult, ALU.add)
            yield
            for lev in range(5):
                pA_ = [self.ps(), self.ps()]
                if lev < 4:
                    pN_ = [self.ps(), self.ps()]
                for j in range(4):
                    for p_ in range(2):
                        sl = slice(j * 64, (j + 1) * 64)
                        if lev < 4:
                            k.mm(pN_[p_][P2[p_], sl], Ak[P2[p_], sl], Nk[P2[p_], sl])
                        k.mm(pA_[p_][P2[p_], sl], Nk[P2[p_], sl], Ak[P2[p_], sl])
                if lev < 4:
                    evac2(Nk, pN_, "act", "act")
                evac2(Ak, pA_, "dve", "dve")
                yield
                pP = [self.ps(), self.ps()]
                for j in range(4):
                    for p_ in range(2):
                        sl = slice(j * 64, (j + 1) * 64)
                        k.mm(pP[p_][P2[p_], sl], Ak[P2[p_], sl], P[P2[p_], sl])
                for p_ in range(2):
                    k.tt(P[P2[p_], 0:256], P[P2[p_], 0:256], pP[p_][P2[p_], 0:256], ALU.add)
                yield
            k.tt(v4(X1), VTM4[:, c0:c0 + 2, :, :], bc4(BCOL3[:, c0:c0 + 2, :]), ALU.mult, eng=GP)
            k.tt(v4(X2), KTM4[:, c0:c0 + 2, :, :], bc4(BEG3[:, c0:c0 + 2, :]), ALU.mult, eng=GP)
            pu = [self.ps(), self.ps()]
            pw_ = [self.ps(), self.ps()]
            for j in range(4):
                for p_ in range(2):
                    sl = slice(j * 64, (j + 1) * 64)
                    k.mm(pu[p_][P2[p_], sl], P[P2[p_], sl], X1[P2[p_], sl])
                    k.mm(pw_[p_][P2[p_], sl], P[P2[p_], sl], X2[P2[p_], sl])
            evac2(U, pu, "act", "act")
            evac2(Wm, pw_, "dve", "dve")
            yield
            k.tt(v4(Dm), KTM4[:, c0:c0 + 2, :, :], bc4(KHS3[:, c0:c0 + 2, :]), ALU.mult, eng=GP)
            pM = [self.ps(), self.ps()]
            pB = [self.ps(), self.ps()]
            Mraw, Braw = A(9 + 11 * sbi), A(10 + 11 * sbi)
            for j in range(4):
                for p_ in range(2):
                    sl = slice(j * 64, (j + 1) * 64)
                    k.mm(pM[p_][P2[p_], sl], Wm[P2[p_], sl], Dm[P2[p_], sl])
                    k.mm(pB[p_][P2[p_], sl], Dm[P2[p_], sl], U[P2[p_], sl])
            evac2(Mraw, pM, "act", "act")
            evac2(Braw, pB, "dve", "dve")
            yield
            for tl in range(2):
                for cc in range(2):
                    sl = slice((cc * 2 + tl) * 64, (cc * 2 + tl + 1) * 64)
                    k.stt(MT[:, tl, c0 + cc, :], ID2, eGL[:, tl, c0 + cc:c0 + cc + 1], Mraw[:, sl], ALU.mult, ALU.subtract)
                k.cp(BC[:, tl, c0:c0 + 2, :], v4(Braw)[:, :, tl, :], eng=GP)
            pQ = [self.ps(), self.ps()]
            pOL = [self.ps(), self.ps()]
            for j in range(4):
                for p_ in range(2):
                    sl = slice(j * 64, (j + 1) * 64)
                    k.mm(pQ[p_][P2[p_], sl], Wm[P2[p_], sl], LT[P2[p_], sl])
                    k.mm(pOL[p_][P2[p_], sl], U[P2[p_], sl], LT[P2[p_], sl])
            evac2(Mraw, pQ, "act", "act")
            evac2(Braw, pOL, "dve", "dve")
            yield
            for tl in range(2):
                qv = QpT[:, tl, c0 * 64:c0 * 64 + 128].rearrange("p (c t) -> p c t", t=64)
                gv = QGx[:, tl, c0 * 64:c0 * 64 + 128].rearrange("p (c t) -> p c t", t=64)
                k.tt(qv, gv, v4(Mraw)[:, :, tl, :], ALU.subtract, eng=GP)
                k.cp(OL[:, tl, c0 * 64:c0 * 64 + 128].rearrange("p (c t) -> p c t", t=64), v4(Braw)[:, :, tl, :], eng=GP)
        gens = [solve(sbi) for sbi in range(NCH // 2)]
        while gens:
            for g_ in list(gens):
                try:
                    next(g_)
                except StopIteration:
                    gens.remove(g_)
        if CUTG == 7:
            return
        if not is_s:
            if blk == 0:
                k.ms(SALL[:, :, 0, :], 0.0)
            else:
                k.cp(SALL[:, :, 0, :], SALL[:, :, NCH, :])
        else:
            k.dma("sp", SIN, self.d_st["gdn"][l][:, :, s0:s0 + NCH, :])
        SB = (lambda tl, c: SIN[:, tl, c, :]) if is_s else (lambda tl, c: SALL[:, tl, c, :])
        for c in range(NCH):
            pS = [self.ps(), self.ps()]
            for p_ in range(2):
                for tl in range(2):
                    k.mm(pS[p_][P2[p_], tl * 64:(tl + 1) * 64], MT[P2[p_], tl, c, :], SB(tl, c)[P2[p_], :])
                k.tt(SALL[P2[p_], :, c + 1, :], pS[p_][P2[p_], 0:128].rearrange("p (a v) -> p a v", v=64), BC[P2[p_], :, c, :], ALU.add)
        if is_s:
            k.dma("sp", self.o_ss["gdn"][l][:, :, s0:s0 + NCH, :], SALL[:, :, 1:NCH + 1, :], is_out=True)
        elif blk == NPB - 1:
            k.dma("sp", self.o_ps["gdn"][l], SALL[:, :, NCH, :], is_out=True)
        if CUTG == 8:
            return
        pO = [self.ps(), self.ps()]
        for p_ in range(2):
            for c in range(NCH):
                for tl in range(2):
                    k.mm(pO[p_][P2[p_], (tl * NCH + c) * 64:(tl * NCH + c + 1) * 64], SB(tl, c)[P2[p_], :], QpT[P2[p_], tl, c * 64:(c + 1) * 64])
            k.tt(T2[P2[p_], 0:2 * MB], pO[p_][P2[p_], 0:2 * MB], OL.rearrange("p a t -> p (a t)")[P2[p_]], ALU.add)
        if _os.environ.get('KDBG', '') == 'gdn' and blk == 0 and l == 0:
            for nm_, ap_ in (("GC", GC[0:4, 0:MB]), ("BET", BET[0:4, 0:MB]), ("GCOL", GCOL[:, 0:8]), ("BCOL", BCOL[:, 0:8]),
                             ("KHS", KHS[:, 0:8]), ("Qp", Qp), ("Kp", Kp), ("KTM", KTM[:, 0:512]), ("VTM", VTM[:, 0:512]),
                             ("MT", MT), ("BC", BC), ("QpT", QpT), ("OL", OL), ("SALL", SALL), ("T2o", T2[:, 0:512]), ("QGx", QGx)):
                k.dbg(nm_, ap_)
        self.o_post(l, m, T2, GATE, OB, OSQ, T1, T2, self.pv(l, PV_NORM + 2), Wo, is_s, t0, N, s0)

    def build(self, skip_mixers=(), stop_after=None, no_ffn=False, nblk=None):
        self.skip_mixers = set(skip_mixers)
        self.nblk = nblk
        self.alloc()
        self.setup()
        self.mod_group(0, 0)
        self.initial_h()
        for l in range(DEPTH):
            if no_ffn:
                self.mod_group(l, 1)
                self.ln_pass(l, 0)
            else:
                self.ffn(l, 0, mid_hook=lambda: self.mod_group_gen(l, 1))
            if stop_after == ("ffn1", l):
                break
            self.mix_layer(l, mid_hook=lambda: self.mod_group(l, 2))
            if stop_after == ("mix", l):
                break
            hook = (lambda: self.mod_group_gen(l + 1, 0)) if l + 1 < DEPTH else None
            self.ffn(l, 1, mid_hook=hook)
        self.store_y()
        nw = self.k.S.emit()
        self.nwaits = nw
        return self.nc


def _const_pack():
    c = np.zeros((128, CST_N), np.float32)
    c[:, CST_IDENT:CST_IDENT + 128] = np.eye(128, dtype=np.float32)
    s = np.arange(64)[:, None]
    t = np.arange(64)[None, :]
    for hf in range(2):
        c[64 * hf:64 * hf + 64, CST_MINCL:CST_MINCL + 64] = (s <= t)
        c[64 * hf:64 * hf + 64, CST_MSTR:CST_MSTR + 64] = (s < t)
        c[64 * hf:64 * hf + 64, CST_MSTRT:CST_MSTRT + 64] = (t < s)
    c[:, CST_ID2:CST_ID2 + 64] = np.tile(np.eye(64, dtype=np.float32), (2, 1))
    r = np.ones((MB,), np.float32)
    r[0::64] = 0.0
    c[:, CST_RESET:CST_RESET + MB] = r[None, :]
    for h in range(4):
        c[h, CST_SEL + h * 64:CST_SEL + (h + 1) * 64] = 1.0
        tl, p = divmod(h, 2)
        c[h, CST_SELF + tl * 128 + p * 64:CST_SELF + tl * 128 + (p + 1) * 64] = 1.0
    c[:, CST_ONES:CST_ONES + 128] = 1.0
    bo = np.zeros((128, 128), np.float32)
    bo[0:64, 0:64] = 1.0
    bo[64:128, 64:128] = 1.0
    c[:, CST_BONES:CST_BONES + 128] = bo
    pm = np.zeros((128, 128), np.float32)
    for m_ in range(128):
        kk = (m_ // 64) * 64 + ((m_ % 64) + 32) % 64
        pm[kk, m_] = 1.0
    c[:, CST_PM:CST_PM + 128] = pm
    for p_ in range(2):
        for tl in range(2):
            c[2 * tl + p_, CST_SELHP + p_ * 2 + tl] = 1.0
    return c


def _rot_tables():
    half = 32
    inv = (np.float32(10000.0) ** (-np.arange(half, dtype=np.float32) / np.float32(half))).astype(np.float32)
    pos = np.concatenate([np.arange(TP, dtype=np.float32),
                          np.tile(np.float32(16384.0) + np.arange(TS, dtype=np.float32), NSQ)]).astype(np.float32)
    ang = (pos[:, None] * inv[None, :]).astype(np.float32)
    cos = np.cos(ang).astype(np.float32).T
    sin = np.sin(ang).astype(np.float32).T
    cosT = np.zeros((128, NTOK), np.float32)
    sinT = np.zeros((128, NTOK), np.float32)
    for p in range(128):
        d = p % 64
        i = d % 32
        cosT[p] = cos[i]
        sinT[p] = -sin[i] if d < 32 else sin[i]
    return cosT, sinT


def _gret_tables():
    heads = np.arange(4, dtype=np.float32)
    lg = np.log(np.float32(1.0) - np.float32(2.0) ** (np.float32(-5.0) - heads)).astype(np.float32)
    g = np.zeros((2, 128, 2, MB), np.float32)
    j = np.arange(MB) % 64
    for tl in range(2):
        for p in range(128):
            h = 2 * tl + p // 64
            g[0, p, tl, :] = (j + 1).astype(np.float32) * lg[h]
            g[1, p, tl, :] = (np.minimum(j, TS - 1) + 1).astype(np.float32) * lg[h]
    return g


def _tile_w(w, cols):
    out = np.zeros((128, KC, 128), np.float32)
    cols = np.asarray(cols)
    valid = cols >= 0
    sub = w[:, cols[valid]].reshape(KC, 128, -1)
    out[:, :, np.nonzero(valid)[0]] = np.transpose(sub, (1, 0, 2))
    return out.reshape(128, KC * 128)


def _win_tiles(w):
    tiles = []
    r = lambda a, n: list(range(a, a + n))
    pad = lambda lst: lst + [-1] * (128 - len(lst))
    for base in (C_RQ, C_RK, C_RV, C_RG):
        for tl in range(2):
            tiles.append(r(base + tl * 128, 128))
    for base in (C_AQ, C_AK):
        for tl in range(2):
            cols = []
            for p in range(2):
                h = 2 * tl + p
                cols += r(base + h * 32, 32) + [-1] * 32
            tiles.append(cols)
    for base in (C_AV, C_AG):
        for tl in range(2):
            tiles.append(r(base + tl * 128, 128))
    tiles.append(pad(r(C_ALR, 16)))
    for base in (C_HQ, C_HF, C_HI, C_HG):
        for tl in range(2):
            tiles.append(r(base + tl * 128, 128))
    for base in (C_DQ, C_DK, C_DV, C_DG):
        for tl in range(2):
            tiles.append(r(base + tl * 128, 128))
    tiles.append(pad(r(C_DB, 4)))
    tiles.append(pad(r(C_DA, 4)))
    assert len(tiles) == NWT
    return np.stack([_tile_w(w, c) for c in tiles], 0)


def _state_in(st, dk):
    out = np.zeros((DEPTH, 2, 64, 2, NSQ, 64), np.float32)
    x = st.reshape(DEPTH, NSQ, 2, 2, dk, 64)
    out[:, :, 0:dk] = np.transpose(x, (0, 3, 4, 2, 1, 5))
    return out.reshape(DEPTH, 128, 2, NSQ, 64)


def _state_out_s(o, dk):
    x = o.reshape(DEPTH, 2, 64, 2, NSQ, 64)[:, :, 0:dk]
    return np.ascontiguousarray(np.transpose(x, (0, 4, 3, 1, 2, 5)).reshape(DEPTH, NSQ, 4, dk, 64))


def _state_out_p(o, dk):
    x = o.reshape(DEPTH, 2, 64, 2, 64)[:, :, 0:dk]
    return np.ascontiguousarray(np.transpose(x, (0, 3, 1, 2, 4)).reshape(DEPTH, 4, dk, 64))


_NC_CACHE = {}


def _get_nc(key=(), **kw):
    if key not in _NC_CACHE:
        p = Prog4()
        nc = p.build(**kw)
        _NC_CACHE[key] = (nc, p)
    return _NC_CACHE[key]


def _prepare_inputs(x_prompt, x_sample, state_ret, state_gla, state_hgrn, state_gdn, state_gdn_conv,
                    c_prompt, c_sample, ada_w, ada_b, ln_g, ln_b, ffn1_wi, ffn1_wo, ffn2_wi, ffn2_wo,
                    w_in, gla_wg, gla_bg, hg_lb, gdn_conv, gdn_a_log, gdn_dt_bias,
                    gla_norm, hg_norm, gdn_norm, w_out):
    f = lambda a: np.ascontiguousarray(np.asarray(a, dtype=np.float32))
    shared = {}
    for nm, wi in (("wi1", ffn1_wi), ("wi2", ffn2_wi)):
        wi = f(wi)
        shared[nm] = np.stack([np.stack([_tile_w(wi[l], list(range(c * 128, (c + 1) * 128))) for c in range(2 * NJ)], 0)
                               for l in range(DEPTH)], 0)
    shared["wo1"] = f(ffn1_wo)
    shared["wo2"] = f(ffn2_wo)
    w_in = f(w_in)
    shared["win"] = np.stack([_win_tiles(w_in[l]) for l in range(DEPTH)], 0)
    shared["wout"] = f(w_out).reshape(DEPTH, 8, 128, 1024)
    ada_w = f(ada_w)
    shared["adaw"] = np.stack([np.stack([_tile_w(ada_w[l], list(range(c * 128, (c + 1) * 128))) for c in range(72)], 0)
                               for l in range(DEPTH)], 0)
    pv = np.zeros((DEPTH, 128, NPV), np.float32)
    ada_b = f(ada_b); ln_g = f(ln_g); ln_b = f(ln_b); gdn_conv = f(gdn_conv); gla_bg = f(gla_bg); hg_lb = f(hg_lb)
    gla_norm = f(gla_norm); hg_norm = f(hg_norm); gdn_norm = f(gdn_norm); gdn_a_log = f(gdn_a_log); gdn_dt_bias = f(gdn_dt_bias)
    for l in range(DEPTH):
        pv[l, :, PV_ADAB:PV_ADAB + 72] = ada_b[l].reshape(72, 128).T
        for i in range(3):
            pv[l, :, PV_LNG + i * 8:PV_LNG + (i + 1) * 8] = ln_g[l, i].reshape(8, 128).T
            pv[l, :, PV_LNB + i * 8:PV_LNB + (i + 1) * 8] = ln_b[l, i].reshape(8, 128).T
        cw = gdn_conv[l].reshape(4, 6, 128)
        pv[l, :, PV_CONV:PV_CONV + 24] = np.transpose(cw, (2, 1, 0)).reshape(128, 24)
        bg = np.zeros((2, 2, 64), np.float32)
        bg[:, :, 0:32] = gla_bg[l].reshape(2, 2, 32)
        pv[l, :, PV_BG:PV_BG + 2] = bg.reshape(2, 128).T
        pv[l, :, PV_NORM + 0] = np.tile(gla_norm[l], 2)
        pv[l, :, PV_NORM + 1] = np.tile(hg_norm[l], 2)
        pv[l, :, PV_NORM + 2] = np.tile(gdn_norm[l], 2)
        pv[l, :, PV_LB0:PV_LB0 + 2] = hg_lb[0].reshape(2, 128).T
        pv[l, :, PV_LB1:PV_LB1 + 2] = hg_lb[1].reshape(2, 128).T
        pv[l, 0:4, PV_ALOG] = gdn_a_log[l]
        pv[l, 0:4, PV_DTB] = gdn_dt_bias[l]
    shared["pvec"] = pv
    gla_wg = f(gla_wg)
    wgp = np.zeros((DEPTH, 16, 2, 2, 64), np.float32)
    wgp[:, :, :, :, 0:32] = gla_wg.reshape(DEPTH, 16, 2, 2, 32)
    shared["wgpad"] = wgp.reshape(DEPTH, 16, 256)
    cosT, sinT = _rot_tables()
    shared["cosT"] = cosT
    shared["sinT"] = sinT
    shared["cst"] = _const_pack()
    shared["gret"] = _gret_tables()
    x_prompt = f(x_prompt); x_sample = f(x_sample); c_prompt = f(c_prompt); c_sample = f(c_sample)
    sts = {"ret": (f(state_ret), 64), "gla": (f(state_gla), 32), "hg": (f(state_hgrn), 64), "gdn": (f(state_gdn), 64)}
    state_gdn_conv = f(state_gdn_conv)
    in_maps = []
    for c in range(NCORES):
        d = dict(shared)
        sq = slice(c * NSQ, (c + 1) * NSQ)
        xs = x_sample[sq].reshape(NSQ * TS, D)
        d["xT"] = np.ascontiguousarray(np.concatenate([x_prompt[c], xs], 0).T)
        d["cT"] = np.ascontiguousarray(np.concatenate([c_prompt[c:c + 1], c_sample[sq]], 0).T)
        for nm, (st, dk) in sts.items():
            d["st_" + nm] = _state_in(st[:, sq], dk)
        cvs = state_gdn_conv[:, sq].reshape(DEPTH, NSQ, 3, 6, 128)
        d["st_conv"] = np.ascontiguousarray(np.transpose(cvs, (0, 4, 3, 1, 2)))
        in_maps.append(d)
    return in_maps


def _assemble(results):
    y_p = np.zeros((NCORES, TP, D), np.float32)
    y_s = np.zeros((NCORES * NSQ, TS, D), np.float32)
    dks = {"ret": 64, "gla": 32, "hg": 64, "gdn": 64}
    p_st = {nm: np.zeros((DEPTH, NCORES, 4, dk, 64), np.float32) for nm, dk in dks.items()}
    s_st = {nm: np.zeros((DEPTH, NCORES * NSQ, 4, dk, 64), np.float32) for nm, dk in dks.items()}
    p_conv = np.zeros((DEPTH, NCORES, 3, 768), np.float32)
    s_conv = np.zeros((DEPTH, NCORES * NSQ, 3, 768), np.float32)
    for c, r in enumerate(results):
        yT = np.asarray(r["yT"])
        y_p[c] = yT[:, 0:TP].T
        y_s[c * NSQ:(c + 1) * NSQ] = yT[:, TP:].T.reshape(NSQ, TS, D)
        for nm, dk in dks.items():
            p_st[nm][:, c] = _state_out_p(np.asarray(r["ops_" + nm]), dk)
            s_st[nm][:, c * NSQ:(c + 1) * NSQ] = _state_out_s(np.asarray(r["oss_" + nm]), dk)
        pc = np.asarray(r["opconv"])
        p_conv[:, c] = np.transpose(pc, (0, 3, 2, 1)).reshape(DEPTH, 3, 768)
        sc = np.asarray(r["osconv"])
        s_conv[:, c * NSQ:(c + 1) * NSQ] = np.transpose(sc, (0, 3, 4, 2, 1)).reshape(DEPTH, NSQ, 3, 768)
    return (y_p, y_s, p_st["ret"], p_st["gla"], p_st["hg"], p_st["gdn"], p_conv,
            s_st["ret"], s_st["gla"], s_st["hg"], s_st["gdn"], s_conv)


def kernel(**inputs):
    in_maps = _prepare_inputs(**inputs)
    nc, _ = _get_nc()
    res = run_bass_kernel_spmd(nc, in_maps, core_ids=list(range(NCORES)))
    return _assemble(res.results)
```

```python
import numpy as np
from contextlib import ExitStack
import concourse.bass as bass
import concourse.mybir as mybir
from concourse.bass_utils import run_bass_kernel_spmd

F32 = mybir.dt.float32
BF16 = mybir.dt.bfloat16
AF = mybir.ActivationFunctionType
ALU = mybir.AluOpType

NCORES = 8
D = 1024
KC = 8
DFF = 2816
NJ = 22
TP = 2048
NSQ = 16
TS = 4
NTOK = TP + NSQ * TS
DEPTH = 2
ALPHA = float((2.0 * DEPTH) ** 0.25)
LN_EPS = 1e-5
RMS_EPS = 1e-6
TBS = [(0, 512), (512, 512), (1024, 512), (1536, 512), (2048, 64)]
JG = [list(range(0, 8)), list(range(8, 15)), list(range(15, 22))]
MB = 256
NCH = MB // 64
NPB = TP // MB
NSB = NSQ // NCH
NW = 12
NDS = 8
import os as _os
CUT = int(_os.environ.get('KCUT', '0'))
CUTN = int(_os.environ.get('KCUTN', '-1'))
STRICT = int(_os.environ.get('KSTRICT', '1'))
CUTG = int(_os.environ.get('KCUTG', '0'))
ATTACH = int(_os.environ.get('KATTACH', '1'))
GP = _os.environ.get('KGP', 'pool')

C_RQ, C_RK, C_RV, C_RG = 0, 256, 512, 768
C_AQ, C_AK, C_AV, C_ALR, C_AG = 1024, 1152, 1280, 1536, 1552
C_HQ, C_HF, C_HI, C_HG = 1808, 2064, 2320, 2576
C_DQ, C_DK, C_DV, C_DB, C_DA, C_DG = 2832, 3088, 3344, 3600, 3604, 3608
WT = {}
_names = (["rq0", "rq1", "rk0", "rk1", "rv0", "rv1", "rg0", "rg1"] +
          ["aq0", "aq1", "ak0", "ak1", "av0", "av1", "ag0", "ag1", "alr"] +
          ["hq0", "hq1", "hf0", "hf1", "hi0", "hi1", "hg0", "hg1"] +
          ["dq0", "dq1", "dk0", "dk1", "dv0", "dv1", "dg0", "dg1", "db", "da"])
for _i, _n in enumerate(_names):
    WT[_n] = _i
NWT = len(_names)
PV_ADAB, PV_LNG, PV_LNB, PV_CONV, PV_BG, PV_NORM, PV_LB0, PV_LB1, PV_ALOG, PV_DTB, NPV = 0, 72, 96, 120, 144, 146, 149, 151, 153, 154, 160


def _esize(dt):
    return mybir.dt.size(dt)


class _Op:
    __slots__ = ("eng", "fn", "deps", "dmaq", "signal", "sigval", "dslot", "dval", "clock", "prio")

    def __init__(self, eng, fn, deps, dmaq):
        self.eng = eng
        self.fn = fn
        self.deps = deps
        self.dmaq = dmaq
        self.signal = False
        self.sigval = 0
        self.dslot = 0
        self.dval = 0
        self.clock = None
        self.prio = ()


class Sched:
    BUCK = 1024

    def __init__(self, nc, es):
        self.nc = nc
        self.engs = {"pe": nc.tensor, "act": nc.scalar, "dve": nc.vector, "pool": nc.gpsimd, "sp": nc.sync}
        self.ops = []
        self.buckets = {}
        self.mloc = {}
        self.sem = {e: es.enter_context(nc.semaphore("s_" + e)) for e in self.engs}
        self.dsem = {q: [es.enter_context(nc.semaphore("d_%s%d" % (q, i))) for i in range(NDS)]
                     for q in ("sp", "act", "pool")}
        self.out_dmas = []

    def _box(self, ap):
        sp = str(ap.space)
        if sp not in ("SB", "PSUM"):
            return None
        t = ap.tensor
        key = t.name
        info = self.mloc.get(key)
        if info is None:
            ml = self.nc.lookup_mloc(t)
            base = int(ml.addr)
            if sp == "PSUM":
                base += int(ml.bank) * 2048
            info = base
            self.mloc[key] = info
        shape = t.shape
        F = 1
        for s in shape[1:]:
            F *= int(s)
        off = int(ap.offset)
        p0 = off // F
        f0 = off % F
        dims = ap.ap
        pc = int(dims[0][1])
        lo = f0
        hi = f0
        for (st, cnt) in dims[1:]:
            ext = int(st) * (int(cnt) - 1)
            if ext < 0:
                lo += ext
            else:
                hi += ext
        hi += 1
        es_ = _esize(ap.dtype)
        if sp == "PSUM" and STRICT:
            b0 = ((info + lo * es_) // 2048) * 2048
            b1 = ((info + hi * es_ - 1) // 2048 + 1) * 2048
            return (sp, (p0 // 32) * 32, ((p0 + pc + 31) // 32) * 32, b0, b1)
        return (sp, p0, p0 + pc, info + lo * es_, info + hi * es_)

    @staticmethod
    def _ov(a, b):
        return a[1] < b[2] and b[1] < a[2] and a[3] < b[4] and b[3] < a[4]

    @staticmethod
    def _cov(a, b):
        return a[1] <= b[1] and a[2] >= b[2] and a[3] <= b[3] and a[4] >= b[4]

    def _keys(self, box):
        return [(box[0], k) for k in range(box[3] // self.BUCK, (box[4] - 1) // self.BUCK + 1)]

    def add(self, eng, fn, reads, writes, dmaq=None, prio=()):
        idx = len(self.ops)
        raw = set()
        praw = set()
        for ap in prio:
            b = self._box(ap)
            if b is not None:
                for k in self._keys(b):
                    for rec in self.buckets.get(k, ()):
                        if rec[2] and self._ov(b, rec[0]):
                            praw.add(rec[1])
        oth = set()
        rboxes = []
        wboxes = []
        for ap in reads:
            b = self._box(ap)
            if b is not None:
                rboxes.append(b)
        for ap in writes:
            b = self._box(ap)
            if b is not None:
                wboxes.append(b)
        for b in rboxes:
            for k in self._keys(b):
                for rec in self.buckets.get(k, ()):
                    if rec[2] and self._ov(b, rec[0]):
                        raw.add(rec[1])
        for b in wboxes:
            for k in self._keys(b):
                lst = self.buckets.get(k)
                if not lst:
                    continue
                keep = []
                for rec in lst:
                    if self._ov(b, rec[0]):
                        oth.add(rec[1])
                        if self._cov(b, rec[0]):
                            continue
                    keep.append(rec)
                self.buckets[k] = keep
        myeng = eng
        for b in rboxes:
            rec = (b, idx, False, myeng, dmaq is not None)
            for k in self._keys(b):
                lst = self.buckets.setdefault(k, [])
                for i2, r2 in enumerate(lst):
                    if (not r2[2]) and r2[0] == b and r2[3] == myeng and (not r2[4]) and dmaq is None:
                        lst[i2] = rec
                        break
                else:
                    lst.append(rec)
        for b in wboxes:
            rec = (b, idx, True, myeng, dmaq is not None)
            for k in self._keys(b):
                self.buckets.setdefault(k, []).append(rec)
        deps = []
        for d in raw | oth:
            dop = self.ops[d]
            if dop.dmaq is None and dmaq is None and dop.eng == eng:
                if eng == "pe":
                    continue
                if d not in raw and not STRICT:
                    continue
            deps.append(d)
            if dop.dmaq is None:
                dop.signal = True
        op = _Op(eng, fn, deps, dmaq)
        op.prio = praw
        self.ops.append(op)
        return idx

    def emit(self):
        nc = self.nc
        cnt = {e: 0 for e in self.engs}
        for op in self.ops:
            if op.dmaq is None and op.signal:
                cnt[op.eng] += 1
                op.sigval = cnt[op.eng]
        seen = {e: {} for e in self.engs}
        self.opidx = {id(o): i for i, o in enumerate(self.ops)}
        dcount = {q: 0 for q in self.dsem}
        dlast = {q: [None] * NDS for q in self.dsem}
        nwaits = 0
        for op in self.ops:
            e = op.eng
            eng = self.engs[e]
            sn = seen[e]
            needs = []
            for d in op.deps:
                dop = self.ops[d]
                if dop.dmaq is None:
                    needs.append((("c", dop.eng), dop.sigval, dop))
                else:
                    needs.append((("d", dop.dmaq, dop.dslot), dop.dval, dop))
            if op.dmaq is not None:
                q = op.dmaq
                slot = dcount[q] % NDS
                dcount[q] += 1
                prev = dlast[q][slot]
                op.dslot = slot
                op.dval = (prev.dval if prev is not None else 0) + 16
                if prev is not None:
                    needs.append((("d", q, slot), prev.dval, prev))
                dlast[q][slot] = op
            needs.sort(key=lambda x: -self.opidx[id(x[2])])
            pending = []
            for (key, val, dop) in needs:
                if sn.get(key, 0) >= val:
                    continue
                semh = self.sem[key[1]] if key[0] == "c" else self.dsem[key[1]][key[2]]
                pending.append((semh, val, self.opidx[id(dop)] in op.prio))
                nwaits += 1
                for k2, v2 in dop.clock.items():
                    if sn.get(k2, 0) < v2:
                        sn[k2] = v2
            attach = None
            if pending and ATTACH:
                pi = [i for i, x in enumerate(pending) if x[2]]
                attach = pending.pop(pi[-1] if pi else -1)
            for (semh, val, _) in pending:
                eng.wait_ge(semh, val)
            ins = op.fn(eng)
            if attach is not None:
                ins._wait_ge(attach[0], attach[1])
            clock = dict(sn)
            if op.dmaq is not None:
                ins.then_inc(self.dsem[op.dmaq][op.dslot], 16)
                clock[("d", op.dmaq, op.dslot)] = op.dval
            elif op.signal:
                ins.then_inc(self.sem[e], 1)
                clock[("c", e)] = op.sigval
            op.clock = clock
            op.fn = None
        sp = self.engs["sp"]
        sn = seen["sp"]
        for d in self.out_dmas:
            dop = self.ops[d]
            key = ("d", dop.dmaq, dop.dslot)
            if sn.get(key, 0) >= dop.dval:
                continue
            sp.wait_ge(self.dsem[dop.dmaq][dop.dslot], dop.dval)
            sn[key] = dop.dval
        return nwaits


class KB:
    def __init__(self, nc, es):
        self.nc = nc
        self.es = es
        self.S = Sched(nc, es)
        self.dbg_outs = []

    def sb(self, name, shape, dt):
        return self.es.enter_context(self.nc.sbuf_tensor("sb_" + name, shape, dt))

    def psum(self, name, shape, dt):
        return self.es.enter_context(self.nc.psum_tensor(name, shape, dt))

    def dram_in(self, name, shape):
        return self.nc.dram_tensor(name, list(shape), F32, kind="ExternalInput").ap()

    def dram_out(self, name, shape):
        return self.nc.dram_tensor(name, list(shape), F32, kind="ExternalOutput").ap()

    def mm(self, out, lhsT, rhs, start=True, stop=True):
        self.S.add("pe", lambda e: e.matmul(out, lhsT=lhsT, rhs=rhs, start=start, stop=stop), [lhsT, rhs], [out], prio=[lhsT])

    def tr(self, out, in_, ident):
        self.S.add("pe", lambda e: e.transpose(out, in_, ident), [in_, ident], [out], prio=[in_])

    def act(self, out, in_, func, bias=None, scale=None, eng="act"):
        kw = {}
        reads = [in_]
        if bias is not None:
            kw["bias"] = bias
            if not isinstance(bias, (int, float)):
                reads.append(bias)
        if scale is not None:
            kw["scale"] = scale
            if not isinstance(scale, (int, float)):
                reads.append(scale)
        self.S.add(eng, lambda e: e.activation(out=out, in_=in_, func=func, **kw), reads, [out])

    def tt(self, out, a, b, op, eng="dve"):
        self.S.add(eng, lambda e: e.tensor_tensor(out=out, in0=a, in1=b, op=op), [a, b], [out])

    def ts(self, out, a, s1, op0, s2=None, op1=None, eng="dve"):
        reads = [a]
        if not isinstance(s1, (int, float)):
            reads.append(s1)
        if s2 is not None and not isinstance(s2, (int, float)):
            reads.append(s2)
        if s2 is None:
            self.S.add(eng, lambda e: e.tensor_scalar(out=out, in0=a, scalar1=s1, scalar2=None, op0=op0), reads, [out])
        else:
            self.S.add(eng, lambda e: e.tensor_scalar(out=out, in0=a, scalar1=s1, scalar2=s2, op0=op0, op1=op1), reads, [out])

    def stt(self, out, a, s, b, op0, op1):
        reads = [a, b]
        if not isinstance(s, (int, float)):
            reads.append(s)
        self.S.add("dve", lambda e: e.scalar_tensor_tensor(out=out, in0=a, scalar=s, in1=b, op0=op0, op1=op1), reads, [out])

    def cp(self, out, in_, eng="dve"):
        if eng == "act":
            fn_ = AF.Identity if _os.environ.get('KVAR3', '') == 'ident' else AF.Copy
            self.S.add("act", lambda e: e.activation(out=out, in_=in_, func=fn_), [in_], [out])
        else:
            self.S.add(eng, lambda e: e.tensor_copy(out=out, in_=in_), [in_], [out])

    def ms(self, ap, val, eng="dve"):
        self.S.add(eng, lambda e: e.memset(ap, val), [], [ap])

    def scan(self, out, d0, d1, init, op0, op1):
        reads = [d0, d1]
        self.S.add("dve", lambda e: e.tensor_tensor_scan(out=out, data0=d0, data1=d1, initial=init, op0=op0, op1=op1), reads, [out])

    def recip(self, out, in_):
        self.S.add("dve", lambda e: e.reciprocal(out=out, in_=in_), [in_], [out])

    def dma(self, q, out, in_, is_out=False):
        idx = self.S.add(q, lambda e: e.dma_start(out=out, in_=in_), [in_], [out], dmaq=q)
        if is_out:
            self.S.out_dmas.append(idx)
        return idx

    def dbg(self, name, ap):
        shape = [int(s) for s in ap.shape]
        o = self.dram_out("dbg_" + name, shape)
        self.dma("sp", o, ap, is_out=True)
        self.dbg_outs.append(("dbg_" + name, shape))


class Prog:
    def __init__(self, debug=None):
        self.debug = debug or set()
        self.nc = bass.Bass("TRN2", target_bir_lowering=False)
        self.es = ExitStack()
        self.k = KB(self.nc, self.es)
        self.wcnt = 0
        self.pscnt = 0

    def alloc(self):
        k = self.k
        self.d_xT = k.dram_in("xT", [D, NTOK])
        self.d_cT = k.dram_in("cT", [D, 17])
        self.d_wi = [k.dram_in("wi1", [DEPTH, 2 * NJ, 128, 1024]), k.dram_in("wi2", [DEPTH, 2 * NJ, 128, 1024])]
        self.d_wo = [k.dram_in("wo1", [DEPTH, DFF, D]), k.dram_in("wo2", [DEPTH, DFF, D])]
        self.d_win = k.dram_in("win", [DEPTH, NWT, 128, 1024])
        self.d_wout = k.dram_in("wout", [DEPTH, 8, 128, 1024])
        self.d_adaw = k.dram_in("adaw", [DEPTH, 72, 128, 1024])
        self.d_pvec = k.dram_in("pvec", [DEPTH, 128, NPV])
        self.d_wg = k.dram_in("wgpad", [DEPTH, 16, 256])
        self.d_cos = k.dram_in("cosT", [128, NTOK])
        self.d_sin = k.dram_in("sinT", [128, NTOK])
        self.d_cst = k.dram_in("cst", [128, CST_N])
        self.d_gret = k.dram_in("gret", [2, 128, 2, MB])
        self.d_st = {}
        for nm in ("ret", "gla", "hg", "gdn"):
            self.d_st[nm] = k.dram_in("st_" + nm, [DEPTH, 128, 2, NSQ, 64])
        self.d_stconv = k.dram_in("st_conv", [DEPTH, 128, 6, NSQ, 3])
        self.o_yT = k.dram_out("yT", [D, NTOK])
        self.o_ps = {}
        self.o_ss = {}
        for nm in ("ret", "gla", "hg", "gdn"):
            self.o_ps[nm] = k.dram_out("ops_" + nm, [DEPTH, 128, 2, 64])
            self.o_ss[nm] = k.dram_out("oss_" + nm, [DEPTH, 128, 2, NSQ, 64])
        self.o_pconv = k.dram_out("opconv", [DEPTH, 128, 6, 3])
        self.o_sconv = k.dram_out("osconv", [DEPTH, 128, 6, NSQ, 3])
        self.xres = k.sb("xres", [128, KC, NTOK], F32)
        self.hT = k.sb("hT", [128, KC, NTOK], BF16)
        self.gbig = k.sb("gbig", [128, 8 * NTOK], BF16)
        self.wpool = [k.sb("w%d" % i, [128, 1024], BF16) for i in range(NW)]
        self.wmod = [k.sb("wm%d" % i, [128, 1024], BF16) for i in range(2)]
        self.wmodc = 0
        self.cst = k.sb("cst", [128, CST_N], F32)
        self.cstb = k.sb("cstb", [128, CSTB_N], BF16)
        self.pvec = [k.sb("pvec%d" % l, [128, NPV], F32) for l in range(DEPTH)]
        self.cTs = k.sb("cTs", [128, KC, 17], F32)
        self.csl = k.sb("csl", [128, KC, 17], BF16)
        self.modg = k.sb("modg", [128, 24, 17], F32)
        self.cF = [[k.sb("cF%d%d" % (l, i), [128, KC, 17], F32) for i in range(3)] for l in range(DEPTH)]
        self.hs = [[k.sb("hs%d%d" % (l, i), [128, KC, 17], F32) for i in range(3)] for l in range(DEPTH)]
        self.hb = [[k.sb("hb%d%d" % (l, i), [128, KC, 17], F32) for i in range(3)] for l in range(DEPTH)]
        self.xs = [k.sb("xs%d" % l, [128, 3, KC], F32) for l in range(DEPTH)]
        self.xb = [k.sb("xb%d" % l, [128, 3, KC], F32) for l in range(DEPTH)]
        self.wg = k.sb("wg", [16, 256], F32)
        self.wgb = k.sb("wgb", [16, 256], BF16)
        self.small = k.sb("small", [128, 64], F32)
        self.arena2 = k.sb("arena2", [128, A2_BYTES // 4], F32)
        self.psb = [k.psum("ps%d" % i, [128, 512], F32) for i in range(8)]

    def carve(self, region, off, nbytes, dt, parts=128):
        if region == "g":
            base = self.gbig
            es = 2
            tot = 8 * NTOK * 2
        else:
            base = self.arena2
            es = 4
            tot = A2_BYTES
        assert off % 4 == 0 and nbytes % 4 == 0 and off + nbytes <= tot, (region, off, nbytes, tot)
        ap = base[0:parts, off // es:(off + nbytes) // es]
        if dt == F32 and es == 2:
            ap = ap.bitcast(F32)
        elif dt == BF16 and es == 4:
            ap = ap.bitcast(BF16)
        return ap

    def ps(self):
        p = self.psb[self.pscnt % 6]
        self.pscnt += 1
        return p

    def wload(self, dram_ap):
        w = self.wpool[self.wcnt % NW]
        self.wcnt += 1
        self.k.dma("pool", w[:, :], dram_ap)
        return w

    def pv(self, l, col, n=1):
        return self.pvec[l][:, col:col + n]


CST_IDENT = 0
CST_MINCL = 128
CST_MSTR = 192
CST_MSTRT = 256
CST_ID2 = 320
CST_RESET = 384
CST_SEL = 384 + MB
CST_SELF = CST_SEL + 256
CST_ONES = CST_SELF + 256
CST_BONES = CST_ONES + 128
CST_PM = CST_BONES + 128
CST_SELHP = CST_PM + 128
CST_N = CST_SELHP + 4
CB_IDENT, CB_ONES, CB_BONES, CB_PM, CSTB_N = 0, 128, 256, 384, 512
A2_BYTES = 24 * 1024


def _bc(ap, shape):
    return ap.to_broadcast(list(shape))


class Prog2(Prog):
    def setup(self):
        k = self.k
        k.dma("sp", self.cst[:, :], self.d_cst)
        for l in range(DEPTH):
            k.dma("sp", self.pvec[l][:, :], self.d_pvec[l])
        k.dma("sp", self.cTs[:, :, :], self.d_cT.rearrange("(kc p) b -> p kc b", p=128))
        k.dma("sp", self.xres[:, :, :], self.d_xT.rearrange("(kc p) t -> p kc t", p=128))
        for (src, dst) in ((CST_IDENT, CB_IDENT), (CST_ONES, CB_ONES), (CST_BONES, CB_BONES), (CST_PM, CB_PM)):
            k.cp(self.cstb[:, dst:dst + 128], self.cst[:, src:src + 128])
        k.ms(self.small[:, 8:9], 1.0)
        k.ms(self.small[:, 9:10], RMS_EPS)
        k.ms(self.small[:, 10:11], 0.0)
        self.c_one = self.small[:, 8:9]
        self.c_eps = self.small[:, 9:10]
        k.act(self.csl[:, :, :], self.cTs[:, :, :], AF.Silu)
        for l in range(DEPTH):
            last = (l == DEPTH - 1)
            for i in range(3):
                a = 1.0 if (last and i == 2) else ALPHA
                k.ts(self.xs[l][:, i, :], self.pv(l, PV_LNG + i * 8, 8), a, ALU.mult)
                k.ts(self.xb[l][:, i, :], self.pv(l, PV_LNB + i * 8, 8), a, ALU.mult)

    def ident_b(self):
        return self.cstb[:, CB_IDENT:CB_IDENT + 128]

    def mod_group(self, l, i):
        for _ in self.mod_group_gen(l, i):
            pass

    def mod_group_gen(self, l, i, private=False):
        k = self.k
        for f in range(24):
            ft = i * 24 + f
            if private:
                w = self.wmod[self.wmodc % 2]
                self.wmodc += 1
                k.dma("pool", w[:, :], self.d_adaw[l, ft])
            else:
                w = self.wload(self.d_adaw[l, ft])
            p = self.ps()
            for kc in range(KC):
                k.mm(p[:, 0:17], w[:, kc * 128:(kc + 1) * 128], self.csl[:, kc, :], start=(kc == 0), stop=(kc == KC - 1))
            k.ts(self.modg[:, f, :], p[:, 0:17], self.pv(l, PV_ADAB + ft), ALU.add)
            yield
        sh = self.modg[:, 0:8, :]
        sc = self.modg[:, 8:16, :]
        gt = self.modg[:, 16:24, :]
        coef = 1.0 if i == 1 else 0.5
        k.ts(self.cF[l][i][:, :, :], gt, 1.0, ALU.add, coef, ALU.mult)
        if i == 0 and l == 0:
            k.ts(self.hs[l][i][:, :, :], sc, 1.0, ALU.add)
            k.cp(self.hb[l][i][:, :, :], sh)
        else:
            pl, pi = (l, i - 1) if i > 0 else (l - 1, 2)
            gp = _bc(self.pv(pl, PV_LNG + pi * 8, 8).unsqueeze(2), [128, 8, 17])
            bp = _bc(self.pv(pl, PV_LNB + pi * 8, 8).unsqueeze(2), [128, 8, 17])
            k.ts(self.hs[l][i][:, :, :], sc, 1.0, ALU.add)
            k.tt(self.hb[l][i][:, :, :], self.hs[l][i][:, :, :], bp, ALU.mult)
            k.tt(self.hb[l][i][:, :, :], self.hb[l][i][:, :, :], sh, ALU.add)
            k.tt(self.hs[l][i][:, :, :], self.hs[l][i][:, :, :], gp, ALU.mult)

    def affine_h(self, out, in_, sv, bv, kc, tbi, tmp):
        k = self.k
        if tbi < 4:
            k.act(out, in_, AF.Identity, bias=bv[:, kc, 0:1], scale=sv[:, kc, 0:1])
        else:
            s3 = _bc(sv[:, kc, 1:17].unsqueeze(2), [128, NSQ, TS])
            b3 = _bc(bv[:, kc, 1:17].unsqueeze(2), [128, NSQ, TS])
            i3 = in_.rearrange("p (s j) -> p s j", j=TS)
            o3 = out.rearrange("p (s j) -> p s j", j=TS)
            t3 = tmp.rearrange("p (s j) -> p s j", j=TS)
            k.tt(t3, i3, s3, ALU.mult)
            k.tt(o3, t3, b3, ALU.add)

    def initial_h(self):
        k = self.k
        tmpS = self.carve("a", 18432, 256, F32)
        for tbi, (t0, n) in enumerate(TBS):
            for kc in range(KC):
                self.affine_h(self.hT[:, kc, t0:t0 + n], self.xres[:, kc, t0:t0 + n], self.hs[0][0], self.hb[0][0], kc, tbi, tmpS[:, 0:64])
        for kc in range(KC):
            k.ts(self.xres[:, kc, :], self.xres[:, kc, :], ALPHA, ALU.mult)

    def resid_acc(self, ps_ap, l, i, kc, tbi, t0, n, s0=0, ns=NSQ):
        k = self.k
        xr = self.xres[:, kc, t0:t0 + n]
        cf = self.cF[l][i]
        if tbi < 4:
            k.stt(xr, ps_ap, cf[:, kc, 0:1], xr, ALU.mult, ALU.add)
        else:
            tmpS = self.carve("a", 18432, 256, F32)[:, 0:n]
            c3 = _bc(cf[:, kc, 1 + s0:1 + s0 + ns].unsqueeze(2), [128, ns, TS])
            k.tt(tmpS.rearrange("p (s j) -> p s j", j=TS), ps_ap.rearrange("p (s j) -> p s j", j=TS), c3, ALU.mult)
            k.tt(xr, xr, tmpS, ALU.add)

    def stats_acc(self, ps1, ps2, kc, t0, n):
        k = self.k
        rb = self.carve("a", 4096 + (kc % 2) * 1024, 1024, BF16)[:, 0:n]
        rsq = self.carve("a", 6144 + (kc % 2) * 1024, 1024, BF16)[:, 0:n]
        xr = self.xres[:, kc, t0:t0 + n]
        k.cp(rb, xr, eng="act")
        k.act(rsq, xr, AF.Square)
        ones = self.cstb[:, CB_ONES:CB_ONES + 128]
        k.mm(ps1[:, 0:n], ones, rb, start=(kc == 0), stop=(kc == KC - 1))
        k.mm(ps2[:, 0:n], ones, rsq, start=(kc == 0), stop=(kc == KC - 1))

    def ln_block(self, ps1, ps2, l, i, tbi, t0, n):
        k = self.k
        mean = self.carve("a", 8192, 2048, F32)[:, 0:n]
        rstd = self.carve("a", 10240, 2048, F32)[:, 0:n]
        nmr = self.carve("a", 12288, 2048, F32)[:, 0:n]
        ex2 = self.carve("a", 18688, 2048, F32)[:, 0:n]
        tmpS = self.carve("a", 18432, 256, F32)
        k.ts(mean, ps1[:, 0:n], 1.0 / D, ALU.mult)
        k.tt(ex2, mean, mean, ALU.mult)
        k.stt(ex2, ps2[:, 0:n], 1.0 / D, ex2, ALU.mult, ALU.subtract)
        k.ts(ex2, ex2, 0.0, ALU.max, LN_EPS, ALU.add)
        k.act(ex2, ex2, AF.Sqrt)
        k.recip(rstd, ex2)
        k.stt(nmr, mean, -1.0, rstd, ALU.mult, ALU.mult)
        last = (l == DEPTH - 1 and i == 2)
        if i < 2:
            nl, ni = l, i + 1
        else:
            nl, ni = l + 1, 0
        for kc in range(KC):
            xn = self.carve("a", 14336 + (kc % 2) * 2048, 2048, F32)[:, 0:n]
            xr = self.xres[:, kc, t0:t0 + n]
            k.tt(xn, xr, rstd, ALU.mult)
            k.tt(xn, xn, nmr, ALU.add)
            k.act(xr, xn, AF.Identity, bias=self.xb[l][:, i, kc:kc + 1], scale=self.xs[l][:, i, kc:kc + 1])
            if not last:
                self.affine_h(self.hT[:, kc, t0:t0 + n], xn, self.hs[nl][ni], self.hb[nl][ni], kc, tbi, tmpS[:, 0:64])

    def ffn(self, l, which, mid_hook=None):
        k = self.k
        i = 0 if which == 0 else 2
        d_wi = self.d_wi[which]
        d_wo = self.d_wo[which]
        gb = self.gbig
        mid_gen = mid_hook() if mid_hook is not None else None
        for gi, js in enumerate(JG):
            for jj, j in enumerate(js):
                wa = self.wload(d_wi[l, j])
                wb = self.wload(d_wi[l, NJ + j])
                for tbi, (t0, n) in enumerate(TBS):
                    pa = self.ps()
                    pb = self.ps()
                    for kc in range(KC):
                        k.mm(pa[:, 0:n], wa[:, kc * 128:(kc + 1) * 128], self.hT[:, kc, t0:t0 + n], start=(kc == 0), stop=(kc == KC - 1))
                    for kc in range(KC):
                        k.mm(pb[:, 0:n], wb[:, kc * 128:(kc + 1) * 128], self.hT[:, kc, t0:t0 + n], start=(kc == 0), stop=(kc == KC - 1))
                    sA = self.carve("a", ((jj * 5 + tbi) % 2) * 2048, 2048, F32)[:, 0:n]
                    k.act(sA, pa[:, 0:n], AF.Silu)
                    k.tt(gb[:, jj * NTOK + t0: jj * NTOK + t0 + n], sA, pb[:, 0:n], ALU.mult)
                if gi == 0 and mid_gen is not None:
                    for _ in range(3):
                        next(mid_gen, None)
            if gi == 0 and mid_gen is not None:
                for _ in mid_gen:
                    pass
            wos = [self.wload(d_wo[l, j * 128:(j + 1) * 128, :]) for j in js]
            final = (gi == len(JG) - 1)
            for tbi, (t0, n) in enumerate(TBS):
                if final:
                    ps1 = self.psb[6]
                    ps2 = self.psb[7]
                for oc in range(KC):
                    po = self.ps()
                    for jj in range(len(js)):
                        k.mm(po[:, 0:n], wos[jj][:, oc * 128:(oc + 1) * 128], gb[:, jj * NTOK + t0: jj * NTOK + t0 + n],
                             start=(jj == 0), stop=(jj == len(js) - 1))
                    self.resid_acc(po[:, 0:n], l, i, oc, tbi, t0, n)
                    if final:
                        self.stats_acc(ps1, ps2, oc, t0, n)
                if final:
                    self.ln_block(ps1, ps2, l, i, tbi, t0, n)

    def ln_pass(self, l, i):
        for tbi, (t0, n) in enumerate(TBS):
            ps1 = self.psb[6]
            ps2 = self.psb[7]
            for kc in range(KC):
                self.stats_acc(ps1, ps2, kc, t0, n)
            self.ln_block(ps1, ps2, l, i, tbi, t0, n)

    def store_y(self):
        self.k.dma("sp", self.o_yT.rearrange("(kc p) t -> p kc t", p=128), self.xres[:, :, :], is_out=True)


class Prog3(Prog2):
    def blkinfo(self, blk):
        if blk < NPB:
            return False, blk * MB, MB, 0
        sb_ = blk - NPB
        return True, TP + sb_ * NCH * TS, NCH * TS, sb_ * NCH

    def cv(self, ap, is_s, N):
        a = ap[:, 0:N]
        if is_s:
            a = a.rearrange("p (s j) -> p s j", j=TS)
        return a

    def pw(self, ap, is_s):
        if is_s:
            return ap.rearrange("p (s t) -> p s t", t=64)[:, :, 0:TS]
        return ap

    def proj(self, w, t0, N, M=128):
        k = self.k
        p = self.ps()
        for kc in range(KC):
            k.mm(p[0:M, 0:N], w[:, kc * 128:kc * 128 + M], self.hT[:, kc, t0:t0 + N], start=(kc == 0), stop=(kc == KC - 1))
        return p

    def layer_small(self, l):
        k = self.k
        sm = self.small
        k.ts(sm[:, 0:2], self.pv(l, PV_BG, 2), -1.0, ALU.mult)
        if l == 0:
            k.ms(sm[:, 2:4], 0.0)
        else:
            k.tt(sm[:, 2:4], self.pv(l, PV_LB1, 2), self.pv(l, PV_LB0, 2), ALU.subtract)
            k.act(sm[:, 2:4], sm[:, 2:4], AF.Exp, scale=-1.0)
            k.ts(sm[:, 2:4], sm[:, 2:4], 1.0, ALU.add)
            k.recip(sm[:, 2:4], sm[:, 2:4])
        k.ts(sm[:, 4:6], sm[:, 2:4], -1.0, ALU.mult, 1.0, ALU.add)
        k.ts(sm[:, 6:8], sm[:, 4:6], -1.0, ALU.mult)
        k.act(sm[0:4, 11:12], self.pv(l, PV_ALOG)[0:4, :], AF.Exp)
        k.ts(sm[0:4, 11:12], sm[0:4, 11:12], -1.0, ALU.mult)
        k.dma("sp", self.wg[:, :], self.d_wg[l])
        k.cp(self.wgb[:, :], self.wg[:, :])

    def mix_layer(self, l, mid_hook=None):
        k = self.k
        self.layer_small(l)
        mixers = [("ret", ["rq0", "rq1", "rk0", "rk1", "rv0", "rv1", "rg0", "rg1"]),
                  ("gla", ["aq0", "aq1", "ak0", "ak1", "av0", "av1", "ag0", "ag1", "alr"]),
                  ("hg", ["hq0", "hq1", "hf0", "hf1", "hi0", "hi1", "hg0", "hg1"]),
                  ("gdn", ["dq0", "dq1", "dk0", "dk1", "dv0", "dv1", "dg0", "dg1", "db", "da"])]
        mid_gen = mid_hook() if mid_hook is not None else None
        for m, (name, wn) in enumerate(mixers):
            if name in self.skip_mixers:
                continue
            W = {n: self.wload(self.d_win[l, WT[n]]) for n in wn}
            Wo = [self.wload(self.d_wout[l, 2 * m + tl]) for tl in range(2)]
            blks = list(range(NPB + NSB) if self.nblk is None else self.nblk)
            if name == "gdn":
                for blk in blks:
                    self.mix_gdn(l, m, blk, W, Wo)
            else:
                def run_to_prep(g_):
                    for v_ in g_:
                        if v_ == "PREP_DONE":
                            return
                cur = self.mix_gla(l, m, name, blks[0], W, Wo, 0)
                run_to_prep(cur)
                for bi in range(len(blks)):
                    if mid_gen is not None:
                        for _ in range(2):
                            next(mid_gen, None)
                    nxt = self.mix_gla(l, m, name, blks[bi + 1], W, Wo, (bi + 1) % 2) if bi + 1 < len(blks) else None
                    cur_live, nxt_live = True, nxt is not None
                    while cur_live or nxt_live:
                        if cur_live:
                            try:
                                next(cur)
                            except StopIteration:
                                cur_live = False
                        if nxt_live:
                            try:
                                if next(nxt) == "PREP_DONE":
                                    nxt_live = False
                            except StopIteration:
                                nxt_live = False
                    cur = nxt
            if mid_gen is not None:
                for _ in mid_gen:
                    pass
                mid_gen = None
        if mid_gen is not None:
            for _ in mid_gen:
                pass
        self.ln_pass(l, 1)

    def chk(self):
        self.chkc = getattr(self, "chkc", 0) + 1
        return self.chkc == CUTN

    def gbuf(self, off, nbytes, dt, parts=128):
        return self.carve("g", off, nbytes, dt, parts)

    def mix_gla(self, l, m, name, blk, W, Wo, st=0):
        k = self.k
        is_s, t0, N, s0 = self.blkinfo(blk)
        cv = lambda ap: self.cv(ap, is_s, N)
        pw = lambda ap: self.pw(ap, is_s)
        r3 = lambda ap: ap.rearrange("p (a t) -> p a t", a=2)
        Qp = r3(self.gbuf(0, 1024, BF16))
        Kp = r3(self.gbuf(1024, 1024, BF16))
        if st == 0:
            V = r3(self.gbuf(2048, 1024, BF16))
            GATE = r3(self.gbuf(31488, 2048, F32))
            QT = r3(self.gbuf(12288, 1024, BF16))
            KT = r3(self.gbuf(13312, 1024, BF16))
            QG = r3(self.gbuf(14336, 1024, BF16))
            av = self.gbuf(31232, 32, F32).rearrange("p (a c) -> p a c", a=2)
            bv = self.gbuf(31264, 32, F32).rearrange("p (a c) -> p a c", a=2)
        else:
            QT = r3(self.carve("a", 4096, 1024, BF16))
            KT = r3(self.carve("a", 5120, 1024, BF16))
            QG = r3(self.carve("a", 6144, 1024, BF16))
            V = r3(self.carve("a", 7168, 1024, BF16))
            GATE = r3(self.carve("a", 8192, 2048, F32))
            av = self.carve("a", 10240, 32, F32).rearrange("p (a c) -> p a c", a=2)
            bv = self.carve("a", 10272, 32, F32).rearrange("p (a c) -> p a c", a=2)
        T1r = self.carve("a", 12288, 2048, F32)
        T2r = self.carve("a", 14336, 2048, F32)
        Gs = r3(self.gbuf(4096, 2048, F32))
        T1 = self.gbuf(6144, 2048, F32)
        T2 = self.gbuf(8192, 2048, F32)
        G0 = r3(self.gbuf(10240, 2048, F32))
        KTM = self.gbuf(15360, 2048, BF16)
        VTM = self.gbuf(17408, 2048, BF16)
        ATT = self.gbuf(19456, 2048, BF16)
        UP = self.gbuf(21504, 2048, F32).rearrange("p (a c v) -> p a c v", a=2, c=NCH)
        SALL = self.gbuf(23552, 2560, F32).rearrange("p (a c v) -> p a c v", a=2, c=NCH + 1)
        SBF = self.gbuf(26112, 1024, BF16).rearrange("p (a c v) -> p a c v", a=2, c=NCH)
        SIN = self.gbuf(27136, 2048, F32).rearrange("p (a c v) -> p a c v", a=2, c=NCH)
        OSQ = self.gbuf(29184, 1024, BF16)
        OB = r3(self.gbuf(30208, 1024, BF16))
        cosb = self.carve("a", 0, 1024, F32)
        sinb = self.carve("a", 1024, 1024, F32)
        alrb = self.carve("a", 2048, 512, BF16)
        qb = self.carve("a", 2560, 512, BF16)
        t1c = T1[:, 0:MB]
        t2c = T2[:, 0:MB]
        identb = self.ident_b()
        sm = self.small

        if is_s:
            for tl in range(2):
                k.ms(Qp[:, tl, :], 0.0)
                k.ms(Kp[:, tl, :], 0.0)
                k.ms(V[:, tl, :], 0.0)
                if name != "ret":
                    k.ms(G0[:, tl, :], 0.0)

        def simple_v_gate(vn, gn, tl):
            p = self.proj(W[vn + str(tl)], t0, N)
            k.cp(pw(V[:, tl, :]), cv(p))
            p = self.proj(W[gn + str(tl)], t0, N)
            k.act(GATE[:, tl, 0:N], p[:, 0:N], AF.Silu)

        if name == "ret":
            if _os.environ.get('KVAR', '') != 'nodma':
                k.dma("sp", cosb[:, 0:N], self.d_cos[:, t0:t0 + N])
                k.dma("sp", sinb[:, 0:N], self.d_sin[:, t0:t0 + N])
                k.dma("sp", Gs, self.d_gret[1 if is_s else 0])
            pmb = self.cstb[:, CB_PM:CB_PM + 128]
            if CUT == 11:
                return
            for tl in range(2):
                for (wn_, dst, scale) in (("rq", Qp, 1.0), ("rk", Kp, 0.125)):
                    p = self.proj(W[wn_ + str(tl)], t0, N)
                    if self.chk():
                        return
                    k.cp(qb[:, 0:N], p[:, 0:N])
                    pp = self.ps()
                    k.mm(pp[:, 0:N], pmb, qb[:, 0:N])
                    if self.chk():
                        return
                    k.stt(t1c[:, 0:N], p[:, 0:N], scale, cosb[:, 0:N], ALU.mult, ALU.mult)
                    k.stt(t2c[:, 0:N], pp[:, 0:N], scale, sinb[:, 0:N], ALU.mult, ALU.mult)
                    if self.chk():
                        return
                    _v = _os.environ.get('KVAR', '')
                    if _v == 'A' and tl == 1:
                        k.tt(pw(dst[:, 0, :]), cv(t1c), cv(t2c), ALU.add)
                    elif _v == 'B' and tl == 1:
                        k.tt(pw(dst[:, tl, :]), cv(t1c), cv(t2c), ALU.add, eng="pool")
                    else:
                        k.tt(pw(dst[:, tl, :]), cv(t1c), cv(t2c), ALU.add, eng=GP)
                    if self.chk():
                        return
                yield
                simple_v_gate("rv", "rg", tl)
                yield
                if self.chk():
                    return
        elif name == "gla":
            p = self.proj(W["alr"], t0, N, M=16)
            k.cp(alrb[0:16, 0:N], p[0:16, 0:N])
            for tl in range(2):
                pg = self.ps()
                k.mm(pg[:, 0:N], self.wgb[0:16, tl * 128:(tl + 1) * 128], alrb[0:16, 0:N])
                k.act(t1c[:, 0:N], pg[:, 0:N], AF.Exp, bias=sm[:, tl:tl + 1], scale=-1.0)
                k.act(t1c[:, 0:N], t1c[:, 0:N], AF.Ln, bias=self.c_one)
                k.ts(pw(G0[:, tl, :]), cv(t1c), -1.0 / 16.0, ALU.mult)
                p = self.proj(W["aq" + str(tl)], t0, N)
                k.ts(pw(Qp[:, tl, :]), cv(p), float(32.0 ** -0.5), ALU.mult)
                p = self.proj(W["ak" + str(tl)], t0, N)
                k.cp(pw(Kp[:, tl, :]), cv(p))
                yield
                simple_v_gate("av", "ag", tl)
                yield
        else:
            for tl in range(2):
                p = self.proj(W["hq" + str(tl)], t0, N)
                k.act(t1c[:, 0:N], p[:, 0:N], AF.Silu)
                k.ts(pw(Qp[:, tl, :]), cv(t1c), 0.125, ALU.mult)
                p = self.proj(W["hf" + str(tl)], t0, N)
                k.act(t1c[:, 0:N], p[:, 0:N], AF.Exp, scale=-1.0)
                k.ts(t1c[:, 0:N], t1c[:, 0:N], 1.0, ALU.add)
                k.recip(t1c[:, 0:N], t1c[:, 0:N])
                k.act(t2c[:, 0:N], t1c[:, 0:N], AF.Ln, bias=sm[:, 2 + tl:3 + tl], scale=sm[:, 4 + tl:5 + tl])
                k.cp(pw(G0[:, tl, :]), cv(t2c))
                k.ts(pw(Kp[:, tl, :]), cv(t1c), sm[:, 6 + tl:7 + tl], ALU.mult, sm[:, 4 + tl:5 + tl], ALU.add)
                yield
                simple_v_gate("hi", "hg", tl)
                yield
        if name != "ret":
            reset = self.cst[:, CST_RESET:CST_RESET + MB]
            for tl in range(2):
                k.scan(Gs[:, tl, :], reset, G0[:, tl, :], 0.0, ALU.mult, ALU.add)

        if CUT == 1:
            return
        for tl in range(2):
            G3 = Gs[:, tl, :].rearrange("p (c t) -> p c t", t=64)
            D3 = t1c.rearrange("p (c t) -> p c t", t=64)
            k.tt(D3, G3, _bc(G3[:, :, 31:32], [128, NCH, 64]), ALU.subtract)
            k.act(t2c, t1c, AF.Exp)
            k.tt(QT[:, tl, :], Qp[:, tl, :], t2c, ALU.mult, eng=GP)
            k.act(t2c, t1c, AF.Exp, scale=-1.0)
            k.tt(KT[:, tl, :], Kp[:, tl, :], t2c, ALU.mult, eng=GP)
            k.act(t2c, Gs[:, tl, :], AF.Exp)
            k.tt(QG[:, tl, :], Qp[:, tl, :], t2c, ALU.mult, eng=GP)
            k.act(av[:, tl, :], G3[:, :, 63], AF.Exp)
            k.act(bv[:, tl, :], D3[:, :, 63], AF.Exp)
            yield
        yield "PREP_DONE"

        P2 = [slice(0, 64), slice(64, 128)]
        for (src, dstm) in ((KT, KTM), (V, VTM)):
            pT = [self.ps(), self.ps()]
            for p_ in range(2):
                pTb = pT[p_][:, :].bitcast(BF16)
                for c in range(NCH):
                    for tl in range(2):
                        k.tr(pTb[P2[p_], (c * 2 + tl) * 64:(c * 2 + tl + 1) * 64], src[P2[p_], tl, c * 64:(c + 1) * 64],
                             identb[P2[p_], 64 * p_:64 * p_ + 64])
                k.cp(dstm[P2[p_], 0:NCH * 128], pTb[P2[p_], 0:NCH * 128])
            yield

        mincl2 = self.cst[:, CST_MINCL:CST_MINCL + 64]
        pA = [self.ps(), self.ps()]
        for p_ in range(2):
            for c in range(NCH):
                for tl in range(2):
                    sl = slice((c * 2 + tl) * 64, (c * 2 + tl + 1) * 64)
                    k.mm(pA[p_][P2[p_], sl], KT[P2[p_], tl, c * 64:(c + 1) * 64], QT[P2[p_], tl, c * 64:(c + 1) * 64])
            k.tt(ATT[P2[p_], 0:NCH * 128].rearrange("p (a t) -> p a t", t=64),
                 pA[p_][P2[p_], 0:NCH * 128].rearrange("p (a t) -> p a t", t=64),
                 _bc(mincl2[P2[p_], :].unsqueeze(1), [64, NCH * 2, 64]), ALU.mult)

        yield
        pU = [self.ps(), self.ps()]
        for p_ in range(2):
            for c in range(NCH):
                for tl in range(2):
                    sl = slice((c * 2 + tl) * 64, (c * 2 + tl + 1) * 64)
                    k.mm(pU[p_][P2[p_], (tl * NCH + c) * 64:(tl * NCH + c + 1) * 64], KTM[P2[p_], sl], VTM[P2[p_], sl])
            k.tt(UP.rearrange("p a c v -> p (a c) v")[P2[p_]], pU[p_][P2[p_], :].rearrange("p (a v) -> p a v", v=64),
                 _bc(bv.rearrange("p a c -> p (a c)")[P2[p_]].unsqueeze(2), [64, 2 * NCH, 64]), ALU.mult)

        yield
        if not is_s:
            if blk == 0:
                k.ms(SALL[:, :, 0, :], 0.0)
            else:
                k.cp(SALL[:, :, 0, :], SALL[:, :, NCH, :])
            SBv = SALL
        else:
            k.dma("sp", SIN, self.d_st[name][l][:, :, s0:s0 + NCH, :])
            SBv = SIN
        for c in range(NCH):
            for tl in range(2):
                src = SALL[:, tl, c, :] if not is_s else SIN[:, tl, c, :]
                k.stt(SALL[:, tl, c + 1, :], src, av[:, tl, c:c + 1], UP[:, tl, c, :], ALU.mult, ALU.add)
        if is_s:
            k.dma("sp", self.o_ss[name][l][:, :, s0:s0 + NCH, :], SALL[:, :, 1:NCH + 1, :], is_out=True)
        elif blk == NPB - 1:
            k.dma("sp", self.o_ps[name][l], SALL[:, :, NCH, :], is_out=True)
        k.cp(SBF, SBv[:, :, 0:NCH, :], eng=GP)

        yield
        pO = [self.ps(), self.ps()]
        for p_ in range(2):
            for c in range(NCH):
                for tl in range(2):
                    sl = slice((c * 2 + tl) * 64, (c * 2 + tl + 1) * 64)
                    o_ap = pO[p_][P2[p_], (tl * NCH + c) * 64:(tl * NCH + c + 1) * 64]
                    k.mm(o_ap, VTM[P2[p_], sl], ATT[P2[p_], sl], start=True, stop=False)
                    k.mm(o_ap, SBF[P2[p_], tl, c, :], QG[P2[p_], tl, c * 64:(c + 1) * 64], start=False, stop=True)
        for p_ in range(2):
            k.cp(T2r[P2[p_], 0:2 * MB], pO[p_][P2[p_], 0:2 * MB], eng="act")
        yield
        normw = None if name == "ret" else self.pv(l, PV_NORM + {"gla": 0, "hg": 1}[name])
        self.o_post(l, m, T2r, GATE, OB, OSQ, T1r, T2r, normw, Wo, is_s, t0, N, s0)

    def o_post(self, l, m, pO, GATE, OB, OSQ, T1, T2, normw, Wo, is_s, t0, N, s0):
        k = self.k
        cv = lambda ap: self.cv(ap, is_s, N)
        pw = lambda ap: self.pw(ap, is_s)
        bones = self.cstb[:, CB_BONES:CB_BONES + 128]
        if str(pO.space) == "PSUM":
            k.cp(T2[:, 0:2 * MB], pO[:, 0:2 * MB], eng="act")
            pO = T2
        k.act(OSQ[:, 0:2 * MB], pO[:, 0:2 * MB], AF.Square)
        pS = self.ps()
        for tl in range(2):
            k.mm(pS[:, tl * MB:(tl + 1) * MB], bones, OSQ[:, tl * MB:(tl + 1) * MB])
        k.act(T1[:, 0:2 * MB], pS[:, 0:2 * MB], AF.Sqrt, bias=self.c_eps, scale=1.0 / 64.0)
        k.recip(T1[:, 0:2 * MB], T1[:, 0:2 * MB])
        k.tt(T2[:, 0:2 * MB], pO[:, 0:2 * MB], T1[:, 0:2 * MB], ALU.mult)
        for tl in range(2):
            src = pw(T2[:, tl * MB:(tl + 1) * MB])
            if normw is None:
                k.tt(cv(OB[:, tl, :]), src, cv(GATE[:, tl, :]), ALU.mult)
            else:
                k.stt(cv(OB[:, tl, :]), src, normw, cv(GATE[:, tl, :]), ALU.mult, ALU.mult)
        for oc in range(KC):
            po = self.ps()
            for tl in range(2):
                k.mm(po[:, 0:N], Wo[tl][:, oc * 128:(oc + 1) * 128], OB[:, tl, 0:N], start=(tl == 0), stop=(tl == 1))
            self.resid_acc(po[:, 0:N], l, 1, oc, 4 if is_s else 0, t0, N, s0, NCH)


class Prog4(Prog3):
    def mix_gdn(self, l, m, blk, W, Wo):
        k = self.k
        is_s, t0, N, s0 = self.blkinfo(blk)
        cv = lambda ap: self.cv(ap, is_s, N)
        pw = lambda ap: self.pw(ap, is_s)
        r3 = lambda ap: ap.rearrange("p (a t) -> p a t", a=2)
        g = self.gbuf
        P2 = [slice(0, 64), slice(64, 128)]
        Qp = r3(g(0, 2048, F32))
        Kp = r3(g(2048, 2048, F32))
        QGx = r3(g(4096, 2048, F32))
        QpT = r3(g(6144, 2048, F32))
        VTM = g(8192, 2048, F32)
        KTM = g(10240, 2048, F32)
        MT = g(12288, 2048, F32).rearrange("p (a c v) -> p a c v", a=2, c=NCH)
        BC = g(14336, 2048, F32).rearrange("p (a c v) -> p a c v", a=2, c=NCH)
        SALL = g(16384, 2560, F32).rearrange("p (a c v) -> p a c v", a=2, c=NCH + 1)
        SIN = g(18944, 2048, F32).rearrange("p (a c v) -> p a c v", a=2, c=NCH)
        OB = r3(g(20992, 1024, BF16))
        OSQ = g(22016, 1024, BF16)
        GD = g(23040, 1024, F32)
        GC = g(24064, 1024, F32)
        BET = g(25088, 1024, F32)
        c3v = lambda ap: ap[:, 0:NCH * 2].rearrange("p (c l) -> p c l", l=2)
        GCOL = g(26112, 64, F32)
        BCOL = g(26176, 64, F32)
        EGC = g(26240, 64, F32)
        KHS = g(26304, 64, F32)
        BEG = g(26368, 64, F32)
        eGL = g(26432, 32, F32).rearrange("p (a c) -> p a c", a=2)
        halo = g(26496, 72, F32).rearrange("p (i r) -> p i r", r=3)
        OL = r3(g(26624, 2048, F32))
        GATE = r3(g(28672, 2048, F32))
        A = lambda i: self.carve("a", i * 1024, 1024, F32)
        UQ = r3(self.carve("a", 9216, 2048, F32))
        UK = r3(self.carve("a", 11264, 2048, F32))
        VT = r3(self.carve("a", 13312, 2048, F32))
        SQ = self.carve("a", 15360, 1024, BF16)
        ubuf = self.carve("a", 16384, 1040, F32)
        acc = self.carve("a", 17424, 1024, F32)
        T1 = self.carve("a", 18448, 2048, F32)
        T2 = self.carve("a", 20496, 2048, F32)
        sm = self.small
        identF = self.cst[:, CST_IDENT:CST_IDENT + 128]
        bones = self.cstb[:, CB_BONES:CB_BONES + 128]
        selF = self.cst[0:4, CST_SELF:CST_SELF + 256]
        selHP = self.cst[0:4, CST_SELHP:CST_SELHP + 4]
        mincl = self.cst[:, CST_MINCL:CST_MINCL + 64]
        mstr = self.cst[:, CST_MSTR:CST_MSTR + 64]
        mstrT = self.cst[:, CST_MSTRT:CST_MSTRT + 64]
        ID2 = self.cst[:, CST_ID2:CST_ID2 + 64]

        if is_s:
            for tl in range(2):
                k.ms(Qp[:, tl, :], 0.0)
                k.ms(Kp[:, tl, :], 0.0)
                k.ms(VT[:, tl, :], 0.0)
            k.ms(GD[0:4, :], 0.0)
            k.ms(BET[0:4, :], 0.0)

        names = ["dq0", "dq1", "dk0", "dk1", "dv0", "dv1"]
        for idx, wn_ in enumerate(names):
            p = self.proj(W[wn_], t0, N)
            tl = idx % 2
            cw = lambda i: self.pv(l, PV_CONV + idx * 4 + i)
            if not is_s:
                if blk == 0:
                    k.ms(ubuf[:, 0:3], 0.0)
                else:
                    k.cp(ubuf[:, 0:3], halo[:, idx, :], eng=GP)
                k.cp(ubuf[:, 3:3 + N], p[:, 0:N], eng="act")
                k.ts(acc[:, 0:N], ubuf[:, 0:N], cw(0), ALU.mult)
                for i in range(1, 4):
                    k.stt(acc[:, 0:N], ubuf[:, i:i + N], cw(i), acc[:, 0:N], ALU.mult, ALU.add)
                k.cp(halo[:, idx, :], ubuf[:, N:N + 3], eng=GP)
                if blk == NPB - 1:
                    k.dma("sp", self.o_pconv[l][:, idx, :], halo[:, idx, :], is_out=True)
            else:
                ubs = ubuf[:, 0:NCH * 7].rearrange("p (s r) -> p s r", r=7)
                k.dma("sp", ubs[:, :, 0:3], self.d_stconv[l][:, idx, s0:s0 + NCH, :])
                k.cp(ubs[:, :, 3:7], p[:, 0:N].rearrange("p (s j) -> p s j", j=TS), eng="act")
                a3 = acc[:, 0:N].rearrange("p (s j) -> p s j", j=TS)
                k.ts(a3, ubs[:, :, 0:4], cw(0), ALU.mult)
                for i in range(1, 4):
                    k.stt(a3, ubs[:, :, i:i + 4], cw(i), a3, ALU.mult, ALU.add)
                k.dma("sp", self.o_sconv[l][:, idx, s0:s0 + NCH, :], ubs[:, :, 4:7], is_out=True)
            accv = acc[:, 0:N]
            if idx < 2:
                k.act(UQ[:, tl, 0:N], accv, AF.Silu)
            elif idx < 4:
                k.act(UK[:, tl, 0:N], accv, AF.Silu)
            else:
                k.act(T1[:, 0:N], accv, AF.Silu)
                k.cp(pw(VT[:, tl, :]), cv(T1[:, 0:MB]), eng=GP)
        if CUTG == 1:
            return
        for (X, dst, scale) in ((UQ, Qp, 0.125), (UK, Kp, 1.0)):
            pS = self.ps()
            for tl in range(2):
                k.act(SQ[:, tl * MB:tl * MB + N], X[:, tl, 0:N], AF.Square)
                k.mm(pS[:, tl * MB:tl * MB + N], bones, SQ[:, tl * MB:tl * MB + N])
                k.act(T1[:, tl * MB:tl * MB + N], pS[:, tl * MB:tl * MB + N], AF.Sqrt, bias=self.c_eps, scale=1.0)
                k.recip(T1[:, tl * MB:tl * MB + N], T1[:, tl * MB:tl * MB + N])
                k.stt(pw(dst[:, tl, :]), cv(X[:, tl, :]), scale, cv(T1[:, tl * MB:(tl + 1) * MB]), ALU.mult, ALU.mult)
        if CUTG == 2:
            return
        for tl in range(2):
            p = self.proj(W["dg" + str(tl)], t0, N)
            k.act(GATE[:, tl, 0:N], p[:, 0:N], AF.Silu)
        if CUTG == 3:
            return
        p = self.proj(W["db"], t0, N, M=4)
        k.act(T2[0:4, 0:N], p[0:4, 0:N], AF.Exp, scale=-1.0)
        k.ts(T2[0:4, 0:N], T2[0:4, 0:N], 1.0, ALU.add)
        k.recip(T2[0:4, 0:N], T2[0:4, 0:N])
        k.cp(pw(BET[0:4, 0:MB]), cv(T2[0:4, 0:MB]))
        p = self.proj(W["da"], t0, N, M=4)
        k.act(T2[0:4, 0:N], p[0:4, 0:N], AF.Exp, bias=self.pv(l, PV_DTB)[0:4, :], scale=1.0)
        k.act(T2[0:4, 0:N], T2[0:4, 0:N], AF.Ln, bias=self.c_one[0:4, :])
        k.ts(pw(GD[0:4, 0:MB]), cv(T2[0:4, 0:MB]), sm[0:4, 11:12], ALU.mult)
        k.scan(GC[0:4, 0:MB], self.cst[0:4, CST_RESET:CST_RESET + MB], GD[0:4, 0:MB], 0.0, ALU.mult, ALU.add)
        if CUTG == 4:
            return
        pX = self.ps()
        for tl in range(2):
            k.mm(pX[:, tl * MB:(tl + 1) * MB], selF[0:4, tl * 128:(tl + 1) * 128], GC[0:4, 0:MB])
        k.act(T1[:, 0:2 * MB], pX[:, 0:2 * MB], AF.Exp)
        for tl in range(2):
            k.tt(QGx[:, tl, :], Qp[:, tl, :], T1[:, tl * MB:(tl + 1) * MB], ALU.mult, eng=GP)
            k.cp(eGL[:, tl, :], T1[:, tl * MB:(tl + 1) * MB].rearrange("p (c t) -> p c t", t=64)[:, :, 63], eng=GP)
        pC = [self.ps(), self.ps()]
        for p_ in range(2):
            for c in range(NCH):
                k.mm(pC[p_][P2[p_], c * 2:(c + 1) * 2], GC[0:4, c * 64:(c + 1) * 64], selHP[0:4, p_ * 2:p_ * 2 + 2])
                k.mm(pC[p_][P2[p_], 64 + c * 2:64 + (c + 1) * 2], BET[0:4, c * 64:(c + 1) * 64], selHP[0:4, p_ * 2:p_ * 2 + 2])
            k.cp(GCOL[P2[p_], 0:NCH * 2], pC[p_][P2[p_], 0:NCH * 2])
            k.cp(BCOL[P2[p_], 0:NCH * 2], pC[p_][P2[p_], 64:64 + NCH * 2])
        k.act(EGC[:, 0:NCH * 2], GCOL[:, 0:NCH * 2], AF.Exp)
        k.tt(BEG[:, 0:NCH * 2], BCOL[:, 0:NCH * 2], EGC[:, 0:NCH * 2], ALU.mult)
        pL = self.ps()
        glast = GC[0:4, 0:MB].rearrange("p (c t) -> p c t", t=64)[:, :, 63]
        for tl in range(2):
            k.mm(pL[:, tl * NCH:(tl + 1) * NCH], selF[0:4, tl * 128:(tl + 1) * 128], glast)
        k.tt(c3v(KHS), pL[:, 0:NCH * 2].rearrange("p (l c) -> p c l", l=2), c3v(GCOL), ALU.subtract)
        k.act(KHS[:, 0:NCH * 2], KHS[:, 0:NCH * 2], AF.Exp)
        if CUTG == 5:
            return
        for (src, dstm) in ((Kp, KTM), (VT, VTM)):
            pT = [self.ps(), self.ps()]
            for p_ in range(2):
                for c in range(NCH):
                    for tl in range(2):
                        k.mm(pT[p_][P2[p_], (c * 2 + tl) * 64:(c * 2 + tl + 1) * 64], src[P2[p_], tl, c * 64:(c + 1) * 64],
                             identF[P2[p_], 64 * p_:64 * p_ + 64])
                k.cp(dstm[P2[p_], 0:NCH * 128], pT[p_][P2[p_], 0:NCH * 128], eng=("act" if p_ else "dve"))
        KTM4 = KTM[:, 0:NCH * 128].rearrange("p (c l d) -> p c l d", c=NCH, l=2)
        VTM4 = VTM[:, 0:NCH * 128].rearrange("p (c l d) -> p c l d", c=NCH, l=2)
        GCOL3 = c3v(GCOL)
        BCOL3 = c3v(BCOL)
        BEG3 = c3v(BEG)
        KHS3 = c3v(KHS)
        v4 = lambda ap: ap[:, 0:256].rearrange("p (c l t) -> p c l t", c=2, l=2)
        v3 = lambda ap: ap[:, 0:256].rearrange("p (a t) -> p a t", t=64)
        bc4 = lambda ap3: _bc(ap3.unsqueeze(3), [128, 2, 2, 64])
        bm = lambda mk: _bc(mk.unsqueeze(1), [128, 4, 64])

        def evac2(dst, pp, eng0="dve", eng1="act"):
            k.cp(dst[P2[0], 0:256], pp[0][P2[0], 0:256], eng=eng0)
            k.cp(dst[P2[1], 0:256], pp[1][P2[1], 0:256], eng=eng1)

        if CUTG == 6:
            return
        def solve(sbi):
            c0 = sbi * 2
            Dm, X1, X2, LT, Nk, Ak, P, U, Wm = [A(i + 11 * sbi) for i in range(9)]
            pGr = self.ps()
            pBr = self.ps()
            for tl in range(2):
                k.mm(pGr[:, tl * 128:(tl + 1) * 128], selF[0:4, tl * 128:(tl + 1) * 128], GC[0:4, c0 * 64:c0 * 64 + 128])
                k.mm(pBr[:, tl * 128:(tl + 1) * 128], selF[0:4, tl * 128:(tl + 1) * 128], BET[0:4, c0 * 64:c0 * 64 + 128])
            gr4 = pGr[:, 0:256].rearrange("p (l c t) -> p c l t", l=2, c=2)
            br4 = pBr[:, 0:256].rearrange("p (l c t) -> p c l t", l=2, c=2)
            k.tt(v4(Dm), gr4, bc4(GCOL3[:, c0:c0 + 2, :]), ALU.subtract)
            k.ts(X1[:, 0:256], Dm[:, 0:256], 0.0, ALU.min)
            k.act(X1[:, 0:256], X1[:, 0:256], AF.Exp)
            k.ts(X2[:, 0:256], Dm[:, 0:256], -1.0, ALU.mult, 0.0, ALU.min)
            k.act(X2[:, 0:256], X2[:, 0:256], AF.Exp)
            k.tt(v3(LT), v3(X1), bm(mincl), ALU.mult, eng=GP)
            k.tt(v3(X1), v3(X1), bm(mstr), ALU.mult, eng=GP)
            k.tt(v4(X1), v4(X1), br4, ALU.mult)
            k.tt(v3(X2), v3(X2), bm(mstrT), ALU.mult, eng=GP)
            k.tt(v4(X2), v4(X2), bc4(BCOL3[:, c0:c0 + 2, :]), ALU.mult, eng=GP)
            yield
            pKK = [self.ps(), self.ps()]
            pQK = [self.ps(), self.ps()]
            for cc in range(2):
                c = c0 + cc
                for tl in range(2):
                    for p_ in range(2):
                        sl = slice((cc * 2 + tl) * 64, (cc * 2 + tl + 1) * 64)
                        kk = Kp[P2[p_], tl, c * 64:(c + 1) * 64]
                        qq = Qp[P2[p_], tl, c * 64:(c + 1) * 64]
                        k.mm(pKK[p_][P2[p_], sl], kk, kk)
                        k.mm(pQK[p_][P2[p_], sl], kk, qq)
            for p_ in range(2):
                k.tt(Nk[P2[p_], 0:256], pKK[p_][P2[p_], 0:256], X1[P2[p_], 0:256], ALU.mult)
                k.tt(Ak[P2[p_], 0:256], pKK[p_][P2[p_], 0:256], X2[P2[p_], 0:256], ALU.mult)
                k.tt(LT[P2[p_], 0:256], pQK[p_][P2[p_], 0:256], LT[P2[p_], 0:256], ALU.mult)
            k.stt(v3(P), v3(Nk), -1.0, bm(ID2), ALU.mult, ALU.add)
            yield
            for lev in range(5):
                pA_ = [self.ps(), self.ps()]
                if lev < 4:
                    pN_ = [self.ps(), self.ps()]
                for j in range(4):
                    for p_ in range(2):
                        sl = slice(j * 64, (j + 1) * 64)
                        if lev < 4:
                            k.mm(pN_[p_][P2[p_], sl], Ak[P2[p_], sl], Nk[P2[p_], sl])
                        k.mm(pA_[p_][P2[p_], sl], Nk[P2[p_], sl], Ak[P2[p_], sl])
                if lev < 4:
                    evac2(Nk, pN_, "act", "act")
                evac2(Ak, pA_, "dve", "dve")
                yield
                pP = [self.ps(), self.ps()]
                for j in range(4):
                    for p_ in range(2):
                        sl = slice(j * 64, (j + 1) * 64)
                        k.mm(pP[p_][P2[p_], sl], Ak[P2[p_], sl], P[P2[p_], sl])
                for p_ in range(2):
                    k.tt(P[P2[p_], 0:256], P[P2[p_], 0:256], pP[p_][P2[p_], 0:256], ALU.add)
                yield
            k.tt(v4(X1), VTM4[:, c0:c0 + 2, :, :], bc4(BCOL3[:, c0:c0 + 2, :]), ALU.mult, eng=GP)
            k.tt(v4(X2), KTM4[:, c0:c0 + 2, :, :], bc4(BEG3[:, c0:c0 + 2, :]), ALU.mult, eng=GP)
            pu = [self.ps(), self.ps()]
            pw_ = [self.ps(), self.ps()]
            for j in range(4):
                for p_ in range(2):
                    sl = slice(j * 64, (j + 1) * 64)
                    k.mm(pu[p_][P2[p_], sl], P[P2[p_], sl], X1[P2[p_], sl])
                    k.mm(pw_[p_][P2[p_], sl], P[P2[p_], sl], X2[P2[p_], sl])
            evac2(U, pu, "act", "act")
            evac2(Wm, pw_, "dve", "dve")
            yield
            k.tt(v4(Dm), KTM4[:, c0:c0 + 2, :, :], bc4(KHS3[:, c0:c0 + 2, :]), ALU.mult, eng=GP)
            pM = [self.ps(), self.ps()]
            pB = [self.ps(), self.ps()]
            Mraw, Braw = A(9 + 11 * sbi), A(10 + 11 * sbi)
            for j in range(4):
                for p_ in range(2):
                    sl = slice(j * 64, (j + 1) * 64)
                    k.mm(pM[p_][P2[p_], sl], Wm[P2[p_], sl], Dm[P2[p_], sl])
                    k.mm(pB[p_][P2[p_], sl], Dm[P2[p_], sl], U[P2[p_], sl])
            evac2(Mraw, pM, "act", "act")
            evac2(Braw, pB, "dve", "dve")
            yield
            for tl in range(2):
                for cc in range(2):
                    sl = slice((cc * 2 + tl) * 64, (cc * 2 + tl + 1) * 64)
                    k.stt(MT[:, tl, c0 + cc, :], ID2, eGL[:, tl, c0 + cc:c0 + cc + 1], Mraw[:, sl], ALU.mult, ALU.subtract)
                k.cp(BC[:, tl, c0:c0 + 2, :], v4(Braw)[:, :, tl, :], eng=GP)
            pQ = [self.ps(), self.ps()]
            pOL = [self.ps(), self.ps()]
            for j in range(4):
                for p_ in range(2):
                    sl = slice(j * 64, (j + 1) * 64)
                    k.mm(pQ[p_][P2[p_], sl], Wm[P2[p_], sl], LT[P2[p_], sl])
                    k.mm(pOL[p_][P2[p_], sl], U[P2[p_], sl], LT[P2[p_], sl])
            evac2(Mraw, pQ, "act", "act")
            evac2(Braw, pOL, "dve", "dve")
            yield
            for tl in range(2):
                qv = QpT[:, tl, c0 * 64:c0 * 64 + 128].rearrange("p (c t) -> p c t", t=64)
                gv = QGx[:, tl, c0 * 64:c0 * 64 + 128].rearrange("p (c t) -> p c t", t=64)
                k.tt(qv, gv, v4(Mraw)[:, :, tl, :], ALU.subtract, eng=GP)
                k.cp(OL[:, tl, c0 * 64:c0 * 64 + 128].rearrange("p (c t) -> p c t", t=64), v4(Braw)[:, :, tl, :], eng=GP)
        gens = [solve(sbi) for sbi in range(NCH // 2)]
        while gens:
            for g_ in list(gens):
                try:
                    next(g_)
                except StopIteration:
                    gens.remove(g_)
        if CUTG == 7:
            return
        if not is_s:
            if blk == 0:
                k.ms(SALL[:, :, 0, :], 0.0)
            else:
                k.cp(SALL[:, :, 0, :], SALL[:, :, NCH, :])
        else:
            k.dma("sp", SIN, self.d_st["gdn"][l][:, :, s0:s0 + NCH, :])
        SB = (lambda tl, c: SIN[:, tl, c, :]) if is_s else (lambda tl, c: SALL[:, tl, c, :])
        for c in range(NCH):
            pS = [self.ps(), self.ps()]
            for p_ in range(2):
                for tl in range(2):
                    k.mm(pS[p_][P2[p_], tl * 64:(tl + 1) * 64], MT[P2[p_], tl, c, :], SB(tl, c)[P2[p_], :])
                k.tt(SALL[P2[p_], :, c + 1, :], pS[p_][P2[p_], 0:128].rearrange("p (a v) -> p a v", v=64), BC[P2[p_], :, c, :], ALU.add)
        if is_s:
            k.dma("sp", self.o_ss["gdn"][l][:, :, s0:s0 + NCH, :], SALL[:, :, 1:NCH + 1, :], is_out=True)
        elif blk == NPB - 1:
            k.dma("sp", self.o_ps["gdn"][l], SALL[:, :, NCH, :], is_out=True)
        if CUTG == 8:
            return
        pO = [self.ps(), self.ps()]
        for p_ in range(2):
            for c in range(NCH):
                for tl in range(2):
                    k.mm(pO[p_][P2[p_], (tl * NCH + c) * 64:(tl * NCH + c + 1) * 64], SB(tl, c)[P2[p_], :], QpT[P2[p_], tl, c * 64:(c + 1) * 64])
            k.tt(T2[P2[p_], 0:2 * MB], pO[p_][P2[p_], 0:2 * MB], OL.rearrange("p a t -> p (a t)")[P2[p_]], ALU.add)
        if _os.environ.get('KDBG', '') == 'gdn' and blk == 0 and l == 0:
            for nm_, ap_ in (("GC", GC[0:4, 0:MB]), ("BET", BET[0:4, 0:MB]), ("GCOL", GCOL[:, 0:8]), ("BCOL", BCOL[:, 0:8]),
                             ("KHS", KHS[:, 0:8]), ("Qp", Qp), ("Kp", Kp), ("KTM", KTM[:, 0:512]), ("VTM", VTM[:, 0:512]),
                             ("MT", MT), ("BC", BC), ("QpT", QpT), ("OL", OL), ("SALL", SALL), ("T2o", T2[:, 0:512]), ("QGx", QGx)):
                k.dbg(nm_, ap_)
        self.o_post(l, m, T2, GATE, OB, OSQ, T1, T2, self.pv(l, PV_NORM + 2), Wo, is_s, t0, N, s0)

    def build(self, skip_mixers=(), stop_after=None, no_ffn=False, nblk=None):
        self.skip_mixers = set(skip_mixers)
        self.nblk = nblk
        self.alloc()
        self.setup()
        self.mod_group(0, 0)
        self.initial_h()
        for l in range(DEPTH):
            if no_ffn:
                self.mod_group(l, 1)
                self.ln_pass(l, 0)
            else:
                self.ffn(l, 0, mid_hook=lambda: self.mod_group_gen(l, 1))
            if stop_after == ("ffn1", l):
                break
            self.mix_layer(l, mid_hook=lambda: self.mod_group_gen(l, 2, True))
            if stop_after == ("mix", l):
                break
            hook = (lambda: self.mod_group_gen(l + 1, 0)) if l + 1 < DEPTH else None
            self.ffn(l, 1, mid_hook=hook)
        self.store_y()
        nw = self.k.S.emit()
        self.nwaits = nw
        return self.nc


def _const_pack():
    c = np.zeros((128, CST_N), np.float32)
    c[:, CST_IDENT:CST_IDENT + 128] = np.eye(128, dtype=np.float32)
    s = np.arange(64)[:, None]
    t = np.arange(64)[None, :]
    for hf in range(2):
        c[64 * hf:64 * hf + 64, CST_MINCL:CST_MINCL + 64] = (s <= t)
        c[64 * hf:64 * hf + 64, CST_MSTR:CST_MSTR + 64] = (s < t)
        c[64 * hf:64 * hf + 64, CST_MSTRT:CST_MSTRT + 64] = (t < s)
    c[:, CST_ID2:CST_ID2 + 64] = np.tile(np.eye(64, dtype=np.float32), (2, 1))
    r = np.ones((MB,), np.float32)
    r[0::64] = 0.0
    c[:, CST_RESET:CST_RESET + MB] = r[None, :]
    for h in range(4):
        c[h, CST_SEL + h * 64:CST_SEL + (h + 1) * 64] = 1.0
        tl, p = divmod(h, 2)
        c[h, CST_SELF + tl * 128 + p * 64:CST_SELF + tl * 128 + (p + 1) * 64] = 1.0
    c[:, CST_ONES:CST_ONES + 128] = 1.0
    bo = np.zeros((128, 128), np.float32)
    bo[0:64, 0:64] = 1.0
    bo[64:128, 64:128] = 1.0
    c[:, CST_BONES:CST_BONES + 128] = bo
    pm = np.zeros((128, 128), np.float32)
    for m_ in range(128):
        kk = (m_ // 64) * 64 + ((m_ % 64) + 32) % 64
        pm[kk, m_] = 1.0
    c[:, CST_PM:CST_PM + 128] = pm
    for p_ in range(2):
        for tl in range(2):
            c[2 * tl + p_, CST_SELHP + p_ * 2 + tl] = 1.0
    return c


def _rot_tables():
    half = 32
    inv = (np.float32(10000.0) ** (-np.arange(half, dtype=np.float32) / np.float32(half))).astype(np.float32)
    pos = np.concatenate([np.arange(TP, dtype=np.float32),
                          np.tile(np.float32(16384.0) + np.arange(TS, dtype=np.float32), NSQ)]).astype(np.float32)
    ang = (pos[:, None] * inv[None, :]).astype(np.float32)
    cos = np.cos(ang).astype(np.float32).T
    sin = np.sin(ang).astype(np.float32).T
    cosT = np.zeros((128, NTOK), np.float32)
    sinT = np.zeros((128, NTOK), np.float32)
    for p in range(128):
        d = p % 64
        i = d % 32
        cosT[p] = cos[i]
        sinT[p] = -sin[i] if d < 32 else sin[i]
    return cosT, sinT


def _gret_tables():
    heads = np.arange(4, dtype=np.float32)
    lg = np.log(np.float32(1.0) - np.float32(2.0) ** (np.float32(-5.0) - heads)).astype(np.float32)
    g = np.zeros((2, 128, 2, MB), np.float32)
    j = np.arange(MB) % 64
    for tl in range(2):
        for p in range(128):
            h = 2 * tl + p // 64
            g[0, p, tl, :] = (j + 1).astype(np.float32) * lg[h]
            g[1, p, tl, :] = (np.minimum(j, TS - 1) + 1).astype(np.float32) * lg[h]
    return g


def _tile_w(w, cols):
    out = np.zeros((128, KC, 128), np.float32)
    cols = np.asarray(cols)
    valid = cols >= 0
    sub = w[:, cols[valid]].reshape(KC, 128, -1)
    out[:, :, np.nonzero(valid)[0]] = np.transpose(sub, (1, 0, 2))
    return out.reshape(128, KC * 128)


def _win_tiles(w):
    tiles = []
    r = lambda a, n: list(range(a, a + n))
    pad = lambda lst: lst + [-1] * (128 - len(lst))
    for base in (C_RQ, C_RK, C_RV, C_RG):
        for tl in range(2):
            tiles.append(r(base + tl * 128, 128))
    for base in (C_AQ, C_AK):
        for tl in range(2):
            cols = []
            for p in range(2):
                h = 2 * tl + p
                cols += r(base + h * 32, 32) + [-1] * 32
            tiles.append(cols)
    for base in (C_AV, C_AG):
        for tl in range(2):
            tiles.append(r(base + tl * 128, 128))
    tiles.append(pad(r(C_ALR, 16)))
    for base in (C_HQ, C_HF, C_HI, C_HG):
        for tl in range(2):
            tiles.append(r(base + tl * 128, 128))
    for base in (C_DQ, C_DK, C_DV, C_DG):
        for tl in range(2):
            tiles.append(r(base + tl * 128, 128))
    tiles.append(pad(r(C_DB, 4)))
    tiles.append(pad(r(C_DA, 4)))
    assert len(tiles) == NWT
    return np.stack([_tile_w(w, c) for c in tiles], 0)


def _state_in(st, dk):
    out = np.zeros((DEPTH, 2, 64, 2, NSQ, 64), np.float32)
    x = st.reshape(DEPTH, NSQ, 2, 2, dk, 64)
    out[:, :, 0:dk] = np.transpose(x, (0, 3, 4, 2, 1, 5))
    return out.reshape(DEPTH, 128, 2, NSQ, 64)


def _state_out_s(o, dk):
    x = o.reshape(DEPTH, 2, 64, 2, NSQ, 64)[:, :, 0:dk]
    return np.ascontiguousarray(np.transpose(x, (0, 4, 3, 1, 2, 5)).reshape(DEPTH, NSQ, 4, dk, 64))


def _state_out_p(o, dk):
    x = o.reshape(DEPTH, 2, 64, 2, 64)[:, :, 0:dk]
    return np.ascontiguousarray(np.transpose(x, (0, 3, 1, 2, 4)).reshape(DEPTH, 4, dk, 64))


_NC_CACHE = {}


def _get_nc(key=(), **kw):
    if key not in _NC_CACHE:
        p = Prog4()
        nc = p.build(**kw)
        _NC_CACHE[key] = (nc, p)
    return _NC_CACHE[key]


def _prepare_inputs(x_prompt, x_sample, state_ret, state_gla, state_hgrn, state_gdn, state_gdn_conv,
                    c_prompt, c_sample, ada_w, ada_b, ln_g, ln_b, ffn1_wi, ffn1_wo, ffn2_wi, ffn2_wo,
                    w_in, gla_wg, gla_bg, hg_lb, gdn_conv, gdn_a_log, gdn_dt_bias,
                    gla_norm, hg_norm, gdn_norm, w_out):
    f = lambda a: np.ascontiguousarray(np.asarray(a, dtype=np.float32))
    shared = {}
    for nm, wi in (("wi1", ffn1_wi), ("wi2", ffn2_wi)):
        wi = f(wi)
        shared[nm] = np.stack([np.stack([_tile_w(wi[l], list(range(c * 128, (c + 1) * 128))) for c in range(2 * NJ)], 0)
                               for l in range(DEPTH)], 0)
    shared["wo1"] = f(ffn1_wo)
    shared["wo2"] = f(ffn2_wo)
    w_in = f(w_in)
    shared["win"] = np.stack([_win_tiles(w_in[l]) for l in range(DEPTH)], 0)
    shared["wout"] = f(w_out).reshape(DEPTH, 8, 128, 1024)
    ada_w = f(ada_w)
    shared["adaw"] = np.stack([np.stack([_tile_w(ada_w[l], list(range(c * 128, (c + 1) * 128))) for c in range(72)], 0)
                               for l in range(DEPTH)], 0)
    pv = np.zeros((DEPTH, 128, NPV), np.float32)
    ada_b = f(ada_b); ln_g = f(ln_g); ln_b = f(ln_b); gdn_conv = f(gdn_conv); gla_bg = f(gla_bg); hg_lb = f(hg_lb)
    gla_norm = f(gla_norm); hg_norm = f(hg_norm); gdn_norm = f(gdn_norm); gdn_a_log = f(gdn_a_log); gdn_dt_bias = f(gdn_dt_bias)
    for l in range(DEPTH):
        pv[l, :, PV_ADAB:PV_ADAB + 72] = ada_b[l].reshape(72, 128).T
        for i in range(3):
            pv[l, :, PV_LNG + i * 8:PV_LNG + (i + 1) * 8] = ln_g[l, i].reshape(8, 128).T
            pv[l, :, PV_LNB + i * 8:PV_LNB + (i + 1) * 8] = ln_b[l, i].reshape(8, 128).T
        cw = gdn_conv[l].reshape(4, 6, 128)
        pv[l, :, PV_CONV:PV_CONV + 24] = np.transpose(cw, (2, 1, 0)).reshape(128, 24)
        bg = np.zeros((2, 2, 64), np.float32)
        bg[:, :, 0:32] = gla_bg[l].reshape(2, 2, 32)
        pv[l, :, PV_BG:PV_BG + 2] = bg.reshape(2, 128).T
        pv[l, :, PV_NORM + 0] = np.tile(gla_norm[l], 2)
        pv[l, :, PV_NORM + 1] = np.tile(hg_norm[l], 2)
        pv[l, :, PV_NORM + 2] = np.tile(gdn_norm[l], 2)
        pv[l, :, PV_LB0:PV_LB0 + 2] = hg_lb[0].reshape(2, 128).T
        pv[l, :, PV_LB1:PV_LB1 + 2] = hg_lb[1].reshape(2, 128).T
        pv[l, 0:4, PV_ALOG] = gdn_a_log[l]
        pv[l, 0:4, PV_DTB] = gdn_dt_bias[l]
    shared["pvec"] = pv
    gla_wg = f(gla_wg)
    wgp = np.zeros((DEPTH, 16, 2, 2, 64), np.float32)
    wgp[:, :, :, :, 0:32] = gla_wg.reshape(DEPTH, 16, 2, 2, 32)
    shared["wgpad"] = wgp.reshape(DEPTH, 16, 256)
    cosT, sinT = _rot_tables()
    shared["cosT"] = cosT
    shared["sinT"] = sinT
    shared["cst"] = _const_pack()
    shared["gret"] = _gret_tables()
    x_prompt = f(x_prompt); x_sample = f(x_sample); c_prompt = f(c_prompt); c_sample = f(c_sample)
    sts = {"ret": (f(state_ret), 64), "gla": (f(state_gla), 32), "hg": (f(state_hgrn), 64), "gdn": (f(state_gdn), 64)}
    state_gdn_conv = f(state_gdn_conv)
    in_maps = []
    for c in range(NCORES):
        d = dict(shared)
        sq = slice(c * NSQ, (c + 1) * NSQ)
        xs = x_sample[sq].reshape(NSQ * TS, D)
        d["xT"] = np.ascontiguousarray(np.concatenate([x_prompt[c], xs], 0).T)
        d["cT"] = np.ascontiguousarray(np.concatenate([c_prompt[c:c + 1], c_sample[sq]], 0).T)
        for nm, (st, dk) in sts.items():
            d["st_" + nm] = _state_in(st[:, sq], dk)
        cvs = state_gdn_conv[:, sq].reshape(DEPTH, NSQ, 3, 6, 128)
        d["st_conv"] = np.ascontiguousarray(np.transpose(cvs, (0, 4, 3, 1, 2)))
        in_maps.append(d)
    return in_maps


def _assemble(results):
    y_p = np.zeros((NCORES, TP, D), np.float32)
    y_s = np.zeros((NCORES * NSQ, TS, D), np.float32)
    dks = {"ret": 64, "gla": 32, "hg": 64, "gdn": 64}
    p_st = {nm: np.zeros((DEPTH, NCORES, 4, dk, 64), np.float32) for nm, dk in dks.items()}
    s_st = {nm: np.zeros((DEPTH, NCORES * NSQ, 4, dk, 64), np.float32) for nm, dk in dks.items()}
    p_conv = np.zeros((DEPTH, NCORES, 3, 768), np.float32)
    s_conv = np.zeros((DEPTH, NCORES * NSQ, 3, 768), np.float32)
    for c, r in enumerate(results):
        yT = np.asarray(r["yT"])
        y_p[c] = yT[:, 0:TP].T
        y_s[c * NSQ:(c + 1) * NSQ] = yT[:, TP:].T.reshape(NSQ, TS, D)
        for nm, dk in dks.items():
            p_st[nm][:, c] = _state_out_p(np.asarray(r["ops_" + nm]), dk)
            s_st[nm][:, c * NSQ:(c + 1) * NSQ] = _state_out_s(np.asarray(r["oss_" + nm]), dk)
        pc = np.asarray(r["opconv"])
        p_conv[:, c] = np.transpose(pc, (0, 3, 2, 1)).reshape(DEPTH, 3, 768)
        sc = np.asarray(r["osconv"])
        s_conv[:, c * NSQ:(c + 1) * NSQ] = np.transpose(sc, (0, 3, 4, 2, 1)).reshape(DEPTH, NSQ, 3, 768)
    return (y_p, y_s, p_st["ret"], p_st["gla"], p_st["hg"], p_st["gdn"], p_conv,
            s_st["ret"], s_st["gla"], s_st["hg"], s_st["gdn"], s_conv)


def kernel(**inputs):
    in_maps = _prepare_inputs(**inputs)
    nc, _ = _get_nc()
    res = run_bass_kernel_spmd(nc, in_maps, core_ids=list(range(NCORES)))
    return _assemble(res.results)
```

```python
import numpy as np
from contextlib import ExitStack
import concourse.bass as bass
import concourse.mybir as mybir
from concourse.bass_utils import run_bass_kernel_spmd

F32 = mybir.dt.float32
BF16 = mybir.dt.bfloat16
AF = mybir.ActivationFunctionType
ALU = mybir.AluOpType

NCORES = 8
D = 1024
KC = 8
DFF = 2816
NJ = 22
TP = 2048
NSQ = 16
TS = 4
NTOK = TP + NSQ * TS
DEPTH = 2
ALPHA = float((2.0 * DEPTH) ** 0.25)
LN_EPS = 1e-5
RMS_EPS = 1e-6
TBS = [(0, 512), (512, 512), (1024, 512), (1536, 512), (2048, 64)]
JG = [list(range(0, 8)), list(range(8, 15)), list(range(15, 22))]
MB = 256
NCH = MB // 64
NPB = TP // MB
NSB = NSQ // NCH
NW = 12
NDS = 8
import os as _os
CUT = int(_os.environ.get('KCUT', '0'))
CUTN = int(_os.environ.get('KCUTN', '-1'))
STRICT = int(_os.environ.get('KSTRICT', '1'))
CUTG = int(_os.environ.get('KCUTG', '0'))
ATTACH = int(_os.environ.get('KATTACH', '1'))
GP = _os.environ.get('KGP', 'pool')

C_RQ, C_RK, C_RV, C_RG = 0, 256, 512, 768
C_AQ, C_AK, C_AV, C_ALR, C_AG = 1024, 1152, 1280, 1536, 1552
C_HQ, C_HF, C_HI, C_HG = 1808, 2064, 2320, 2576
C_DQ, C_DK, C_DV, C_DB, C_DA, C_DG = 2832, 3088, 3344, 3600, 3604, 3608
WT = {}
_names = (["rq0", "rq1", "rk0", "rk1", "rv0", "rv1", "rg0", "rg1"] +
          ["aq0", "aq1", "ak0", "ak1", "av0", "av1", "ag0", "ag1", "alr"] +
          ["hq0", "hq1", "hf0", "hf1", "hi0", "hi1", "hg0", "hg1"] +
          ["dq0", "dq1", "dk0", "dk1", "dv0", "dv1", "dg0", "dg1", "db", "da"])
for _i, _n in enumerate(_names):
    WT[_n] = _i
NWT = len(_names)
PV_ADAB, PV_LNG, PV_LNB, PV_CONV, PV_BG, PV_NORM, PV_LB0, PV_LB1, PV_ALOG, PV_DTB, NPV = 0, 72, 96, 120, 144, 146, 149, 151, 153, 154, 160


def _esize(dt):
    return mybir.dt.size(dt)


class _Op:
    __slots__ = ("eng", "fn", "deps", "dmaq", "signal", "sigval", "dslot", "dval", "clock", "prio")

    def __init__(self, eng, fn, deps, dmaq):
        self.eng = eng
        self.fn = fn
        self.deps = deps
        self.dmaq = dmaq
        self.signal = False
        self.sigval = 0
        self.dslot = 0
        self.dval = 0
        self.clock = None
        self.prio = ()


class Sched:
    BUCK = 1024

    def __init__(self, nc, es):
        self.nc = nc
        self.engs = {"pe": nc.tensor, "act": nc.scalar, "dve": nc.vector, "pool": nc.gpsimd, "sp": nc.sync}
        self.ops = []
        self.buckets = {}
        self.mloc = {}
        self.sem = {e: es.enter_context(nc.semaphore("s_" + e)) for e in self.engs}
        self.dsem = {q: [es.enter_context(nc.semaphore("d_%s%d" % (q, i))) for i in range(NDS)]
                     for q in ("sp", "act", "pool")}
        self.out_dmas = []

    def _box(self, ap):
        sp = str(ap.space)
        if sp not in ("SB", "PSUM"):
            return None
        t = ap.tensor
        key = t.name
        info = self.mloc.get(key)
        if info is None:
            ml = self.nc.lookup_mloc(t)
            base = int(ml.addr)
            if sp == "PSUM":
                base += int(ml.bank) * 2048
            info = base
            self.mloc[key] = info
        shape = t.shape
        F = 1
        for s in shape[1:]:
            F *= int(s)
        off = int(ap.offset)
        p0 = off // F
        f0 = off % F
        dims = ap.ap
        pc = int(dims[0][1])
        lo = f0
        hi = f0
        for (st, cnt) in dims[1:]:
            ext = int(st) * (int(cnt) - 1)
            if ext < 0:
                lo += ext
            else:
                hi += ext
        hi += 1
        es_ = _esize(ap.dtype)
        if sp == "PSUM" and STRICT:
            b0 = ((info + lo * es_) // 2048) * 2048
            b1 = ((info + hi * es_ - 1) // 2048 + 1) * 2048
            return (sp, (p0 // 32) * 32, ((p0 + pc + 31) // 32) * 32, b0, b1)
        return (sp, p0, p0 + pc, info + lo * es_, info + hi * es_)

    @staticmethod
    def _ov(a, b):
        return a[1] < b[2] and b[1] < a[2] and a[3] < b[4] and b[3] < a[4]

    @staticmethod
    def _cov(a, b):
        return a[1] <= b[1] and a[2] >= b[2] and a[3] <= b[3] and a[4] >= b[4]

    def _keys(self, box):
        return [(box[0], k) for k in range(box[3] // self.BUCK, (box[4] - 1) // self.BUCK + 1)]

    def add(self, eng, fn, reads, writes, dmaq=None, prio=()):
        idx = len(self.ops)
        raw = set()
        praw = set()
        for ap in prio:
            b = self._box(ap)
            if b is not None:
                for k in self._keys(b):
                    for rec in self.buckets.get(k, ()):
                        if rec[2] and self._ov(b, rec[0]):
                            praw.add(rec[1])
        oth = set()
        rboxes = []
        wboxes = []
        for ap in reads:
            b = self._box(ap)
            if b is not None:
                rboxes.append(b)
        for ap in writes:
            b = self._box(ap)
            if b is not None:
                wboxes.append(b)
        for b in rboxes:
            for k in self._keys(b):
                for rec in self.buckets.get(k, ()):
                    if rec[2] and self._ov(b, rec[0]):
                        raw.add(rec[1])
        for b in wboxes:
            for k in self._keys(b):
                lst = self.buckets.get(k)
                if not lst:
                    continue
                keep = []
                for rec in lst:
                    if self._ov(b, rec[0]):
                        oth.add(rec[1])
                        if self._cov(b, rec[0]):
                            continue
                    keep.append(rec)
                self.buckets[k] = keep
        myeng = eng
        for b in rboxes:
            rec = (b, idx, False, myeng, dmaq is not None)
            for k in self._keys(b):
                lst = self.buckets.setdefault(k, [])
                for i2, r2 in enumerate(lst):
                    if (not r2[2]) and r2[0] == b and r2[3] == myeng and (not r2[4]) and dmaq is None:
                        lst[i2] = rec
                        break
                else:
                    lst.append(rec)
        for b in wboxes:
            rec = (b, idx, True, myeng, dmaq is not None)
            for k in self._keys(b):
                self.buckets.setdefault(k, []).append(rec)
        deps = []
        for d in raw | oth:
            dop = self.ops[d]
            if dop.dmaq is None and dmaq is None and dop.eng == eng:
                if eng == "pe":
                    continue
                if d not in raw and not STRICT:
                    continue
            deps.append(d)
            if dop.dmaq is None:
                dop.signal = True
        op = _Op(eng, fn, deps, dmaq)
        op.prio = praw
        self.ops.append(op)
        return idx

    def emit(self):
        nc = self.nc
        cnt = {e: 0 for e in self.engs}
        for op in self.ops:
            if op.dmaq is None and op.signal:
                cnt[op.eng] += 1
                op.sigval = cnt[op.eng]
        seen = {e: {} for e in self.engs}
        self.opidx = {id(o): i for i, o in enumerate(self.ops)}
        dcount = {q: 0 for q in self.dsem}
        dlast = {q: [None] * NDS for q in self.dsem}
        nwaits = 0
        for op in self.ops:
            e = op.eng
            eng = self.engs[e]
            sn = seen[e]
            needs = []
            for d in op.deps:
                dop = self.ops[d]
                if dop.dmaq is None:
                    needs.append((("c", dop.eng), dop.sigval, dop))
                else:
                    needs.append((("d", dop.dmaq, dop.dslot), dop.dval, dop))
            if op.dmaq is not None:
                q = op.dmaq
                slot = dcount[q] % NDS
                dcount[q] += 1
                prev = dlast[q][slot]
                op.dslot = slot
                op.dval = (prev.dval if prev is not None else 0) + 16
                if prev is not None:
                    needs.append((("d", q, slot), prev.dval, prev))
                dlast[q][slot] = op
            needs.sort(key=lambda x: -self.opidx[id(x[2])])
            pending = []
            for (key, val, dop) in needs:
                if sn.get(key, 0) >= val:
                    continue
                semh = self.sem[key[1]] if key[0] == "c" else self.dsem[key[1]][key[2]]
                pending.append((semh, val, self.opidx[id(dop)] in op.prio))
                nwaits += 1
                for k2, v2 in dop.clock.items():
                    if sn.get(k2, 0) < v2:
                        sn[k2] = v2
            attach = None
            if pending and ATTACH:
                pi = [i for i, x in enumerate(pending) if x[2]]
                attach = pending.pop(pi[-1] if pi else -1)
            for (semh, val, _) in pending:
                eng.wait_ge(semh, val)
            ins = op.fn(eng)
            if attach is not None:
                ins._wait_ge(attach[0], attach[1])
            clock = dict(sn)
            if op.dmaq is not None:
                ins.then_inc(self.dsem[op.dmaq][op.dslot], 16)
                clock[("d", op.dmaq, op.dslot)] = op.dval
            elif op.signal:
                ins.then_inc(self.sem[e], 1)
                clock[("c", e)] = op.sigval
            op.clock = clock
            op.fn = None
        sp = self.engs["sp"]
        sn = seen["sp"]
        for d in self.out_dmas:
            dop = self.ops[d]
            key = ("d", dop.dmaq, dop.dslot)
            if sn.get(key, 0) >= dop.dval:
                continue
            sp.wait_ge(self.dsem[dop.dmaq][dop.dslot], dop.dval)
            sn[key] = dop.dval
        return nwaits


class KB:
    def __init__(self, nc, es):
        self.nc = nc
        self.es = es
        self.S = Sched(nc, es)
        self.dbg_outs = []

    def sb(self, name, shape, dt):
        return self.es.enter_context(self.nc.sbuf_tensor("sb_" + name, shape, dt))

    def psum(self, name, shape, dt):
        return self.es.enter_context(self.nc.psum_tensor(name, shape, dt))

    def dram_in(self, name, shape):
        return self.nc.dram_tensor(name, list(shape), F32, kind="ExternalInput").ap()

    def dram_out(self, name, shape):
        return self.nc.dram_tensor(name, list(shape), F32, kind="ExternalOutput").ap()

    def mm(self, out, lhsT, rhs, start=True, stop=True):
        self.S.add("pe", lambda e: e.matmul(out, lhsT=lhsT, rhs=rhs, start=start, stop=stop), [lhsT, rhs], [out], prio=[lhsT])

    def tr(self, out, in_, ident):
        self.S.add("pe", lambda e: e.transpose(out, in_, ident), [in_, ident], [out], prio=[in_])

    def act(self, out, in_, func, bias=None, scale=None, eng="act"):
        kw = {}
        reads = [in_]
        if bias is not None:
            kw["bias"] = bias
            if not isinstance(bias, (int, float)):
                reads.append(bias)
        if scale is not None:
            kw["scale"] = scale
            if not isinstance(scale, (int, float)):
                reads.append(scale)
        self.S.add(eng, lambda e: e.activation(out=out, in_=in_, func=func, **kw), reads, [out])

    def tt(self, out, a, b, op, eng="dve"):
        self.S.add(eng, lambda e: e.tensor_tensor(out=out, in0=a, in1=b, op=op), [a, b], [out])

    def ts(self, out, a, s1, op0, s2=None, op1=None, eng="dve"):
        reads = [a]
        if not isinstance(s1, (int, float)):
            reads.append(s1)
        if s2 is not None and not isinstance(s2, (int, float)):
            reads.append(s2)
        if s2 is None:
            self.S.add(eng, lambda e: e.tensor_scalar(out=out, in0=a, scalar1=s1, scalar2=None, op0=op0), reads, [out])
        else:
            self.S.add(eng, lambda e: e.tensor_scalar(out=out, in0=a, scalar1=s1, scalar2=s2, op0=op0, op1=op1), reads, [out])

    def stt(self, out, a, s, b, op0, op1):
        reads = [a, b]
        if not isinstance(s, (int, float)):
            reads.append(s)
        self.S.add("dve", lambda e: e.scalar_tensor_tensor(out=out, in0=a, scalar=s, in1=b, op0=op0, op1=op1), reads, [out])

    def cp(self, out, in_, eng="dve"):
        if eng == "act":
            fn_ = AF.Identity if _os.environ.get('KVAR3', '') == 'ident' else AF.Copy
            self.S.add("act", lambda e: e.activation(out=out, in_=in_, func=fn_), [in_], [out])
        else:
            self.S.add(eng, lambda e: e.tensor_copy(out=out, in_=in_), [in_], [out])

    def ms(self, ap, val, eng="dve"):
        self.S.add(eng, lambda e: e.memset(ap, val), [], [ap])

    def scan(self, out, d0, d1, init, op0, op1):
        reads = [d0, d1]
        self.S.add("dve", lambda e: e.tensor_tensor_scan(out=out, data0=d0, data1=d1, initial=init, op0=op0, op1=op1), reads, [out])

    def recip(self, out, in_):
        self.S.add("dve", lambda e: e.reciprocal(out=out, in_=in_), [in_], [out])

    def dma(self, q, out, in_, is_out=False):
        idx = self.S.add(q, lambda e: e.dma_start(out=out, in_=in_), [in_], [out], dmaq=q)
        if is_out:
            self.S.out_dmas.append(idx)
        return idx

    def dbg(self, name, ap):
        shape = [int(s) for s in ap.shape]
        o = self.dram_out("dbg_" + name, shape)
        self.dma("sp", o, ap, is_out=True)
        self.dbg_outs.append(("dbg_" + name, shape))


class Prog:
    def __init__(self, debug=None):
        self.debug = debug or set()
        self.nc = bass.Bass("TRN2", target_bir_lowering=False)
        self.es = ExitStack()
        self.k = KB(self.nc, self.es)
        self.wcnt = 0
        self.pscnt = 0

    def alloc(self):
        k = self.k
        self.d_xT = k.dram_in("xT", [D, NTOK])
        self.d_cT = k.dram_in("cT", [D, 17])
        self.d_wi = [k.dram_in("wi1", [DEPTH, 2 * NJ, 128, 1024]), k.dram_in("wi2", [DEPTH, 2 * NJ, 128, 1024])]
        self.d_wo = [k.dram_in("wo1", [DEPTH, DFF, D]), k.dram_in("wo2", [DEPTH, DFF, D])]
        self.d_win = k.dram_in("win", [DEPTH, NWT, 128, 1024])
        self.d_wout = k.dram_in("wout", [DEPTH, 8, 128, 1024])
        self.d_adaw = k.dram_in("adaw", [DEPTH, 72, 128, 1024])
        self.d_pvec = k.dram_in("pvec", [DEPTH, 128, NPV])
        self.d_wg = k.dram_in("wgpad", [DEPTH, 16, 256])
        self.d_cos = k.dram_in("cosT", [128, NTOK])
        self.d_sin = k.dram_in("sinT", [128, NTOK])
        self.d_cst = k.dram_in("cst", [128, CST_N])
        self.d_gret = k.dram_in("gret", [2, 128, 2, MB])
        self.d_st = {}
        for nm in ("ret", "gla", "hg", "gdn"):
            self.d_st[nm] = k.dram_in("st_" + nm, [DEPTH, 128, 2, NSQ, 64])
        self.d_stconv = k.dram_in("st_conv", [DEPTH, 128, 6, NSQ, 3])
        self.o_yT = k.dram_out("yT", [D, NTOK])
        self.o_ps = {}
        self.o_ss = {}
        for nm in ("ret", "gla", "hg", "gdn"):
            self.o_ps[nm] = k.dram_out("ops_" + nm, [DEPTH, 128, 2, 64])
            self.o_ss[nm] = k.dram_out("oss_" + nm, [DEPTH, 128, 2, NSQ, 64])
        self.o_pconv = k.dram_out("opconv", [DEPTH, 128, 6, 3])
        self.o_sconv = k.dram_out("osconv", [DEPTH, 128, 6, NSQ, 3])
        self.xres = k.sb("xres", [128, KC, NTOK], F32)
        self.hT = k.sb("hT", [128, KC, NTOK], BF16)
        self.gbig = k.sb("gbig", [128, 8 * NTOK], BF16)
        self.wpool = [k.sb("w%d" % i, [128, 1024], BF16) for i in range(NW)]
        self.wmod = [k.sb("wm%d" % i, [128, 1024], BF16) for i in range(2)]
        self.wmodc = 0
        self.cst = k.sb("cst", [128, CST_N], F32)
        self.cstb = k.sb("cstb", [128, CSTB_N], BF16)
        self.pvec = [k.sb("pvec%d" % l, [128, NPV], F32) for l in range(DEPTH)]
        self.cTs = k.sb("cTs", [128, KC, 17], F32)
        self.csl = k.sb("csl", [128, KC, 17], BF16)
        self.modg = k.sb("modg", [128, 24, 17], F32)
        self.cF = [[k.sb("cF%d%d" % (l, i), [128, KC, 17], F32) for i in range(3)] for l in range(DEPTH)]
        self.hs = [[k.sb("hs%d%d" % (l, i), [128, KC, 17], F32) for i in range(3)] for l in range(DEPTH)]
        self.hb = [[k.sb("hb%d%d" % (l, i), [128, KC, 17], F32) for i in range(3)] for l in range(DEPTH)]
        self.xs = [k.sb("xs%d" % l, [128, 3, KC], F32) for l in range(DEPTH)]
        self.xb = [k.sb("xb%d" % l, [128, 3, KC], F32) for l in range(DEPTH)]
        self.wg = k.sb("wg", [16, 256], F32)
        self.wgb = k.sb("wgb", [16, 256], BF16)
        self.small = k.sb("small", [128, 64], F32)
        self.arena2 = k.sb("arena2", [128, A2_BYTES // 4], F32)
        self.psb = [k.psum("ps%d" % i, [128, 512], F32) for i in range(8)]

    def carve(self, region, off, nbytes, dt, parts=128):
        if region == "g":
            base = self.gbig
            es = 2
            tot = 8 * NTOK * 2
        else:
            base = self.arena2
            es = 4
            tot = A2_BYTES
        assert off % 4 == 0 and nbytes % 4 == 0 and off + nbytes <= tot, (region, off, nbytes, tot)
        ap = base[0:parts, off // es:(off + nbytes) // es]
        if dt == F32 and es == 2:
            ap = ap.bitcast(F32)
        elif dt == BF16 and es == 4:
            ap = ap.bitcast(BF16)
        return ap

    def ps(self):
        p = self.psb[self.pscnt % getattr(self, "psn", 6)]
        self.pscnt += 1
        return p

    def wload(self, dram_ap):
        w = self.wpool[self.wcnt % NW]
        self.wcnt += 1
        self.k.dma("pool", w[:, :], dram_ap)
        return w

    def pv(self, l, col, n=1):
        return self.pvec[l][:, col:col + n]


CST_IDENT = 0
CST_MINCL = 128
CST_MSTR = 192
CST_MSTRT = 256
CST_ID2 = 320
CST_RESET = 384
CST_SEL = 384 + MB
CST_SELF = CST_SEL + 256
CST_ONES = CST_SELF + 256
CST_BONES = CST_ONES + 128
CST_PM = CST_BONES + 128
CST_SELHP = CST_PM + 128
CST_N = CST_SELHP + 4
CB_IDENT, CB_ONES, CB_BONES, CB_PM, CSTB_N = 0, 128, 256, 384, 512
A2_BYTES = 24 * 1024


def _bc(ap, shape):
    return ap.to_broadcast(list(shape))


class Prog2(Prog):
    def setup(self):
        k = self.k
        k.dma("sp", self.cst[:, :], self.d_cst)
        for l in range(DEPTH):
            k.dma("sp", self.pvec[l][:, :], self.d_pvec[l])
        k.dma("sp", self.cTs[:, :, :], self.d_cT.rearrange("(kc p) b -> p kc b", p=128))
        k.dma("sp", self.xres[:, :, :], self.d_xT.rearrange("(kc p) t -> p kc t", p=128))
        for (src, dst) in ((CST_IDENT, CB_IDENT), (CST_ONES, CB_ONES), (CST_BONES, CB_BONES), (CST_PM, CB_PM)):
            k.cp(self.cstb[:, dst:dst + 128], self.cst[:, src:src + 128])
        k.ms(self.small[:, 8:9], 1.0)
        k.ms(self.small[:, 9:10], RMS_EPS)
        k.ms(self.small[:, 10:11], 0.0)
        self.c_one = self.small[:, 8:9]
        self.c_eps = self.small[:, 9:10]
        k.act(self.csl[:, :, :], self.cTs[:, :, :], AF.Silu)
        for l in range(DEPTH):
            last = (l == DEPTH - 1)
            for i in range(3):
                a = 1.0 if (last and i == 2) else ALPHA
                k.ts(self.xs[l][:, i, :], self.pv(l, PV_LNG + i * 8, 8), a, ALU.mult)
                k.ts(self.xb[l][:, i, :], self.pv(l, PV_LNB + i * 8, 8), a, ALU.mult)

    def ident_b(self):
        return self.cstb[:, CB_IDENT:CB_IDENT + 128]

    def mod_group(self, l, i):
        for _ in self.mod_group_gen(l, i):
            pass

    def mod_group_gen(self, l, i, private=False):
        k = self.k
        for f in range(24):
            ft = i * 24 + f
            if private:
                w = self.wmod[self.wmodc % 2]
                self.wmodc += 1
                k.dma("pool", w[:, :], self.d_adaw[l, ft])
            else:
                w = self.wload(self.d_adaw[l, ft])
            p = self.ps()
            for kc in range(KC):
                k.mm(p[:, 0:17], w[:, kc * 128:(kc + 1) * 128], self.csl[:, kc, :], start=(kc == 0), stop=(kc == KC - 1))
            k.ts(self.modg[:, f, :], p[:, 0:17], self.pv(l, PV_ADAB + ft), ALU.add)
            yield
        sh = self.modg[:, 0:8, :]
        sc = self.modg[:, 8:16, :]
        gt = self.modg[:, 16:24, :]
        coef = 1.0 if i == 1 else 0.5
        k.ts(self.cF[l][i][:, :, :], gt, 1.0, ALU.add, coef, ALU.mult)
        if i == 0 and l == 0:
            k.ts(self.hs[l][i][:, :, :], sc, 1.0, ALU.add)
            k.cp(self.hb[l][i][:, :, :], sh)
        else:
            pl, pi = (l, i - 1) if i > 0 else (l - 1, 2)
            gp = _bc(self.pv(pl, PV_LNG + pi * 8, 8).unsqueeze(2), [128, 8, 17])
            bp = _bc(self.pv(pl, PV_LNB + pi * 8, 8).unsqueeze(2), [128, 8, 17])
            k.ts(self.hs[l][i][:, :, :], sc, 1.0, ALU.add)
            k.tt(self.hb[l][i][:, :, :], self.hs[l][i][:, :, :], bp, ALU.mult)
            k.tt(self.hb[l][i][:, :, :], self.hb[l][i][:, :, :], sh, ALU.add)
            k.tt(self.hs[l][i][:, :, :], self.hs[l][i][:, :, :], gp, ALU.mult)

    def affine_h(self, out, in_, sv, bv, kc, tbi, tmp):
        k = self.k
        if tbi < 4:
            k.act(out, in_, AF.Identity, bias=bv[:, kc, 0:1], scale=sv[:, kc, 0:1])
        else:
            s3 = _bc(sv[:, kc, 1:17].unsqueeze(2), [128, NSQ, TS])
            b3 = _bc(bv[:, kc, 1:17].unsqueeze(2), [128, NSQ, TS])
            i3 = in_.rearrange("p (s j) -> p s j", j=TS)
            o3 = out.rearrange("p (s j) -> p s j", j=TS)
            t3 = tmp.rearrange("p (s j) -> p s j", j=TS)
            k.tt(t3, i3, s3, ALU.mult)
            k.tt(o3, t3, b3, ALU.add)

    def initial_h(self):
        k = self.k
        tmpS = self.carve("a", 18432, 256, F32)
        for tbi, (t0, n) in enumerate(TBS):
            for kc in range(KC):
                self.affine_h(self.hT[:, kc, t0:t0 + n], self.xres[:, kc, t0:t0 + n], self.hs[0][0], self.hb[0][0], kc, tbi, tmpS[:, 0:64])
        for kc in range(KC):
            k.ts(self.xres[:, kc, :], self.xres[:, kc, :], ALPHA, ALU.mult)

    def resid_acc(self, ps_ap, l, i, kc, tbi, t0, n, s0=0, ns=NSQ):
        k = self.k
        xr = self.xres[:, kc, t0:t0 + n]
        cf = self.cF[l][i]
        if tbi < 4:
            k.stt(xr, ps_ap, cf[:, kc, 0:1], xr, ALU.mult, ALU.add)
        else:
            tmpS = self.carve("a", 18432, 256, F32)[:, 0:n]
            c3 = _bc(cf[:, kc, 1 + s0:1 + s0 + ns].unsqueeze(2), [128, ns, TS])
            k.tt(tmpS.rearrange("p (s j) -> p s j", j=TS), ps_ap.rearrange("p (s j) -> p s j", j=TS), c3, ALU.mult)
            k.tt(xr, xr, tmpS, ALU.add)

    def stats_acc(self, ps1, ps2, kc, t0, n):
        k = self.k
        rb = self.carve("a", 4096 + (kc % 2) * 1024, 1024, BF16)[:, 0:n]
        rsq = self.carve("a", 6144 + (kc % 2) * 1024, 1024, BF16)[:, 0:n]
        xr = self.xres[:, kc, t0:t0 + n]
        k.cp(rb, xr, eng="act")
        k.act(rsq, xr, AF.Square)
        ones = self.cstb[:, CB_ONES:CB_ONES + 128]
        k.mm(ps1[:, 0:n], ones, rb, start=(kc == 0), stop=(kc == KC - 1))
        k.mm(ps2[:, 0:n], ones, rsq, start=(kc == 0), stop=(kc == KC - 1))

    def ln_block(self, ps1, ps2, l, i, tbi, t0, n):
        k = self.k
        mean = self.carve("a", 8192, 2048, F32)[:, 0:n]
        rstd = self.carve("a", 10240, 2048, F32)[:, 0:n]
        nmr = self.carve("a", 12288, 2048, F32)[:, 0:n]
        ex2 = self.carve("a", 18688, 2048, F32)[:, 0:n]
        tmpS = self.carve("a", 18432, 256, F32)
        k.ts(mean, ps1[:, 0:n], 1.0 / D, ALU.mult)
        k.tt(ex2, mean, mean, ALU.mult)
        k.stt(ex2, ps2[:, 0:n], 1.0 / D, ex2, ALU.mult, ALU.subtract)
        k.ts(ex2, ex2, 0.0, ALU.max, LN_EPS, ALU.add)
        k.act(ex2, ex2, AF.Sqrt)
        k.recip(rstd, ex2)
        k.stt(nmr, mean, -1.0, rstd, ALU.mult, ALU.mult)
        last = (l == DEPTH - 1 and i == 2)
        if i < 2:
            nl, ni = l, i + 1
        else:
            nl, ni = l + 1, 0
        for kc in range(KC):
            xn = self.carve("a", 14336 + (kc % 2) * 2048, 2048, F32)[:, 0:n]
            xr = self.xres[:, kc, t0:t0 + n]
            k.tt(xn, xr, rstd, ALU.mult)
            k.tt(xn, xn, nmr, ALU.add)
            k.act(xr, xn, AF.Identity, bias=self.xb[l][:, i, kc:kc + 1], scale=self.xs[l][:, i, kc:kc + 1])
            if not last:
                self.affine_h(self.hT[:, kc, t0:t0 + n], xn, self.hs[nl][ni], self.hb[nl][ni], kc, tbi, tmpS[:, 0:64])

    def ffn(self, l, which, mid_hook=None):
        k = self.k
        i = 0 if which == 0 else 2
        d_wi = self.d_wi[which]
        d_wo = self.d_wo[which]
        gb = self.gbig
        mid_gen = mid_hook() if mid_hook is not None else None
        for gi, js in enumerate(JG):
            for jj, j in enumerate(js):
                wa = self.wload(d_wi[l, j])
                wb = self.wload(d_wi[l, NJ + j])
                for tbi, (t0, n) in enumerate(TBS):
                    pa = self.ps()
                    pb = self.ps()
                    for kc in range(KC):
                        k.mm(pa[:, 0:n], wa[:, kc * 128:(kc + 1) * 128], self.hT[:, kc, t0:t0 + n], start=(kc == 0), stop=(kc == KC - 1))
                    for kc in range(KC):
                        k.mm(pb[:, 0:n], wb[:, kc * 128:(kc + 1) * 128], self.hT[:, kc, t0:t0 + n], start=(kc == 0), stop=(kc == KC - 1))
                    sA = self.carve("a", ((jj * 5 + tbi) % 2) * 2048, 2048, F32)[:, 0:n]
                    k.act(sA, pa[:, 0:n], AF.Silu)
                    k.tt(gb[:, jj * NTOK + t0: jj * NTOK + t0 + n], sA, pb[:, 0:n], ALU.mult)
                if gi == 0 and mid_gen is not None:
                    for _ in range(3):
                        next(mid_gen, None)
            if gi == 0 and mid_gen is not None:
                for _ in mid_gen:
                    pass
            wos = [self.wload(d_wo[l, j * 128:(j + 1) * 128, :]) for j in js]
            final = (gi == len(JG) - 1)
            for tbi, (t0, n) in enumerate(TBS):
                if final:
                    ps1 = self.psb[6]
                    ps2 = self.psb[7]
                for oc in range(KC):
                    po = self.ps()
                    for jj in range(len(js)):
                        k.mm(po[:, 0:n], wos[jj][:, oc * 128:(oc + 1) * 128], gb[:, jj * NTOK + t0: jj * NTOK + t0 + n],
                             start=(jj == 0), stop=(jj == len(js) - 1))
                    self.resid_acc(po[:, 0:n], l, i, oc, tbi, t0, n)
                    if final:
                        self.stats_acc(ps1, ps2, oc, t0, n)
                if final:
                    self.ln_block(ps1, ps2, l, i, tbi, t0, n)

    def ln_pass(self, l, i):
        for tbi, (t0, n) in enumerate(TBS):
            ps1 = self.psb[6]
            ps2 = self.psb[7]
            for kc in range(KC):
                self.stats_acc(ps1, ps2, kc, t0, n)
            self.ln_block(ps1, ps2, l, i, tbi, t0, n)

    def store_y(self):
        self.k.dma("sp", self.o_yT.rearrange("(kc p) t -> p kc t", p=128), self.xres[:, :, :], is_out=True)


class Prog3(Prog2):
    def blkinfo(self, blk):
        if blk < NPB:
            return False, blk * MB, MB, 0
        sb_ = blk - NPB
        return True, TP + sb_ * NCH * TS, NCH * TS, sb_ * NCH

    def cv(self, ap, is_s, N):
        a = ap[:, 0:N]
        if is_s:
            a = a.rearrange("p (s j) -> p s j", j=TS)
        return a

    def pw(self, ap, is_s):
        if is_s:
            return ap.rearrange("p (s t) -> p s t", t=64)[:, :, 0:TS]
        return ap

    def proj(self, w, t0, N, M=128):
        k = self.k
        p = self.ps()
        for kc in range(KC):
            k.mm(p[0:M, 0:N], w[:, kc * 128:kc * 128 + M], self.hT[:, kc, t0:t0 + N], start=(kc == 0), stop=(kc == KC - 1))
        return p

    def layer_small(self, l):
        k = self.k
        sm = self.small
        k.ts(sm[:, 0:2], self.pv(l, PV_BG, 2), -1.0, ALU.mult)
        if l == 0:
            k.ms(sm[:, 2:4], 0.0)
        else:
            k.tt(sm[:, 2:4], self.pv(l, PV_LB1, 2), self.pv(l, PV_LB0, 2), ALU.subtract)
            k.act(sm[:, 2:4], sm[:, 2:4], AF.Exp, scale=-1.0)
            k.ts(sm[:, 2:4], sm[:, 2:4], 1.0, ALU.add)
            k.recip(sm[:, 2:4], sm[:, 2:4])
        k.ts(sm[:, 4:6], sm[:, 2:4], -1.0, ALU.mult, 1.0, ALU.add)
        k.ts(sm[:, 6:8], sm[:, 4:6], -1.0, ALU.mult)
        k.act(sm[0:4, 11:12], self.pv(l, PV_ALOG)[0:4, :], AF.Exp)
        k.ts(sm[0:4, 11:12], sm[0:4, 11:12], -1.0, ALU.mult)
        k.dma("sp", self.wg[:, :], self.d_wg[l])
        k.cp(self.wgb[:, :], self.wg[:, :])

    def mix_layer(self, l, mid_hook=None):
        k = self.k
        self.layer_small(l)
        mixers = [("ret", ["rq0", "rq1", "rk0", "rk1", "rv0", "rv1", "rg0", "rg1"]),
                  ("gla", ["aq0", "aq1", "ak0", "ak1", "av0", "av1", "ag0", "ag1", "alr"]),
                  ("hg", ["hq0", "hq1", "hf0", "hf1", "hi0", "hi1", "hg0", "hg1"]),
                  ("gdn", ["dq0", "dq1", "dk0", "dk1", "dv0", "dv1", "dg0", "dg1", "db", "da"])]
        mid_gen = mid_hook() if mid_hook is not None else None
        self.psn = 8
        for m, (name, wn) in enumerate(mixers):
            if name in self.skip_mixers:
                continue
            W = {n: self.wload(self.d_win[l, WT[n]]) for n in wn}
            Wo = [self.wload(self.d_wout[l, 2 * m + tl]) for tl in range(2)]
            blks = list(range(NPB + NSB) if self.nblk is None else self.nblk)
            if name == "gdn":
                for blk in blks:
                    self.mix_gdn(l, m, blk, W, Wo)
            else:
                def run_to_prep(g_):
                    for v_ in g_:
                        if v_ == "PREP_DONE":
                            return
                cur = self.mix_gla(l, m, name, blks[0], W, Wo, 0)
                run_to_prep(cur)
                for bi in range(len(blks)):
                    if mid_gen is not None:
                        for _ in range(2):
                            next(mid_gen, None)
                    nxt = self.mix_gla(l, m, name, blks[bi + 1], W, Wo, (bi + 1) % 2) if bi + 1 < len(blks) else None
                    cur_live, nxt_live = True, nxt is not None
                    while cur_live or nxt_live:
                        if cur_live:
                            try:
                                next(cur)
                            except StopIteration:
                                cur_live = False
                        if nxt_live:
                            try:
                                if next(nxt) == "PREP_DONE":
                                    nxt_live = False
                            except StopIteration:
                                nxt_live = False
                    cur = nxt
            if mid_gen is not None:
                for _ in mid_gen:
                    pass
                mid_gen = None
        if mid_gen is not None:
            for _ in mid_gen:
                pass
        self.psn = 6
        self.ln_pass(l, 1)

    def chk(self):
        self.chkc = getattr(self, "chkc", 0) + 1
        return self.chkc == CUTN

    def gbuf(self, off, nbytes, dt, parts=128):
        return self.carve("g", off, nbytes, dt, parts)

    def mix_gla(self, l, m, name, blk, W, Wo, st=0):
        k = self.k
        is_s, t0, N, s0 = self.blkinfo(blk)
        cv = lambda ap: self.cv(ap, is_s, N)
        pw = lambda ap: self.pw(ap, is_s)
        r3 = lambda ap: ap.rearrange("p (a t) -> p a t", a=2)
        Qp = r3(self.gbuf(0, 1024, BF16))
        Kp = r3(self.gbuf(1024, 1024, BF16))
        if st == 0:
            V = r3(self.gbuf(2048, 1024, BF16))
            GATE = r3(self.gbuf(31488, 2048, F32))
            QT = r3(self.gbuf(12288, 1024, BF16))
            KT = r3(self.gbuf(13312, 1024, BF16))
            QG = r3(self.gbuf(14336, 1024, BF16))
            av = self.gbuf(31232, 32, F32).rearrange("p (a c) -> p a c", a=2)
            bv = self.gbuf(31264, 32, F32).rearrange("p (a c) -> p a c", a=2)
        else:
            QT = r3(self.carve("a", 4096, 1024, BF16))
            KT = r3(self.carve("a", 5120, 1024, BF16))
            QG = r3(self.carve("a", 6144, 1024, BF16))
            V = r3(self.carve("a", 7168, 1024, BF16))
            GATE = r3(self.carve("a", 8192, 2048, F32))
            av = self.carve("a", 10240, 32, F32).rearrange("p (a c) -> p a c", a=2)
            bv = self.carve("a", 10272, 32, F32).rearrange("p (a c) -> p a c", a=2)
        T1r = self.carve("a", 12288, 2048, F32)
        T2r = self.carve("a", 14336, 2048, F32)
        Gs = r3(self.gbuf(4096, 2048, F32))
        T1 = self.gbuf(6144, 2048, F32)
        T2 = self.gbuf(8192, 2048, F32)
        G0 = r3(self.gbuf(10240, 2048, F32))
        KTM = self.gbuf(15360, 2048, BF16)
        VTM = self.gbuf(17408, 2048, BF16)
        ATT = self.gbuf(19456, 2048, BF16)
        UP = self.gbuf(21504, 2048, F32).rearrange("p (a c v) -> p a c v", a=2, c=NCH)
        SALL = self.gbuf(23552, 2560, F32).rearrange("p (a c v) -> p a c v", a=2, c=NCH + 1)
        SBF = self.gbuf(26112, 1024, BF16).rearrange("p (a c v) -> p a c v", a=2, c=NCH)
        SIN = self.gbuf(27136, 2048, F32).rearrange("p (a c v) -> p a c v", a=2, c=NCH)
        OSQ = self.gbuf(29184, 1024, BF16)
        OB = r3(self.gbuf(30208, 1024, BF16))
        cosb = self.carve("a", 0, 1024, F32)
        sinb = self.carve("a", 1024, 1024, F32)
        alrb = self.carve("a", 2048, 512, BF16)
        qb = self.carve("a", 2560, 512, BF16)
        t1c = T1[:, 0:MB]
        t2c = T2[:, 0:MB]
        identb = self.ident_b()
        sm = self.small

        if is_s:
            for tl in range(2):
                k.ms(Qp[:, tl, :], 0.0)
                k.ms(Kp[:, tl, :], 0.0)
                k.ms(V[:, tl, :], 0.0)
                if name != "ret":
                    k.ms(G0[:, tl, :], 0.0)

        def simple_v_gate(vn, gn, tl):
            p = self.proj(W[vn + str(tl)], t0, N)
            k.cp(pw(V[:, tl, :]), cv(p))
            p = self.proj(W[gn + str(tl)], t0, N)
            k.act(GATE[:, tl, 0:N], p[:, 0:N], AF.Silu)

        if name == "ret":
            if _os.environ.get('KVAR', '') != 'nodma':
                k.dma("sp", cosb[:, 0:N], self.d_cos[:, t0:t0 + N])
                k.dma("sp", sinb[:, 0:N], self.d_sin[:, t0:t0 + N])
                k.dma("sp", Gs, self.d_gret[1 if is_s else 0])
            pmb = self.cstb[:, CB_PM:CB_PM + 128]
            if CUT == 11:
                return
            for tl in range(2):
                for (wn_, dst, scale) in (("rq", Qp, 1.0), ("rk", Kp, 0.125)):
                    p = self.proj(W[wn_ + str(tl)], t0, N)
                    if self.chk():
                        return
                    k.cp(qb[:, 0:N], p[:, 0:N])
                    pp = self.ps()
                    k.mm(pp[:, 0:N], pmb, qb[:, 0:N])
                    if self.chk():
                        return
                    k.stt(t1c[:, 0:N], p[:, 0:N], scale, cosb[:, 0:N], ALU.mult, ALU.mult)
                    k.stt(t2c[:, 0:N], pp[:, 0:N], scale, sinb[:, 0:N], ALU.mult, ALU.mult)
                    if self.chk():
                        return
                    _v = _os.environ.get('KVAR', '')
                    if _v == 'A' and tl == 1:
                        k.tt(pw(dst[:, 0, :]), cv(t1c), cv(t2c), ALU.add)
                    elif _v == 'B' and tl == 1:
                        k.tt(pw(dst[:, tl, :]), cv(t1c), cv(t2c), ALU.add, eng="pool")
                    else:
                        k.tt(pw(dst[:, tl, :]), cv(t1c), cv(t2c), ALU.add, eng=GP)
                    if self.chk():
                        return
                yield
                simple_v_gate("rv", "rg", tl)
                yield
                if self.chk():
                    return
        elif name == "gla":
            p = self.proj(W["alr"], t0, N, M=16)
            k.cp(alrb[0:16, 0:N], p[0:16, 0:N])
            for tl in range(2):
                pg = self.ps()
                k.mm(pg[:, 0:N], self.wgb[0:16, tl * 128:(tl + 1) * 128], alrb[0:16, 0:N])
                k.act(t1c[:, 0:N], pg[:, 0:N], AF.Exp, bias=sm[:, tl:tl + 1], scale=-1.0)
                k.act(t1c[:, 0:N], t1c[:, 0:N], AF.Ln, bias=self.c_one)
                k.ts(pw(G0[:, tl, :]), cv(t1c), -1.0 / 16.0, ALU.mult)
                p = self.proj(W["aq" + str(tl)], t0, N)
                k.ts(pw(Qp[:, tl, :]), cv(p), float(32.0 ** -0.5), ALU.mult)
                p = self.proj(W["ak" + str(tl)], t0, N)
                k.cp(pw(Kp[:, tl, :]), cv(p))
                yield
                simple_v_gate("av", "ag", tl)
                yield
        else:
            for tl in range(2):
                p = self.proj(W["hq" + str(tl)], t0, N)
                k.act(t1c[:, 0:N], p[:, 0:N], AF.Silu)
                k.ts(pw(Qp[:, tl, :]), cv(t1c), 0.125, ALU.mult)
                p = self.proj(W["hf" + str(tl)], t0, N)
                k.act(t1c[:, 0:N], p[:, 0:N], AF.Exp, scale=-1.0)
                k.ts(t1c[:, 0:N], t1c[:, 0:N], 1.0, ALU.add)
                k.recip(t1c[:, 0:N], t1c[:, 0:N])
                k.act(t2c[:, 0:N], t1c[:, 0:N], AF.Ln, bias=sm[:, 2 + tl:3 + tl], scale=sm[:, 4 + tl:5 + tl])
                k.cp(pw(G0[:, tl, :]), cv(t2c))
                k.ts(pw(Kp[:, tl, :]), cv(t1c), sm[:, 6 + tl:7 + tl], ALU.mult, sm[:, 4 + tl:5 + tl], ALU.add)
                yield
                simple_v_gate("hi", "hg", tl)
                yield
        if name != "ret":
            reset = self.cst[:, CST_RESET:CST_RESET + MB]
            for tl in range(2):
                k.scan(Gs[:, tl, :], reset, G0[:, tl, :], 0.0, ALU.mult, ALU.add)

        if CUT == 1:
            return
        for tl in range(2):
            G3 = Gs[:, tl, :].rearrange("p (c t) -> p c t", t=64)
            D3 = t1c.rearrange("p (c t) -> p c t", t=64)
            k.tt(D3, G3, _bc(G3[:, :, 31:32], [128, NCH, 64]), ALU.subtract)
            k.act(t2c, t1c, AF.Exp)
            k.tt(QT[:, tl, :], Qp[:, tl, :], t2c, ALU.mult, eng=GP)
            k.act(t2c, t1c, AF.Exp, scale=-1.0)
            k.tt(KT[:, tl, :], Kp[:, tl, :], t2c, ALU.mult, eng=GP)
            k.act(t2c, Gs[:, tl, :], AF.Exp)
            k.tt(QG[:, tl, :], Qp[:, tl, :], t2c, ALU.mult, eng=GP)
            k.act(av[:, tl, :], G3[:, :, 63], AF.Exp)
            k.act(bv[:, tl, :], D3[:, :, 63], AF.Exp)
            yield
        yield "PREP_DONE"

        P2 = [slice(0, 64), slice(64, 128)]
        for (src, dstm) in ((KT, KTM), (V, VTM)):
            pT = [self.ps(), self.ps()]
            for p_ in range(2):
                pTb = pT[p_][:, :].bitcast(BF16)
                for c in range(NCH):
                    for tl in range(2):
                        k.tr(pTb[P2[p_], (c * 2 + tl) * 64:(c * 2 + tl + 1) * 64], src[P2[p_], tl, c * 64:(c + 1) * 64],
                             identb[P2[p_], 64 * p_:64 * p_ + 64])
                k.cp(dstm[P2[p_], 0:NCH * 128], pTb[P2[p_], 0:NCH * 128])
            yield

        mincl2 = self.cst[:, CST_MINCL:CST_MINCL + 64]
        pA = [self.ps(), self.ps()]
        for p_ in range(2):
            for c in range(NCH):
                for tl in range(2):
                    sl = slice((c * 2 + tl) * 64, (c * 2 + tl + 1) * 64)
                    k.mm(pA[p_][P2[p_], sl], KT[P2[p_], tl, c * 64:(c + 1) * 64], QT[P2[p_], tl, c * 64:(c + 1) * 64])
            k.tt(ATT[P2[p_], 0:NCH * 128].rearrange("p (a t) -> p a t", t=64),
                 pA[p_][P2[p_], 0:NCH * 128].rearrange("p (a t) -> p a t", t=64),
                 _bc(mincl2[P2[p_], :].unsqueeze(1), [64, NCH * 2, 64]), ALU.mult)

        yield
        pU = [self.ps(), self.ps()]
        for p_ in range(2):
            for c in range(NCH):
                for tl in range(2):
                    sl = slice((c * 2 + tl) * 64, (c * 2 + tl + 1) * 64)
                    k.mm(pU[p_][P2[p_], (tl * NCH + c) * 64:(tl * NCH + c + 1) * 64], KTM[P2[p_], sl], VTM[P2[p_], sl])
            k.tt(UP.rearrange("p a c v -> p (a c) v")[P2[p_]], pU[p_][P2[p_], :].rearrange("p (a v) -> p a v", v=64),
                 _bc(bv.rearrange("p a c -> p (a c)")[P2[p_]].unsqueeze(2), [64, 2 * NCH, 64]), ALU.mult)

        yield
        if not is_s:
            if blk == 0:
                k.ms(SALL[:, :, 0, :], 0.0)
            else:
                k.cp(SALL[:, :, 0, :], SALL[:, :, NCH, :])
            SBv = SALL
        else:
            k.dma("sp", SIN, self.d_st[name][l][:, :, s0:s0 + NCH, :])
            SBv = SIN
        for c in range(NCH):
            for tl in range(2):
                src = SALL[:, tl, c, :] if not is_s else SIN[:, tl, c, :]
                k.stt(SALL[:, tl, c + 1, :], src, av[:, tl, c:c + 1], UP[:, tl, c, :], ALU.mult, ALU.add)
        if is_s:
            k.dma("sp", self.o_ss[name][l][:, :, s0:s0 + NCH, :], SALL[:, :, 1:NCH + 1, :], is_out=True)
        elif blk == NPB - 1:
            k.dma("sp", self.o_ps[name][l], SALL[:, :, NCH, :], is_out=True)
        k.cp(SBF, SBv[:, :, 0:NCH, :], eng=GP)

        yield
        pO = [self.ps(), self.ps()]
        for p_ in range(2):
            for c in range(NCH):
                for tl in range(2):
                    sl = slice((c * 2 + tl) * 64, (c * 2 + tl + 1) * 64)
                    o_ap = pO[p_][P2[p_], (tl * NCH + c) * 64:(tl * NCH + c + 1) * 64]
                    k.mm(o_ap, VTM[P2[p_], sl], ATT[P2[p_], sl], start=True, stop=False)
                    k.mm(o_ap, SBF[P2[p_], tl, c, :], QG[P2[p_], tl, c * 64:(c + 1) * 64], start=False, stop=True)
        for p_ in range(2):
            k.cp(T2r[P2[p_], 0:2 * MB], pO[p_][P2[p_], 0:2 * MB], eng="act")
        yield
        normw = None if name == "ret" else self.pv(l, PV_NORM + {"gla": 0, "hg": 1}[name])
        self.o_post(l, m, T2r, GATE, OB, OSQ, T1r, T2r, normw, Wo, is_s, t0, N, s0)

    def o_post(self, l, m, pO, GATE, OB, OSQ, T1, T2, normw, Wo, is_s, t0, N, s0):
        k = self.k
        cv = lambda ap: self.cv(ap, is_s, N)
        pw = lambda ap: self.pw(ap, is_s)
        bones = self.cstb[:, CB_BONES:CB_BONES + 128]
        if str(pO.space) == "PSUM":
            k.cp(T2[:, 0:2 * MB], pO[:, 0:2 * MB], eng="act")
            pO = T2
        k.act(OSQ[:, 0:2 * MB], pO[:, 0:2 * MB], AF.Square)
        pS = self.ps()
        for tl in range(2):
            k.mm(pS[:, tl * MB:(tl + 1) * MB], bones, OSQ[:, tl * MB:(tl + 1) * MB])
        k.act(T1[:, 0:2 * MB], pS[:, 0:2 * MB], AF.Sqrt, bias=self.c_eps, scale=1.0 / 64.0)
        k.recip(T1[:, 0:2 * MB], T1[:, 0:2 * MB])
        k.tt(T2[:, 0:2 * MB], pO[:, 0:2 * MB], T1[:, 0:2 * MB], ALU.mult)
        for tl in range(2):
            src = pw(T2[:, tl * MB:(tl + 1) * MB])
            if normw is None:
                k.tt(cv(OB[:, tl, :]), src, cv(GATE[:, tl, :]), ALU.mult)
            else:
                k.stt(cv(OB[:, tl, :]), src, normw, cv(GATE[:, tl, :]), ALU.mult, ALU.mult)
        for oc in range(KC):
            po = self.ps()
            for tl in range(2):
                k.mm(po[:, 0:N], Wo[tl][:, oc * 128:(oc + 1) * 128], OB[:, tl, 0:N], start=(tl == 0), stop=(tl == 1))
            self.resid_acc(po[:, 0:N], l, 1, oc, 4 if is_s else 0, t0, N, s0, NCH)


class Prog4(Prog3):
    def mix_gdn(self, l, m, blk, W, Wo):
        k = self.k
        is_s, t0, N, s0 = self.blkinfo(blk)
        cv = lambda ap: self.cv(ap, is_s, N)
        pw = lambda ap: self.pw(ap, is_s)
        r3 = lambda ap: ap.rearrange("p (a t) -> p a t", a=2)
        g = self.gbuf
        P2 = [slice(0, 64), slice(64, 128)]
        Qp = r3(g(0, 2048, F32))
        Kp = r3(g(2048, 2048, F32))
        QGx = r3(g(4096, 2048, F32))
        QpT = r3(g(6144, 2048, F32))
        VTM = g(8192, 2048, F32)
        KTM = g(10240, 2048, F32)
        MT = g(12288, 2048, F32).rearrange("p (a c v) -> p a c v", a=2, c=NCH)
        BC = g(14336, 2048, F32).rearrange("p (a c v) -> p a c v", a=2, c=NCH)
        SALL = g(16384, 2560, F32).rearrange("p (a c v) -> p a c v", a=2, c=NCH + 1)
        SIN = g(18944, 2048, F32).rearrange("p (a c v) -> p a c v", a=2, c=NCH)
        OB = r3(g(20992, 1024, BF16))
        OSQ = g(22016, 1024, BF16)
        GD = g(23040, 1024, F32)
        GC = g(24064, 1024, F32)
        BET = g(25088, 1024, F32)
        c3v = lambda ap: ap[:, 0:NCH * 2].rearrange("p (c l) -> p c l", l=2)
        GCOL = g(26112, 64, F32)
        BCOL = g(26176, 64, F32)
        EGC = g(26240, 64, F32)
        KHS = g(26304, 64, F32)
        BEG = g(26368, 64, F32)
        eGL = g(26432, 32, F32).rearrange("p (a c) -> p a c", a=2)
        halo = g(26496, 72, F32).rearrange("p (i r) -> p i r", r=3)
        OL = r3(g(26624, 2048, F32))
        GATE = r3(g(28672, 2048, F32))
        A = lambda i: self.carve("a", i * 1024, 1024, F32)
        UQ = r3(self.carve("a", 9216, 2048, F32))
        UK = r3(self.carve("a", 11264, 2048, F32))
        VT = r3(self.carve("a", 13312, 2048, F32))
        SQ = self.carve("a", 15360, 1024, BF16)
        ubuf = self.carve("a", 16384, 1040, F32)
        acc = self.carve("a", 17424, 1024, F32)
        T1 = self.carve("a", 18448, 2048, F32)
        T2 = self.carve("a", 20496, 2048, F32)
        sm = self.small
        identF = self.cst[:, CST_IDENT:CST_IDENT + 128]
        bones = self.cstb[:, CB_BONES:CB_BONES + 128]
        selF = self.cst[0:4, CST_SELF:CST_SELF + 256]
        selHP = self.cst[0:4, CST_SELHP:CST_SELHP + 4]
        mincl = self.cst[:, CST_MINCL:CST_MINCL + 64]
        mstr = self.cst[:, CST_MSTR:CST_MSTR + 64]
        mstrT = self.cst[:, CST_MSTRT:CST_MSTRT + 64]
        ID2 = self.cst[:, CST_ID2:CST_ID2 + 64]

        if is_s:
            for tl in range(2):
                k.ms(Qp[:, tl, :], 0.0)
                k.ms(Kp[:, tl, :], 0.0)
                k.ms(VT[:, tl, :], 0.0)
            k.ms(GD[0:4, :], 0.0)
            k.ms(BET[0:4, :], 0.0)

        names = ["dq0", "dq1", "dk0", "dk1", "dv0", "dv1"]
        for idx, wn_ in enumerate(names):
            p = self.proj(W[wn_], t0, N)
            tl = idx % 2
            cw = lambda i: self.pv(l, PV_CONV + idx * 4 + i)
            if not is_s:
                if blk == 0:
                    k.ms(ubuf[:, 0:3], 0.0)
                else:
                    k.cp(ubuf[:, 0:3], halo[:, idx, :], eng=GP)
                k.cp(ubuf[:, 3:3 + N], p[:, 0:N], eng="act")
                k.ts(acc[:, 0:N], ubuf[:, 0:N], cw(0), ALU.mult)
                for i in range(1, 4):
                    k.stt(acc[:, 0:N], ubuf[:, i:i + N], cw(i), acc[:, 0:N], ALU.mult, ALU.add)
                k.cp(halo[:, idx, :], ubuf[:, N:N + 3], eng=GP)
                if blk == NPB - 1:
                    k.dma("sp", self.o_pconv[l][:, idx, :], halo[:, idx, :], is_out=True)
            else:
                ubs = ubuf[:, 0:NCH * 7].rearrange("p (s r) -> p s r", r=7)
                k.dma("sp", ubs[:, :, 0:3], self.d_stconv[l][:, idx, s0:s0 + NCH, :])
                k.cp(ubs[:, :, 3:7], p[:, 0:N].rearrange("p (s j) -> p s j", j=TS), eng="act")
                a3 = acc[:, 0:N].rearrange("p (s j) -> p s j", j=TS)
                k.ts(a3, ubs[:, :, 0:4], cw(0), ALU.mult)
                for i in range(1, 4):
                    k.stt(a3, ubs[:, :, i:i + 4], cw(i), a3, ALU.mult, ALU.add)
                k.dma("sp", self.o_sconv[l][:, idx, s0:s0 + NCH, :], ubs[:, :, 4:7], is_out=True)
            accv = acc[:, 0:N]
            if idx < 2:
                k.act(UQ[:, tl, 0:N], accv, AF.Silu)
            elif idx < 4:
                k.act(UK[:, tl, 0:N], accv, AF.Silu)
            else:
                k.act(T1[:, 0:N], accv, AF.Silu)
                k.cp(pw(VT[:, tl, :]), cv(T1[:, 0:MB]), eng=GP)
        if CUTG == 1:
            return
        for (X, dst, scale) in ((UQ, Qp, 0.125), (UK, Kp, 1.0)):
            pS = self.ps()
            for tl in range(2):
                k.act(SQ[:, tl * MB:tl * MB + N], X[:, tl, 0:N], AF.Square)
                k.mm(pS[:, tl * MB:tl * MB + N], bones, SQ[:, tl * MB:tl * MB + N])
                k.act(T1[:, tl * MB:tl * MB + N], pS[:, tl * MB:tl * MB + N], AF.Sqrt, bias=self.c_eps, scale=1.0)
                k.recip(T1[:, tl * MB:tl * MB + N], T1[:, tl * MB:tl * MB + N])
                k.stt(pw(dst[:, tl, :]), cv(X[:, tl, :]), scale, cv(T1[:, tl * MB:(tl + 1) * MB]), ALU.mult, ALU.mult)
        if CUTG == 2:
            return
        for tl in range(2):
            p = self.proj(W["dg" + str(tl)], t0, N)
            k.act(GATE[:, tl, 0:N], p[:, 0:N], AF.Silu)
        if CUTG == 3:
            return
        p = self.proj(W["db"], t0, N, M=4)
        k.act(T2[0:4, 0:N], p[0:4, 0:N], AF.Exp, scale=-1.0)
        k.ts(T2[0:4, 0:N], T2[0:4, 0:N], 1.0, ALU.add)
        k.recip(T2[0:4, 0:N], T2[0:4, 0:N])
        k.cp(pw(BET[0:4, 0:MB]), cv(T2[0:4, 0:MB]))
        p = self.proj(W["da"], t0, N, M=4)
        k.act(T2[0:4, 0:N], p[0:4, 0:N], AF.Exp, bias=self.pv(l, PV_DTB)[0:4, :], scale=1.0)
        k.act(T2[0:4, 0:N], T2[0:4, 0:N], AF.Ln, bias=self.c_one[0:4, :])
        k.ts(pw(GD[0:4, 0:MB]), cv(T2[0:4, 0:MB]), sm[0:4, 11:12], ALU.mult)
        k.scan(GC[0:4, 0:MB], self.cst[0:4, CST_RESET:CST_RESET + MB], GD[0:4, 0:MB], 0.0, ALU.mult, ALU.add)
        if CUTG == 4:
            return
        pX = self.ps()
        for tl in range(2):
            k.mm(pX[:, tl * MB:(tl + 1) * MB], selF[0:4, tl * 128:(tl + 1) * 128], GC[0:4, 0:MB])
        k.act(T1[:, 0:2 * MB], pX[:, 0:2 * MB], AF.Exp)
        for tl in range(2):
            k.tt(QGx[:, tl, :], Qp[:, tl, :], T1[:, tl * MB:(tl + 1) * MB], ALU.mult, eng=GP)
            k.cp(eGL[:, tl, :], T1[:, tl * MB:(tl + 1) * MB].rearrange("p (c t) -> p c t", t=64)[:, :, 63], eng=GP)
        pC = [self.ps(), self.ps()]
        for p_ in range(2):
            for c in range(NCH):
                k.mm(pC[p_][P2[p_], c * 2:(c + 1) * 2], GC[0:4, c * 64:(c + 1) * 64], selHP[0:4, p_ * 2:p_ * 2 + 2])
                k.mm(pC[p_][P2[p_], 64 + c * 2:64 + (c + 1) * 2], BET[0:4, c * 64:(c + 1) * 64], selHP[0:4, p_ * 2:p_ * 2 + 2])
            k.cp(GCOL[P2[p_], 0:NCH * 2], pC[p_][P2[p_], 0:NCH * 2])
            k.cp(BCOL[P2[p_], 0:NCH * 2], pC[p_][P2[p_], 64:64 + NCH * 2])
        k.act(EGC[:, 0:NCH * 2], GCOL[:, 0:NCH * 2], AF.Exp)
        k.tt(BEG[:, 0:NCH * 2], BCOL[:, 0:NCH * 2], EGC[:, 0:NCH * 2], ALU.mult)
        pL = self.ps()
        glast = GC[0:4, 0:MB].rearrange("p (c t) -> p c t", t=64)[:, :, 63]
        for tl in range(2):
            k.mm(pL[:, tl * NCH:(tl + 1) * NCH], selF[0:4, tl * 128:(tl + 1) * 128], glast)
        k.tt(c3v(KHS), pL[:, 0:NCH * 2].rearrange("p (l c) -> p c l", l=2), c3v(GCOL), ALU.subtract)
        k.act(KHS[:, 0:NCH * 2], KHS[:, 0:NCH * 2], AF.Exp)
        if CUTG == 5:
            return
        for (src, dstm) in ((Kp, KTM), (VT, VTM)):
            pT = [self.ps(), self.ps()]
            for p_ in range(2):
                for c in range(NCH):
                    for tl in range(2):
                        k.mm(pT[p_][P2[p_], (c * 2 + tl) * 64:(c * 2 + tl + 1) * 64], src[P2[p_], tl, c * 64:(c + 1) * 64],
                             identF[P2[p_], 64 * p_:64 * p_ + 64])
                k.cp(dstm[P2[p_], 0:NCH * 128], pT[p_][P2[p_], 0:NCH * 128], eng=("act" if p_ else "dve"))
        KTM4 = KTM[:, 0:NCH * 128].rearrange("p (c l d) -> p c l d", c=NCH, l=2)
        VTM4 = VTM[:, 0:NCH * 128].rearrange("p (c l d) -> p c l d", c=NCH, l=2)
        GCOL3 = c3v(GCOL)
        BCOL3 = c3v(BCOL)
        BEG3 = c3v(BEG)
        KHS3 = c3v(KHS)
        v4 = lambda ap: ap[:, 0:256].rearrange("p (c l t) -> p c l t", c=2, l=2)
        v3 = lambda ap: ap[:, 0:256].rearrange("p (a t) -> p a t", t=64)
        bc4 = lambda ap3: _bc(ap3.unsqueeze(3), [128, 2, 2, 64])
        bm = lambda mk: _bc(mk.unsqueeze(1), [128, 4, 64])

        def evac2(dst, pp, eng0="dve", eng1="act"):
            k.cp(dst[P2[0], 0:256], pp[0][P2[0], 0:256], eng=eng0)
            k.cp(dst[P2[1], 0:256], pp[1][P2[1], 0:256], eng=eng1)

        if CUTG == 6:
            return
        def solve(sbi):
            c0 = sbi * 2
            Dm, X1, X2, LT, Nk, Ak, P, U, Wm = [A(i + 11 * sbi) for i in range(9)]
            pGr = self.ps()
            pBr = self.ps()
            for tl in range(2):
                k.mm(pGr[:, tl * 128:(tl + 1) * 128], selF[0:4, tl * 128:(tl + 1) * 128], GC[0:4, c0 * 64:c0 * 64 + 128])
                k.mm(pBr[:, tl * 128:(tl + 1) * 128], selF[0:4, tl * 128:(tl + 1) * 128], BET[0:4, c0 * 64:c0 * 64 + 128])
            gr4 = pGr[:, 0:256].rearrange("p (l c t) -> p c l t", l=2, c=2)
            br4 = pBr[:, 0:256].rearrange("p (l c t) -> p c l t", l=2, c=2)
            k.tt(v4(Dm), gr4, bc4(GCOL3[:, c0:c0 + 2, :]), ALU.subtract)
            k.ts(X1[:, 0:256], Dm[:, 0:256], 0.0, ALU.min)
            k.act(X1[:, 0:256], X1[:, 0:256], AF.Exp)
            k.ts(X2[:, 0:256], Dm[:, 0:256], -1.0, ALU.mult, 0.0, ALU.min)
            k.act(X2[:, 0:256], X2[:, 0:256], AF.Exp)
            k.tt(v3(LT), v3(X1), bm(mincl), ALU.mult, eng=GP)
            k.tt(v3(X1), v3(X1), bm(mstr), ALU.mult, eng=GP)
            k.tt(v4(X1), v4(X1), br4, ALU.mult)
            k.tt(v3(X2), v3(X2), bm(mstrT), ALU.mult, eng=GP)
            k.tt(v4(X2), v4(X2), bc4(BCOL3[:, c0:c0 + 2, :]), ALU.mult, eng=GP)
            yield
            pKK = [self.ps(), self.ps()]
            pQK = [self.ps(), self.ps()]
            for cc in range(2):
                c = c0 + cc
                for tl in range(2):
                    for p_ in range(2):
                        sl = slice((cc * 2 + tl) * 64, (cc * 2 + tl + 1) * 64)
                        kk = Kp[P2[p_], tl, c * 64:(c + 1) * 64]
                        qq = Qp[P2[p_], tl, c * 64:(c + 1) * 64]
                        k.mm(pKK[p_][P2[p_], sl], kk, kk)
                        k.mm(pQK[p_][P2[p_], sl], kk, qq)
            for p_ in range(2):
                k.tt(Nk[P2[p_], 0:256], pKK[p_][P2[p_], 0:256], X1[P2[p_], 0:256], ALU.mult)
                k.tt(Ak[P2[p_], 0:256], pKK[p_][P2[p_], 0:256], X2[P2[p_], 0:256], ALU.mult)
                k.tt(LT[P2[p_], 0:256], pQK[p_][P2[p_], 0:256], LT[P2[p_], 0:256], ALU.mult)
            k.stt(v3(P), v3(Nk), -1.0, bm(ID2), ALU.mult, ALU.add)
            yield
            for lev in range(5):
                pA_ = [self.ps(), self.ps()]
                if lev < 4:
                    pN_ = [self.ps(), self.ps()]
                for j in range(4):
                    for p_ in range(2):
                        sl = slice(j * 64, (j + 1) * 64)
                        if lev < 4:
                            k.mm(pN_[p_][P2[p_], sl], Ak[P2[p_], sl], Nk[P2[p_], sl])
                        k.mm(pA_[p_][P2[p_], sl], Nk[P2[p_], sl], Ak[P2[p_], sl])
                if lev < 4:
                    evac2(Nk, pN_, "act", "act")
                evac2(Ak, pA_, "dve", "dve")
                yield
                pP = [self.ps(), self.ps()]
                for j in range(4):
                    for p_ in range(2):
                        sl = slice(j * 64, (j + 1) * 64)
                        k.mm(pP[p_][P2[p_], sl], Ak[P2[p_], sl], P[P2[p_], sl])
                for p_ in range(2):
                    k.tt(P[P2[p_], 0:256], P[P2[p_], 0:256], pP[p_][P2[p_], 0:256], ALU.add)
                yield
            k.tt(v4(X1), VTM4[:, c0:c0 + 2, :, :], bc4(BCOL3[:, c0:c0 + 2, :]), ALU.mult, eng=GP)
            k.tt(v4(X2), KTM4[:, c0:c0 + 2, :, :], bc4(BEG3[:, c0:c0 + 2, :]), ALU.mult, eng=GP)
            pu = [self.ps(), self.ps()]
            pw_ = [self.ps(), self.ps()]
            for j in range(4):
                for p_ in range(2):
                    sl = slice(j * 64, (j + 1) * 64)
                    k.mm(pu[p_][P2[p_], sl], P[P2[p_], sl], X1[P2[p_], sl])
                    k.mm(pw_[p_][P2[p_], sl], P[P2[p_], sl], X2[P2[p_], sl])
            evac2(U, pu, "act", "act")
            evac2(Wm, pw_, "dve", "dve")
            yield
            k.tt(v4(Dm), KTM4[:, c0:c0 + 2, :, :], bc4(KHS3[:, c0:c0 + 2, :]), ALU.mult, eng=GP)
            pM = [self.ps(), self.ps()]
            pB = [self.ps(), self.ps()]
            Mraw, Braw = A(9 + 11 * sbi), A(10 + 11 * sbi)
            for j in range(4):
                for p_ in range(2):
                    sl = slice(j * 64, (j + 1) * 64)
                    k.mm(pM[p_][P2[p_], sl], Wm[P2[p_], sl], Dm[P2[p_], sl])
                    k.mm(pB[p_][P2[p_], sl], Dm[P2[p_], sl], U[P2[p_], sl])
            evac2(Mraw, pM, "act", "act")
            evac2(Braw, pB, "dve", "dve")
            yield
            for tl in range(2):
                for cc in range(2):
                    sl = slice((cc * 2 + tl) * 64, (cc * 2 + tl + 1) * 64)
                    k.stt(MT[:, tl, c0 + cc, :], ID2, eGL[:, tl, c0 + cc:c0 + cc + 1], Mraw[:, sl], ALU.mult, ALU.subtract)
                k.cp(BC[:, tl, c0:c0 + 2, :], v4(Braw)[:, :, tl, :], eng=GP)
            pQ = [self.ps(), self.ps()]
            pOL = [self.ps(), self.ps()]
            for j in range(4):
                for p_ in range(2):
                    sl = slice(j * 64, (j + 1) * 64)
                    k.mm(pQ[p_][P2[p_], sl], Wm[P2[p_], sl], LT[P2[p_], sl])
                    k.mm(pOL[p_][P2[p_], sl], U[P2[p_], sl], LT[P2[p_], sl])
            evac2(Mraw, pQ, "act", "act")
            evac2(Braw, pOL, "dve", "dve")
            yield
            for tl in range(2):
                qv = QpT[:, tl, c0 * 64:c0 * 64 + 128].rearrange("p (c t) -> p c t", t=64)
                gv = QGx[:, tl, c0 * 64:c0 * 64 + 128].rearrange("p (c t) -> p c t", t=64)
                k.tt(qv, gv, v4(Mraw)[:, :, tl, :], ALU.subtract, eng=GP)
                k.cp(OL[:, tl, c0 * 64:c0 * 64 + 128].rearrange("p (c t) -> p c t", t=64), v4(Braw)[:, :, tl, :], eng=GP)
        gens = [solve(sbi) for sbi in range(NCH // 2)]
        while gens:
            for g_ in list(gens):
                try:
                    next(g_)
                except StopIteration:
                    gens.remove(g_)
        if CUTG == 7:
            return
        if not is_s:
            if blk == 0:
                k.ms(SALL[:, :, 0, :], 0.0)
            else:
                k.cp(SALL[:, :, 0, :], SALL[:, :, NCH, :])
        else:
            k.dma("sp", SIN, self.d_st["gdn"][l][:, :, s0:s0 + NCH, :])
        SB = (lambda tl, c: SIN[:, tl, c, :]) if is_s else (lambda tl, c: SALL[:, tl, c, :])
        for c in range(NCH):
            pS = [self.ps(), self.ps()]
            for p_ in range(2):
                for tl in range(2):
                    k.mm(pS[p_][P2[p_], tl * 64:(tl + 1) * 64], MT[P2[p_], tl, c, :], SB(tl, c)[P2[p_], :])
                k.tt(SALL[P2[p_], :, c + 1, :], pS[p_][P2[p_], 0:128].rearrange("p (a v) -> p a v", v=64), BC[P2[p_], :, c, :], ALU.add)
        if is_s:
            k.dma("sp", self.o_ss["gdn"][l][:, :, s0:s0 + NCH, :], SALL[:, :, 1:NCH + 1, :], is_out=True)
        elif blk == NPB - 1:
            k.dma("sp", self.o_ps["gdn"][l], SALL[:, :, NCH, :], is_out=True)
        if CUTG == 8:
            return
        pO = [self.ps(), self.ps()]
        for p_ in range(2):
            for c in range(NCH):
                for tl in range(2):
                    k.mm(pO[p_][P2[p_], (tl * NCH + c) * 64:(tl * NCH + c + 1) * 64], SB(tl, c)[P2[p_], :], QpT[P2[p_], tl, c * 64:(c + 1) * 64])
            k.tt(T2[P2[p_], 0:2 * MB], pO[p_][P2[p_], 0:2 * MB], OL.rearrange("p a t -> p (a t)")[P2[p_]], ALU.add)
        if _os.environ.get('KDBG', '') == 'gdn' and blk == 0 and l == 0:
            for nm_, ap_ in (("GC", GC[0:4, 0:MB]), ("BET", BET[0:4, 0:MB]), ("GCOL", GCOL[:, 0:8]), ("BCOL", BCOL[:, 0:8]),
                             ("KHS", KHS[:, 0:8]), ("Qp", Qp), ("Kp", Kp), ("KTM", KTM[:, 0:512]), ("VTM", VTM[:, 0:512]),
                             ("MT", MT), ("BC", BC), ("QpT", QpT), ("OL", OL), ("SALL", SALL), ("T2o", T2[:, 0:512]), ("QGx", QGx)):
                k.dbg(nm_, ap_)
        self.o_post(l, m, T2, GATE, OB, OSQ, T1, T2, self.pv(l, PV_NORM + 2), Wo, is_s, t0, N, s0)

    def build(self, skip_mixers=(), stop_after=None, no_ffn=False, nblk=None):
        self.skip_mixers = set(skip_mixers)
        self.nblk = nblk
        self.alloc()
        self.setup()
        self.mod_group(0, 0)
        self.initial_h()
        for l in range(DEPTH):
            if no_ffn:
                self.mod_group(l, 1)
                self.ln_pass(l, 0)
            else:
                self.ffn(l, 0, mid_hook=lambda: self.mod_group_gen(l, 1))
            if stop_after == ("ffn1", l):
                break
            self.mix_layer(l, mid_hook=lambda: self.mod_group_gen(l, 2, True))
            if stop_after == ("mix", l):
                break
            hook = (lambda: self.mod_group_gen(l + 1, 0)) if l + 1 < DEPTH else None
            self.ffn(l, 1, mid_hook=hook)
        self.store_y()
        nw = self.k.S.emit()
        self.nwaits = nw
        return self.nc


def _const_pack():
    c = np.zeros((128, CST_N), np.float32)
    c[:, CST_IDENT:CST_IDENT + 128] = np.eye(128, dtype=np.float32)
    s = np.arange(64)[:, None]
    t = np.arange(64)[None, :]
    for hf in range(2):
        c[64 * hf:64 * hf + 64, CST_MINCL:CST_MINCL + 64] = (s <= t)
        c[64 * hf:64 * hf + 64, CST_MSTR:CST_MSTR + 64] = (s < t)
        c[64 * hf:64 * hf + 64, CST_MSTRT:CST_MSTRT + 64] = (t < s)
    c[:, CST_ID2:CST_ID2 + 64] = np.tile(np.eye(64, dtype=np.float32), (2, 1))
    r = np.ones((MB,), np.float32)
    r[0::64] = 0.0
    c[:, CST_RESET:CST_RESET + MB] = r[None, :]
    for h in range(4):
        c[h, CST_SEL + h * 64:CST_SEL + (h + 1) * 64] = 1.0
        tl, p = divmod(h, 2)
        c[h, CST_SELF + tl * 128 + p * 64:CST_SELF + tl * 128 + (p + 1) * 64] = 1.0
    c[:, CST_ONES:CST_ONES + 128] = 1.0
    bo = np.zeros((128, 128), np.float32)
    bo[0:64, 0:64] = 1.0
    bo[64:128, 64:128] = 1.0
    c[:, CST_BONES:CST_BONES + 128] = bo
    pm = np.zeros((128, 128), np.float32)
    for m_ in range(128):
        kk = (m_ // 64) * 64 + ((m_ % 64) + 32) % 64
        pm[kk, m_] = 1.0
    c[:, CST_PM:CST_PM + 128] = pm
    for p_ in range(2):
        for tl in range(2):
            c[2 * tl + p_, CST_SELHP + p_ * 2 + tl] = 1.0
    return c


def _rot_tables():
    half = 32
    inv = (np.float32(10000.0) ** (-np.arange(half, dtype=np.float32) / np.float32(half))).astype(np.float32)
    pos = np.concatenate([np.arange(TP, dtype=np.float32),
                          np.tile(np.float32(16384.0) + np.arange(TS, dtype=np.float32), NSQ)]).astype(np.float32)
    ang = (pos[:, None] * inv[None, :]).astype(np.float32)
    cos = np.cos(ang).astype(np.float32).T
    sin = np.sin(ang).astype(np.float32).T
    cosT = np.zeros((128, NTOK), np.float32)
    sinT = np.zeros((128, NTOK), np.float32)
    for p in range(128):
        d = p % 64
        i = d % 32
        cosT[p] = cos[i]
        sinT[p] = -sin[i] if d < 32 else sin[i]
    return cosT, sinT


def _gret_tables():
    heads = np.arange(4, dtype=np.float32)
    lg = np.log(np.float32(1.0) - np.float32(2.0) ** (np.float32(-5.0) - heads)).astype(np.float32)
    g = np.zeros((2, 128, 2, MB), np.float32)
    j = np.arange(MB) % 64
    for tl in range(2):
        for p in range(128):
            h = 2 * tl + p // 64
            g[0, p, tl, :] = (j + 1).astype(np.float32) * lg[h]
            g[1, p, tl, :] = (np.minimum(j, TS - 1) + 1).astype(np.float32) * lg[h]
    return g


def _tile_w(w, cols):
    out = np.zeros((128, KC, 128), np.float32)
    cols = np.asarray(cols)
    valid = cols >= 0
    sub = w[:, cols[valid]].reshape(KC, 128, -1)
    out[:, :, np.nonzero(valid)[0]] = np.transpose(sub, (1, 0, 2))
    return out.reshape(128, KC * 128)


def _win_tiles(w):
    tiles = []
    r = lambda a, n: list(range(a, a + n))
    pad = lambda lst: lst + [-1] * (128 - len(lst))
    for base in (C_RQ, C_RK, C_RV, C_RG):
        for tl in range(2):
            tiles.append(r(base + tl * 128, 128))
    for base in (C_AQ, C_AK):
        for tl in range(2):
            cols = []
            for p in range(2):
                h = 2 * tl + p
                cols += r(base + h * 32, 32) + [-1] * 32
            tiles.append(cols)
    for base in (C_AV, C_AG):
        for tl in range(2):
            tiles.append(r(base + tl * 128, 128))
    tiles.append(pad(r(C_ALR, 16)))
    for base in (C_HQ, C_HF, C_HI, C_HG):
        for tl in range(2):
            tiles.append(r(base + tl * 128, 128))
    for base in (C_DQ, C_DK, C_DV, C_DG):
        for tl in range(2):
            tiles.append(r(base + tl * 128, 128))
    tiles.append(pad(r(C_DB, 4)))
    tiles.append(pad(r(C_DA, 4)))
    assert len(tiles) == NWT
    return np.stack([_tile_w(w, c) for c in tiles], 0)


def _state_in(st, dk):
    out = np.zeros((DEPTH, 2, 64, 2, NSQ, 64), np.float32)
    x = st.reshape(DEPTH, NSQ, 2, 2, dk, 64)
    out[:, :, 0:dk] = np.transpose(x, (0, 3, 4, 2, 1, 5))
    return out.reshape(DEPTH, 128, 2, NSQ, 64)


def _state_out_s(o, dk):
    x = o.reshape(DEPTH, 2, 64, 2, NSQ, 64)[:, :, 0:dk]
    return np.ascontiguousarray(np.transpose(x, (0, 4, 3, 1, 2, 5)).reshape(DEPTH, NSQ, 4, dk, 64))


def _state_out_p(o, dk):
    x = o.reshape(DEPTH, 2, 64, 2, 64)[:, :, 0:dk]
    return np.ascontiguousarray(np.transpose(x, (0, 3, 1, 2, 4)).reshape(DEPTH, 4, dk, 64))


_NC_CACHE = {}


def _get_nc(key=(), **kw):
    if key not in _NC_CACHE:
        p = Prog4()
        nc = p.build(**kw)
        _NC_CACHE[key] = (nc, p)
    return _NC_CACHE[key]


def _prepare_inputs(x_prompt, x_sample, state_ret, state_gla, state_hgrn, state_gdn, state_gdn_conv,
                    c_prompt, c_sample, ada_w, ada_b, ln_g, ln_b, ffn1_wi, ffn1_wo, ffn2_wi, ffn2_wo,
                    w_in, gla_wg, gla_bg, hg_lb, gdn_conv, gdn_a_log, gdn_dt_bias,
                    gla_norm, hg_norm, gdn_norm, w_out):
    f = lambda a: np.ascontiguousarray(np.asarray(a, dtype=np.float32))
    shared = {}
    for nm, wi in (("wi1", ffn1_wi), ("wi2", ffn2_wi)):
        wi = f(wi)
        shared[nm] = np.stack([np.stack([_tile_w(wi[l], list(range(c * 128, (c + 1) * 128))) for c in range(2 * NJ)], 0)
                               for l in range(DEPTH)], 0)
    shared["wo1"] = f(ffn1_wo)
    shared["wo2"] = f(ffn2_wo)
    w_in = f(w_in)
    shared["win"] = np.stack([_win_tiles(w_in[l]) for l in range(DEPTH)], 0)
    shared["wout"] = f(w_out).reshape(DEPTH, 8, 128, 1024)
    ada_w = f(ada_w)
    shared["adaw"] = np.stack([np.stack([_tile_w(ada_w[l], list(range(c * 128, (c + 1) * 128))) for c in range(72)], 0)
                               for l in range(DEPTH)], 0)
    pv = np.zeros((DEPTH, 128, NPV), np.float32)
    ada_b = f(ada_b); ln_g = f(ln_g); ln_b = f(ln_b); gdn_conv = f(gdn_conv); gla_bg = f(gla_bg); hg_lb = f(hg_lb)
    gla_norm = f(gla_norm); hg_norm = f(hg_norm); gdn_norm = f(gdn_norm); gdn_a_log = f(gdn_a_log); gdn_dt_bias = f(gdn_dt_bias)
    for l in range(DEPTH):
        pv[l, :, PV_ADAB:PV_ADAB + 72] = ada_b[l].reshape(72, 128).T
        for i in range(3):
            pv[l, :, PV_LNG + i * 8:PV_LNG + (i + 1) * 8] = ln_g[l, i].reshape(8, 128).T
            pv[l, :, PV_LNB + i * 8:PV_LNB + (i + 1) * 8] = ln_b[l, i].reshape(8, 128).T
        cw = gdn_conv[l].reshape(4, 6, 128)
        pv[l, :, PV_CONV:PV_CONV + 24] = np.transpose(cw, (2, 1, 0)).reshape(128, 24)
        bg = np.zeros((2, 2, 64), np.float32)
        bg[:, :, 0:32] = gla_bg[l].reshape(2, 2, 32)
        pv[l, :, PV_BG:PV_BG + 2] = bg.reshape(2, 128).T
        pv[l, :, PV_NORM + 0] = np.tile(gla_norm[l], 2)
        pv[l, :, PV_NORM + 1] = np.tile(hg_norm[l], 2)
        pv[l, :, PV_NORM + 2] = np.tile(gdn_norm[l], 2)
        pv[l, :, PV_LB0:PV_LB0 + 2] = hg_lb[0].reshape(2, 128).T
        pv[l, :, PV_LB1:PV_LB1 + 2] = hg_lb[1].reshape(2, 128).T
        pv[l, 0:4, PV_ALOG] = gdn_a_log[l]
        pv[l, 0:4, PV_DTB] = gdn_dt_bias[l]
    shared["pvec"] = pv
    gla_wg = f(gla_wg)
    wgp = np.zeros((DEPTH, 16, 2, 2, 64), np.float32)
    wgp[:, :, :, :, 0:32] = gla_wg.reshape(DEPTH, 16, 2, 2, 32)
    shared["wgpad"] = wgp.reshape(DEPTH, 16, 256)
    cosT, sinT = _rot_tables()
    shared["cosT"] = cosT
    shared["sinT"] = sinT
    shared["cst"] = _const_pack()
    shared["gret"] = _gret_tables()
    x_prompt = f(x_prompt); x_sample = f(x_sample); c_prompt = f(c_prompt); c_sample = f(c_sample)
    sts = {"ret": (f(state_ret), 64), "gla": (f(state_gla), 32), "hg": (f(state_hgrn), 64), "gdn": (f(state_gdn), 64)}
    state_gdn_conv = f(state_gdn_conv)
    in_maps = []
    for c in range(NCORES):
        d = dict(shared)
        sq = slice(c * NSQ, (c + 1) * NSQ)
        xs = x_sample[sq].reshape(NSQ * TS, D)
        d["xT"] = np.ascontiguousarray(np.concatenate([x_prompt[c], xs], 0).T)
        d["cT"] = np.ascontiguousarray(np.concatenate([c_prompt[c:c + 1], c_sample[sq]], 0).T)
        for nm, (st, dk) in sts.items():
            d["st_" + nm] = _state_in(st[:, sq], dk)
        cvs = state_gdn_conv[:, sq].reshape(DEPTH, NSQ, 3, 6, 128)
        d["st_conv"] = np.ascontiguousarray(np.transpose(cvs, (0, 4, 3, 1, 2)))
        in_maps.append(d)
    return in_maps


def _assemble(results):
    y_p = np.zeros((NCORES, TP, D), np.float32)
    y_s = np.zeros((NCORES * NSQ, TS, D), np.float32)
    dks = {"ret": 64, "gla": 32, "hg": 64, "gdn": 64}
    p_st = {nm: np.zeros((DEPTH, NCORES, 4, dk, 64), np.float32) for nm, dk in dks.items()}
    s_st = {nm: np.zeros((DEPTH, NCORES * NSQ, 4, dk, 64), np.float32) for nm, dk in dks.items()}
    p_conv = np.zeros((DEPTH, NCORES, 3, 768), np.float32)
    s_conv = np.zeros((DEPTH, NCORES * NSQ, 3, 768), np.float32)
    for c, r in enumerate(results):
        yT = np.asarray(r["yT"])
        y_p[c] = yT[:, 0:TP].T
        y_s[c * NSQ:(c + 1) * NSQ] = yT[:, TP:].T.reshape(NSQ, TS, D)
        for nm, dk in dks.items():
            p_st[nm][:, c] = _state_out_p(np.asarray(r["ops_" + nm]), dk)
            s_st[nm][:, c * NSQ:(c + 1) * NSQ] = _state_out_s(np.asarray(r["oss_" + nm]), dk)
        pc = np.asarray(r["opconv"])
        p_conv[:, c] = np.transpose(pc, (0, 3, 2, 1)).reshape(DEPTH, 3, 768)
        sc = np.asarray(r["osconv"])
        s_conv[:, c * NSQ:(c + 1) * NSQ] = np.transpose(sc, (0, 3, 4, 2, 1)).reshape(DEPTH, NSQ, 3, 768)
    return (y_p, y_s, p_st["ret"], p_st["gla"], p_st["hg"], p_st["gdn"], p_conv,
            s_st["ret"], s_st["gla"], s_st["hg"], s_st["gdn"], s_conv)


def kernel(**inputs):
    in_maps = _prepare_inputs(**inputs)
    nc, _ = _get_nc()
    res = run_bass_kernel_spmd(nc, in_maps, core_ids=list(range(NCORES)))
    return _assemble(res.results)
```
